# Optimizing a Trainium2 kernel written in Bass

```python
import math
import jax
import jax.numpy as jnp
from jax import lax
import numpy as np

D_MODEL = 1024
BATCH = 8
SEQ = 8192
DEPTH = 1

N_NSA_HEADS = 8
NSA_KV_GROUPS = 2
NSA_HEAD_DIM = 64
CMP_LEN = 32
CMP_STRIDE = 16
CMP_HIDDEN = 256
SLC_BLOCK = 64
SLC_TOP_N = 16
WINDOW = 512
N_MLA_HEADS = 8
MLA_NOPE_DIM = 64
MLA_ROPE_DIM = 32
MLA_V_DIM = 64
MLA_Q_LORA = 256
MLA_KV_LORA = 128
ROPE_THETA = 10000.0
MIX_WIDTH = N_NSA_HEADS * NSA_HEAD_DIM + N_MLA_HEADS * MLA_V_DIM
D_FF = -(-8 * D_MODEL // (3 * 256)) * 256
D_PLE = 256
QBLK = 128
ALPHA = (2 * DEPTH) ** 0.25
BETA = (8 * DEPTH) ** -0.25
LN_EPS = 1e-5
RMS_EPS = 1e-6
NEG_INF = -1e30
FORCE_SCORE = 1e6
NSA_KV_W = NSA_KV_GROUPS * NSA_HEAD_DIM
IN_SPLITS = (N_NSA_HEADS * NSA_HEAD_DIM, NSA_KV_W, NSA_KV_W, NSA_KV_W, NSA_KV_W, NSA_KV_W, NSA_KV_W,
             3 * N_NSA_HEADS, MLA_Q_LORA, MLA_KV_LORA, MLA_ROPE_DIM)
D_IN = sum(IN_SPLITS)

kernel_name = 'hybrid_nsa_mla_deepnorm_block'


def _split_cols(h):
    parts, off = [], 0
    for n in IN_SPLITS:
        parts.append(h[..., off:off + n])
        off += n
    return parts


def _layer_norm(x, g, b):
    xf = x.astype(jnp.float32)
    mu = xf.mean(-1, keepdims=True)
    var = jnp.square(xf - mu).mean(-1, keepdims=True)
    return ((xf - mu) * lax.rsqrt(var + LN_EPS) * g + b).astype(x.dtype)


def _rms_norm(x, g):
    xf = x.astype(jnp.float32)
    return (xf * lax.rsqrt(jnp.square(xf).mean(-1, keepdims=True) + RMS_EPS) * g).astype(x.dtype)


def _rope(x, cos, sin):
    half = x.shape[-1] // 2
    x1 = x[..., :half].astype(jnp.float32)
    x2 = x[..., half:].astype(jnp.float32)
    return jnp.concatenate([x1 * cos - x2 * sin, x2 * cos + x1 * sin], axis=-1).astype(x.dtype)


def _masked_softmax(s, mask):
    p = jax.nn.softmax(jnp.where(mask, s, NEG_INF), axis=-1)
    return jnp.where(mask, p, 0.0)


def _alibi_slopes(n):
    return jnp.exp2(-8.0 * jnp.arange(1, n + 1, dtype=jnp.float32) / n)


def _compress(kv, pos, w1, w2):
    B, T, G, D = kv.shape
    n_sub = T // CMP_STRIDE
    r = CMP_LEN // CMP_STRIDE
    nc = n_sub - r + 1
    sub = kv.reshape(B, n_sub, CMP_STRIDE, G, D)
    blk = jnp.concatenate([sub[:, j:j + nc] for j in range(r)], axis=2) + pos[None, None, :, None, :]
    blk = blk.transpose(0, 1, 3, 2, 4).reshape(B, nc, G, CMP_LEN * D)
    return jax.nn.gelu(blk @ w1) @ w2


def _nsa(q, kc, vc, ks, vs, kw, vw, gate_logits, w_ck1, w_ck2, pos_ck, w_cv1, w_cv2, pos_cv):
    B, T = q.shape[:2]
    G, H, D = NSA_KV_GROUPS, N_NSA_HEADS // NSA_KV_GROUPS, NSA_HEAD_DIM
    f32 = jnp.float32
    q = q.reshape(B, T, G, H, D) * (D ** -0.5)
    k_cmp = _compress(kc.reshape(B, T, G, D), pos_ck, w_ck1, w_ck2)
    v_cmp = _compress(vc.reshape(B, T, G, D), pos_cv, w_cv1, w_cv2)
    nc = k_cmp.shape[1]
    r = CMP_LEN // CMP_STRIDE
    cmp_end = jnp.arange(nc) * CMP_STRIDE + (CMP_LEN - 1)
    nb = T // SLC_BLOCK
    ratio = SLC_BLOCK // CMP_STRIDE
    n_top = min(SLC_TOP_N, nb)
    k_blk = ks.reshape(B, nb, SLC_BLOCK, G, D).transpose(0, 3, 1, 2, 4)
    v_blk = vs.reshape(B, nb, SLC_BLOCK, G, D).transpose(0, 3, 1, 2, 4)
    pad = ((0, 0), (WINDOW, 0), (0, 0), (0, 0))
    k_win = jnp.pad(kw.reshape(B, T, G, D), pad)
    v_win = jnp.pad(vw.reshape(B, T, G, D), pad)
    gates = jax.nn.sigmoid(gate_logits.reshape(B, T, G, H, 3).astype(f32)).astype(q.dtype)
    slopes = _alibi_slopes(N_NSA_HEADS).reshape(G, H)
    b_ix = jnp.arange(B)[:, None, None, None]
    g_ix = jnp.arange(G)[None, None, :, None]
    jb = jnp.arange(nb)

    def block(qb):
        q0 = qb * QBLK
        t = q0 + jnp.arange(QBLK)
        qblk = lax.dynamic_slice_in_dim(q, q0, QBLK, axis=1)
        s = jnp.einsum('bqghd,bcgd->bqghc', qblk, k_cmp).astype(f32)
        dist = (t[:, None] - cmp_end[None, :]).astype(f32)
        bias = -slopes[:, :, None] * dist[:, None, None, :]
        mask = (cmp_end[None, :] <= t[:, None])[:, None, None, :]
        p_cmp = _masked_softmax(s + bias, mask)
        o_cmp = jnp.einsum('bqghc,bcgd->bqghd', p_cmp.astype(v_cmp.dtype), v_cmp)
        cpad = ((0, 0),) * 4
        p_sub = sum(jnp.pad(p_cmp, cpad + ((j, r - 1 - j),)) for j in range(r))
        imp = p_sub.sum(3).reshape(B, QBLK, G, nb, ratio).sum(-1)
        cur = t // SLC_BLOCK
        forced = (jb[None, :] == 0) | (jb[None, :] == cur[:, None]) | (jb[None, :] == cur[:, None] - 1)
        future = jb[None, :] > cur[:, None]
        imp = jnp.where(forced[None, :, None, :], FORCE_SCORE, imp)
        imp = jnp.where(future[None, :, None, :], -1.0, imp)
        _, idx = lax.top_k(imp, n_top)
        k_sel = k_blk[b_ix, g_ix, idx]
        v_sel = v_blk[b_ix, g_ix, idx]
        s_pos = idx[..., None] * SLC_BLOCK + jnp.arange(SLC_BLOCK)
        dist = t[None, :, None, None, None] - s_pos
        s = jnp.einsum('bqghd,bqgnsd->bqghns', qblk, k_sel).astype(f32)
        bias = -slopes[None, None, :, :, None, None] * dist[:, :, :, None].astype(f32)
        mask = (dist >= 0)[:, :, :, None].reshape(B, QBLK, G, 1, n_top * SLC_BLOCK)
        p = _masked_softmax((s + bias).reshape(B, QBLK, G, H, n_top * SLC_BLOCK), mask)
        p = p.reshape(B, QBLK, G, H, n_top, SLC_BLOCK).astype(v_sel.dtype)
        o_slc = jnp.einsum('bqghns,bqgnsd->bqghd', p, v_sel)
        kwb = lax.dynamic_slice_in_dim(k_win, q0, QBLK + WINDOW, axis=1)
        vwb = lax.dynamic_slice_in_dim(v_win, q0, QBLK + WINDOW, axis=1)
        w_pos = q0 - WINDOW + jnp.arange(QBLK + WINDOW)
        dist = t[:, None] - w_pos[None, :]
        mask = ((dist >= 0) & (dist < WINDOW) & (w_pos[None, :] >= 0))[:, None, None, :]
        s = jnp.einsum('bqghd,bkgd->bqghk', qblk, kwb).astype(f32)
        bias = -slopes[:, :, None] * dist[:, None, None, :].astype(f32)
        p = _masked_softmax(s + bias, mask)
        o_win = jnp.einsum('bqghk,bkgd->bqghd', p.astype(vwb.dtype), vwb)
        g = lax.dynamic_slice_in_dim(gates, q0, QBLK, axis=1)
        o = g[..., 0:1] * o_cmp + g[..., 1:2] * o_slc + g[..., 2:3] * o_win
        return o.reshape(B, QBLK, N_NSA_HEADS * D)

    out = lax.map(block, jnp.arange(T // QBLK))
    return out.transpose(1, 0, 2, 3).reshape(B, T, N_NSA_HEADS * D)


def _mla(c_q, c_kv, k_pe, g_qn, w_uq, g_kvn, w_ukv, cos, sin):
    B, T = c_q.shape[:2]
    H = N_MLA_HEADS
    f32 = jnp.float32
    q = (_rms_norm(c_q, g_qn) @ w_uq).reshape(B, T, H, MLA_NOPE_DIM + MLA_ROPE_DIM)
    scale = (MLA_NOPE_DIM + MLA_ROPE_DIM) ** -0.5
    q_nope = q[..., :MLA_NOPE_DIM] * scale
    q_pe = _rope(q[..., MLA_NOPE_DIM:], cos[:, None, :], sin[:, None, :]) * scale
    kv = (_rms_norm(c_kv, g_kvn) @ w_ukv).reshape(B, T, H, MLA_NOPE_DIM + MLA_V_DIM)
    k_nope = kv[..., :MLA_NOPE_DIM]
    v = kv[..., MLA_NOPE_DIM:]
    k_pe = _rope(k_pe, cos, sin)
    kpos = jnp.arange(T)

    def block(qb):
        q0 = qb * QBLK
        t = q0 + jnp.arange(QBLK)
        qn = lax.dynamic_slice_in_dim(q_nope, q0, QBLK, axis=1)
        qr = lax.dynamic_slice_in_dim(q_pe, q0, QBLK, axis=1)
        s = (jnp.einsum('bqhd,bkhd->bhqk', qn, k_nope)
             + jnp.einsum('bqhr,bkr->bhqk', qr, k_pe)).astype(f32)
        p = _masked_softmax(s, (kpos[None, :] <= t[:, None])[None, None])
        o = jnp.einsum('bhqk,bkhd->bqhd', p.astype(v.dtype), v)
        return o.reshape(B, QBLK, H * MLA_V_DIM)

    out = lax.map(block, jnp.arange(T // QBLK))
    return out.transpose(1, 0, 2, 3).reshape(B, T, H * MLA_V_DIM)


def setup_inputs(seed: int = 0) -> dict:
    key = jax.random.key(seed)
    ks = jax.random.split(key, 26)

    def nrm(k, shape, scale):
        return jax.random.normal(k, shape, jnp.float32) * scale

    def gain(k, shape):
        return 1.0 + 0.01 * jax.random.normal(k, shape, jnp.float32)

    L, D, DH = DEPTH, D_MODEL, NSA_HEAD_DIM
    return {
        'x': nrm(ks[0], (BATCH, SEQ, D), 1.0),
        'p': nrm(ks[1], (DEPTH, BATCH, SEQ, D_PLE), 1.0),
        'w_in': nrm(ks[2], (L, D, D_IN), D ** -0.5),
        'w_ck1': nrm(ks[3], (L, CMP_LEN * DH, CMP_HIDDEN), (CMP_LEN * DH) ** -0.5),
        'w_ck2': nrm(ks[4], (L, CMP_HIDDEN, DH), CMP_HIDDEN ** -0.5),
        'pos_ck': nrm(ks[5], (L, CMP_LEN, DH), 0.1),
        'w_cv1': nrm(ks[6], (L, CMP_LEN * DH, CMP_HIDDEN), (CMP_LEN * DH) ** -0.5),
        'w_cv2': nrm(ks[7], (L, CMP_HIDDEN, DH), CMP_HIDDEN ** -0.5),
        'pos_cv': nrm(ks[8], (L, CMP_LEN, DH), 0.1),
        'mla_q_norm': gain(ks[9], (L, MLA_Q_LORA)),
        'w_uq': nrm(ks[10], (L, MLA_Q_LORA, N_MLA_HEADS * (MLA_NOPE_DIM + MLA_ROPE_DIM)), MLA_Q_LORA ** -0.5),
        'mla_kv_norm': gain(ks[11], (L, MLA_KV_LORA)),
        'w_ukv': nrm(ks[12], (L, MLA_KV_LORA, N_MLA_HEADS * (MLA_NOPE_DIM + MLA_V_DIM)), MLA_KV_LORA ** -0.5),
        'w_out': nrm(ks[13], (L, MIX_WIDTH, D), BETA * MIX_WIDTH ** -0.5),
        'ln1_g': gain(ks[14], (L, D)),
        'ln1_b': nrm(ks[15], (L, D), 0.01),
        'w_up': nrm(ks[16], (L, D, 2 * D_FF), D ** -0.5),
        'w_down': nrm(ks[17], (L, D_FF, D), BETA * D_FF ** -0.5),
        'ln2_g': gain(ks[18], (L, D)),
        'ln2_b': nrm(ks[19], (L, D), 0.01),
        'w_ple_gate': nrm(ks[20], (L, D, D), D ** -0.5),
        'w_ple': nrm(ks[21], (L, D_PLE, D), BETA * D_PLE ** -0.5),
        'ln3_g': gain(ks[22], (L, D)),
        'ln3_b': nrm(ks[23], (L, D), 0.01),
    }


def reference(x, p, w_in, w_ck1, w_ck2, pos_ck, w_cv1, w_cv2, pos_cv, mla_q_norm, w_uq,
              mla_kv_norm, w_ukv, w_out, ln1_g, ln1_b, w_up, w_down, ln2_g, ln2_b,
              w_ple_gate, w_ple, ln3_g, ln3_b):
    T = x.shape[1]
    pos = jnp.arange(T, dtype=jnp.float32)
    half = MLA_ROPE_DIM // 2
    inv_freq = ROPE_THETA ** (-jnp.arange(half, dtype=jnp.float32) / half)
    ang = pos[:, None] * inv_freq[None, :]
    cos, sin = jnp.cos(ang), jnp.sin(ang)
    for i in range(DEPTH):
        (nq, kc, vc, ks_, vs_, kw, vw, gl, c_q, c_kv, k_pe) = _split_cols(x @ w_in[i])
        o_nsa = _nsa(nq, kc, vc, ks_, vs_, kw, vw, gl,
                     w_ck1[i], w_ck2[i], pos_ck[i], w_cv1[i], w_cv2[i], pos_cv[i])
        o_mla = _mla(c_q, c_kv, k_pe, mla_q_norm[i], w_uq[i], mla_kv_norm[i], w_ukv[i], cos, sin)
        mix = jnp.concatenate([o_nsa, o_mla], axis=-1) @ w_out[i]
        x = _layer_norm(ALPHA * x + mix, ln1_g[i], ln1_b[i])
        gu = x @ w_up[i]
        ffn = (jax.nn.silu(gu[..., :D_FF]) * gu[..., D_FF:]) @ w_down[i]
        x = _layer_norm(ALPHA * x + ffn, ln2_g[i], ln2_b[i])
        gate = jax.nn.sigmoid((x @ w_ple_gate[i]).astype(jnp.float32)).astype(x.dtype)
        ple = gate * (p[i] @ w_ple[i])
        x = _layer_norm(ALPHA * x + ple, ln3_g[i], ln3_b[i])
    return x
```

```python
from contextlib import ExitStack
import numpy as np
import ml_dtypes
import concourse.bass as bass
import concourse.mybir as mybir
from concourse.bass_utils import run_bass_kernel_spmd

F32 = mybir.dt.float32
BF16 = mybir.dt.bfloat16
AF = mybir.ActivationFunctionType
ALU = mybir.AluOpType
AX = mybir.AxisListType
bf16 = ml_dtypes.bfloat16

T = 8192
D = 1024
NCH = T // 512
NKT = T // 128
DFF = 2816
BIG = 30000.0
ALPHA = 2.0 ** 0.25
LN_EPS = 1e-5
RMS_EPS = 1e-6
MLA_SCALE = 96.0 ** -0.5
ALIBI_CUT = 200.0


class Buf:
    __slots__ = ("name", "last_w", "readers")

    def __init__(self, name=""):
        self.name = name
        self.last_w = None
        self.readers = {}


class KB:
    NSLOT = 8

    def __init__(self, nc, es):
        self.nc = nc
        self.E = {"pe": nc.tensor, "act": nc.scalar, "dve": nc.vector, "pool": nc.gpsimd, "sp": nc.sync}
        self.sem = {}
        self.seq = {}
        self.known = {}
        for e in self.E:
            self.sem[("e", e)] = es.enter_context(nc.semaphore("s_" + e))
            self.seq[e] = 0
            self.known[e] = {}
        self.slots = {}
        self.slot_rr = {}
        for q in ("sp", "pool"):
            self.slots[q] = []
            self.slot_rr[q] = 0
            for i in range(self.NSLOT):
                key = ("d", q, i)
                self.sem[key] = es.enter_context(nc.semaphore("d_%s%d" % (q, i)))
                self.slots[q].append([key, 0])

    def _collect(self, reads, writes):
        deps = {}
        for b in reads:
            if b.last_w is not None:
                k, v = b.last_w
                o = deps.get(k)
                deps[k] = (max(v, o[0]) if o else v, True)
        for b in writes:
            if b.last_w is not None:
                k, v = b.last_w
                o = deps.get(k)
                if o is None:
                    deps[k] = (v, False)
                elif v > o[0]:
                    deps[k] = (v, o[1])
            for k, v in b.readers.items():
                o = deps.get(k)
                if o is None:
                    deps[k] = (v, False)
                elif v > o[0]:
                    deps[k] = (v, o[1])
        return deps

    def _wait(self, eng, deps):
        kn = self.known[eng]
        for k, (v, raw) in deps.items():
            if k == ("e", eng):
                if eng == "pe" or not raw:
                    continue
            if kn.get(k, 0) >= v:
                continue
            self.E[eng].wait_ge(self.sem[k], v)
            kn[k] = v

    def _record(self, dep, reads, writes):
        k, v = dep
        for b in reads:
            if b.readers.get(k, 0) < v:
                b.readers[k] = v
        for b in writes:
            b.last_w = dep
            b.readers = {}

    def op(self, eng, fn, reads=(), writes=(), inc=True):
        self._wait(eng, self._collect(reads, writes))
        ins = fn(self.E[eng])
        if inc:
            ins.then_inc(self.sem[("e", eng)], 1)
            self.seq[eng] += 1
            dep = (("e", eng), self.seq[eng])
        else:
            dep = (("e", eng), self.seq[eng] + 1)
        self._record(dep, reads, writes)
        return ins

    def dma(self, out, in_, reads=(), writes=(), q="sp", **kw):
        sl = self.slots[q][self.slot_rr[q]]
        self.slot_rr[q] = (self.slot_rr[q] + 1) % self.NSLOT
        deps = self._collect(reads, writes)
        if sl[1] > 0:
            o = deps.get(sl[0])
            if o is None or o[0] < sl[1]:
                deps[sl[0]] = (sl[1], True)
        self._wait(q, deps)
        self.E[q].dma_start(out=out, in_=in_, **kw).then_inc(self.sem[sl[0]], 16)
        sl[1] += 16
        self._record((sl[0], sl[1]), reads, writes)

    def barrier(self):
        cur = {}
        for e in self.E:
            cur[("e", e)] = self.seq[e]
        for q in self.slots:
            for key, cnt in self.slots[q]:
                cur[key] = cnt
        for e in self.E:
            kn = self.known[e]
            for k, v in cur.items():
                if v == 0 or k == ("e", e):
                    continue
                if kn.get(k, 0) >= v:
                    continue
                self.E[e].wait_ge(self.sem[k], v)
                kn[k] = v


class Rot:
    def __init__(self, items):
        self.items = items
        self.i = 0

    def next(self):
        it = self.items[self.i]
        self.i = (self.i + 1) % len(self.items)
        return it


def host_constants():
    c = {}
    c["ident"] = np.eye(128, dtype=np.float32).astype(bf16)
    pos = np.arange(T, dtype=np.float32)
    inv_freq = (10000.0 ** (-np.arange(16, dtype=np.float32) / 16)).astype(np.float32)
    ang = (pos[:, None] * inv_freq[None, :]).astype(np.float32)
    cos = np.cos(ang).astype(np.float32).T
    sin = np.sin(ang).astype(np.float32).T
    c["ropeC"] = np.ascontiguousarray(np.concatenate([cos, cos], 0))
    c["ropeS"] = np.ascontiguousarray(np.concatenate([-sin, sin], 0))
    p = np.arange(128)[:, None]
    j = np.arange(512)[None, :]
    def addmask(valid):
        return np.where(valid, 0.0, -BIG).astype(np.float32).astype(bf16)
    c["cmk"] = addmask(np.stack([(p + 128 * d <= j) for d in range(4)], 1))
    c["wmk"] = addmask(np.stack([(j < 128 * d + p) for d in range(4)], 1))
    c["pmk"] = addmask(np.stack([(16 * p + 31 <= 512 * m + j) for m in range(5)], 1))
    slopes = 2.0 ** (-np.arange(1, 9, dtype=np.float64))
    t = np.arange(T)
    qaug = np.zeros((8, 5, T), np.float32)
    for h in range(8):
        s = slopes[h]
        qaug[h, 0] = s
        qaug[h, 1] = s
        qaug[h, 2] = -s * (t // 128) * 128
        qaug[h, 3] = -s * (t % 128)
        qaug[h, 4] = -s * ((t % 128) - 31)
    c["qaug"] = qaug.astype(bf16)
    ka = np.zeros((5, T), np.float32)
    ka[0] = (t // 128) * 128
    ka[1] = t % 128
    ka[2] = 1
    ka[3] = 1
    c["kaug"] = ka.astype(bf16)
    cc = np.arange(512)
    kc = np.zeros((5, 512), np.float32)
    kc[0] = (cc // 128) * 2048
    kc[1] = (cc % 128) * 16
    kc[2] = 1
    kc[4] = 1
    c["kaugc"] = kc.astype(bf16)
    for k in ("qaug", "kaug", "kaugc"):
        assert np.all(c[k].astype(np.float32) == {"qaug": qaug, "kaug": ka, "kaugc": kc}[k])
    E = np.zeros((128, 64, 128), np.float32)
    for kt in range(64):
        E[2 * kt, kt, :64] = BIG
        E[2 * kt + 1, kt, 64:] = BIG
    c["eexp"] = E.astype(bf16)
    M = np.zeros((512, 128), np.float32)
    for cidx in range(511):
        for s in (cidx, cidx + 1):
            M[cidx, s // 4] += 1
    c["impm"] = M.astype(bf16)
    keep = np.zeros((128, 256), np.float32)
    add = np.zeros((128, 256), np.float32)
    pp = np.arange(128)
    for col in range(256):
        r = col - 128
        if r < -1:
            keep[:, col] = 1
        elif r == -1:
            keep[:, col] = (pp >= 64)
            add[:, col] = 1e6 * (pp < 64)
        elif r == 0:
            add[:, col] = 1e6
        elif r == 1:
            add[:, col] = np.where(pp >= 64, 1e6, -1.0)
        else:
            add[:, col] = -1.0
    c["keept"] = keep
    c["addt"] = add
    return c


CONST_SPECS = {
    "ident": ([128, 128], BF16), "ropeC": ([32, T], F32), "ropeS": ([32, T], F32),
    "cmk": ([128, 4, 512], BF16), "wmk": ([128, 4, 512], BF16), "pmk": ([128, 5, 512], BF16),
    "qaug": ([8, 5, T], BF16), "kaug": ([5, T], BF16), "kaugc": ([5, 512], BF16),
    "eexp": ([128, 64, 128], BF16), "impm": ([512, 128], BF16),
    "keept": ([128, 256], F32), "addt": ([128, 256], F32),
}

IN_SPECS = {
    "x": [T, D], "p": [T, 256], "w_in": [D, 1720], "w_ck1": [2048, 256], "w_ck2": [256, 64],
    "pos_ck": [32, 64], "w_cv1": [2048, 256], "w_cv2": [256, 64], "pos_cv": [32, 64],
    "mla_q_norm": [256, 1], "w_uq": [256, 768], "mla_kv_norm": [128, 1], "w_ukv": [128, 1024],
    "w_out": [D, D], "ln1_g": [1, D], "ln1_b": [1, D], "w_up": [D, 2 * DFF], "w_down": [DFF, D],
    "ln2_g": [1, D], "ln2_b": [1, D], "w_ple_gate": [D, D], "w_ple": [256, D],
    "ln3_g": [1, D], "ln3_b": [1, D],
}

SCRATCH_SPECS = {
    "QN": ([512, T], BF16),
    "KX": ([4, 128, T], BF16),
    "VSW": ([T, 256], BF16),
    "G": ([24, T], F32),
    "KPE": ([32, T], BF16),
    "QM": ([8, 96, T], BF16),
    "KN": ([512, T], BF16),
    "VM": ([T, 512], BF16),
    "ON": ([512, T], BF16),
    "OM": ([512, T], BF16),
    "WUP": ([22, 128, 2, 8, 128], BF16),
}


def sb(nc, es, name, shape, dt):
    return es.enter_context(nc.sbuf_tensor(name, list(shape), dt))


def ps(nc, es, name, shape, dt):
    return es.enter_context(nc.psum_tensor(name, list(shape), dt))


def rot_tiles(nc, es, name, shape, dt, n, psum=False):
    items = []
    for i in range(n):
        t = (ps if psum else sb)(nc, es, "%s%d" % (name, i), shape, dt)
        items.append((t, Buf("%s%d" % (name, i))))
    return Rot(items)


class Ctx:
    pass


def build_program(stop_after=None, debug_scratch=False, stop_mid=None):
    nc = bass.Bass("TRN2", target_bir_lowering=False)
    c = Ctx()
    c.nc = nc
    c.din = {k: nc.dram_tensor(k, v, F32, kind="ExternalInput").ap() for k, v in IN_SPECS.items()}
    c.dc = {k: nc.dram_tensor("c_" + k, v[0], v[1], kind="ExternalInput").ap() for k, v in CONST_SPECS.items()}
    skind = "ExternalOutput" if debug_scratch else "Internal"
    c.ds = {k: nc.dram_tensor("s_" + k, v[0], v[1], kind=skind).ap() for k, v in SCRATCH_SPECS.items()}
    c.dsb = {k: Buf("s_" + k) for k in SCRATCH_SPECS}
    c.out = nc.dram_tensor("out", [T, D], F32, kind="ExternalOutput").ap()
    c.out_b = Buf("out")
    c.stop_mid = stop_mid
    if stop_mid == "compress":
        c.dbg = {"kcmp": nc.dram_tensor("dbg_kcmp", [69, 2, 512], BF16, kind="ExternalOutput").ap(),
                 "vcmp": nc.dram_tensor("dbg_vcmp", [128, 4, 2, 128], BF16, kind="ExternalOutput").ap()}
    with ExitStack() as es0:
        kb = KB(nc, es0)
        c.kb = kb
        c.ident = sb(nc, es0, "ident", [128, 128], BF16)
        c.ident_b = Buf("ident")
        kb.dma(c.ident[:], c.dc["ident"][:], writes=[c.ident_b])
        phases = [phase1, phase2, phase3]
        for ph in phases:
            ph(c)
            kb.barrier()
            if stop_after == ph.__name__:
                break
        kb.barrier()
    return nc


def phase1(c):
    nc, kb = c.nc, c.kb
    din, dc, ds, dsb = c.din, c.dc, c.ds, c.dsb
    with ExitStack() as es:
        wtr = sb(nc, es, "wtr", [128, 8, 1048], BF16)
        wtok = sb(nc, es, "wtok", [128, 8, 640], BF16)
        wkpA = sb(nc, es, "wkpA", [128, 8, 96], BF16)
        wkpB = sb(nc, es, "wkpB", [128, 8, 96], BF16)
        wuqA = sb(nc, es, "wuqA", [128, 2, 8, 96], BF16)
        wuqB = sb(nc, es, "wuqB", [128, 2, 8, 96], BF16)
        wkvK = sb(nc, es, "wkvK", [128, 8, 64], BF16)
        wkvV = sb(nc, es, "wkvV", [128, 8, 64], BF16)
        gq = sb(nc, es, "gq", [128, 2], F32)
        gkv = sb(nc, es, "gkv", [128, 1], F32)
        W = Buf("p1w")
        es_stg = ExitStack()
        stg = rot_tiles(nc, es_stg, "stg", [128, 1720], F32, 2)
        kb.op("pool", lambda e: e.memset(wkpA[:], 0.0), writes=[W])
        kb.op("pool", lambda e: e.memset(wkpB[:], 0.0), writes=[W])
        kb.op("pool", lambda e: e.memset(wuqB[:], 0.0), writes=[W])
        for kt in range(2):
            kb.dma(gq[:, kt:kt + 1], din["mla_q_norm"][kt * 128:(kt + 1) * 128, :], writes=[W])
        kb.dma(gkv[:], din["mla_kv_norm"][:, :], writes=[W])
        for ft in range(8):
            s, sbuf_ = stg.next()
            kb.dma(s[:], din["w_in"][ft * 128:(ft + 1) * 128, :], writes=[sbuf_])
            kb.op("dve", lambda e: e.tensor_copy(out=wtr[:, ft, 0:896], in_=s[:, 0:896]), reads=[sbuf_], writes=[W])
            kb.op("pool", lambda e: e.tensor_copy(out=wtr[:, ft, 896:1024], in_=s[:, 1024:1152]), reads=[sbuf_], writes=[W])
            kb.op("pool", lambda e: e.tensor_copy(out=wtr[:, ft, 1024:1048], in_=s[:, 1280:1304]), reads=[sbuf_], writes=[W])
            kb.op("dve", lambda e: e.tensor_copy(out=wtok[:, ft, 0:384], in_=s[:, 1304:1688]), reads=[sbuf_], writes=[W])
            kb.op("pool", lambda e: e.tensor_copy(out=wtok[:, ft, 384:512], in_=s[:, 896:1024]), reads=[sbuf_], writes=[W])
            kb.op("pool", lambda e: e.tensor_copy(out=wtok[:, ft, 512:640], in_=s[:, 1152:1280]), reads=[sbuf_], writes=[W])
            kb.op("dve", lambda e: e.tensor_copy(out=wkpA[:, ft, 64:96], in_=s[:, 1688:1720]), reads=[sbuf_], writes=[W])
            kb.op("dve", lambda e: e.tensor_copy(out=wkpB[:, ft, 64:80], in_=s[:, 1704:1720]), reads=[sbuf_], writes=[W])
            kb.op("dve", lambda e: e.tensor_copy(out=wkpB[:, ft, 80:96], in_=s[:, 1688:1704]), reads=[sbuf_], writes=[W])
        for kt in range(2):
            s, sbuf_ = stg.next()
            kb.dma(s[:, 0:768], din["w_uq"][kt * 128:(kt + 1) * 128, :], writes=[sbuf_])
            s3 = s[:, 0:768].rearrange("p (h c) -> p h c", c=96)
            kb.op("dve", lambda e: e.tensor_scalar(out=wuqA[:, kt, :, :], in0=s3, scalar1=gq[:, kt:kt + 1], scalar2=None,
                                                   op0=ALU.mult), reads=[sbuf_, W], writes=[W])
            kb.op("dve", lambda e: e.tensor_scalar(out=wuqB[:, kt, :, 64:80], in0=s3[:, :, 80:96], scalar1=gq[:, kt:kt + 1],
                                                   scalar2=None, op0=ALU.mult), reads=[sbuf_, W], writes=[W])
            kb.op("dve", lambda e: e.tensor_scalar(out=wuqB[:, kt, :, 80:96], in0=s3[:, :, 64:80], scalar1=gq[:, kt:kt + 1],
                                                   scalar2=None, op0=ALU.mult), reads=[sbuf_, W], writes=[W])
        s, sbuf_ = stg.next()
        kb.dma(s[:, 0:1024], din["w_ukv"][:, :], writes=[sbuf_])
        s3 = s[:, 0:1024].rearrange("p (h c) -> p h c", c=128)
        kb.op("dve", lambda e: e.tensor_scalar(out=wkvK[:, :, :], in0=s3[:, :, 0:64], scalar1=gkv[:, 0:1], scalar2=None,
                                               op0=ALU.mult), reads=[sbuf_, W], writes=[W])
        kb.op("dve", lambda e: e.tensor_scalar(out=wkvV[:, :, :], in0=s3[:, :, 64:128], scalar1=gkv[:, 0:1], scalar2=None,
                                               op0=ALU.mult), reads=[sbuf_, W], writes=[W])
        kb.barrier()
        es_stg.close()
        XF = rot_tiles(nc, es, "xf", [128, 4, 1024], F32, 2)
        XT = rot_tiles(nc, es, "xT", [128, 8, 512], BF16, 2)
        STT = rot_tiles(nc, es, "stT", [128, 8, 512], BF16, 2)
        GST = rot_tiles(nc, es, "gst", [24, 512], F32, 2)
        RC = rot_tiles(nc, es, "rc", [96, 2, 512], F32, 3)
        KST = rot_tiles(nc, es, "kst", [96, 512], BF16, 2)
        RT = rot_tiles(nc, es, "rt", [96, 2, 512], F32, 2)
        VST = rot_tiles(nc, es, "vst", [128, 4, 256], BF16, 2)
        SS = rot_tiles(nc, es, "ss", [128, 4], F32, 2)
        JK = rot_tiles(nc, es, "jk", [128, 256], F32, 2)
        CN = rot_tiles(nc, es, "cn", [128, 384], BF16, 2)
        CT = rot_tiles(nc, es, "cT", [128, 3, 512], BF16, 2)
        QST = rot_tiles(nc, es, "qst", [96, 8, 512], BF16, 2)
        KNST = rot_tiles(nc, es, "knst", [128, 4, 512], BF16, 2)
        VMST = rot_tiles(nc, es, "vmst", [128, 4, 512], BF16, 2)
        TP = rot_tiles(nc, es, "tp", [128, 1024], BF16, 2, psum=True)
        TPQ = rot_tiles(nc, es, "tpq", [128, 3, 128], BF16, 1, psum=True)
        PS = rot_tiles(nc, es, "ps", [128, 512], F32, 5, psum=True)
        ident, ident_b = c.ident, c.ident_b
        xv = din["x"]
        XB = rot_tiles(nc, es, "xbb", [128, 4, 1024], BF16, 2)
        LS = {}

        def L_load(ch):
            t0 = ch * 512
            xf, xf_b = XF.next()
            kb.dma(xf[:], xv[t0:t0 + 512, :].rearrange("(s p) d -> p s d", p=128), writes=[xf_b])
            rc, rc_b = RC.next()
            kb.dma(rc[64:96, 0, :], dc["ropeC"][:, t0:t0 + 512], writes=[rc_b])
            kb.dma(rc[64:96, 1, :], dc["ropeS"][:, t0:t0 + 512], writes=[rc_b])
            LS[ch] = {"xf": (xf, xf_b), "rc": (rc, rc_b)}

        def L_cast(ch):
            xf, xf_b = LS[ch]["xf"]
            xb, xb_b = XB.next()
            kb.op("dve", lambda e: e.tensor_copy(out=xb[:, 0:3, :], in_=xf[:, 0:3, :]), reads=[xf_b], writes=[xb_b])
            kb.op("pool", lambda e: e.tensor_copy(out=xb[:, 3:4, :], in_=xf[:, 3:4, :]), reads=[xf_b], writes=[xb_b])
            LS[ch]["xb"] = (xb, xb_b)

        def L_tr(ch):
            xb, xb_b = LS[ch]["xb"]
            xT, xT_b = XT.next()
            for f2 in range(4):
                tp, tp_b = TP.next()
                for fi in range(2):
                    ft = f2 * 2 + fi
                    for s_ in range(4):
                        last = (fi == 1 and s_ == 3)
                        kb.op("pe", lambda e: e.transpose(out=tp[:, fi * 512 + s_ * 128: fi * 512 + (s_ + 1) * 128],
                                                          in_=xb[:, s_, ft * 128:(ft + 1) * 128], identity=ident[:]),
                              reads=[xb_b, ident_b], writes=[tp_b], inc=last)
                dst = xT[:, 2 * f2:2 * f2 + 2, :].rearrange("p a t -> p (a t)")
                kb.op("act" if f2 % 2 else "dve",
                      (lambda e: e.activation(out=dst, in_=tp[:], func=AF.Copy)) if f2 % 2 else
                      (lambda e: e.tensor_copy(out=dst, in_=tp[:])), reads=[tp_b], writes=[xT_b])
            LS[ch]["xT"] = (xT, xT_b)

        L_load(0)
        L_cast(0)
        L_tr(0)
        if NCH > 1:
            L_load(1)
        for ch in range(NCH):
            t0 = ch * 512
            xT, xT_b = LS[ch]["xT"]
            rc, rc_b = LS[ch]["rc"]
            if ch + 2 < NCH:
                L_load(ch + 2)
            if ch + 1 < NCH:
                L_cast(ch + 1)
            stT, stT_b = STT.next()
            for ct in range(8):
                pt, pt_b = PS.next()
                for ft in range(8):
                    kb.op("pe", lambda e: e.matmul(pt[:, :], lhsT=wtr[:, ft, ct * 128:(ct + 1) * 128], rhs=xT[:, ft, :],
                                                   start=(ft == 0), stop=(ft == 7)),
                          reads=[W, xT_b], writes=[pt_b], inc=(ft == 7))
                if ct < 4:
                    kb.op("act", lambda e: e.activation(out=stT[:, ct, :], in_=pt[:, :], func=AF.Copy, scale=0.125),
                          reads=[pt_b], writes=[stT_b])
                else:
                    kb.op("dve", lambda e: e.tensor_copy(out=stT[:, ct, :], in_=pt[:, :]), reads=[pt_b], writes=[stT_b])
            kb.dma(ds["QN"].rearrange("(c p) t -> p c t", p=128)[:, :, t0:t0 + 512], stT[:, 0:4, :],
                   reads=[stT_b], writes=[dsb["QN"]])
            kb.dma(ds["KX"].rearrange("k p t -> p k t")[:, :, t0:t0 + 512], stT[:, 4:8, :],
                   reads=[stT_b], writes=[dsb["KX"]])
            pt, pt_b = PS.next()
            for ft in range(8):
                kb.op("pe", lambda e: e.matmul(pt[0:24, :], lhsT=wtr[:, ft, 1024:1048], rhs=xT[:, ft, :],
                                               start=(ft == 0), stop=(ft == 7)),
                      reads=[W, xT_b], writes=[pt_b], inc=(ft == 7))
            gst, gst_b = GST.next()
            kb.op("act", lambda e: e.activation(out=gst[:, :], in_=pt[0:24, :], func=AF.Sigmoid), reads=[pt_b], writes=[gst_b])
            kb.dma(ds["G"][:, t0:t0 + 512], gst[:, :], reads=[gst_b], writes=[dsb["G"]])
            pa, pa_b = PS.next()
            for ft in range(8):
                kb.op("pe", lambda e: e.matmul(pa[0:96, :], lhsT=wkpA[:, ft, :], rhs=xT[:, ft, :], start=(ft == 0), stop=(ft == 7)),
                      reads=[W, xT_b], writes=[pa_b], inc=(ft == 7))
            pb, pb_b = PS.next()
            for ft in range(8):
                kb.op("pe", lambda e: e.matmul(pb[0:96, :], lhsT=wkpB[:, ft, :], rhs=xT[:, ft, :], start=(ft == 0), stop=(ft == 7)),
                      reads=[W, xT_b], writes=[pb_b], inc=(ft == 7))
            rt, rt_b = RT.next()
            kst, kst_b = KST.next()
            kb.op("dve", lambda e: e.tensor_tensor(out=rt[64:96, 0, :], in0=pa[64:96, :], in1=rc[64:96, 0, :], op=ALU.mult),
                  reads=[pa_b, rc_b], writes=[rt_b])
            kb.op("dve", lambda e: e.tensor_tensor(out=rt[64:96, 1, :], in0=pb[64:96, :], in1=rc[64:96, 1, :], op=ALU.mult),
                  reads=[pb_b, rc_b], writes=[rt_b])
            kb.op("pool", lambda e: e.tensor_tensor(out=kst[64:96, :], in0=rt[64:96, 0, :], in1=rt[64:96, 1, :], op=ALU.add),
                  reads=[rt_b], writes=[kst_b])
            kb.dma(ds["KPE"][:, t0:t0 + 512], kst[64:96, :], reads=[kst_b], writes=[dsb["KPE"]])
            if ch + 1 < NCH:
                L_tr(ch + 1)
            vst, vst_b = VST.next()
            cT, cT_b = CT.next()
            for s_ in range(4):
                p1, p1_b = PS.next()
                for ft in range(8):
                    kb.op("pe", lambda e: e.matmul(p1[:, 0:384], lhsT=xT[:, ft, s_ * 128:(s_ + 1) * 128], rhs=wtok[:, ft, 0:384],
                                                   start=(ft == 0), stop=(ft == 7)),
                          reads=[W, xT_b], writes=[p1_b], inc=(ft == 7))
                p2, p2_b = PS.next()
                for ft in range(8):
                    kb.op("pe", lambda e: e.matmul(p2[:, 0:256], lhsT=xT[:, ft, s_ * 128:(s_ + 1) * 128], rhs=wtok[:, ft, 384:640],
                                                   start=(ft == 0), stop=(ft == 7)),
                          reads=[W, xT_b], writes=[p2_b], inc=(ft == 7))
                kb.op("dve", lambda e: e.tensor_copy(out=vst[:, s_, :], in_=p2[:, 0:256]), reads=[p2_b], writes=[vst_b])
                ss, ss_b = SS.next()
                jk, jk_b = JK.next()
                kb.op("act", lambda e: e.activation(out=jk[:, 0:256], in_=p1[:, 0:256], func=AF.Square, accum_out=ss[:, 0:1]),
                      reads=[p1_b], writes=[jk_b, ss_b])
                kb.op("act", lambda e: e.activation(out=jk[:, 0:128], in_=p1[:, 256:384], func=AF.Square, accum_out=ss[:, 1:2]),
                      reads=[p1_b], writes=[jk_b, ss_b])
                kb.op("dve", lambda e: e.tensor_scalar(out=ss[:, 0:1], in0=ss[:, 0:1], scalar1=1.0 / 256, scalar2=RMS_EPS,
                                                       op0=ALU.mult, op1=ALU.add), reads=[ss_b], writes=[ss_b])
                kb.op("dve", lambda e: e.tensor_scalar(out=ss[:, 1:2], in0=ss[:, 1:2], scalar1=1.0 / 128, scalar2=RMS_EPS,
                                                       op0=ALU.mult, op1=ALU.add), reads=[ss_b], writes=[ss_b])
                kb.op("act", lambda e: e.activation(out=ss[:, 2:4], in_=ss[:, 0:2], func=AF.Sqrt), reads=[ss_b], writes=[ss_b])
                kb.op("dve", lambda e: e.reciprocal(out=ss[:, 0:2], in_=ss[:, 2:4]), reads=[ss_b], writes=[ss_b])
                cn, cn_b = CN.next()
                kb.op("dve", lambda e: e.tensor_scalar(out=cn[:, 0:256], in0=p1[:, 0:256], scalar1=ss[:, 0:1], scalar2=None,
                                                       op0=ALU.mult), reads=[p1_b, ss_b], writes=[cn_b])
                kb.op("dve", lambda e: e.tensor_scalar(out=cn[:, 256:384], in0=p1[:, 256:384], scalar1=ss[:, 1:2], scalar2=None,
                                                       op0=ALU.mult), reads=[p1_b, ss_b], writes=[cn_b])
                tq, tq_b = TPQ.next()
                for i in range(3):
                    kb.op("pe", lambda e: e.transpose(out=tq[:, i, :], in_=cn[:, i * 128:(i + 1) * 128], identity=ident[:]),
                          reads=[cn_b, ident_b], writes=[tq_b], inc=(i == 2))
                kb.op("act", lambda e: e.activation(out=cT[:, :, s_ * 128:(s_ + 1) * 128], in_=tq[:, :, :], func=AF.Copy),
                      reads=[tq_b], writes=[cT_b])
            kb.dma(ds["VSW"][t0:t0 + 512, :].rearrange("(s p) c -> p s c", p=128), vst[:, :, :], reads=[vst_b], writes=[dsb["VSW"]])
            qst, qst_b = QST.next()
            for h in range(8):
                pa, pa_b = PS.next()
                for kt in range(2):
                    kb.op("pe", lambda e: e.matmul(pa[0:96, :], lhsT=wuqA[:, kt, h, :], rhs=cT[:, kt, :], start=(kt == 0), stop=(kt == 1)),
                          reads=[W, cT_b], writes=[pa_b], inc=(kt == 1))
                pb, pb_b = PS.next()
                for kt in range(2):
                    kb.op("pe", lambda e: e.matmul(pb[0:96, :], lhsT=wuqB[:, kt, h, :], rhs=cT[:, kt, :], start=(kt == 0), stop=(kt == 1)),
                          reads=[W, cT_b], writes=[pb_b], inc=(kt == 1))
                kb.op("act", lambda e: e.activation(out=qst[0:64, h, :], in_=pa[0:64, :], func=AF.Copy), reads=[pa_b], writes=[qst_b])
                rt, rt_b = RT.next()
                kb.op("dve", lambda e: e.tensor_tensor(out=rt[64:96, 0, :], in0=pa[64:96, :], in1=rc[64:96, 0, :], op=ALU.mult),
                      reads=[pa_b, rc_b], writes=[rt_b])
                kb.op("dve", lambda e: e.tensor_tensor(out=rt[64:96, 1, :], in0=pb[64:96, :], in1=rc[64:96, 1, :], op=ALU.mult),
                      reads=[pb_b, rc_b], writes=[rt_b])
                kb.op("pool", lambda e: e.tensor_tensor(out=qst[64:96, h, :], in0=rt[64:96, 0, :], in1=rt[64:96, 1, :], op=ALU.add),
                      reads=[rt_b], writes=[qst_b])
            kb.dma(ds["QM"].rearrange("h r t -> r h t")[:, :, t0:t0 + 512], qst[:, :, :], reads=[qst_b], writes=[dsb["QM"]])
            knst, knst_b = KNST.next()
            for hp in range(4):
                pt, pt_b = PS.next()
                kb.op("pe", lambda e: e.matmul(pt[:, :], lhsT=wkvK[:, 2 * hp:2 * hp + 2, :].rearrange("p a c -> p (a c)"),
                                               rhs=cT[:, 2, :], start=True, stop=True), reads=[W, cT_b], writes=[pt_b])
                kb.op("act" if hp % 2 else "dve",
                      (lambda e: e.activation(out=knst[:, hp, :], in_=pt[:, :], func=AF.Copy)) if hp % 2 else
                      (lambda e: e.tensor_copy(out=knst[:, hp, :], in_=pt[:, :])), reads=[pt_b], writes=[knst_b])
            kb.dma(ds["KN"].rearrange("(c p) t -> p c t", p=128)[:, :, t0:t0 + 512], knst[:, :, :], reads=[knst_b], writes=[dsb["KN"]])
            vmst, vmst_b = VMST.next()
            for s_ in range(4):
                pt, pt_b = PS.next()
                kb.op("pe", lambda e: e.matmul(pt[:, :], lhsT=cT[:, 2, s_ * 128:(s_ + 1) * 128],
                                               rhs=wkvV[:, :, :].rearrange("p a c -> p (a c)"), start=True, stop=True),
                      reads=[W, cT_b], writes=[pt_b])
                kb.op("act" if s_ % 2 else "dve",
                      (lambda e: e.activation(out=vmst[:, s_, :], in_=pt[:, :], func=AF.Copy)) if s_ % 2 else
                      (lambda e: e.tensor_copy(out=vmst[:, s_, :], in_=pt[:, :])), reads=[pt_b], writes=[vmst_b])
            kb.dma(ds["VM"][t0:t0 + 512, :].rearrange("(s p) c -> p s c", p=128), vmst[:, :, :], reads=[vmst_b], writes=[dsb["VM"]])


class TileD:
    __slots__ = ("qk", "scale", "pv", "epi")

    def __init__(self, qk, scale, pv, epi=None):
        self.qk = qk
        self.scale = scale
        self.pv = pv
        self.epi = epi


class Stream:
    def __init__(self, c, R, L=2):
        self.c, self.R, self.L = c, R, L
        self.pending = []

    def push(self, t):
        kb, R = self.c.kb, self.R
        S, S_b = R.SPS.next()
        nq = len(t.qk)
        for a, (lh, rh, rb) in enumerate(t.qk):
            kb.op("pe", lambda e: e.matmul(S[:, :], lhsT=lh, rhs=rh, start=(a == 0), stop=(a == nq - 1)),
                  reads=rb, writes=[S_b], inc=(a == nq - 1))
        P, P_b = R.PSB.next()
        kb.op("act", lambda e: e.activation(out=P[:, :], in_=S[:, :], func=AF.Exp, scale=t.scale), reads=[S_b], writes=[P_b])
        self.pending.append((t, P, P_b))
        if len(self.pending) > self.L:
            self._pop()

    def _pop(self):
        t, P, P_b = self.pending.pop(0)
        t.pv(P, P_b)
        if t.epi is not None:
            t.epi()

    def flush(self):
        while self.pending:
            self._pop()


def run_stream(c, R, tiles, L=2):
    st = c.stream
    for t in tiles:
        st.push(t)


def phase2(c):
    nc, kb = c.nc, c.kb
    din, dc, ds, dsb = c.din, c.dc, c.ds, c.dsb
    ident, ident_b = c.ident, c.ident_b
    with ExitStack() as esA:
        kcmp = sb(nc, esA, "kcmp", [69, 2, 512], BF16)
        kcmp_b = Buf("kcmp")
        vcmp = sb(nc, esA, "vcmp", [128, 4, 2, 128], BF16)
        vcmp_b = Buf("vcmp")
        CB = Buf("consts2")
        cmk = sb(nc, esA, "cmk", [128, 4, 512], BF16)
        wmk = sb(nc, esA, "wmk", [128, 4, 512], BF16)
        pmk = sb(nc, esA, "pmk", [128, 5, 512], BF16)
        eexp = sb(nc, esA, "eexp", [128, 64, 128], BF16)
        impm = sb(nc, esA, "impm", [128, 4, 128], BF16)
        keept = sb(nc, esA, "keept", [128, 256], F32)
        addt = sb(nc, esA, "addt", [128, 256], F32)
        kb.dma(cmk[:], dc["cmk"][:], writes=[CB])
        kb.dma(wmk[:], dc["wmk"][:], writes=[CB])
        kb.dma(pmk[:], dc["pmk"][:], writes=[CB])
        kb.dma(eexp[:], dc["eexp"][:], writes=[CB])
        kb.dma(impm[:], dc["impm"].rearrange("(ct p) b -> p ct b", p=128), writes=[CB])
        kb.dma(keept[:], dc["keept"][:], writes=[CB])
        kb.dma(addt[:], dc["addt"][:], writes=[CB])
        kb.op("pool", lambda e: e.memset(kcmp[:], 0.0), writes=[kcmp_b])
        kb.op("pool", lambda e: e.memset(vcmp[:], 1.0), writes=[vcmp_b])
        for g in range(2):
            kb.dma(kcmp[64:69, g, :], dc["kaugc"][:, :], writes=[kcmp_b])
        with ExitStack() as es:
            kcg = sb(nc, es, "kcg", [64, 4, T], BF16)
            kcg_b = Buf("kcg")
            for kind in range(2):
                for g in range(2):
                    kb.dma(kcg[:, kind * 2 + g, :], ds["KX"][kind, g * 64:(g + 1) * 64, :], reads=[dsb["KX"]], writes=[kcg_b])
            w1s = sb(nc, es, "w1s", [64, 32, 256], F32)
            w1s_b = Buf("w1s")
            w1 = [sb(nc, es, "w1_%d" % k, [64, 32, 256], BF16) for k in range(2)]
            w2s = sb(nc, es, "w2s", [128, 2, 64], F32)
            w2s_b = Buf("w2s")
            w2 = [sb(nc, es, "w2_%d" % k, [128, 2, 64], BF16) for k in range(2)]
            posS = sb(nc, es, "posS", [32, 64], F32)
            posS_b = Buf("posS")
            posB = sb(nc, es, "posB", [32, 64], BF16)
            posB_b = Buf("posB")
            posT = sb(nc, es, "posT", [64, 2, 32], BF16)
            b1 = sb(nc, es, "b1", [128, 4], F32)
            Wc = Buf("wc")
            tpp = ps(nc, es, "tpp", [64, 32], BF16)
            tpp_b = Buf("tpp")
            pb1 = ps(nc, es, "pb1", [128, 4], F32)
            pb1_b = Buf("pb1")
            PSC = rot_tiles(nc, es, "psc", [128, 512], F32, 4, psum=True)
            XG = rot_tiles(nc, es, "xg", [128, 512], F32, 2)
            X2 = rot_tiles(nc, es, "x2", [128, 512], F32, 2)
            GL = rot_tiles(nc, es, "gl", [128, 512], BF16, 4)
            for gl, gl_b in GL.items:
                kb.op("pool", lambda e: e.memset(gl[:], 0.0), writes=[gl_b])
            names = [("w_ck1", "w_ck2", "pos_ck"), ("w_cv1", "w_cv2", "pos_cv")]
            for kind, (n1, n2, npos) in enumerate(names):
                kb.dma(w1s[:], din[n1].rearrange("(l d) n -> d l n", d=64), writes=[w1s_b])
                kb.op("dve", lambda e: e.tensor_copy(out=w1[kind][:, 0:16, :], in_=w1s[:, 0:16, :]), reads=[w1s_b], writes=[Wc])
                kb.op("pool", lambda e: e.tensor_copy(out=w1[kind][:, 16:32, :], in_=w1s[:, 16:32, :]), reads=[w1s_b], writes=[Wc])
                kb.dma(w2s[:], din[n2].rearrange("(c p) n -> p c n", p=128), writes=[w2s_b])
                kb.op("dve", lambda e: e.tensor_copy(out=w2[kind][:], in_=w2s[:]), reads=[w2s_b], writes=[Wc])
                kb.dma(posS[:], din[npos][:, :], writes=[posS_b])
                kb.op("dve", lambda e: e.tensor_copy(out=posB[:], in_=posS[:]), reads=[posS_b], writes=[posB_b])
                kb.op("pe", lambda e: e.transpose(out=tpp[:, :], in_=posB[:, :], identity=ident[0:32, 0:32]),
                      reads=[posB_b, ident_b], writes=[tpp_b])
                kb.op("dve", lambda e: e.tensor_copy(out=posT[:, kind, :], in_=tpp[:, :]), reads=[tpp_b], writes=[Wc])
            for kind in range(2):
                for hc in range(2):
                    idx = kind * 2 + hc
                    for l in range(32):
                        kb.op("pe", lambda e: e.matmul(pb1[:, idx:idx + 1], lhsT=w1[kind][:, l, hc * 128:(hc + 1) * 128],
                                                       rhs=posT[:, kind, l:l + 1], start=(l == 0), stop=(l == 31)),
                              reads=[Wc], writes=[pb1_b], inc=(l == 31))
            kb.op("dve", lambda e: e.tensor_copy(out=b1[:], in_=pb1[:]), reads=[pb1_b], writes=[Wc])
            for kind in range(2):
                for g in range(2):
                    gls = []
                    for hc in range(2):
                        idx = kind * 2 + hc
                        pm, pm_b = PSC.next()
                        for l in range(32):
                            kb.op("pe", lambda e: e.matmul(pm[:, 0:511], lhsT=w1[kind][:, l, hc * 128:(hc + 1) * 128],
                                                           rhs=kcg[:, kind * 2 + g, l:l + 8161:16], start=(l == 0), stop=(l == 31)),
                                  reads=[Wc, kcg_b], writes=[pm_b], inc=(l == 31))
                        xg, xg_b = XG.next()
                        x2, x2_b = X2.next()
                        gl, gl_b = GL.next()
                        kb.op("dve", lambda e: e.tensor_scalar(out=xg[:, 0:511], in0=pm[:, 0:511], scalar1=b1[:, idx:idx + 1],
                                                               scalar2=None, op0=ALU.add), reads=[pm_b, Wc], writes=[xg_b])
                        kb.op("pool", lambda e: e.tensor_tensor(out=x2[:, 0:511], in0=xg[:, 0:511], in1=xg[:, 0:511], op=ALU.mult),
                              reads=[xg_b], writes=[x2_b])
                        kb.op("dve", lambda e: e.tensor_scalar(out=x2[:, 0:511], in0=x2[:, 0:511], scalar1=0.044715, scalar2=1.0,
                                                               op0=ALU.mult, op1=ALU.add), reads=[x2_b], writes=[x2_b])
                        kb.op("pool", lambda e: e.tensor_tensor(out=x2[:, 0:511], in0=x2[:, 0:511], in1=xg[:, 0:511], op=ALU.mult),
                              reads=[x2_b, xg_b], writes=[x2_b])
                        kb.op("act", lambda e: e.activation(out=x2[:, 0:511], in_=x2[:, 0:511], func=AF.Sigmoid, scale=1.5957691216),
                              reads=[x2_b], writes=[x2_b])
                        kb.op("dve", lambda e: e.tensor_tensor(out=gl[:, 0:511], in0=x2[:, 0:511], in1=xg[:, 0:511], op=ALU.mult),
                              reads=[x2_b, xg_b], writes=[gl_b])
                        gls.append((gl, gl_b))
                    if kind == 0:
                        pk, pk_b = PSC.next()
                        for hc in range(2):
                            kb.op("pe", lambda e: e.matmul(pk[0:64, 0:511], lhsT=w2[0][:, hc, :], rhs=gls[hc][0][:, 0:511],
                                                           start=(hc == 0), stop=(hc == 1)),
                                  reads=[Wc, gls[hc][1]], writes=[pk_b], inc=(hc == 1))
                        kb.op("act", lambda e: e.activation(out=kcmp[0:64, g, 0:511], in_=pk[0:64, 0:511], func=AF.Copy),
                              reads=[pk_b], writes=[kcmp_b])
                    else:
                        for ct in range(4):
                            pv, pv_b = PSC.next()
                            for hc in range(2):
                                kb.op("pe", lambda e: e.matmul(pv[:, 0:64], lhsT=gls[hc][0][:, ct * 128:(ct + 1) * 128], rhs=w2[1][:, hc, :],
                                                               start=(hc == 0), stop=(hc == 1)),
                                      reads=[Wc, gls[hc][1]], writes=[pv_b], inc=(hc == 1))
                            kb.op("act", lambda e: e.activation(out=vcmp[:, ct, g, 0:64], in_=pv[:, 0:64], func=AF.Copy),
                                  reads=[pv_b], writes=[vcmp_b])
        kb.barrier()
        if getattr(c, "stop_mid", None) == "compress":
            dbg = c.dbg
            kb.dma(dbg["kcmp"][:], kcmp[:], reads=[kcmp_b])
            kb.dma(dbg["vcmp"][:], vcmp[:], reads=[vcmp_b])
            return
        with ExitStack() as es:
            R = Ctx()
            R.SPS = rot_tiles(nc, es, "sps", [128, 512], F32, 3, psum=True)
            R.OPS = rot_tiles(nc, es, "ops", [128, 512], F32, 2, psum=True)
            R.UPS = rot_tiles(nc, es, "ups", [128, 4, 128], F32, 2, psum=True)
            R.TPN = rot_tiles(nc, es, "tpn", [128, 4, 128], BF16, 1, psum=True)
            R.PSB = rot_tiles(nc, es, "psb", [128, 512], BF16, 4)
            KA = rot_tiles(nc, es, "ka", [96, T], BF16, 2).items
            VA = rot_tiles(nc, es, "va", [128, 64, 128], BF16, 2).items
            for va, va_b in VA:
                kb.op("pool", lambda e: e.memset(va[:], 1.0), writes=[va_b])
            QA = [[(sb(nc, es, "qa%d%d" % (pr, hi), [69, 512], BF16), Buf("qa")) for hi in range(4)] for pr in range(2)]
            OC = [[(sb(nc, es, "oc%d%d" % (pr, hi), [64, 512], F32), Buf("oc")) for hi in range(4)] for pr in range(2)]
            QMT = rot_tiles(nc, es, "qmt", [96, 512], BF16, 3)
            GB = rot_tiles(nc, es, "gb", [64, 3, 512], F32, 4)
            NMT = rot_tiles(nc, es, "nmt", [128, 512], BF16, 2)
            IMP = rot_tiles(nc, es, "imp", [128, 4, 128], F32, 2)
            DEN = rot_tiles(nc, es, "den", [64, 512], F32, 2)
            REC = rot_tiles(nc, es, "rec", [64, 512], F32, 2)
            RG = rot_tiles(nc, es, "rg", [64, 512], F32, 2)
            T1 = rot_tiles(nc, es, "t1", [64, 512], F32, 4)
            ACC = rot_tiles(nc, es, "acc", [64, 512], F32, 2)
            OST = rot_tiles(nc, es, "ost", [64, 512], BF16, 3)
            W1 = rot_tiles(nc, es, "w1k", [128, 128], F32, 2)
            W2 = rot_tiles(nc, es, "w2k", [128, 128], F32, 2)
            M8 = rot_tiles(nc, es, "m8", [128, 16], F32, 2)
            NMS = rot_tiles(nc, es, "nms", [128, 128], BF16, 2)
            RS = rot_tiles(nc, es, "rs", [128, 8], F32, 2)
            Gt = ds["G"].tensor
            c.stream = Stream(c, R)

            def norm_rec(O, O_b):
                den, den_b = DEN.next()
                rec, rec_b = REC.next()
                kb.op("dve", lambda e: e.tensor_scalar_max(out=den[:, :], in0=O[64:128, :], scalar1=1e-30), reads=[O_b], writes=[den_b])
                kb.op("dve", lambda e: e.reciprocal(out=rec[:, :], in_=den[:, :]), reads=[den_b], writes=[rec_b])
                return rec, rec_b

            for g in range(2):
                (ks, ks_b), (kw, kw_b) = KA
                (vs, vs_b), (vw, vw_b) = VA
                kb.dma(ks[0:64, :], ds["KX"][2, g * 64:(g + 1) * 64, :], reads=[dsb["KX"]], writes=[ks_b])
                kb.dma(ks[64:69, :], dc["kaug"][:, :], writes=[ks_b])
                kb.dma(kw[0:64, :], ds["KX"][3, g * 64:(g + 1) * 64, :], reads=[dsb["KX"]], writes=[kw_b])
                kb.dma(kw[64:69, :], dc["kaug"][:, :], writes=[kw_b])
                for half in range(2):
                    hs = slice(half * 32, (half + 1) * 32)
                    rs_ = slice(half * 4096, (half + 1) * 4096)
                    kb.dma(vs[:, hs, 0:64], ds["VSW"][rs_, g * 64:(g + 1) * 64].rearrange("(kt p) d -> p kt d", p=128),
                           reads=[dsb["VSW"]], writes=[vs_b])
                    kb.dma(vw[:, hs, 0:64], ds["VSW"][rs_, 128 + g * 64:128 + (g + 1) * 64].rearrange("(kt p) d -> p kt d", p=128),
                           reads=[dsb["VSW"]], writes=[vw_b])
                for qc in range(NCH):
                    t0 = qc * 512
                    pr = qc % 2
                    imp, imp_b = IMP.next()
                    nmt, nmt_b = NMT.next()
                    tiles = []
                    for hi in range(4):
                        h = 4 * g + hi
                        qa, qa_b = QA[pr][hi]
                        kb.dma(qa[0:64, :], ds["QN"][h * 64:(h + 1) * 64, t0:t0 + 512], reads=[dsb["QN"]], writes=[qa_b])
                        kb.dma(qa[64:69, :], dc["qaug"][h, :, t0:t0 + 512], writes=[qa_b])
                        O, O_b = R.OPS.next()
                        U, U_b = R.UPS.next()
                        nct = qc // 4 + 1
                        for ct in range(nct):
                            m = qc - 4 * ct
                            qk = [(kcmp[0:69, g, ct * 128:(ct + 1) * 128], qa[0:69, :], [kcmp_b, qa_b])]
                            if m <= 4:
                                qk.append((ident[:, :], pmk[:, m, :], [ident_b, CB]))

                            def pv(P, P_b, ct=ct, nct=nct, O=O, O_b=O_b, U=U, U_b=U_b):
                                kb.op("pe", lambda e: e.matmul(O[:, :], lhsT=vcmp[:, ct, g, :], rhs=P[:, :], start=(ct == 0), stop=(ct == nct - 1)),
                                      reads=[vcmp_b, P_b], writes=[O_b], inc=False)
                                for s_ in range(4):
                                    kb.op("pe", lambda e: e.matmul(U[:, s_, :], lhsT=P[:, s_ * 128:(s_ + 1) * 128], rhs=impm[:, ct, :],
                                                                   start=(ct == 0), stop=(ct == nct - 1)),
                                          reads=[CB, P_b], writes=[U_b], inc=(s_ == 3))

                            def epi(hi=hi, O=O, O_b=O_b, U=U, U_b=U_b):
                                rec, rec_b = norm_rec(O, O_b)
                                oc, oc_b = OC[pr][hi]
                                kb.op("dve", lambda e: e.tensor_tensor(out=oc[:, :], in0=O[0:64, :], in1=rec[:, :], op=ALU.mult),
                                      reads=[O_b, rec_b], writes=[oc_b])
                                rs, rs_b = RS.next()
                                kb.op("dve", lambda e: e.tensor_reduce(out=rs[:, 0:4], in_=U[:, :, :], axis=AX.X, op=ALU.add),
                                      reads=[U_b], writes=[rs_b])
                                kb.op("dve", lambda e: e.tensor_scalar(out=rs[:, 0:4], in0=rs[:, 0:4], scalar1=1e-30, scalar2=0.5,
                                                                       op0=ALU.max, op1=ALU.mult), reads=[rs_b], writes=[rs_b])
                                kb.op("dve", lambda e: e.reciprocal(out=rs[:, 4:8], in_=rs[:, 0:4]), reads=[rs_b], writes=[rs_b])
                                for s_ in range(4):
                                    if hi == 0:
                                        kb.op("dve", lambda e: e.tensor_scalar(out=imp[:, s_, :], in0=U[:, s_, :], scalar1=rs[:, 4 + s_:5 + s_],
                                                                               scalar2=None, op0=ALU.mult), reads=[U_b, rs_b], writes=[imp_b])
                                    else:
                                        kb.op("dve", lambda e: e.scalar_tensor_tensor(out=imp[:, s_, :], in0=U[:, s_, :], scalar=rs[:, 4 + s_:5 + s_],
                                                                                      in1=imp[:, s_, :], op0=ALU.mult, op1=ALU.add),
                                              reads=[U_b, rs_b, imp_b], writes=[imp_b])

                            tiles.append(TileD(qk, 1.0, pv, epi if ct == nct - 1 else None))
                    run_stream(c, R, tiles)
                    c.stream.flush()
                    tpn, tpn_b = R.TPN.next()
                    for s_ in range(4):
                        qt = 4 * qc + s_
                        lo = 128 - 2 * qt
                        w, w_b = W1.next()
                        w2, w2_b = W2.next()
                        m8, m8_b = M8.next()
                        nms, nms_b = NMS.next()
                        kb.op("dve", lambda e: e.tensor_tensor(out=w[:, :], in0=imp[:, s_, :], in1=keept[:, lo:lo + 128], op=ALU.mult),
                              reads=[imp_b, CB], writes=[w_b])
                        kb.op("dve", lambda e: e.tensor_tensor(out=w[:, :], in0=w[:, :], in1=addt[:, lo:lo + 128], op=ALU.add),
                              reads=[w_b, CB], writes=[w_b])
                        kb.op("dve", lambda e: e.memset(w[:, 0:1], 1e6), reads=[w_b], writes=[w_b])
                        kb.op("dve", lambda e: e.max(out=m8[:, 0:8], in_=w[:, :]), reads=[w_b], writes=[m8_b])
                        kb.op("dve", lambda e: e.match_replace(out=w2[:, :], in_to_replace=m8[:, 0:8], in_values=w[:, :], imm_value=-2.0),
                              reads=[w_b, m8_b], writes=[w2_b])
                        kb.op("dve", lambda e: e.max(out=m8[:, 8:16], in_=w2[:, :]), reads=[w2_b, m8_b], writes=[m8_b])
                        kb.op("dve", lambda e: e.tensor_scalar(out=nms[:, :], in0=w[:, :], scalar1=m8[:, 15:16], scalar2=-1.0,
                                                               op0=ALU.is_ge, op1=ALU.add), reads=[w_b, m8_b], writes=[nms_b])
                        kb.op("pe", lambda e: e.transpose(out=tpn[:, s_, :], in_=nms[:, :], identity=ident[:, :]),
                              reads=[nms_b, ident_b], writes=[tpn_b])
                    kb.op("act", lambda e: e.activation(out=nmt[:, :], in_=tpn[:, :, :].rearrange("p a b -> p (a b)"), func=AF.Copy),
                          reads=[tpn_b], writes=[nmt_b])
                    tiles = []
                    for hi in range(4):
                        h = 4 * g + hi
                        qa, qa_b = QA[pr][hi]
                        gb, gb_b = GB.next()
                        kb.dma(gb[:, :, :], bass.AP(Gt, h * 3 * T + t0, [[0, 64], [T, 3], [1, 512]]), reads=[dsb["G"]], writes=[gb_b])
                        st = {}
                        O, O_b = R.OPS.next()
                        nk = 4 * qc + 4
                        k0 = max(0, int((t0 - 127 - ALIBI_CUT * (2.0 ** (h + 1))) // 128) + 1)
                        for kt in range(k0, nk):
                            qk = [(ks[0:69, kt * 128:(kt + 1) * 128], qa[0:69, :], [ks_b, qa_b]),
                                  (eexp[:, kt, :], nmt[:, :], [CB, nmt_b])]
                            d = kt - 4 * qc
                            if d >= 0:
                                qk.append((ident[:, :], cmk[:, d, :], [ident_b, CB]))

                            def pv(P, P_b, kt=kt, nk=nk, k0=k0, O=O, O_b=O_b):
                                kb.op("pe", lambda e: e.matmul(O[:, :], lhsT=vs[:, kt, :], rhs=P[:, :], start=(kt == k0), stop=(kt == nk - 1)),
                                      reads=[vs_b, P_b], writes=[O_b], inc=(kt == nk - 1))

                            def epi(O=O, O_b=O_b, gb=gb, gb_b=gb_b, st=st):
                                rec, rec_b = norm_rec(O, O_b)
                                rg, rg_b = RG.next()
                                t1, t1_b = T1.next()
                                kb.op("pool", lambda e: e.tensor_tensor(out=rg[:, :], in0=rec[:, :], in1=gb[:, 1, :], op=ALU.mult),
                                      reads=[rec_b, gb_b], writes=[rg_b])
                                kb.op("dve", lambda e: e.tensor_tensor(out=t1[:, :], in0=O[0:64, :], in1=rg[:, :], op=ALU.mult),
                                      reads=[O_b, rg_b], writes=[t1_b])
                                st["slc"] = (t1, t1_b)

                            tiles.append(TileD(qk, 1.0, pv, epi if kt == nk - 1 else None))
                        O, O_b = R.OPS.next()
                        wl = []
                        for d in range(4):
                            wl.append((4 * qc + d, cmk[:, d, :]))
                        for d in range(4):
                            kt = 4 * qc - 4 + d
                            if kt >= 0:
                                wl.append((kt, wmk[:, d, :]))
                        nw = len(wl)
                        for wi, (kt, mk) in enumerate(wl):
                            qk = [(kw[0:69, kt * 128:(kt + 1) * 128], qa[0:69, :], [kw_b, qa_b]),
                                  (ident[:, :], mk, [ident_b, CB])]

                            def pv(P, P_b, kt=kt, wi=wi, nw=nw, O=O, O_b=O_b):
                                kb.op("pe", lambda e: e.matmul(O[:, :], lhsT=vw[:, kt, :], rhs=P[:, :], start=(wi == 0), stop=(wi == nw - 1)),
                                      reads=[vw_b, P_b], writes=[O_b], inc=(wi == nw - 1))

                            def epi(h=h, hi=hi, O=O, O_b=O_b, gb=gb, gb_b=gb_b, st=st):
                                rec, rec_b = norm_rec(O, O_b)
                                rg, rg_b = RG.next()
                                t2, t2_b = T1.next()
                                kb.op("pool", lambda e: e.tensor_tensor(out=rg[:, :], in0=rec[:, :], in1=gb[:, 2, :], op=ALU.mult),
                                      reads=[rec_b, gb_b], writes=[rg_b])
                                kb.op("dve", lambda e: e.tensor_tensor(out=t2[:, :], in0=O[0:64, :], in1=rg[:, :], op=ALU.mult),
                                      reads=[O_b, rg_b], writes=[t2_b])
                                t1, t1_b = st["slc"]
                                oc, oc_b = OC[pr][hi]
                                acc, acc_b = ACC.next()
                                ost, ost_b = OST.next()
                                kb.op("pool", lambda e: e.tensor_tensor(out=acc[:, :], in0=oc[:, :], in1=gb[:, 0, :], op=ALU.mult),
                                      reads=[oc_b, gb_b], writes=[acc_b])
                                kb.op("pool", lambda e: e.tensor_tensor(out=acc[:, :], in0=acc[:, :], in1=t1[:, :], op=ALU.add),
                                      reads=[acc_b, t1_b], writes=[acc_b])
                                kb.op("pool", lambda e: e.tensor_tensor(out=ost[:, :], in0=acc[:, :], in1=t2[:, :], op=ALU.add),
                                      reads=[acc_b, t2_b], writes=[ost_b])
                                kb.dma(ds["ON"][h * 64:(h + 1) * 64, t0:t0 + 512], ost[:, :], reads=[ost_b], writes=[dsb["ON"]], q="pool")

                            tiles.append(TileD(qk, 1.0, pv, epi if wi == nw - 1 else None))
                    run_stream(c, R, tiles)
                    c.stream.flush()
            c.stream.flush()
            if getattr(c, "stop_mid", None) == "nsa":
                return
            for h in range(8):
                ka, ka_b = KA[h % 2]
                va, va_b = VA[h % 2]
                kb.dma(ka[0:64, :], ds["KN"][h * 64:(h + 1) * 64, :], reads=[dsb["KN"]], writes=[ka_b])
                kb.dma(ka[64:96, :], ds["KPE"][:, :], reads=[dsb["KPE"]], writes=[ka_b])
                for half in range(2):
                    hs = slice(half * 32, (half + 1) * 32)
                    rs_ = slice(half * 4096, (half + 1) * 4096)
                    kb.dma(va[:, hs, 0:64], ds["VM"][rs_, h * 64:(h + 1) * 64].rearrange("(kt p) d -> p kt d", p=128),
                           reads=[dsb["VM"]], writes=[va_b])
                qms = {}

                def qload(qc, h=h, qms=qms):
                    qm, qm_b = QMT.next()
                    kb.dma(qm[:, :], ds["QM"][h, :, qc * 512:(qc + 1) * 512], reads=[dsb["QM"]], writes=[qm_b])
                    qms[qc] = (qm, qm_b)

                qload(0)
                for qc in range(NCH):
                    t0 = qc * 512
                    if qc + 1 < NCH:
                        qload(qc + 1)
                    qm, qm_b = qms.pop(qc)
                    O, O_b = R.OPS.next()
                    nk = 4 * qc + 4
                    for kt in range(nk):
                        qk = [(ka[0:96, kt * 128:(kt + 1) * 128], qm[0:96, :], [ka_b, qm_b])]
                        d = kt - 4 * qc
                        if d >= 0:
                            qk.append((ident[:, :], cmk[:, d, :], [ident_b, CB]))

                        def pv(P, P_b, kt=kt, nk=nk, O=O, O_b=O_b, va=va, va_b=va_b):
                            kb.op("pe", lambda e: e.matmul(O[:, :], lhsT=va[:, kt, :], rhs=P[:, :], start=(kt == 0), stop=(kt == nk - 1)),
                                  reads=[va_b, P_b], writes=[O_b], inc=True)

                        def epi(h=h, t0=t0, O=O, O_b=O_b):
                            rec, rec_b = norm_rec(O, O_b)
                            ost, ost_b = OST.next()
                            kb.op("dve", lambda e: e.tensor_tensor(out=ost[:, :], in0=O[0:64, :], in1=rec[:, :], op=ALU.mult),
                                  reads=[O_b, rec_b], writes=[ost_b])
                            kb.dma(ds["OM"][h * 64:(h + 1) * 64, t0:t0 + 512], ost[:, :], reads=[ost_b], writes=[dsb["OM"]], q="pool")

                        c.stream.push(TileD(qk, MLA_SCALE, pv, epi if kt == nk - 1 else None))
            c.stream.flush()

CH3 = 256
NCH3 = T // CH3


def phase3(c):
    nc, kb = c.nc, c.kb
    din, dc, ds, dsb = c.din, c.dc, c.ds, c.dsb
    ident, ident_b = c.ident, c.ident_b
    with ExitStack() as es:
        W = Buf("p3w")
        wo = sb(nc, es, "wo", [128, 8, 1024], BF16)
        wpg = sb(nc, es, "wpg", [128, 8, 1024], BF16)
        wpl = sb(nc, es, "wpl", [128, 2, 1024], BF16)
        wd = sb(nc, es, "wd", [128, 22, 1024], BF16)
        lnv = sb(nc, es, "lnv", [128, 6, 1024], F32)
        for i, nm in enumerate(("ln1_g", "ln1_b", "ln2_g", "ln2_b", "ln3_g", "ln3_b")):
            kb.dma(lnv[:, i, :], din[nm][0:1, :].partition_broadcast(128), writes=[W])
        with ExitStack() as es1:
            STG = rot_tiles(nc, es1, "stg3", [128, 2816], F32, 2)
            STB = rot_tiles(nc, es1, "stb3", [128, 2816], BF16, 2)
            k = 0
            for name, dst in (("w_out", wo), ("w_ple_gate", wpg)):
                for ft in range(8):
                    s_, s_b = STG.next()
                    kb.dma(s_[:, 0:1024], din[name][ft * 128:(ft + 1) * 128, :], writes=[s_b])
                    kb.op("dve" if k % 2 else "pool", lambda e: e.tensor_copy(out=dst[:, ft, :], in_=s_[:, 0:1024]), reads=[s_b], writes=[W])
                    k += 1
            for kt in range(2):
                s_, s_b = STG.next()
                kb.dma(s_[:, 0:1024], din["w_ple"][kt * 128:(kt + 1) * 128, :], writes=[s_b])
                kb.op("dve", lambda e: e.tensor_copy(out=wpl[:, kt, :], in_=s_[:, 0:1024]), reads=[s_b], writes=[W])
            for j in range(22):
                s_, s_b = STG.next()
                kb.dma(s_[:, 0:1024], din["w_down"][j * 128:(j + 1) * 128, :], writes=[s_b])
                kb.op("dve" if j % 2 else "pool", lambda e: e.tensor_copy(out=wd[:, j, :], in_=s_[:, 0:1024]), reads=[s_b], writes=[W])
            for ft in range(8):
                for a in range(2):
                    s_, s_b = STG.next()
                    sb_, sb_b = STB.next()
                    kb.dma(s_[:, :], din["w_up"][ft * 128:(ft + 1) * 128, a * DFF:(a + 1) * DFF], writes=[s_b])
                    kb.op("dve" if a else "pool", lambda e: e.tensor_copy(out=sb_[:, :], in_=s_[:, :]), reads=[s_b], writes=[sb_b])
                    kb.dma(ds["WUP"][:, :, a, ft, :].rearrange("j p c -> p j c"), sb_[:, :].rearrange("p (j c) -> p j c", c=128),
                           reads=[sb_b], writes=[dsb["WUP"]])
        kb.barrier()
        XF = rot_tiles(nc, es, "xf3", [128, 1024], F32, 1)
        OT = rot_tiles(nc, es, "oT3", [128, 8, CH3], BF16, 1)
        PF = rot_tiles(nc, es, "pf3", [128, 2, 256], F32, 1)
        PB = rot_tiles(nc, es, "pb3", [128, 2, 256], BF16, 3)
        PT = rot_tiles(nc, es, "pT3", [128, 2, CH3], BF16, 1)
        Y = rot_tiles(nc, es, "y3", [128, 1024], F32, 2)
        Z = rot_tiles(nc, es, "z3", [128, 1024], F32, 1)
        X1 = rot_tiles(nc, es, "x1r", [128, 2, 1024], F32, 2)
        X2 = rot_tiles(nc, es, "x2r", [128, 2, 1024], F32, 2)
        XBF = rot_tiles(nc, es, "xbf3", [128, 1024], BF16, 2)
        X1T = rot_tiles(nc, es, "x1T", [128, 8, CH3], BF16, 2)
        X2T = rot_tiles(nc, es, "x2T", [128, 8, CH3], BF16, 2)
        AT = rot_tiles(nc, es, "aT", [128, 22, CH3], BF16, 1)
        WU = rot_tiles(nc, es, "wu", [128, 2, 8, 128], BF16, 3)
        SG = rot_tiles(nc, es, "sg", [128, CH3], F32, 2)
        ST = rot_tiles(nc, es, "st3", [128, 8], F32, 4)
        PA = rot_tiles(nc, es, "pa3", [128, 512], F32, 3, psum=True)
        PG = rot_tiles(nc, es, "pg3", [128, 2, CH3], F32, 3, psum=True)
        TP = rot_tiles(nc, es, "tp3", [128, 8, 128], BF16, 2, psum=True)

        def ln_stats(y, y_b):
            st, st_b = ST.next()
            z, z_b = Z.next()
            kb.op("act", lambda e: e.activation(out=z[:, :], in_=y[:, :], func=AF.Copy, accum_out=st[:, 0:1]), reads=[y_b], writes=[z_b, st_b])
            kb.op("act", lambda e: e.activation(out=z[:, :], in_=y[:, :], func=AF.Square, accum_out=st[:, 1:2]), reads=[y_b], writes=[z_b, st_b])
            kb.op("pool", lambda e: e.tensor_scalar(out=st[:, 2:4], in0=st[:, 0:2], scalar1=1.0 / D, scalar2=None, op0=ALU.mult), reads=[st_b], writes=[st_b])
            kb.op("pool", lambda e: e.tensor_tensor(out=st[:, 4:5], in0=st[:, 2:3], in1=st[:, 2:3], op=ALU.mult), reads=[st_b], writes=[st_b])
            kb.op("pool", lambda e: e.tensor_tensor(out=st[:, 5:6], in0=st[:, 3:4], in1=st[:, 4:5], op=ALU.subtract), reads=[st_b], writes=[st_b])
            kb.op("pool", lambda e: e.tensor_scalar_add(out=st[:, 5:6], in0=st[:, 5:6], scalar1=LN_EPS), reads=[st_b], writes=[st_b])
            kb.op("act", lambda e: e.activation(out=st[:, 6:7], in_=st[:, 5:6], func=AF.Sqrt), reads=[st_b], writes=[st_b])
            kb.op("dve", lambda e: e.reciprocal(out=st[:, 7:8], in_=st[:, 6:7]), reads=[st_b], writes=[st_b])
            return st, st_b, z, z_b

        def layer_norm(y, y_b, gi, out_ap, out_b):
            st, st_b, z, z_b = ln_stats(y, y_b)
            kb.op("pool", lambda e: e.tensor_scalar(out=z[:, :], in0=y[:, :], scalar1=st[:, 2:3], scalar2=st[:, 7:8],
                                                    op0=ALU.subtract, op1=ALU.mult), reads=[y_b, st_b], writes=[z_b])
            kb.op("pool", lambda e: e.tensor_tensor(out=z[:, :], in0=z[:, :], in1=lnv[:, gi, :], op=ALU.mult), reads=[z_b, W], writes=[z_b])
            if out_ap is None:
                out_ap, out_b = z[:, :], z_b
            kb.op("pool", lambda e: e.tensor_tensor(out=out_ap, in0=z[:, :], in1=lnv[:, gi + 1, :], op=ALU.add), reads=[z_b, W], writes=[out_b])

        def cast_bf(src, src_b):
            xb, xb_b = XBF.next()
            kb.op("dve", lambda e: e.tensor_copy(out=xb[:, :], in_=src), reads=[src_b], writes=[xb_b])
            return xb, xb_b

        def transposes(xb, xb_b, s_, dstT, dstT_b):
            tp, tp_b = TP.next()
            for ft in range(8):
                kb.op("pe", lambda e: e.transpose(out=tp[:, ft, :], in_=xb[:, ft * 128:(ft + 1) * 128], identity=ident[:, :]),
                      reads=[xb_b, ident_b], writes=[tp_b], inc=(ft == 7))
            kb.op("act", lambda e: e.activation(out=dstT[:, :, s_ * 128:(s_ + 1) * 128], in_=tp[:, :, :], func=AF.Copy), reads=[tp_b], writes=[dstT_b])

        S = {}

        def A_mm(ch):
            t0 = ch * CH3
            st = S.setdefault(ch, {})
            oT, oT_b = OT.next()
            pf, pf_b = PF.next()
            kb.dma(oT[:, 0:4, :], ds["ON"].rearrange("(c p) t -> p c t", p=128)[:, :, t0:t0 + CH3], reads=[dsb["ON"]], writes=[oT_b])
            kb.dma(oT[:, 4:8, :], ds["OM"].rearrange("(c p) t -> p c t", p=128)[:, :, t0:t0 + CH3], reads=[dsb["OM"]], writes=[oT_b])
            kb.dma(pf[:, :, :], din["p"][t0:t0 + CH3, :].rearrange("(s p) d -> p s d", p=128), writes=[pf_b])
            pb_, pb_b = PB.next()
            kb.op("pool", lambda e: e.tensor_copy(out=pb_[:, :, :], in_=pf[:, :, :]), reads=[pf_b], writes=[pb_b])
            st["pb"] = (pb_, pb_b)
            st["x1"] = X1.next()
            st["x1T"] = X1T.next()
            st["ys"] = []
            for s_ in range(2):
                y, y_b = Y.next()
                xf, xf_b = XF.next()
                kb.dma(xf[:, :], din["x"][t0 + s_ * 128:t0 + (s_ + 1) * 128, :], writes=[xf_b])
                for n in range(2):
                    pa, pa_b = PA.next()
                    for kt in range(8):
                        kb.op("pe", lambda e: e.matmul(pa[:, :], lhsT=oT[:, kt, s_ * 128:(s_ + 1) * 128], rhs=wo[:, kt, n * 512:(n + 1) * 512],
                                                       start=(kt == 0), stop=(kt == 7)), reads=[oT_b, W], writes=[pa_b], inc=(kt == 7))
                    kb.op("dve", lambda e: e.scalar_tensor_tensor(out=y[:, n * 512:(n + 1) * 512], in0=xf[:, n * 512:(n + 1) * 512], scalar=ALPHA,
                                                                  in1=pa[:, :], op0=ALU.mult, op1=ALU.add), reads=[xf_b, pa_b], writes=[y_b])
                st["ys"].append((y, y_b))

        def A_ln(ch, s_):
            st = S[ch]
            x1, x1_b = st["x1"]
            y, y_b = st["ys"][s_]
            layer_norm(y, y_b, 0, x1[:, s_, :], x1_b)
            st["xb%d" % s_] = cast_bf(x1[:, s_, :], x1_b)

        def A_tr(ch, s_):
            st = S[ch]
            x1T, x1T_b = st["x1T"]
            transposes(*st["xb%d" % s_], s_, x1T, x1T_b)

        def B_up(ch, j):
            st = S[ch]
            x1T, x1T_b = st["x1T"]
            if j == 0:
                st["aT"] = AT.next()
            aT, aT_b = st["aT"]
            wu, wu_b = WU.next()
            kb.dma(wu[:, :, :, :], ds["WUP"][j], reads=[dsb["WUP"]], writes=[wu_b])
            pg, pg_b = PG.next()
            for a in range(2):
                for ft in range(8):
                    kb.op("pe", lambda e: e.matmul(pg[:, a, :], lhsT=wu[:, a, ft, :], rhs=x1T[:, ft, :], start=(ft == 0), stop=(ft == 7)),
                          reads=[wu_b, x1T_b], writes=[pg_b], inc=(a == 1 and ft == 7))
            sg, sg_b = SG.next()
            kb.op("act", lambda e: e.activation(out=sg[:, :], in_=pg[:, 0, :], func=AF.Silu), reads=[pg_b], writes=[sg_b])
            kb.op("dve", lambda e: e.tensor_tensor(out=aT[:, j, :], in0=pg[:, 1, :], in1=sg[:, :], op=ALU.mult), reads=[pg_b, sg_b], writes=[aT_b])

        def B_down(ch):
            st = S[ch]
            aT, aT_b = st["aT"]
            x1, x1_b = st["x1"]
            st["x2"] = X2.next()
            st["x2T"] = X2T.next()
            x2, x2_b = st["x2"]
            ys = []
            for s_ in range(2):
                y, y_b = Y.next()
                for n in range(2):
                    pa, pa_b = PA.next()
                    for j in range(22):
                        kb.op("pe", lambda e: e.matmul(pa[:, :], lhsT=aT[:, j, s_ * 128:(s_ + 1) * 128], rhs=wd[:, j, n * 512:(n + 1) * 512],
                                                       start=(j == 0), stop=(j == 21)), reads=[aT_b, W], writes=[pa_b], inc=(j == 21))
                    kb.op("dve", lambda e: e.scalar_tensor_tensor(out=y[:, n * 512:(n + 1) * 512], in0=x1[:, s_, n * 512:(n + 1) * 512], scalar=ALPHA,
                                                                  in1=pa[:, :], op0=ALU.mult, op1=ALU.add), reads=[x1_b, pa_b], writes=[y_b])
                ys.append((y, y_b))
            for s_ in range(2):
                y, y_b = ys[s_]
                layer_norm(y, y_b, 2, x2[:, s_, :], x2_b)
                st["x2b%d" % s_] = cast_bf(x2[:, s_, :], x2_b)

        def B_tr(ch, s_):
            st = S[ch]
            x2T, x2T_b = st["x2T"]
            transposes(*st["x2b%d" % s_], s_, x2T, x2T_b)

        def C_all(ch):
            t0 = ch * CH3
            st = S.pop(ch)
            x2, x2_b = st["x2"]
            x2T, x2T_b = st["x2T"]
            pb_, pb_b = st["pb"]
            pT, pT_b = PT.next()
            tp, tp_b = TP.next()
            for s_ in range(2):
                for kt in range(2):
                    kb.op("pe", lambda e: e.transpose(out=tp[:, s_ * 2 + kt, :], in_=pb_[:, s_, kt * 128:(kt + 1) * 128], identity=ident[:, :]),
                          reads=[pb_b, ident_b], writes=[tp_b], inc=(s_ == 1 and kt == 1))
            for s_ in range(2):
                kb.op("act", lambda e: e.activation(out=pT[:, :, s_ * 128:(s_ + 1) * 128], in_=tp[:, 2 * s_:2 * s_ + 2, :], func=AF.Copy),
                      reads=[tp_b], writes=[pT_b])
            for s_ in range(2):
                y, y_b = Y.next()
                for n in range(2):
                    pa, pa_b = PA.next()
                    for kt in range(8):
                        kb.op("pe", lambda e: e.matmul(pa[:, :], lhsT=x2T[:, kt, s_ * 128:(s_ + 1) * 128], rhs=wpg[:, kt, n * 512:(n + 1) * 512],
                                                       start=(kt == 0), stop=(kt == 7)), reads=[x2T_b, W], writes=[pa_b], inc=(kt == 7))
                    kb.op("act", lambda e: e.activation(out=y[:, n * 512:(n + 1) * 512], in_=pa[:, :], func=AF.Sigmoid), reads=[pa_b], writes=[y_b])
                    pp, pp_b = PA.next()
                    for kt in range(2):
                        kb.op("pe", lambda e: e.matmul(pp[:, :], lhsT=pT[:, kt, s_ * 128:(s_ + 1) * 128], rhs=wpl[:, kt, n * 512:(n + 1) * 512],
                                                       start=(kt == 0), stop=(kt == 1)), reads=[pT_b, W], writes=[pp_b], inc=(kt == 1))
                    kb.op("dve", lambda e: e.tensor_tensor(out=y[:, n * 512:(n + 1) * 512], in0=pp[:, :], in1=y[:, n * 512:(n + 1) * 512], op=ALU.mult),
                          reads=[pp_b, y_b], writes=[y_b])
                    kb.op("dve", lambda e: e.scalar_tensor_tensor(out=y[:, n * 512:(n + 1) * 512], in0=x2[:, s_, n * 512:(n + 1) * 512], scalar=ALPHA,
                                                                  in1=y[:, n * 512:(n + 1) * 512], op0=ALU.mult, op1=ALU.add),
                          reads=[x2_b, y_b], writes=[y_b])
                layer_norm(y, y_b, 4, None, None)
                z, z_b = Z.items[0]
                kb.dma(c.out[t0 + s_ * 128:t0 + (s_ + 1) * 128, :], z[:, :], reads=[z_b], writes=[c.out_b], q="pool")

        A_mm(0)
        for s_ in range(2):
            A_ln(0, s_)
            A_tr(0, s_)
        for it in range(NCH3 + 1):
            nxt = it + 1 < NCH3
            cur = it < NCH3
            prv = it >= 1
            if nxt:
                A_mm(it + 1)
            if cur:
                for j in range(22):
                    B_up(it, j)
                    if prv and j == 3:
                        B_tr(it - 1, 0)
                    if prv and j == 4:
                        B_tr(it - 1, 1)
                    if nxt and j == 5:
                        A_ln(it + 1, 0)
                    if nxt and j == 7:
                        A_ln(it + 1, 1)
                    if nxt and j == 11:
                        A_tr(it + 1, 0)
                    if nxt and j == 13:
                        A_tr(it + 1, 1)
                    if prv and j == 15:
                        C_all(it - 1)
                B_down(it)
            else:
                B_tr(it - 1, 0)
                B_tr(it - 1, 1)
                C_all(it - 1)


_CACHE = {}


def kernel(**inputs):
    if "nc" not in _CACHE:
        _CACHE["nc"] = build_program()
        _CACHE["consts"] = host_constants()
    nc = _CACHE["nc"]
    consts = _CACHE["consts"]
    B = inputs["x"].shape[0]
    shared = {}
    for k, shp in IN_SPECS.items():
        if k in ("x", "p"):
            continue
        shared[k] = np.ascontiguousarray(np.asarray(inputs[k], dtype=np.float32)[0].reshape(shp))
    for k, v in consts.items():
        shared["c_" + k] = v
    in_maps = []
    for b in range(B):
        m = dict(shared)
        m["x"] = np.ascontiguousarray(np.asarray(inputs["x"], dtype=np.float32)[b])
        m["p"] = np.ascontiguousarray(np.asarray(inputs["p"], dtype=np.float32)[0, b])
        in_maps.append(m)
    res = run_bass_kernel_spmd(nc, in_maps, core_ids=list(range(B)))
    return np.stack([np.asarray(r["out"], dtype=np.float32) for r in res.results], 0)
```

```python
from contextlib import ExitStack
import numpy as np
import ml_dtypes
import concourse.bass as bass
import concourse.mybir as mybir
from concourse.bass_utils import run_bass_kernel_spmd

F32 = mybir.dt.float32
BF16 = mybir.dt.bfloat16
AF = mybir.ActivationFunctionType
ALU = mybir.AluOpType
AX = mybir.AxisListType
bf16 = ml_dtypes.bfloat16

T = 8192
D = 1024
NCH = T // 512
NKT = T // 128
DFF = 2816
BIG = 30000.0
ALPHA = 2.0 ** 0.25
LN_EPS = 1e-5
RMS_EPS = 1e-6
MLA_SCALE = 96.0 ** -0.5
ALIBI_CUT = 200.0


class Buf:
    __slots__ = ("name", "last_w", "readers")

    def __init__(self, name=""):
        self.name = name
        self.last_w = None
        self.readers = {}


class KB:
    NSLOT = 8

    def __init__(self, nc, es):
        self.nc = nc
        self.E = {"pe": nc.tensor, "act": nc.scalar, "dve": nc.vector, "pool": nc.gpsimd, "sp": nc.sync}
        self.sem = {}
        self.seq = {}
        self.known = {}
        for e in self.E:
            self.sem[("e", e)] = es.enter_context(nc.semaphore("s_" + e))
            self.seq[e] = 0
            self.known[e] = {}
        self.slots = {}
        self.slot_rr = {}
        for q in ("sp", "pool"):
            self.slots[q] = []
            self.slot_rr[q] = 0
            for i in range(self.NSLOT):
                key = ("d", q, i)
                self.sem[key] = es.enter_context(nc.semaphore("d_%s%d" % (q, i)))
                self.slots[q].append([key, 0])

    def _collect(self, reads, writes):
        deps = {}
        for b in reads:
            if b.last_w is not None:
                k, v = b.last_w
                o = deps.get(k)
                deps[k] = (max(v, o[0]) if o else v, True)
        for b in writes:
            if b.last_w is not None:
                k, v = b.last_w
                o = deps.get(k)
                if o is None:
                    deps[k] = (v, False)
                elif v > o[0]:
                    deps[k] = (v, o[1])
            for k, v in b.readers.items():
                o = deps.get(k)
                if o is None:
                    deps[k] = (v, False)
                elif v > o[0]:
                    deps[k] = (v, o[1])
        return deps

    def _wait(self, eng, deps):
        kn = self.known[eng]
        for k, (v, raw) in deps.items():
            if k == ("e", eng):
                if eng == "pe" or not raw:
                    continue
            if kn.get(k, 0) >= v:
                continue
            self.E[eng].wait_ge(self.sem[k], v)
            kn[k] = v

    def _record(self, dep, reads, writes):
        k, v = dep
        for b in reads:
            if b.readers.get(k, 0) < v:
                b.readers[k] = v
        for b in writes:
            b.last_w = dep
            b.readers = {}

    def op(self, eng, fn, reads=(), writes=(), inc=True):
        self._wait(eng, self._collect(reads, writes))
        ins = fn(self.E[eng])
        if inc:
            ins.then_inc(self.sem[("e", eng)], 1)
            self.seq[eng] += 1
            dep = (("e", eng), self.seq[eng])
        else:
            dep = (("e", eng), self.seq[eng] + 1)
        self._record(dep, reads, writes)
        return ins

    def dma(self, out, in_, reads=(), writes=(), q="sp", **kw):
        sl = self.slots[q][self.slot_rr[q]]
        self.slot_rr[q] = (self.slot_rr[q] + 1) % self.NSLOT
        deps = self._collect(reads, writes)
        if sl[1] > 0:
            o = deps.get(sl[0])
            if o is None or o[0] < sl[1]:
                deps[sl[0]] = (sl[1], True)
        self._wait(q, deps)
        self.E[q].dma_start(out=out, in_=in_, **kw).then_inc(self.sem[sl[0]], 16)
        sl[1] += 16
        self._record((sl[0], sl[1]), reads, writes)

    def barrier(self):
        cur = {}
        for e in self.E:
            cur[("e", e)] = self.seq[e]
        for q in self.slots:
            for key, cnt in self.slots[q]:
                cur[key] = cnt
        for e in self.E:
            kn = self.known[e]
            for k, v in cur.items():
                if v == 0 or k == ("e", e):
                    continue
                if kn.get(k, 0) >= v:
                    continue
                self.E[e].wait_ge(self.sem[k], v)
                kn[k] = v


class Rot:
    def __init__(self, items):
        self.items = items
        self.i = 0

    def next(self):
        it = self.items[self.i]
        self.i = (self.i + 1) % len(self.items)
        return it


def host_constants():
    c = {}
    c["ident"] = np.eye(128, dtype=np.float32).astype(bf16)
    pos = np.arange(T, dtype=np.float32)
    inv_freq = (10000.0 ** (-np.arange(16, dtype=np.float32) / 16)).astype(np.float32)
    ang = (pos[:, None] * inv_freq[None, :]).astype(np.float32)
    cos = np.cos(ang).astype(np.float32).T
    sin = np.sin(ang).astype(np.float32).T
    c["ropeC"] = np.ascontiguousarray(np.concatenate([cos, cos], 0))
    c["ropeS"] = np.ascontiguousarray(np.concatenate([-sin, sin], 0))
    p = np.arange(128)[:, None]
    j = np.arange(512)[None, :]
    def addmask(valid):
        return np.where(valid, 0.0, -BIG).astype(np.float32).astype(bf16)
    c["cmk"] = addmask(np.stack([(p + 128 * d <= j) for d in range(4)], 1))
    c["wmk"] = addmask(np.stack([(j < 128 * d + p) for d in range(4)], 1))
    c["pmk"] = addmask(np.stack([(16 * p + 31 <= 512 * m + j) for m in range(5)], 1))
    slopes = 2.0 ** (-np.arange(1, 9, dtype=np.float64))
    t = np.arange(T)
    qaug = np.zeros((8, 5, T), np.float32)
    for h in range(8):
        s = slopes[h]
        qaug[h, 0] = s
        qaug[h, 1] = s
        qaug[h, 2] = -s * (t // 128) * 128
        qaug[h, 3] = -s * (t % 128)
        qaug[h, 4] = -s * ((t % 128) - 31)
    c["qaug"] = qaug.astype(bf16)
    ka = np.zeros((5, T), np.float32)
    ka[0] = (t // 128) * 128
    ka[1] = t % 128
    ka[2] = 1
    ka[3] = 1
    c["kaug"] = ka.astype(bf16)
    cc = np.arange(512)
    kc = np.zeros((5, 512), np.float32)
    kc[0] = (cc // 128) * 2048
    kc[1] = (cc % 128) * 16
    kc[2] = 1
    kc[4] = 1
    c["kaugc"] = kc.astype(bf16)
    for k in ("qaug", "kaug", "kaugc"):
        assert np.all(c[k].astype(np.float32) == {"qaug": qaug, "kaug": ka, "kaugc": kc}[k])
    E = np.zeros((128, 64, 128), np.float32)
    for kt in range(64):
        E[2 * kt, kt, :64] = BIG
        E[2 * kt + 1, kt, 64:] = BIG
    c["eexp"] = E.astype(bf16)
    M = np.zeros((512, 128), np.float32)
    for cidx in range(511):
        for s in (cidx, cidx + 1):
            M[cidx, s // 4] += 1
    c["impm"] = M.astype(bf16)
    keep = np.zeros((128, 256), np.float32)
    add = np.zeros((128, 256), np.float32)
    pp = np.arange(128)
    for col in range(256):
        r = col - 128
        if r < -1:
            keep[:, col] = 1
        elif r == -1:
            keep[:, col] = (pp >= 64)
            add[:, col] = 1e6 * (pp < 64)
        elif r == 0:
            add[:, col] = 1e6
        elif r == 1:
            add[:, col] = np.where(pp >= 64, 1e6, -1.0)
        else:
            add[:, col] = -1.0
    c["keept"] = keep
    c["addt"] = add
    return c


CONST_SPECS = {
    "ident": ([128, 128], BF16), "ropeC": ([32, T], F32), "ropeS": ([32, T], F32),
    "cmk": ([128, 4, 512], BF16), "wmk": ([128, 4, 512], BF16), "pmk": ([128, 5, 512], BF16),
    "qaug": ([8, 5, T], BF16), "kaug": ([5, T], BF16), "kaugc": ([5, 512], BF16),
    "eexp": ([128, 64, 128], BF16), "impm": ([512, 128], BF16),
    "keept": ([128, 256], F32), "addt": ([128, 256], F32),
}

IN_SPECS = {
    "x": [T, D], "p": [T, 256], "w_in": [D, 1720], "w_ck1": [2048, 256], "w_ck2": [256, 64],
    "pos_ck": [32, 64], "w_cv1": [2048, 256], "w_cv2": [256, 64], "pos_cv": [32, 64],
    "mla_q_norm": [256, 1], "w_uq": [256, 768], "mla_kv_norm": [128, 1], "w_ukv": [128, 1024],
    "w_out": [D, D], "ln1_g": [1, D], "ln1_b": [1, D], "w_up": [D, 2 * DFF], "w_down": [DFF, D],
    "ln2_g": [1, D], "ln2_b": [1, D], "w_ple_gate": [D, D], "w_ple": [256, D],
    "ln3_g": [1, D], "ln3_b": [1, D],
}

SCRATCH_SPECS = {
    "QN": ([512, T], BF16),
    "KX": ([4, 128, T], BF16),
    "VSW": ([T, 256], BF16),
    "G": ([24, T], F32),
    "KPE": ([32, T], BF16),
    "QM": ([8, 96, T], BF16),
    "KN": ([512, T], BF16),
    "VM": ([T, 512], BF16),
    "ON": ([512, T], BF16),
    "OM": ([512, T], BF16),
    "WUP": ([22, 128, 2, 8, 128], BF16),
}


def sb(nc, es, name, shape, dt):
    return es.enter_context(nc.sbuf_tensor(name, list(shape), dt))


def ps(nc, es, name, shape, dt):
    return es.enter_context(nc.psum_tensor(name, list(shape), dt))


def rot_tiles(nc, es, name, shape, dt, n, psum=False):
    items = []
    for i in range(n):
        t = (ps if psum else sb)(nc, es, "%s%d" % (name, i), shape, dt)
        items.append((t, Buf("%s%d" % (name, i))))
    return Rot(items)


class Ctx:
    pass


def build_program(stop_after=None, debug_scratch=False, stop_mid=None):
    nc = bass.Bass("TRN2", target_bir_lowering=False)
    c = Ctx()
    c.nc = nc
    c.din = {k: nc.dram_tensor(k, v, F32, kind="ExternalInput").ap() for k, v in IN_SPECS.items()}
    c.dc = {k: nc.dram_tensor("c_" + k, v[0], v[1], kind="ExternalInput").ap() for k, v in CONST_SPECS.items()}
    skind = "ExternalOutput" if debug_scratch else "Internal"
    c.ds = {k: nc.dram_tensor("s_" + k, v[0], v[1], kind=skind).ap() for k, v in SCRATCH_SPECS.items()}
    c.dsb = {k: Buf("s_" + k) for k in SCRATCH_SPECS}
    c.out = nc.dram_tensor("out", [T, D], F32, kind="ExternalOutput").ap()
    c.out_b = Buf("out")
    c.stop_mid = stop_mid
    if stop_mid == "compress":
        c.dbg = {"kcmp": nc.dram_tensor("dbg_kcmp", [69, 2, 512], BF16, kind="ExternalOutput").ap(),
                 "vcmp": nc.dram_tensor("dbg_vcmp", [128, 4, 2, 128], BF16, kind="ExternalOutput").ap()}
    with ExitStack() as es0:
        kb = KB(nc, es0)
        c.kb = kb
        c.ident = sb(nc, es0, "ident", [128, 128], BF16)
        c.ident_b = Buf("ident")
        kb.dma(c.ident[:], c.dc["ident"][:], writes=[c.ident_b])
        phases = [phase1, phase2, phase3]
        for ph in phases:
            ph(c)
            kb.barrier()
            if stop_after == ph.__name__:
                break
        kb.barrier()
    return nc


def phase1(c):
    nc, kb = c.nc, c.kb
    din, dc, ds, dsb = c.din, c.dc, c.ds, c.dsb
    with ExitStack() as es:
        wtr = sb(nc, es, "wtr", [128, 8, 1048], BF16)
        wtok = sb(nc, es, "wtok", [128, 8, 640], BF16)
        wkpA = sb(nc, es, "wkpA", [128, 8, 96], BF16)
        wkpB = sb(nc, es, "wkpB", [128, 8, 96], BF16)
        wuqA = sb(nc, es, "wuqA", [128, 2, 8, 96], BF16)
        wuqB = sb(nc, es, "wuqB", [128, 2, 8, 96], BF16)
        wkvK = sb(nc, es, "wkvK", [128, 8, 64], BF16)
        wkvV = sb(nc, es, "wkvV", [128, 8, 64], BF16)
        gq = sb(nc, es, "gq", [128, 2], F32)
        gkv = sb(nc, es, "gkv", [128, 1], F32)
        W = Buf("p1w")
        es_stg = ExitStack()
        stg = rot_tiles(nc, es_stg, "stg", [128, 1720], F32, 2)
        kb.op("pool", lambda e: e.memset(wkpA[:], 0.0), writes=[W])
        kb.op("pool", lambda e: e.memset(wkpB[:], 0.0), writes=[W])
        kb.op("pool", lambda e: e.memset(wuqB[:], 0.0), writes=[W])
        for kt in range(2):
            kb.dma(gq[:, kt:kt + 1], din["mla_q_norm"][kt * 128:(kt + 1) * 128, :], writes=[W])
        kb.dma(gkv[:], din["mla_kv_norm"][:, :], writes=[W])
        for ft in range(8):
            s, sbuf_ = stg.next()
            kb.dma(s[:], din["w_in"][ft * 128:(ft + 1) * 128, :], writes=[sbuf_])
            kb.op("dve", lambda e: e.tensor_copy(out=wtr[:, ft, 0:896], in_=s[:, 0:896]), reads=[sbuf_], writes=[W])
            kb.op("pool", lambda e: e.tensor_copy(out=wtr[:, ft, 896:1024], in_=s[:, 1024:1152]), reads=[sbuf_], writes=[W])
            kb.op("pool", lambda e: e.tensor_copy(out=wtr[:, ft, 1024:1048], in_=s[:, 1280:1304]), reads=[sbuf_], writes=[W])
            kb.op("dve", lambda e: e.tensor_copy(out=wtok[:, ft, 0:384], in_=s[:, 1304:1688]), reads=[sbuf_], writes=[W])
            kb.op("pool", lambda e: e.tensor_copy(out=wtok[:, ft, 384:512], in_=s[:, 896:1024]), reads=[sbuf_], writes=[W])
            kb.op("pool", lambda e: e.tensor_copy(out=wtok[:, ft, 512:640], in_=s[:, 1152:1280]), reads=[sbuf_], writes=[W])
            kb.op("dve", lambda e: e.tensor_copy(out=wkpA[:, ft, 64:96], in_=s[:, 1688:1720]), reads=[sbuf_], writes=[W])
            kb.op("dve", lambda e: e.tensor_copy(out=wkpB[:, ft, 64:80], in_=s[:, 1704:1720]), reads=[sbuf_], writes=[W])
            kb.op("dve", lambda e: e.tensor_copy(out=wkpB[:, ft, 80:96], in_=s[:, 1688:1704]), reads=[sbuf_], writes=[W])
        for kt in range(2):
            s, sbuf_ = stg.next()
            kb.dma(s[:, 0:768], din["w_uq"][kt * 128:(kt + 1) * 128, :], writes=[sbuf_])
            s3 = s[:, 0:768].rearrange("p (h c) -> p h c", c=96)
            kb.op("dve", lambda e: e.tensor_scalar(out=wuqA[:, kt, :, :], in0=s3, scalar1=gq[:, kt:kt + 1], scalar2=None,
                                                   op0=ALU.mult), reads=[sbuf_, W], writes=[W])
            kb.op("dve", lambda e: e.tensor_scalar(out=wuqB[:, kt, :, 64:80], in0=s3[:, :, 80:96], scalar1=gq[:, kt:kt + 1],
                                                   scalar2=None, op0=ALU.mult), reads=[sbuf_, W], writes=[W])
            kb.op("dve", lambda e: e.tensor_scalar(out=wuqB[:, kt, :, 80:96], in0=s3[:, :, 64:80], scalar1=gq[:, kt:kt + 1],
                                                   scalar2=None, op0=ALU.mult), reads=[sbuf_, W], writes=[W])
        s, sbuf_ = stg.next()
        kb.dma(s[:, 0:1024], din["w_ukv"][:, :], writes=[sbuf_])
        s3 = s[:, 0:1024].rearrange("p (h c) -> p h c", c=128)
        kb.op("dve", lambda e: e.tensor_scalar(out=wkvK[:, :, :], in0=s3[:, :, 0:64], scalar1=gkv[:, 0:1], scalar2=None,
                                               op0=ALU.mult), reads=[sbuf_, W], writes=[W])
        kb.op("dve", lambda e: e.tensor_scalar(out=wkvV[:, :, :], in0=s3[:, :, 64:128], scalar1=gkv[:, 0:1], scalar2=None,
                                               op0=ALU.mult), reads=[sbuf_, W], writes=[W])
        kb.barrier()
        es_stg.close()
        XF = rot_tiles(nc, es, "xf", [128, 4, 1024], F32, 2)
        XT = rot_tiles(nc, es, "xT", [128, 8, 512], BF16, 2)
        STT = rot_tiles(nc, es, "stT", [128, 8, 512], BF16, 2)
        GST = rot_tiles(nc, es, "gst", [24, 512], F32, 2)
        RC = rot_tiles(nc, es, "rc", [96, 2, 512], F32, 3)
        KST = rot_tiles(nc, es, "kst", [96, 512], BF16, 2)
        RT = rot_tiles(nc, es, "rt", [96, 2, 512], F32, 2)
        VST = rot_tiles(nc, es, "vst", [128, 4, 256], BF16, 2)
        SS = rot_tiles(nc, es, "ss", [128, 4], F32, 2)
        JK = rot_tiles(nc, es, "jk", [128, 256], F32, 2)
        CN = rot_tiles(nc, es, "cn", [128, 384], BF16, 2)
        CT = rot_tiles(nc, es, "cT", [128, 3, 512], BF16, 2)
        QST = rot_tiles(nc, es, "qst", [96, 8, 512], BF16, 2)
        KNST = rot_tiles(nc, es, "knst", [128, 4, 512], BF16, 2)
        VMST = rot_tiles(nc, es, "vmst", [128, 4, 512], BF16, 2)
        TP = rot_tiles(nc, es, "tp", [128, 1024], BF16, 2, psum=True)
        TPQ = rot_tiles(nc, es, "tpq", [128, 3, 128], BF16, 1, psum=True)
        PS = rot_tiles(nc, es, "ps", [128, 512], F32, 5, psum=True)
        ident, ident_b = c.ident, c.ident_b
        xv = din["x"]
        XB = rot_tiles(nc, es, "xbb", [128, 4, 1024], BF16, 2)
        LS = {}

        def L_load(ch):
            t0 = ch * 512
            xf, xf_b = XF.next()
            kb.dma(xf[:], xv[t0:t0 + 512, :].rearrange("(s p) d -> p s d", p=128), writes=[xf_b])
            rc, rc_b = RC.next()
            kb.dma(rc[64:96, 0, :], dc["ropeC"][:, t0:t0 + 512], writes=[rc_b])
            kb.dma(rc[64:96, 1, :], dc["ropeS"][:, t0:t0 + 512], writes=[rc_b])
            LS[ch] = {"xf": (xf, xf_b), "rc": (rc, rc_b)}

        def L_cast(ch):
            xf, xf_b = LS[ch]["xf"]
            xb, xb_b = XB.next()
            kb.op("dve", lambda e: e.tensor_copy(out=xb[:, 0:3, :], in_=xf[:, 0:3, :]), reads=[xf_b], writes=[xb_b])
            kb.op("pool", lambda e: e.tensor_copy(out=xb[:, 3:4, :], in_=xf[:, 3:4, :]), reads=[xf_b], writes=[xb_b])
            LS[ch]["xb"] = (xb, xb_b)

        def L_tr(ch):
            xb, xb_b = LS[ch]["xb"]
            xT, xT_b = XT.next()
            for f2 in range(4):
                tp, tp_b = TP.next()
                for fi in range(2):
                    ft = f2 * 2 + fi
                    for s_ in range(4):
                        last = (fi == 1 and s_ == 3)
                        kb.op("pe", lambda e: e.transpose(out=tp[:, fi * 512 + s_ * 128: fi * 512 + (s_ + 1) * 128],
                                                          in_=xb[:, s_, ft * 128:(ft + 1) * 128], identity=ident[:]),
                              reads=[xb_b, ident_b], writes=[tp_b], inc=last)
                dst = xT[:, 2 * f2:2 * f2 + 2, :].rearrange("p a t -> p (a t)")
                kb.op("act" if f2 % 2 else "dve",
                      (lambda e: e.activation(out=dst, in_=tp[:], func=AF.Copy)) if f2 % 2 else
                      (lambda e: e.tensor_copy(out=dst, in_=tp[:])), reads=[tp_b], writes=[xT_b])
            LS[ch]["xT"] = (xT, xT_b)

        L_load(0)
        L_cast(0)
        L_tr(0)
        if NCH > 1:
            L_load(1)
        for ch in range(NCH):
            t0 = ch * 512
            xT, xT_b = LS[ch]["xT"]
            rc, rc_b = LS[ch]["rc"]
            if ch + 2 < NCH:
                L_load(ch + 2)
            if ch + 1 < NCH:
                L_cast(ch + 1)
            stT, stT_b = STT.next()
            for ct in range(8):
                pt, pt_b = PS.next()
                for ft in range(8):
                    kb.op("pe", lambda e: e.matmul(pt[:, :], lhsT=wtr[:, ft, ct * 128:(ct + 1) * 128], rhs=xT[:, ft, :],
                                                   start=(ft == 0), stop=(ft == 7)),
                          reads=[W, xT_b], writes=[pt_b], inc=(ft == 7))
                if ct < 4:
                    kb.op("act", lambda e: e.activation(out=stT[:, ct, :], in_=pt[:, :], func=AF.Copy, scale=0.125),
                          reads=[pt_b], writes=[stT_b])
                else:
                    kb.op("dve", lambda e: e.tensor_copy(out=stT[:, ct, :], in_=pt[:, :]), reads=[pt_b], writes=[stT_b])
            kb.dma(ds["QN"].rearrange("(c p) t -> p c t", p=128)[:, :, t0:t0 + 512], stT[:, 0:4, :],
                   reads=[stT_b], writes=[dsb["QN"]])
            kb.dma(ds["KX"].rearrange("k p t -> p k t")[:, :, t0:t0 + 512], stT[:, 4:8, :],
                   reads=[stT_b], writes=[dsb["KX"]])
            pt, pt_b = PS.next()
            for ft in range(8):
                kb.op("pe", lambda e: e.matmul(pt[0:24, :], lhsT=wtr[:, ft, 1024:1048], rhs=xT[:, ft, :],
                                               start=(ft == 0), stop=(ft == 7)),
                      reads=[W, xT_b], writes=[pt_b], inc=(ft == 7))
            gst, gst_b = GST.next()
            kb.op("act", lambda e: e.activation(out=gst[:, :], in_=pt[0:24, :], func=AF.Sigmoid), reads=[pt_b], writes=[gst_b])
            kb.dma(ds["G"][:, t0:t0 + 512], gst[:, :], reads=[gst_b], writes=[dsb["G"]])
            pa, pa_b = PS.next()
            for ft in range(8):
                kb.op("pe", lambda e: e.matmul(pa[0:96, :], lhsT=wkpA[:, ft, :], rhs=xT[:, ft, :], start=(ft == 0), stop=(ft == 7)),
                      reads=[W, xT_b], writes=[pa_b], inc=(ft == 7))
            pb, pb_b = PS.next()
            for ft in range(8):
                kb.op("pe", lambda e: e.matmul(pb[0:96, :], lhsT=wkpB[:, ft, :], rhs=xT[:, ft, :], start=(ft == 0), stop=(ft == 7)),
                      reads=[W, xT_b], writes=[pb_b], inc=(ft == 7))
            rt, rt_b = RT.next()
            kst, kst_b = KST.next()
            kb.op("dve", lambda e: e.tensor_tensor(out=rt[64:96, 0, :], in0=pa[64:96, :], in1=rc[64:96, 0, :], op=ALU.mult),
                  reads=[pa_b, rc_b], writes=[rt_b])
            kb.op("dve", lambda e: e.tensor_tensor(out=rt[64:96, 1, :], in0=pb[64:96, :], in1=rc[64:96, 1, :], op=ALU.mult),
                  reads=[pb_b, rc_b], writes=[rt_b])
            kb.op("pool", lambda e: e.tensor_tensor(out=kst[64:96, :], in0=rt[64:96, 0, :], in1=rt[64:96, 1, :], op=ALU.add),
                  reads=[rt_b], writes=[kst_b])
            kb.dma(ds["KPE"][:, t0:t0 + 512], kst[64:96, :], reads=[kst_b], writes=[dsb["KPE"]])
            if ch + 1 < NCH:
                L_tr(ch + 1)
            vst, vst_b = VST.next()
            cT, cT_b = CT.next()
            for s_ in range(4):
                p1, p1_b = PS.next()
                for ft in range(8):
                    kb.op("pe", lambda e: e.matmul(p1[:, 0:384], lhsT=xT[:, ft, s_ * 128:(s_ + 1) * 128], rhs=wtok[:, ft, 0:384],
                                                   start=(ft == 0), stop=(ft == 7)),
                          reads=[W, xT_b], writes=[p1_b], inc=(ft == 7))
                p2, p2_b = PS.next()
                for ft in range(8):
                    kb.op("pe", lambda e: e.matmul(p2[:, 0:256], lhsT=xT[:, ft, s_ * 128:(s_ + 1) * 128], rhs=wtok[:, ft, 384:640],
                                                   start=(ft == 0), stop=(ft == 7)),
                          reads=[W, xT_b], writes=[p2_b], inc=(ft == 7))
                kb.op("dve", lambda e: e.tensor_copy(out=vst[:, s_, :], in_=p2[:, 0:256]), reads=[p2_b], writes=[vst_b])
                ss, ss_b = SS.next()
                jk, jk_b = JK.next()
                kb.op("act", lambda e: e.activation(out=jk[:, 0:256], in_=p1[:, 0:256], func=AF.Square, accum_out=ss[:, 0:1]),
                      reads=[p1_b], writes=[jk_b, ss_b])
                kb.op("act", lambda e: e.activation(out=jk[:, 0:128], in_=p1[:, 256:384], func=AF.Square, accum_out=ss[:, 1:2]),
                      reads=[p1_b], writes=[jk_b, ss_b])
                kb.op("dve", lambda e: e.tensor_scalar(out=ss[:, 0:1], in0=ss[:, 0:1], scalar1=1.0 / 256, scalar2=RMS_EPS,
                                                       op0=ALU.mult, op1=ALU.add), reads=[ss_b], writes=[ss_b])
                kb.op("dve", lambda e: e.tensor_scalar(out=ss[:, 1:2], in0=ss[:, 1:2], scalar1=1.0 / 128, scalar2=RMS_EPS,
                                                       op0=ALU.mult, op1=ALU.add), reads=[ss_b], writes=[ss_b])
                kb.op("act", lambda e: e.activation(out=ss[:, 2:4], in_=ss[:, 0:2], func=AF.Sqrt), reads=[ss_b], writes=[ss_b])
                kb.op("dve", lambda e: e.reciprocal(out=ss[:, 0:2], in_=ss[:, 2:4]), reads=[ss_b], writes=[ss_b])
                cn, cn_b = CN.next()
                kb.op("dve", lambda e: e.tensor_scalar(out=cn[:, 0:256], in0=p1[:, 0:256], scalar1=ss[:, 0:1], scalar2=None,
                                                       op0=ALU.mult), reads=[p1_b, ss_b], writes=[cn_b])
                kb.op("dve", lambda e: e.tensor_scalar(out=cn[:, 256:384], in0=p1[:, 256:384], scalar1=ss[:, 1:2], scalar2=None,
                                                       op0=ALU.mult), reads=[p1_b, ss_b], writes=[cn_b])
                tq, tq_b = TPQ.next()
                for i in range(3):
                    kb.op("pe", lambda e: e.transpose(out=tq[:, i, :], in_=cn[:, i * 128:(i + 1) * 128], identity=ident[:]),
                          reads=[cn_b, ident_b], writes=[tq_b], inc=(i == 2))
                kb.op("act", lambda e: e.activation(out=cT[:, :, s_ * 128:(s_ + 1) * 128], in_=tq[:, :, :], func=AF.Copy),
                      reads=[tq_b], writes=[cT_b])
            kb.dma(ds["VSW"][t0:t0 + 512, :].rearrange("(s p) c -> p s c", p=128), vst[:, :, :], reads=[vst_b], writes=[dsb["VSW"]])
            qst, qst_b = QST.next()
            for h in range(8):
                pa, pa_b = PS.next()
                for kt in range(2):
                    kb.op("pe", lambda e: e.matmul(pa[0:96, :], lhsT=wuqA[:, kt, h, :], rhs=cT[:, kt, :], start=(kt == 0), stop=(kt == 1)),
                          reads=[W, cT_b], writes=[pa_b], inc=(kt == 1))
                pb, pb_b = PS.next()
                for kt in range(2):
                    kb.op("pe", lambda e: e.matmul(pb[0:96, :], lhsT=wuqB[:, kt, h, :], rhs=cT[:, kt, :], start=(kt == 0), stop=(kt == 1)),
                          reads=[W, cT_b], writes=[pb_b], inc=(kt == 1))
                kb.op("act", lambda e: e.activation(out=qst[0:64, h, :], in_=pa[0:64, :], func=AF.Copy), reads=[pa_b], writes=[qst_b])
                rt, rt_b = RT.next()
                kb.op("dve", lambda e: e.tensor_tensor(out=rt[64:96, 0, :], in0=pa[64:96, :], in1=rc[64:96, 0, :], op=ALU.mult),
                      reads=[pa_b, rc_b], writes=[rt_b])
                kb.op("dve", lambda e: e.tensor_tensor(out=rt[64:96, 1, :], in0=pb[64:96, :], in1=rc[64:96, 1, :], op=ALU.mult),
                      reads=[pb_b, rc_b], writes=[rt_b])
                kb.op("pool", lambda e: e.tensor_tensor(out=qst[64:96, h, :], in0=rt[64:96, 0, :], in1=rt[64:96, 1, :], op=ALU.add),
                      reads=[rt_b], writes=[qst_b])
            kb.dma(ds["QM"].rearrange("h r t -> r h t")[:, :, t0:t0 + 512], qst[:, :, :], reads=[qst_b], writes=[dsb["QM"]])
            knst, knst_b = KNST.next()
            for hp in range(4):
                pt, pt_b = PS.next()
                kb.op("pe", lambda e: e.matmul(pt[:, :], lhsT=wkvK[:, 2 * hp:2 * hp + 2, :].rearrange("p a c -> p (a c)"),
                                               rhs=cT[:, 2, :], start=True, stop=True), reads=[W, cT_b], writes=[pt_b])
                kb.op("act" if hp % 2 else "dve",
                      (lambda e: e.activation(out=knst[:, hp, :], in_=pt[:, :], func=AF.Copy)) if hp % 2 else
                      (lambda e: e.tensor_copy(out=knst[:, hp, :], in_=pt[:, :])), reads=[pt_b], writes=[knst_b])
            kb.dma(ds["KN"].rearrange("(c p) t -> p c t", p=128)[:, :, t0:t0 + 512], knst[:, :, :], reads=[knst_b], writes=[dsb["KN"]])
            vmst, vmst_b = VMST.next()
            for s_ in range(4):
                pt, pt_b = PS.next()
                kb.op("pe", lambda e: e.matmul(pt[:, :], lhsT=cT[:, 2, s_ * 128:(s_ + 1) * 128],
                                               rhs=wkvV[:, :, :].rearrange("p a c -> p (a c)"), start=True, stop=True),
                      reads=[W, cT_b], writes=[pt_b])
                kb.op("act" if s_ % 2 else "dve",
                      (lambda e: e.activation(out=vmst[:, s_, :], in_=pt[:, :], func=AF.Copy)) if s_ % 2 else
                      (lambda e: e.tensor_copy(out=vmst[:, s_, :], in_=pt[:, :])), reads=[pt_b], writes=[vmst_b])
            kb.dma(ds["VM"][t0:t0 + 512, :].rearrange("(s p) c -> p s c", p=128), vmst[:, :, :], reads=[vmst_b], writes=[dsb["VM"]])


class TileD:
    __slots__ = ("qk", "scale", "pv", "epi")

    def __init__(self, qk, scale, pv, epi=None):
        self.qk = qk
        self.scale = scale
        self.pv = pv
        self.epi = epi


class Stream:
    def __init__(self, c, R, L=2):
        self.c, self.R, self.L = c, R, L
        self.pending = []

    def push(self, t):
        kb, R = self.c.kb, self.R
        S, S_b = R.SPS.next()
        nq = len(t.qk)
        for a, (lh, rh, rb) in enumerate(t.qk):
            kb.op("pe", lambda e: e.matmul(S[:, :], lhsT=lh, rhs=rh, start=(a == 0), stop=(a == nq - 1)),
                  reads=rb, writes=[S_b], inc=(a == nq - 1))
        P, P_b = R.PSB.next()
        kb.op("act", lambda e: e.activation(out=P[:, :], in_=S[:, :], func=AF.Exp, scale=t.scale), reads=[S_b], writes=[P_b])
        self.pending.append((t, P, P_b))
        if len(self.pending) > self.L:
            self._pop()

    def _pop(self):
        t, P, P_b = self.pending.pop(0)
        t.pv(P, P_b)
        if t.epi is not None:
            t.epi()

    def flush(self):
        while self.pending:
            self._pop()


def run_stream(c, R, tiles, L=2):
    st = c.stream
    for t in tiles:
        st.push(t)


def phase2(c):
    nc, kb = c.nc, c.kb
    din, dc, ds, dsb = c.din, c.dc, c.ds, c.dsb
    ident, ident_b = c.ident, c.ident_b
    with ExitStack() as esA:
        kcmp = sb(nc, esA, "kcmp", [69, 2, 512], BF16)
        kcmp_b = Buf("kcmp")
        vcmp = sb(nc, esA, "vcmp", [128, 4, 2, 128], BF16)
        vcmp_b = Buf("vcmp")
        CB = Buf("consts2")
        cmk = sb(nc, esA, "cmk", [128, 4, 512], BF16)
        wmk = sb(nc, esA, "wmk", [128, 4, 512], BF16)
        pmk = sb(nc, esA, "pmk", [128, 5, 512], BF16)
        eexp = sb(nc, esA, "eexp", [128, 64, 128], BF16)
        impm = sb(nc, esA, "impm", [128, 4, 128], BF16)
        keept = sb(nc, esA, "keept", [128, 256], F32)
        addt = sb(nc, esA, "addt", [128, 256], F32)
        kb.dma(cmk[:], dc["cmk"][:], writes=[CB])
        kb.dma(wmk[:], dc["wmk"][:], writes=[CB])
        kb.dma(pmk[:], dc["pmk"][:], writes=[CB])
        kb.dma(eexp[:], dc["eexp"][:], writes=[CB])
        kb.dma(impm[:], dc["impm"].rearrange("(ct p) b -> p ct b", p=128), writes=[CB])
        kb.dma(keept[:], dc["keept"][:], writes=[CB])
        kb.dma(addt[:], dc["addt"][:], writes=[CB])
        kb.op("pool", lambda e: e.memset(kcmp[:], 0.0), writes=[kcmp_b])
        kb.op("pool", lambda e: e.memset(vcmp[:], 1.0), writes=[vcmp_b])
        for g in range(2):
            kb.dma(kcmp[64:69, g, :], dc["kaugc"][:, :], writes=[kcmp_b])
        with ExitStack() as es:
            kcg = sb(nc, es, "kcg", [64, 4, T], BF16)
            kcg_b = Buf("kcg")
            for kind in range(2):
                for g in range(2):
                    kb.dma(kcg[:, kind * 2 + g, :], ds["KX"][kind, g * 64:(g + 1) * 64, :], reads=[dsb["KX"]], writes=[kcg_b])
            w1s = sb(nc, es, "w1s", [64, 32, 256], F32)
            w1s_b = Buf("w1s")
            w1 = [sb(nc, es, "w1_%d" % k, [64, 32, 256], BF16) for k in range(2)]
            w2s = sb(nc, es, "w2s", [128, 2, 64], F32)
            w2s_b = Buf("w2s")
            w2 = [sb(nc, es, "w2_%d" % k, [128, 2, 64], BF16) for k in range(2)]
            posS = sb(nc, es, "posS", [32, 64], F32)
            posS_b = Buf("posS")
            posB = sb(nc, es, "posB", [32, 64], BF16)
            posB_b = Buf("posB")
            posT = sb(nc, es, "posT", [64, 2, 32], BF16)
            b1 = sb(nc, es, "b1", [128, 4], F32)
            Wc = Buf("wc")
            tpp = ps(nc, es, "tpp", [64, 32], BF16)
            tpp_b = Buf("tpp")
            pb1 = ps(nc, es, "pb1", [128, 4], F32)
            pb1_b = Buf("pb1")
            PSC = rot_tiles(nc, es, "psc", [128, 512], F32, 4, psum=True)
            XG = rot_tiles(nc, es, "xg", [128, 512], F32, 2)
            X2 = rot_tiles(nc, es, "x2", [128, 512], F32, 2)
            GL = rot_tiles(nc, es, "gl", [128, 512], BF16, 4)
            for gl, gl_b in GL.items:
                kb.op("pool", lambda e: e.memset(gl[:], 0.0), writes=[gl_b])
            names = [("w_ck1", "w_ck2", "pos_ck"), ("w_cv1", "w_cv2", "pos_cv")]
            for kind, (n1, n2, npos) in enumerate(names):
                kb.dma(w1s[:], din[n1].rearrange("(l d) n -> d l n", d=64), writes=[w1s_b])
                kb.op("dve", lambda e: e.tensor_copy(out=w1[kind][:, 0:16, :], in_=w1s[:, 0:16, :]), reads=[w1s_b], writes=[Wc])
                kb.op("pool", lambda e: e.tensor_copy(out=w1[kind][:, 16:32, :], in_=w1s[:, 16:32, :]), reads=[w1s_b], writes=[Wc])
                kb.dma(w2s[:], din[n2].rearrange("(c p) n -> p c n", p=128), writes=[w2s_b])
                kb.op("dve", lambda e: e.tensor_copy(out=w2[kind][:], in_=w2s[:]), reads=[w2s_b], writes=[Wc])
                kb.dma(posS[:], din[npos][:, :], writes=[posS_b])
                kb.op("dve", lambda e: e.tensor_copy(out=posB[:], in_=posS[:]), reads=[posS_b], writes=[posB_b])
                kb.op("pe", lambda e: e.transpose(out=tpp[:, :], in_=posB[:, :], identity=ident[0:32, 0:32]),
                      reads=[posB_b, ident_b], writes=[tpp_b])
                kb.op("dve", lambda e: e.tensor_copy(out=posT[:, kind, :], in_=tpp[:, :]), reads=[tpp_b], writes=[Wc])
            for kind in range(2):
                for hc in range(2):
                    idx = kind * 2 + hc
                    for l in range(32):
                        kb.op("pe", lambda e: e.matmul(pb1[:, idx:idx + 1], lhsT=w1[kind][:, l, hc * 128:(hc + 1) * 128],
                                                       rhs=posT[:, kind, l:l + 1], start=(l == 0), stop=(l == 31)),
                              reads=[Wc], writes=[pb1_b], inc=(l == 31))
            kb.op("dve", lambda e: e.tensor_copy(out=b1[:], in_=pb1[:]), reads=[pb1_b], writes=[Wc])
            for kind in range(2):
                for g in range(2):
                    gls = []
                    for hc in range(2):
                        idx = kind * 2 + hc
                        pm, pm_b = PSC.next()
                        for l in range(32):
                            kb.op("pe", lambda e: e.matmul(pm[:, 0:511], lhsT=w1[kind][:, l, hc * 128:(hc + 1) * 128],
                                                           rhs=kcg[:, kind * 2 + g, l:l + 8161:16], start=(l == 0), stop=(l == 31)),
                                  reads=[Wc, kcg_b], writes=[pm_b], inc=(l == 31))
                        xg, xg_b = XG.next()
                        x2, x2_b = X2.next()
                        gl, gl_b = GL.next()
                        kb.op("dve", lambda e: e.tensor_scalar(out=xg[:, 0:511], in0=pm[:, 0:511], scalar1=b1[:, idx:idx + 1],
                                                               scalar2=None, op0=ALU.add), reads=[pm_b, Wc], writes=[xg_b])
                        kb.op("pool", lambda e: e.tensor_tensor(out=x2[:, 0:511], in0=xg[:, 0:511], in1=xg[:, 0:511], op=ALU.mult),
                              reads=[xg_b], writes=[x2_b])
                        kb.op("dve", lambda e: e.tensor_scalar(out=x2[:, 0:511], in0=x2[:, 0:511], scalar1=0.044715, scalar2=1.0,
                                                               op0=ALU.mult, op1=ALU.add), reads=[x2_b], writes=[x2_b])
                        kb.op("pool", lambda e: e.tensor_tensor(out=x2[:, 0:511], in0=x2[:, 0:511], in1=xg[:, 0:511], op=ALU.mult),
                              reads=[x2_b, xg_b], writes=[x2_b])
                        kb.op("act", lambda e: e.activation(out=x2[:, 0:511], in_=x2[:, 0:511], func=AF.Sigmoid, scale=1.5957691216),
                              reads=[x2_b], writes=[x2_b])
                        kb.op("dve", lambda e: e.tensor_tensor(out=gl[:, 0:511], in0=x2[:, 0:511], in1=xg[:, 0:511], op=ALU.mult),
                              reads=[x2_b, xg_b], writes=[gl_b])
                        gls.append((gl, gl_b))
                    if kind == 0:
                        pk, pk_b = PSC.next()
                        for hc in range(2):
                            kb.op("pe", lambda e: e.matmul(pk[0:64, 0:511], lhsT=w2[0][:, hc, :], rhs=gls[hc][0][:, 0:511],
                                                           start=(hc == 0), stop=(hc == 1)),
                                  reads=[Wc, gls[hc][1]], writes=[pk_b], inc=(hc == 1))
                        kb.op("act", lambda e: e.activation(out=kcmp[0:64, g, 0:511], in_=pk[0:64, 0:511], func=AF.Copy),
                              reads=[pk_b], writes=[kcmp_b])
                    else:
                        for ct in range(4):
                            pv, pv_b = PSC.next()
                            for hc in range(2):
                                kb.op("pe", lambda e: e.matmul(pv[:, 0:64], lhsT=gls[hc][0][:, ct * 128:(ct + 1) * 128], rhs=w2[1][:, hc, :],
                                                               start=(hc == 0), stop=(hc == 1)),
                                      reads=[Wc, gls[hc][1]], writes=[pv_b], inc=(hc == 1))
                            kb.op("act", lambda e: e.activation(out=vcmp[:, ct, g, 0:64], in_=pv[:, 0:64], func=AF.Copy),
                                  reads=[pv_b], writes=[vcmp_b])
        kb.barrier()
        if getattr(c, "stop_mid", None) == "compress":
            dbg = c.dbg
            kb.dma(dbg["kcmp"][:], kcmp[:], reads=[kcmp_b])
            kb.dma(dbg["vcmp"][:], vcmp[:], reads=[vcmp_b])
            return
        with ExitStack() as es:
            R = Ctx()
            R.SPS = rot_tiles(nc, es, "sps", [128, 512], F32, 3, psum=True)
            R.OPS = rot_tiles(nc, es, "ops", [128, 512], F32, 2, psum=True)
            R.UPS = rot_tiles(nc, es, "ups", [128, 4, 128], F32, 2, psum=True)
            R.TPN = rot_tiles(nc, es, "tpn", [128, 4, 128], BF16, 1, psum=True)
            R.PSB = rot_tiles(nc, es, "psb", [128, 512], BF16, 4)
            KA = rot_tiles(nc, es, "ka", [96, T], BF16, 2).items
            VA = rot_tiles(nc, es, "va", [128, 64, 128], BF16, 2).items
            for va, va_b in VA:
                kb.op("pool", lambda e: e.memset(va[:], 1.0), writes=[va_b])
            QA = [[(sb(nc, es, "qa%d%d" % (pr, hi), [69, 512], BF16), Buf("qa")) for hi in range(4)] for pr in range(2)]
            OC = [[(sb(nc, es, "oc%d%d" % (pr, hi), [64, 512], F32), Buf("oc")) for hi in range(4)] for pr in range(2)]
            QMT = rot_tiles(nc, es, "qmt", [96, 512], BF16, 3)
            GB = rot_tiles(nc, es, "gb", [64, 3, 512], F32, 4)
            NMT = rot_tiles(nc, es, "nmt", [128, 512], BF16, 2)
            IMP = rot_tiles(nc, es, "imp", [128, 4, 128], F32, 2)
            DEN = rot_tiles(nc, es, "den", [64, 512], F32, 2)
            REC = rot_tiles(nc, es, "rec", [64, 512], F32, 2)
            RG = rot_tiles(nc, es, "rg", [64, 512], F32, 2)
            T1 = rot_tiles(nc, es, "t1", [64, 512], F32, 4)
            ACC = rot_tiles(nc, es, "acc", [64, 512], F32, 2)
            OST = rot_tiles(nc, es, "ost", [64, 512], BF16, 3)
            W1 = rot_tiles(nc, es, "w1k", [128, 128], F32, 2)
            W2 = rot_tiles(nc, es, "w2k", [128, 128], F32, 2)
            M8 = rot_tiles(nc, es, "m8", [128, 16], F32, 2)
            NMS = rot_tiles(nc, es, "nms", [128, 128], BF16, 2)
            RS = rot_tiles(nc, es, "rs", [128, 8], F32, 2)
            Gt = ds["G"].tensor
            c.stream = Stream(c, R)

            def norm_rec(O, O_b):
                den, den_b = DEN.next()
                rec, rec_b = REC.next()
                kb.op("dve", lambda e: e.tensor_scalar_max(out=den[:, :], in0=O[64:128, :], scalar1=1e-30), reads=[O_b], writes=[den_b])
                kb.op("dve", lambda e: e.reciprocal(out=rec[:, :], in_=den[:, :]), reads=[den_b], writes=[rec_b])
                return rec, rec_b

            for g in range(2):
                (ks, ks_b), (kw, kw_b) = KA
                (vs, vs_b), (vw, vw_b) = VA
                kb.dma(ks[0:64, :], ds["KX"][2, g * 64:(g + 1) * 64, :], reads=[dsb["KX"]], writes=[ks_b])
                kb.dma(ks[64:69, :], dc["kaug"][:, :], writes=[ks_b])
                kb.dma(kw[0:64, :], ds["KX"][3, g * 64:(g + 1) * 64, :], reads=[dsb["KX"]], writes=[kw_b])
                kb.dma(kw[64:69, :], dc["kaug"][:, :], writes=[kw_b])
                for half in range(2):
                    hs = slice(half * 32, (half + 1) * 32)
                    rs_ = slice(half * 4096, (half + 1) * 4096)
                    kb.dma(vs[:, hs, 0:64], ds["VSW"][rs_, g * 64:(g + 1) * 64].rearrange("(kt p) d -> p kt d", p=128),
                           reads=[dsb["VSW"]], writes=[vs_b])
                    kb.dma(vw[:, hs, 0:64], ds["VSW"][rs_, 128 + g * 64:128 + (g + 1) * 64].rearrange("(kt p) d -> p kt d", p=128),
                           reads=[dsb["VSW"]], writes=[vw_b])
                for qc in range(NCH):
                    t0 = qc * 512
                    pr = qc % 2
                    imp, imp_b = IMP.next()
                    nmt, nmt_b = NMT.next()
                    tiles = []
                    for hi in range(4):
                        h = 4 * g + hi
                        qa, qa_b = QA[pr][hi]
                        kb.dma(qa[0:64, :], ds["QN"][h * 64:(h + 1) * 64, t0:t0 + 512], reads=[dsb["QN"]], writes=[qa_b])
                        kb.dma(qa[64:69, :], dc["qaug"][h, :, t0:t0 + 512], writes=[qa_b])
                        O, O_b = R.OPS.next()
                        U, U_b = R.UPS.next()
                        nct = qc // 4 + 1
                        for ct in range(nct):
                            m = qc - 4 * ct
                            qk = [(kcmp[0:69, g, ct * 128:(ct + 1) * 128], qa[0:69, :], [kcmp_b, qa_b])]
                            if m <= 4:
                                qk.append((ident[:, :], pmk[:, m, :], [ident_b, CB]))

                            def pv(P, P_b, ct=ct, nct=nct, O=O, O_b=O_b, U=U, U_b=U_b):
                                kb.op("pe", lambda e: e.matmul(O[:, :], lhsT=vcmp[:, ct, g, :], rhs=P[:, :], start=(ct == 0), stop=(ct == nct - 1)),
                                      reads=[vcmp_b, P_b], writes=[O_b], inc=False)
                                for s_ in range(4):
                                    kb.op("pe", lambda e: e.matmul(U[:, s_, :], lhsT=P[:, s_ * 128:(s_ + 1) * 128], rhs=impm[:, ct, :],
                                                                   start=(ct == 0), stop=(ct == nct - 1)),
                                          reads=[CB, P_b], writes=[U_b], inc=(s_ == 3))

                            def epi(hi=hi, O=O, O_b=O_b, U=U, U_b=U_b):
                                rec, rec_b = norm_rec(O, O_b)
                                oc, oc_b = OC[pr][hi]
                                kb.op("dve", lambda e: e.tensor_tensor(out=oc[:, :], in0=O[0:64, :], in1=rec[:, :], op=ALU.mult),
                                      reads=[O_b, rec_b], writes=[oc_b])
                                rs, rs_b = RS.next()
                                kb.op("dve", lambda e: e.tensor_reduce(out=rs[:, 0:4], in_=U[:, :, :], axis=AX.X, op=ALU.add),
                                      reads=[U_b], writes=[rs_b])
                                kb.op("dve", lambda e: e.tensor_scalar(out=rs[:, 0:4], in0=rs[:, 0:4], scalar1=1e-30, scalar2=0.5,
                                                                       op0=ALU.max, op1=ALU.mult), reads=[rs_b], writes=[rs_b])
                                kb.op("dve", lambda e: e.reciprocal(out=rs[:, 4:8], in_=rs[:, 0:4]), reads=[rs_b], writes=[rs_b])
                                for s_ in range(4):
                                    if hi == 0:
                                        kb.op("dve", lambda e: e.tensor_scalar(out=imp[:, s_, :], in0=U[:, s_, :], scalar1=rs[:, 4 + s_:5 + s_],
                                                                               scalar2=None, op0=ALU.mult), reads=[U_b, rs_b], writes=[imp_b])
                                    else:
                                        kb.op("dve", lambda e: e.scalar_tensor_tensor(out=imp[:, s_, :], in0=U[:, s_, :], scalar=rs[:, 4 + s_:5 + s_],
                                                                                      in1=imp[:, s_, :], op0=ALU.mult, op1=ALU.add),
                                              reads=[U_b, rs_b, imp_b], writes=[imp_b])

                            tiles.append(TileD(qk, 1.0, pv, epi if ct == nct - 1 else None))
                    run_stream(c, R, tiles)
                    c.stream.flush()
                    tpn, tpn_b = R.TPN.next()
                    for s_ in range(4):
                        qt = 4 * qc + s_
                        lo = 128 - 2 * qt
                        w, w_b = W1.next()
                        w2, w2_b = W2.next()
                        m8, m8_b = M8.next()
                        nms, nms_b = NMS.next()
                        kb.op("dve", lambda e: e.tensor_tensor(out=w[:, :], in0=imp[:, s_, :], in1=keept[:, lo:lo + 128], op=ALU.mult),
                              reads=[imp_b, CB], writes=[w_b])
                        kb.op("dve", lambda e: e.tensor_tensor(out=w[:, :], in0=w[:, :], in1=addt[:, lo:lo + 128], op=ALU.add),
                              reads=[w_b, CB], writes=[w_b])
                        kb.op("dve", lambda e: e.memset(w[:, 0:1], 1e6), reads=[w_b], writes=[w_b])
                        kb.op("dve", lambda e: e.max(out=m8[:, 0:8], in_=w[:, :]), reads=[w_b], writes=[m8_b])
                        kb.op("dve", lambda e: e.match_replace(out=w2[:, :], in_to_replace=m8[:, 0:8], in_values=w[:, :], imm_value=-2.0),
                              reads=[w_b, m8_b], writes=[w2_b])
                        kb.op("dve", lambda e: e.max(out=m8[:, 8:16], in_=w2[:, :]), reads=[w2_b, m8_b], writes=[m8_b])
                        kb.op("dve", lambda e: e.tensor_scalar(out=nms[:, :], in0=w[:, :], scalar1=m8[:, 15:16], scalar2=-1.0,
                                                               op0=ALU.is_ge, op1=ALU.add), reads=[w_b, m8_b], writes=[nms_b])
                        kb.op("pe", lambda e: e.transpose(out=tpn[:, s_, :], in_=nms[:, :], identity=ident[:, :]),
                              reads=[nms_b, ident_b], writes=[tpn_b])
                    kb.op("act", lambda e: e.activation(out=nmt[:, :], in_=tpn[:, :, :].rearrange("p a b -> p (a b)"), func=AF.Copy),
                          reads=[tpn_b], writes=[nmt_b])
                    tiles = []
                    for hi in range(4):
                        h = 4 * g + hi
                        qa, qa_b = QA[pr][hi]
                        gb, gb_b = GB.next()
                        kb.dma(gb[:, :, :], bass.AP(Gt, h * 3 * T + t0, [[0, 64], [T, 3], [1, 512]]), reads=[dsb["G"]], writes=[gb_b])
                        st = {}
                        O, O_b = R.OPS.next()
                        nk = 4 * qc + 4
                        k0 = max(0, int((t0 - 127 - ALIBI_CUT * (2.0 ** (h + 1))) // 128) + 1)
                        for kt in range(k0, nk):
                            qk = [(ks[0:69, kt * 128:(kt + 1) * 128], qa[0:69, :], [ks_b, qa_b]),
                                  (eexp[:, kt, :], nmt[:, :], [CB, nmt_b])]
                            d = kt - 4 * qc
                            if d >= 0:
                                qk.append((ident[:, :], cmk[:, d, :], [ident_b, CB]))

                            def pv(P, P_b, kt=kt, nk=nk, k0=k0, O=O, O_b=O_b):
                                kb.op("pe", lambda e: e.matmul(O[:, :], lhsT=vs[:, kt, :], rhs=P[:, :], start=(kt == k0), stop=(kt == nk - 1)),
                                      reads=[vs_b, P_b], writes=[O_b], inc=(kt == nk - 1))

                            def epi(O=O, O_b=O_b, gb=gb, gb_b=gb_b, st=st):
                                rec, rec_b = norm_rec(O, O_b)
                                rg, rg_b = RG.next()
                                t1, t1_b = T1.next()
                                kb.op("pool", lambda e: e.tensor_tensor(out=rg[:, :], in0=rec[:, :], in1=gb[:, 1, :], op=ALU.mult),
                                      reads=[rec_b, gb_b], writes=[rg_b])
                                kb.op("dve", lambda e: e.tensor_tensor(out=t1[:, :], in0=O[0:64, :], in1=rg[:, :], op=ALU.mult),
                                      reads=[O_b, rg_b], writes=[t1_b])
                                st["slc"] = (t1, t1_b)

                            tiles.append(TileD(qk, 1.0, pv, epi if kt == nk - 1 else None))
                        O, O_b = R.OPS.next()
                        wl = []
                        for d in range(4):
                            wl.append((4 * qc + d, cmk[:, d, :]))
                        for d in range(4):
                            kt = 4 * qc - 4 + d
                            if kt >= 0:
                                wl.append((kt, wmk[:, d, :]))
                        nw = len(wl)
                        for wi, (kt, mk) in enumerate(wl):
                            qk = [(kw[0:69, kt * 128:(kt + 1) * 128], qa[0:69, :], [kw_b, qa_b]),
                                  (ident[:, :], mk, [ident_b, CB])]

                            def pv(P, P_b, kt=kt, wi=wi, nw=nw, O=O, O_b=O_b):
                                kb.op("pe", lambda e: e.matmul(O[:, :], lhsT=vw[:, kt, :], rhs=P[:, :], start=(wi == 0), stop=(wi == nw - 1)),
                                      reads=[vw_b, P_b], writes=[O_b], inc=(wi == nw - 1))

                            def epi(h=h, hi=hi, O=O, O_b=O_b, gb=gb, gb_b=gb_b, st=st):
                                rec, rec_b = norm_rec(O, O_b)
                                rg, rg_b = RG.next()
                                t2, t2_b = T1.next()
                                kb.op("pool", lambda e: e.tensor_tensor(out=rg[:, :], in0=rec[:, :], in1=gb[:, 2, :], op=ALU.mult),
                                      reads=[rec_b, gb_b], writes=[rg_b])
                                kb.op("dve", lambda e: e.tensor_tensor(out=t2[:, :], in0=O[0:64, :], in1=rg[:, :], op=ALU.mult),
                                      reads=[O_b, rg_b], writes=[t2_b])
                                t1, t1_b = st["slc"]
                                oc, oc_b = OC[pr][hi]
                                acc, acc_b = ACC.next()
                                ost, ost_b = OST.next()
                                kb.op("pool", lambda e: e.tensor_tensor(out=acc[:, :], in0=oc[:, :], in1=gb[:, 0, :], op=ALU.mult),
                                      reads=[oc_b, gb_b], writes=[acc_b])
                                kb.op("pool", lambda e: e.tensor_tensor(out=acc[:, :], in0=acc[:, :], in1=t1[:, :], op=ALU.add),
                                      reads=[acc_b, t1_b], writes=[acc_b])
                                kb.op("pool", lambda e: e.tensor_tensor(out=ost[:, :], in0=acc[:, :], in1=t2[:, :], op=ALU.add),
                                      reads=[acc_b, t2_b], writes=[ost_b])
                                kb.dma(ds["ON"][h * 64:(h + 1) * 64, t0:t0 + 512], ost[:, :], reads=[ost_b], writes=[dsb["ON"]], q="pool")

                            tiles.append(TileD(qk, 1.0, pv, epi if wi == nw - 1 else None))
                    run_stream(c, R, tiles)
                    c.stream.flush()
            c.stream.flush()
            if getattr(c, "stop_mid", None) == "nsa":
                return
            for h in range(8):
                ka, ka_b = KA[h % 2]
                va, va_b = VA[h % 2]
                kb.dma(ka[0:64, :], ds["KN"][h * 64:(h + 1) * 64, :], reads=[dsb["KN"]], writes=[ka_b])
                kb.dma(ka[64:96, :], ds["KPE"][:, :], reads=[dsb["KPE"]], writes=[ka_b])
                for half in range(2):
                    hs = slice(half * 32, (half + 1) * 32)
                    rs_ = slice(half * 4096, (half + 1) * 4096)
                    kb.dma(va[:, hs, 0:64], ds["VM"][rs_, h * 64:(h + 1) * 64].rearrange("(kt p) d -> p kt d", p=128),
                           reads=[dsb["VM"]], writes=[va_b])
                qms = {}

                def qload(qc, h=h, qms=qms):
                    qm, qm_b = QMT.next()
                    kb.dma(qm[:, :], ds["QM"][h, :, qc * 512:(qc + 1) * 512], reads=[dsb["QM"]], writes=[qm_b])
                    qms[qc] = (qm, qm_b)

                qload(0)
                for qc in range(NCH):
                    t0 = qc * 512
                    if qc + 1 < NCH:
                        qload(qc + 1)
                    qm, qm_b = qms.pop(qc)
                    O, O_b = R.OPS.next()
                    nk = 4 * qc + 4
                    for kt in range(nk):
                        qk = [(ka[0:96, kt * 128:(kt + 1) * 128], qm[0:96, :], [ka_b, qm_b])]
                        d = kt - 4 * qc
                        if d >= 0:
                            qk.append((ident[:, :], cmk[:, d, :], [ident_b, CB]))

                        def pv(P, P_b, kt=kt, nk=nk, O=O, O_b=O_b, va=va, va_b=va_b):
                            kb.op("pe", lambda e: e.matmul(O[:, :], lhsT=va[:, kt, :], rhs=P[:, :], start=(kt == 0), stop=(kt == nk - 1)),
                                  reads=[va_b, P_b], writes=[O_b], inc=True)

                        def epi(h=h, t0=t0, O=O, O_b=O_b):
                            rec, rec_b = norm_rec(O, O_b)
                            ost, ost_b = OST.next()
                            kb.op("dve", lambda e: e.tensor_tensor(out=ost[:, :], in0=O[0:64, :], in1=rec[:, :], op=ALU.mult),
                                  reads=[O_b, rec_b], writes=[ost_b])
                            kb.dma(ds["OM"][h * 64:(h + 1) * 64, t0:t0 + 512], ost[:, :], reads=[ost_b], writes=[dsb["OM"]], q="pool")

                        c.stream.push(TileD(qk, MLA_SCALE, pv, epi if kt == nk - 1 else None))
            c.stream.flush()

CH3 = 256
NCH3 = T // CH3


def phase3(c):
    nc, kb = c.nc, c.kb
    din, dc, ds, dsb = c.din, c.dc, c.ds, c.dsb
    ident, ident_b = c.ident, c.ident_b
    with ExitStack() as es:
        W = Buf("p3w")
        wo = sb(nc, es, "wo", [128, 8, 1024], BF16)
        wpg = sb(nc, es, "wpg", [128, 8, 1024], BF16)
        wpl = sb(nc, es, "wpl", [128, 2, 1024], BF16)
        wd = sb(nc, es, "wd", [128, 22, 1024], BF16)
        lnv = sb(nc, es, "lnv", [128, 6, 1024], F32)
        for i, nm in enumerate(("ln1_g", "ln1_b", "ln2_g", "ln2_b", "ln3_g", "ln3_b")):
            kb.dma(lnv[:, i, :], din[nm][0:1, :].partition_broadcast(128), writes=[W])
        with ExitStack() as es1:
            STG = rot_tiles(nc, es1, "stg3", [128, 2816], F32, 2)
            STB = rot_tiles(nc, es1, "stb3", [128, 2816], BF16, 2)
            k = 0
            for name, dst in (("w_out", wo), ("w_ple_gate", wpg)):
                for ft in range(8):
                    s_, s_b = STG.next()
                    kb.dma(s_[:, 0:1024], din[name][ft * 128:(ft + 1) * 128, :], writes=[s_b])
                    kb.op("dve" if k % 2 else "pool", lambda e: e.tensor_copy(out=dst[:, ft, :], in_=s_[:, 0:1024]), reads=[s_b], writes=[W])
                    k += 1
            for kt in range(2):
                s_, s_b = STG.next()
                kb.dma(s_[:, 0:1024], din["w_ple"][kt * 128:(kt + 1) * 128, :], writes=[s_b])
                kb.op("dve", lambda e: e.tensor_copy(out=wpl[:, kt, :], in_=s_[:, 0:1024]), reads=[s_b], writes=[W])
            for j in range(22):
                s_, s_b = STG.next()
                kb.dma(s_[:, 0:1024], din["w_down"][j * 128:(j + 1) * 128, :], writes=[s_b])
                kb.op("dve" if j % 2 else "pool", lambda e: e.tensor_copy(out=wd[:, j, :], in_=s_[:, 0:1024]), reads=[s_b], writes=[W])
            for ft in range(8):
                for a in range(2):
                    s_, s_b = STG.next()
                    sb_, sb_b = STB.next()
                    kb.dma(s_[:, :], din["w_up"][ft * 128:(ft + 1) * 128, a * DFF:(a + 1) * DFF], writes=[s_b])
                    kb.op("dve" if a else "pool", lambda e: e.tensor_copy(out=sb_[:, :], in_=s_[:, :]), reads=[s_b], writes=[sb_b])
                    kb.dma(ds["WUP"][:, :, a, ft, :].rearrange("j p c -> p j c"), sb_[:, :].rearrange("p (j c) -> p j c", c=128),
                           reads=[sb_b], writes=[dsb["WUP"]])
        kb.barrier()
        XF = rot_tiles(nc, es, "xf3", [128, 1024], F32, 1)
        OT = rot_tiles(nc, es, "oT3", [128, 8, CH3], BF16, 1)
        PF = rot_tiles(nc, es, "pf3", [128, 2, 256], F32, 1)
        PB = rot_tiles(nc, es, "pb3", [128, 2, 256], BF16, 3)
        PT = rot_tiles(nc, es, "pT3", [128, 2, CH3], BF16, 1)
        Y = rot_tiles(nc, es, "y3", [128, 1024], F32, 2)
        Z = rot_tiles(nc, es, "z3", [128, 1024], F32, 1)
        X1 = rot_tiles(nc, es, "x1r", [128, 2, 1024], F32, 2)
        X2 = rot_tiles(nc, es, "x2r", [128, 2, 1024], F32, 2)
        XBF = rot_tiles(nc, es, "xbf3", [128, 1024], BF16, 2)
        X1T = rot_tiles(nc, es, "x1T", [128, 8, CH3], BF16, 2)
        X2T = rot_tiles(nc, es, "x2T", [128, 8, CH3], BF16, 2)
        AT = rot_tiles(nc, es, "aT", [128, 22, CH3], BF16, 1)
        WU = rot_tiles(nc, es, "wu", [128, 2, 8, 128], BF16, 3)
        SG = rot_tiles(nc, es, "sg", [128, CH3], F32, 2)
        ST = rot_tiles(nc, es, "st3", [128, 8], F32, 4)
        PA = rot_tiles(nc, es, "pa3", [128, 512], F32, 3, psum=True)
        PG = rot_tiles(nc, es, "pg3", [128, 2, CH3], F32, 3, psum=True)
        TP = rot_tiles(nc, es, "tp3", [128, 8, 128], BF16, 2, psum=True)

        def ln_stats(y, y_b):
            st, st_b = ST.next()
            z, z_b = Z.next()
            kb.op("act", lambda e: e.activation(out=z[:, :], in_=y[:, :], func=AF.Copy, accum_out=st[:, 0:1]), reads=[y_b], writes=[z_b, st_b])
            kb.op("act", lambda e: e.activation(out=z[:, :], in_=y[:, :], func=AF.Square, accum_out=st[:, 1:2]), reads=[y_b], writes=[z_b, st_b])
            kb.op("dve", lambda e: e.tensor_scalar(out=st[:, 2:4], in0=st[:, 0:2], scalar1=1.0 / D, scalar2=None, op0=ALU.mult), reads=[st_b], writes=[st_b])
            kb.op("dve", lambda e: e.tensor_tensor(out=st[:, 4:5], in0=st[:, 2:3], in1=st[:, 2:3], op=ALU.mult), reads=[st_b], writes=[st_b])
            kb.op("dve", lambda e: e.tensor_tensor(out=st[:, 5:6], in0=st[:, 3:4], in1=st[:, 4:5], op=ALU.subtract), reads=[st_b], writes=[st_b])
            kb.op("dve", lambda e: e.tensor_scalar_add(out=st[:, 5:6], in0=st[:, 5:6], scalar1=LN_EPS), reads=[st_b], writes=[st_b])
            kb.op("act", lambda e: e.activation(out=st[:, 6:7], in_=st[:, 5:6], func=AF.Sqrt), reads=[st_b], writes=[st_b])
            kb.op("dve", lambda e: e.reciprocal(out=st[:, 7:8], in_=st[:, 6:7]), reads=[st_b], writes=[st_b])
            return st, st_b, z, z_b

        def layer_norm(y, y_b, gi, out_ap, out_b):
            st, st_b, z, z_b = ln_stats(y, y_b)
            kb.op("dve", lambda e: e.scalar_tensor_tensor(out=z[:, :], in0=y[:, :], scalar=st[:, 2:3], in1=lnv[:, gi, :],
                                                          op0=ALU.subtract, op1=ALU.mult), reads=[y_b, st_b, W], writes=[z_b])
            kb.op("dve", lambda e: e.scalar_tensor_tensor(out=out_ap, in0=z[:, :], scalar=st[:, 7:8], in1=lnv[:, gi + 1, :],
                                                           op0=ALU.mult, op1=ALU.add), reads=[z_b, st_b, W], writes=[out_b])

        def cast_bf(src, src_b):
            xb, xb_b = XBF.next()
            kb.op("pool", lambda e: e.tensor_copy(out=xb[:, :], in_=src), reads=[src_b], writes=[xb_b])
            return xb, xb_b

        def transposes(xb, xb_b, s_, dstT, dstT_b):
            tp, tp_b = TP.next()
            for ft in range(8):
                kb.op("pe", lambda e: e.transpose(out=tp[:, ft, :], in_=xb[:, ft * 128:(ft + 1) * 128], identity=ident[:, :]),
                      reads=[xb_b, ident_b], writes=[tp_b], inc=(ft == 7))
            kb.op("act", lambda e: e.activation(out=dstT[:, :, s_ * 128:(s_ + 1) * 128], in_=tp[:, :, :], func=AF.Copy), reads=[tp_b], writes=[dstT_b])

        S = {}

        def A_mm(ch):
            t0 = ch * CH3
            st = S.setdefault(ch, {})
            oT, oT_b = OT.next()
            pf, pf_b = PF.next()
            kb.dma(oT[:, 0:4, :], ds["ON"].rearrange("(c p) t -> p c t", p=128)[:, :, t0:t0 + CH3], reads=[dsb["ON"]], writes=[oT_b])
            kb.dma(oT[:, 4:8, :], ds["OM"].rearrange("(c p) t -> p c t", p=128)[:, :, t0:t0 + CH3], reads=[dsb["OM"]], writes=[oT_b])
            kb.dma(pf[:, :, :], din["p"][t0:t0 + CH3, :].rearrange("(s p) d -> p s d", p=128), writes=[pf_b])
            pb_, pb_b = PB.next()
            kb.op("pool", lambda e: e.tensor_copy(out=pb_[:, :, :], in_=pf[:, :, :]), reads=[pf_b], writes=[pb_b])
            st["pb"] = (pb_, pb_b)
            st["x1"] = X1.next()
            st["x1T"] = X1T.next()
            st["ys"] = []
            for s_ in range(2):
                y, y_b = Y.next()
                xf, xf_b = XF.next()
                kb.dma(xf[:, :], din["x"][t0 + s_ * 128:t0 + (s_ + 1) * 128, :], writes=[xf_b])
                for n in range(2):
                    pa, pa_b = PA.next()
                    for kt in range(8):
                        kb.op("pe", lambda e: e.matmul(pa[:, :], lhsT=oT[:, kt, s_ * 128:(s_ + 1) * 128], rhs=wo[:, kt, n * 512:(n + 1) * 512],
                                                       start=(kt == 0), stop=(kt == 7)), reads=[oT_b, W], writes=[pa_b], inc=(kt == 7))
                    kb.op("dve", lambda e: e.scalar_tensor_tensor(out=y[:, n * 512:(n + 1) * 512], in0=xf[:, n * 512:(n + 1) * 512], scalar=ALPHA,
                                                                  in1=pa[:, :], op0=ALU.mult, op1=ALU.add), reads=[xf_b, pa_b], writes=[y_b])
                st["ys"].append((y, y_b))

        def A_ln(ch, s_):
            st = S[ch]
            x1, x1_b = st["x1"]
            y, y_b = st["ys"][s_]
            layer_norm(y, y_b, 0, x1[:, s_, :], x1_b)
            st["xb%d" % s_] = cast_bf(x1[:, s_, :], x1_b)

        def A_tr(ch, s_):
            st = S[ch]
            x1T, x1T_b = st["x1T"]
            transposes(*st["xb%d" % s_], s_, x1T, x1T_b)

        def B_up(ch, j):
            st = S[ch]
            x1T, x1T_b = st["x1T"]
            if j == 0:
                st["aT"] = AT.next()
            aT, aT_b = st["aT"]
            wu, wu_b = WU.next()
            kb.dma(wu[:, :, :, :], ds["WUP"][j], reads=[dsb["WUP"]], writes=[wu_b])
            pg, pg_b = PG.next()
            for a in range(2):
                for ft in range(8):
                    kb.op("pe", lambda e: e.matmul(pg[:, a, :], lhsT=wu[:, a, ft, :], rhs=x1T[:, ft, :], start=(ft == 0), stop=(ft == 7)),
                          reads=[wu_b, x1T_b], writes=[pg_b], inc=(a == 1 and ft == 7))
            sg, sg_b = SG.next()
            kb.op("act", lambda e: e.activation(out=sg[:, :], in_=pg[:, 0, :], func=AF.Silu), reads=[pg_b], writes=[sg_b])
            kb.op("dve", lambda e: e.tensor_tensor(out=aT[:, j, :], in0=pg[:, 1, :], in1=sg[:, :], op=ALU.mult), reads=[pg_b, sg_b], writes=[aT_b])

        def B_down(ch):
            st = S[ch]
            aT, aT_b = st["aT"]
            x1, x1_b = st["x1"]
            st["x2"] = X2.next()
            st["x2T"] = X2T.next()
            x2, x2_b = st["x2"]
            ys = []
            for s_ in range(2):
                y, y_b = Y.next()
                for n in range(2):
                    pa, pa_b = PA.next()
                    for j in range(22):
                        kb.op("pe", lambda e: e.matmul(pa[:, :], lhsT=aT[:, j, s_ * 128:(s_ + 1) * 128], rhs=wd[:, j, n * 512:(n + 1) * 512],
                                                       start=(j == 0), stop=(j == 21)), reads=[aT_b, W], writes=[pa_b], inc=(j == 21))
                    kb.op("dve", lambda e: e.scalar_tensor_tensor(out=y[:, n * 512:(n + 1) * 512], in0=x1[:, s_, n * 512:(n + 1) * 512], scalar=ALPHA,
                                                                  in1=pa[:, :], op0=ALU.mult, op1=ALU.add), reads=[x1_b, pa_b], writes=[y_b])
                ys.append((y, y_b))
            st["ys2"] = ys

        def B_ln(ch, s_):
            st = S[ch]
            x2, x2_b = st["x2"]
            y, y_b = st["ys2"][s_]
            layer_norm(y, y_b, 2, x2[:, s_, :], x2_b)
            st["x2b%d" % s_] = cast_bf(x2[:, s_, :], x2_b)

        def B_tr(ch, s_):
            st = S[ch]
            x2T, x2T_b = st["x2T"]
            transposes(*st["x2b%d" % s_], s_, x2T, x2T_b)

        def C_pre(ch):
            st = S[ch]
            pb_, pb_b = st["pb"]
            pT, pT_b = PT.next()
            tp, tp_b = TP.next()
            for s_ in range(2):
                for kt in range(2):
                    kb.op("pe", lambda e: e.transpose(out=tp[:, s_ * 2 + kt, :], in_=pb_[:, s_, kt * 128:(kt + 1) * 128], identity=ident[:, :]),
                          reads=[pb_b, ident_b], writes=[tp_b], inc=(s_ == 1 and kt == 1))
            for s_ in range(2):
                kb.op("act", lambda e: e.activation(out=pT[:, :, s_ * 128:(s_ + 1) * 128], in_=tp[:, 2 * s_:2 * s_ + 2, :], func=AF.Copy),
                      reads=[tp_b], writes=[pT_b])
            st["pT"] = (pT, pT_b)

        def C_sub(ch, s_):
            t0 = ch * CH3
            st = S[ch]
            x2, x2_b = st["x2"]
            x2T, x2T_b = st["x2T"]
            pT, pT_b = st["pT"]
            y, y_b = Y.next()
            for n in range(2):
                pa, pa_b = PA.next()
                for kt in range(8):
                    kb.op("pe", lambda e: e.matmul(pa[:, :], lhsT=x2T[:, kt, s_ * 128:(s_ + 1) * 128], rhs=wpg[:, kt, n * 512:(n + 1) * 512],
                                                   start=(kt == 0), stop=(kt == 7)), reads=[x2T_b, W], writes=[pa_b], inc=(kt == 7))
                kb.op("act", lambda e: e.activation(out=y[:, n * 512:(n + 1) * 512], in_=pa[:, :], func=AF.Sigmoid), reads=[pa_b], writes=[y_b])
                pp, pp_b = PA.next()
                for kt in range(2):
                    kb.op("pe", lambda e: e.matmul(pp[:, :], lhsT=pT[:, kt, s_ * 128:(s_ + 1) * 128], rhs=wpl[:, kt, n * 512:(n + 1) * 512],
                                                   start=(kt == 0), stop=(kt == 1)), reads=[pT_b, W], writes=[pp_b], inc=(kt == 1))
                kb.op("dve", lambda e: e.tensor_tensor(out=y[:, n * 512:(n + 1) * 512], in0=pp[:, :], in1=y[:, n * 512:(n + 1) * 512], op=ALU.mult),
                      reads=[pp_b, y_b], writes=[y_b])
                kb.op("dve", lambda e: e.scalar_tensor_tensor(out=y[:, n * 512:(n + 1) * 512], in0=x2[:, s_, n * 512:(n + 1) * 512], scalar=ALPHA,
                                                              in1=y[:, n * 512:(n + 1) * 512], op0=ALU.mult, op1=ALU.add),
                      reads=[x2_b, y_b], writes=[y_b])
            st_, st_b, z, z_b = ln_stats(y, y_b)
            kb.op("dve", lambda e: e.scalar_tensor_tensor(out=z[:, :], in0=y[:, :], scalar=st_[:, 2:3], in1=lnv[:, 4, :],
                                                          op0=ALU.subtract, op1=ALU.mult), reads=[y_b, st_b, W], writes=[z_b])
            kb.op("dve", lambda e: e.scalar_tensor_tensor(out=z[:, :], in0=z[:, :], scalar=st_[:, 7:8], in1=lnv[:, 5, :],
                                                          op0=ALU.mult, op1=ALU.add), reads=[z_b, st_b, W], writes=[z_b])
            kb.dma(c.out[t0 + s_ * 128:t0 + (s_ + 1) * 128, :], z[:, :], reads=[z_b], writes=[c.out_b], q="pool")
            if s_ == 1:
                S.pop(ch)

        A_mm(0)
        for s_ in range(2):
            A_ln(0, s_)
            A_tr(0, s_)
        hooks = {1: ("B_ln", 0), 4: ("B_ln", 1), 6: ("B_tr", 0), 7: ("B_tr", 1), 9: ("A_mm", 0), 11: ("A_ln", 0), 14: ("A_ln", 1),
                 15: ("C_pre", 0), 16: ("C_sub", 0), 18: ("A_tr", 0), 19: ("C_sub", 1), 21: ("A_tr", 1)}

        def run_hook(it, name, s_):
            nxt = it + 1 < NCH3
            prv = it >= 1
            if name == "B_ln" and prv:
                B_ln(it - 1, s_)
            elif name == "B_tr" and prv:
                B_tr(it - 1, s_)
            elif name == "A_mm" and nxt:
                A_mm(it + 1)
            elif name == "A_ln" and nxt:
                A_ln(it + 1, s_)
            elif name == "A_tr" and nxt:
                A_tr(it + 1, s_)
            elif name == "C_pre" and prv:
                C_pre(it - 1)
            elif name == "C_sub" and prv:
                C_sub(it - 1, s_)

        for it in range(NCH3 + 1):
            if it < NCH3:
                for j in range(22):
                    B_up(it, j)
                    if j in hooks:
                        run_hook(it, *hooks[j])
                B_down(it)
            else:
                for j in sorted(hooks):
                    run_hook(it, *hooks[j])


_CACHE = {}


def kernel(**inputs):
    if "nc" not in _CACHE:
        _CACHE["nc"] = build_program()
        _CACHE["consts"] = host_constants()
    nc = _CACHE["nc"]
    consts = _CACHE["consts"]
    B = inputs["x"].shape[0]
    shared = {}
    for k, shp in IN_SPECS.items():
        if k in ("x", "p"):
            continue
        shared[k] = np.ascontiguousarray(np.asarray(inputs[k], dtype=np.float32)[0].reshape(shp))
    for k, v in consts.items():
        shared["c_" + k] = v
    in_maps = []
    for b in range(B):
        m = dict(shared)
        m["x"] = np.ascontiguousarray(np.asarray(inputs["x"], dtype=np.float32)[b])
        m["p"] = np.ascontiguousarray(np.asarray(inputs["p"], dtype=np.float32)[0, b])
        in_maps.append(m)
    res = run_bass_kernel_spmd(nc, in_maps, core_ids=list(range(B)))
    return np.stack([np.asarray(r["out"], dtype=np.float32) for r in res.results], 0)
```

```python
from contextlib import ExitStack
import numpy as np
import ml_dtypes
import concourse.bass as bass
import concourse.mybir as mybir
from concourse.bass_utils import run_bass_kernel_spmd

F32 = mybir.dt.float32
BF16 = mybir.dt.bfloat16
AF = mybir.ActivationFunctionType
ALU = mybir.AluOpType
AX = mybir.AxisListType
bf16 = ml_dtypes.bfloat16

T = 8192
D = 1024
NCH = T // 512
NKT = T // 128
DFF = 2816
BIG = 30000.0
ALPHA = 2.0 ** 0.25
LN_EPS = 1e-5
RMS_EPS = 1e-6
MLA_SCALE = 96.0 ** -0.5
ALIBI_CUT = 200.0


class Buf:
    __slots__ = ("name", "last_w", "readers")

    def __init__(self, name=""):
        self.name = name
        self.last_w = None
        self.readers = {}


class KB:
    NSLOT = 8

    def __init__(self, nc, es):
        self.nc = nc
        self.E = {"pe": nc.tensor, "act": nc.scalar, "dve": nc.vector, "pool": nc.gpsimd, "sp": nc.sync}
        self.sem = {}
        self.seq = {}
        self.known = {}
        for e in self.E:
            self.sem[("e", e)] = es.enter_context(nc.semaphore("s_" + e))
            self.seq[e] = 0
            self.known[e] = {}
        self.slots = {}
        self.slot_rr = {}
        for q in ("sp", "pool"):
            self.slots[q] = []
            self.slot_rr[q] = 0
            for i in range(self.NSLOT):
                key = ("d", q, i)
                self.sem[key] = es.enter_context(nc.semaphore("d_%s%d" % (q, i)))
                self.slots[q].append([key, 0])

    def _collect(self, reads, writes):
        deps = {}
        for b in reads:
            if b.last_w is not None:
                k, v = b.last_w
                o = deps.get(k)
                deps[k] = (max(v, o[0]) if o else v, True)
        for b in writes:
            if b.last_w is not None:
                k, v = b.last_w
                o = deps.get(k)
                if o is None:
                    deps[k] = (v, False)
                elif v > o[0]:
                    deps[k] = (v, o[1])
            for k, v in b.readers.items():
                o = deps.get(k)
                if o is None:
                    deps[k] = (v, False)
                elif v > o[0]:
                    deps[k] = (v, o[1])
        return deps

    def _wait(self, eng, deps):
        kn = self.known[eng]
        for k, (v, raw) in deps.items():
            if k == ("e", eng):
                if eng == "pe" or not raw:
                    continue
            if kn.get(k, 0) >= v:
                continue
            self.E[eng].wait_ge(self.sem[k], v)
            kn[k] = v

    def _record(self, dep, reads, writes):
        k, v = dep
        for b in reads:
            if b.readers.get(k, 0) < v:
                b.readers[k] = v
        for b in writes:
            b.last_w = dep
            b.readers = {}

    def op(self, eng, fn, reads=(), writes=(), inc=True):
        self._wait(eng, self._collect(reads, writes))
        ins = fn(self.E[eng])
        if inc:
            ins.then_inc(self.sem[("e", eng)], 1)
            self.seq[eng] += 1
            dep = (("e", eng), self.seq[eng])
        else:
            dep = (("e", eng), self.seq[eng] + 1)
        self._record(dep, reads, writes)
        return ins

    def dma(self, out, in_, reads=(), writes=(), q="sp", **kw):
        sl = self.slots[q][self.slot_rr[q]]
        self.slot_rr[q] = (self.slot_rr[q] + 1) % self.NSLOT
        deps = self._collect(reads, writes)
        if sl[1] > 0:
            o = deps.get(sl[0])
            if o is None or o[0] < sl[1]:
                deps[sl[0]] = (sl[1], True)
        self._wait(q, deps)
        self.E[q].dma_start(out=out, in_=in_, **kw).then_inc(self.sem[sl[0]], 16)
        sl[1] += 16
        self._record((sl[0], sl[1]), reads, writes)

    def barrier(self):
        cur = {}
        for e in self.E:
            cur[("e", e)] = self.seq[e]
        for q in self.slots:
            for key, cnt in self.slots[q]:
                cur[key] = cnt
        for e in self.E:
            kn = self.known[e]
            for k, v in cur.items():
                if v == 0 or k == ("e", e):
                    continue
                if kn.get(k, 0) >= v:
                    continue
                self.E[e].wait_ge(self.sem[k], v)
                kn[k] = v


class Rot:
    def __init__(self, items):
        self.items = items
        self.i = 0

    def next(self):
        it = self.items[self.i]
        self.i = (self.i + 1) % len(self.items)
        return it


def host_constants():
    c = {}
    c["ident"] = np.eye(128, dtype=np.float32).astype(bf16)
    pos = np.arange(T, dtype=np.float32)
    inv_freq = (10000.0 ** (-np.arange(16, dtype=np.float32) / 16)).astype(np.float32)
    ang = (pos[:, None] * inv_freq[None, :]).astype(np.float32)
    cos = np.cos(ang).astype(np.float32).T
    sin = np.sin(ang).astype(np.float32).T
    c["ropeC"] = np.ascontiguousarray(np.concatenate([cos, cos], 0))
    c["ropeS"] = np.ascontiguousarray(np.concatenate([-sin, sin], 0))
    p = np.arange(128)[:, None]
    j = np.arange(512)[None, :]
    def addmask(valid):
        return np.where(valid, 0.0, -BIG).astype(np.float32).astype(bf16)
    c["cmk"] = addmask(np.stack([(p + 128 * d <= j) for d in range(4)], 1))
    c["wmk"] = addmask(np.stack([(j < 128 * d + p) for d in range(4)], 1))
    c["pmk"] = addmask(np.stack([(16 * p + 31 <= 512 * m + j) for m in range(5)], 1))
    slopes = 2.0 ** (-np.arange(1, 9, dtype=np.float64))
    t = np.arange(T)
    qaug = np.zeros((8, 5, T), np.float32)
    for h in range(8):
        s = slopes[h]
        qaug[h, 0] = s
        qaug[h, 1] = s
        qaug[h, 2] = -s * (t // 128) * 128
        qaug[h, 3] = -s * (t % 128)
        qaug[h, 4] = -s * ((t % 128) - 31)
    c["qaug"] = qaug.astype(bf16)
    ka = np.zeros((5, T), np.float32)
    ka[0] = (t // 128) * 128
    ka[1] = t % 128
    ka[2] = 1
    ka[3] = 1
    c["kaug"] = ka.astype(bf16)
    cc = np.arange(512)
    kc = np.zeros((5, 512), np.float32)
    kc[0] = (cc // 128) * 2048
    kc[1] = (cc % 128) * 16
    kc[2] = 1
    kc[4] = 1
    c["kaugc"] = kc.astype(bf16)
    for k in ("qaug", "kaug", "kaugc"):
        assert np.all(c[k].astype(np.float32) == {"qaug": qaug, "kaug": ka, "kaugc": kc}[k])
    E = np.zeros((128, 64, 128), np.float32)
    for kt in range(64):
        E[2 * kt, kt, :64] = BIG
        E[2 * kt + 1, kt, 64:] = BIG
    c["eexp"] = E.astype(bf16)
    M = np.zeros((512, 128), np.float32)
    for cidx in range(511):
        for s in (cidx, cidx + 1):
            M[cidx, s // 4] += 1
    c["impm"] = M.astype(bf16)
    keep = np.zeros((128, 256), np.float32)
    add = np.zeros((128, 256), np.float32)
    pp = np.arange(128)
    for col in range(256):
        r = col - 128
        if r < -1:
            keep[:, col] = 1
        elif r == -1:
            keep[:, col] = (pp >= 64)
            add[:, col] = 1e6 * (pp < 64)
        elif r == 0:
            add[:, col] = 1e6
        elif r == 1:
            add[:, col] = np.where(pp >= 64, 1e6, -1.0)
        else:
            add[:, col] = -1.0
    c["keept"] = keep
    c["addt"] = add
    return c


CONST_SPECS = {
    "ident": ([128, 128], BF16), "ropeC": ([32, T], F32), "ropeS": ([32, T], F32),
    "cmk": ([128, 4, 512], BF16), "wmk": ([128, 4, 512], BF16), "pmk": ([128, 5, 512], BF16),
    "qaug": ([8, 5, T], BF16), "kaug": ([5, T], BF16), "kaugc": ([5, 512], BF16),
    "eexp": ([128, 64, 128], BF16), "impm": ([512, 128], BF16),
    "keept": ([128, 256], F32), "addt": ([128, 256], F32),
}

IN_SPECS = {
    "x": [T, D], "p": [T, 256], "w_in": [D, 1720], "w_ck1": [2048, 256], "w_ck2": [256, 64],
    "pos_ck": [32, 64], "w_cv1": [2048, 256], "w_cv2": [256, 64], "pos_cv": [32, 64],
    "mla_q_norm": [256, 1], "w_uq": [256, 768], "mla_kv_norm": [128, 1], "w_ukv": [128, 1024],
    "w_out": [D, D], "ln1_g": [1, D], "ln1_b": [1, D], "w_up": [D, 2 * DFF], "w_down": [DFF, D],
    "ln2_g": [1, D], "ln2_b": [1, D], "w_ple_gate": [D, D], "w_ple": [256, D],
    "ln3_g": [1, D], "ln3_b": [1, D],
}

SCRATCH_SPECS = {
    "QN": ([512, T], BF16),
    "KX": ([4, 128, T], BF16),
    "VSW": ([T, 256], BF16),
    "G": ([24, T], F32),
    "KPE": ([32, T], BF16),
    "QM": ([8, 96, T], BF16),
    "KN": ([512, T], BF16),
    "VM": ([T, 512], BF16),
    "ON": ([512, T], BF16),
    "OM": ([512, T], BF16),
    "WUP": ([22, 128, 2, 8, 128], BF16),
}


def sb(nc, es, name, shape, dt):
    return es.enter_context(nc.sbuf_tensor(name, list(shape), dt))


def ps(nc, es, name, shape, dt):
    return es.enter_context(nc.psum_tensor(name, list(shape), dt))


def rot_tiles(nc, es, name, shape, dt, n, psum=False):
    items = []
    for i in range(n):
        t = (ps if psum else sb)(nc, es, "%s%d" % (name, i), shape, dt)
        items.append((t, Buf("%s%d" % (name, i))))
    return Rot(items)


class Ctx:
    pass


def build_program(stop_after=None, debug_scratch=False, stop_mid=None):
    nc = bass.Bass("TRN2", target_bir_lowering=False)
    c = Ctx()
    c.nc = nc
    c.din = {k: nc.dram_tensor(k, v, F32, kind="ExternalInput").ap() for k, v in IN_SPECS.items()}
    c.dc = {k: nc.dram_tensor("c_" + k, v[0], v[1], kind="ExternalInput").ap() for k, v in CONST_SPECS.items()}
    skind = "ExternalOutput" if debug_scratch else "Internal"
    c.ds = {k: nc.dram_tensor("s_" + k, v[0], v[1], kind=skind).ap() for k, v in SCRATCH_SPECS.items()}
    c.dsb = {k: Buf("s_" + k) for k in SCRATCH_SPECS}
    c.out = nc.dram_tensor("out", [T, D], F32, kind="ExternalOutput").ap()
    c.out_b = Buf("out")
    c.stop_mid = stop_mid
    if stop_mid == "compress":
        c.dbg = {"kcmp": nc.dram_tensor("dbg_kcmp", [69, 2, 512], BF16, kind="ExternalOutput").ap(),
                 "vcmp": nc.dram_tensor("dbg_vcmp", [128, 4, 2, 128], BF16, kind="ExternalOutput").ap()}
    with ExitStack() as es0:
        kb = KB(nc, es0)
        c.kb = kb
        c.ident = sb(nc, es0, "ident", [128, 128], BF16)
        c.ident_b = Buf("ident")
        kb.dma(c.ident[:], c.dc["ident"][:], writes=[c.ident_b])
        phases = [phase1, phase2, phase3]
        for ph in phases:
            ph(c)
            kb.barrier()
            if stop_after == ph.__name__:
                break
        kb.barrier()
    return nc


def phase1(c):
    nc, kb = c.nc, c.kb
    din, dc, ds, dsb = c.din, c.dc, c.ds, c.dsb
    with ExitStack() as es:
        wtr = sb(nc, es, "wtr", [128, 8, 1048], BF16)
        wtok = sb(nc, es, "wtok", [128, 8, 640], BF16)
        wkpA = sb(nc, es, "wkpA", [128, 8, 96], BF16)
        wkpB = sb(nc, es, "wkpB", [128, 8, 96], BF16)
        wuqA = sb(nc, es, "wuqA", [128, 2, 8, 96], BF16)
        wuqB = sb(nc, es, "wuqB", [128, 2, 8, 96], BF16)
        wkvK = sb(nc, es, "wkvK", [128, 8, 64], BF16)
        wkvV = sb(nc, es, "wkvV", [128, 8, 64], BF16)
        gq = sb(nc, es, "gq", [128, 2], F32)
        gkv = sb(nc, es, "gkv", [128, 1], F32)
        W = Buf("p1w")
        es_stg = ExitStack()
        stg = rot_tiles(nc, es_stg, "stg", [128, 1720], F32, 2)
        kb.op("pool", lambda e: e.memset(wkpA[:], 0.0), writes=[W])
        kb.op("pool", lambda e: e.memset(wkpB[:], 0.0), writes=[W])
        kb.op("pool", lambda e: e.memset(wuqB[:], 0.0), writes=[W])
        for kt in range(2):
            kb.dma(gq[:, kt:kt + 1], din["mla_q_norm"][kt * 128:(kt + 1) * 128, :], writes=[W])
        kb.dma(gkv[:], din["mla_kv_norm"][:, :], writes=[W])
        for ft in range(8):
            s, sbuf_ = stg.next()
            kb.dma(s[:], din["w_in"][ft * 128:(ft + 1) * 128, :], writes=[sbuf_])
            kb.op("dve", lambda e: e.tensor_copy(out=wtr[:, ft, 0:896], in_=s[:, 0:896]), reads=[sbuf_], writes=[W])
            kb.op("pool", lambda e: e.tensor_copy(out=wtr[:, ft, 896:1024], in_=s[:, 1024:1152]), reads=[sbuf_], writes=[W])
            kb.op("pool", lambda e: e.tensor_copy(out=wtr[:, ft, 1024:1048], in_=s[:, 1280:1304]), reads=[sbuf_], writes=[W])
            kb.op("dve", lambda e: e.tensor_copy(out=wtok[:, ft, 0:384], in_=s[:, 1304:1688]), reads=[sbuf_], writes=[W])
            kb.op("pool", lambda e: e.tensor_copy(out=wtok[:, ft, 384:512], in_=s[:, 896:1024]), reads=[sbuf_], writes=[W])
            kb.op("pool", lambda e: e.tensor_copy(out=wtok[:, ft, 512:640], in_=s[:, 1152:1280]), reads=[sbuf_], writes=[W])
            kb.op("dve", lambda e: e.tensor_copy(out=wkpA[:, ft, 64:96], in_=s[:, 1688:1720]), reads=[sbuf_], writes=[W])
            kb.op("dve", lambda e: e.tensor_copy(out=wkpB[:, ft, 64:80], in_=s[:, 1704:1720]), reads=[sbuf_], writes=[W])
            kb.op("dve", lambda e: e.tensor_copy(out=wkpB[:, ft, 80:96], in_=s[:, 1688:1704]), reads=[sbuf_], writes=[W])
        for kt in range(2):
            s, sbuf_ = stg.next()
            kb.dma(s[:, 0:768], din["w_uq"][kt * 128:(kt + 1) * 128, :], writes=[sbuf_])
            s3 = s[:, 0:768].rearrange("p (h c) -> p h c", c=96)
            kb.op("dve", lambda e: e.tensor_scalar(out=wuqA[:, kt, :, :], in0=s3, scalar1=gq[:, kt:kt + 1], scalar2=None,
                                                   op0=ALU.mult), reads=[sbuf_, W], writes=[W])
            kb.op("dve", lambda e: e.tensor_scalar(out=wuqB[:, kt, :, 64:80], in0=s3[:, :, 80:96], scalar1=gq[:, kt:kt + 1],
                                                   scalar2=None, op0=ALU.mult), reads=[sbuf_, W], writes=[W])
            kb.op("dve", lambda e: e.tensor_scalar(out=wuqB[:, kt, :, 80:96], in0=s3[:, :, 64:80], scalar1=gq[:, kt:kt + 1],
                                                   scalar2=None, op0=ALU.mult), reads=[sbuf_, W], writes=[W])
        s, sbuf_ = stg.next()
        kb.dma(s[:, 0:1024], din["w_ukv"][:, :], writes=[sbuf_])
        s3 = s[:, 0:1024].rearrange("p (h c) -> p h c", c=128)
        kb.op("dve", lambda e: e.tensor_scalar(out=wkvK[:, :, :], in0=s3[:, :, 0:64], scalar1=gkv[:, 0:1], scalar2=None,
                                               op0=ALU.mult), reads=[sbuf_, W], writes=[W])
        kb.op("dve", lambda e: e.tensor_scalar(out=wkvV[:, :, :], in0=s3[:, :, 64:128], scalar1=gkv[:, 0:1], scalar2=None,
                                               op0=ALU.mult), reads=[sbuf_, W], writes=[W])
        kb.barrier()
        es_stg.close()
        XF = rot_tiles(nc, es, "xf", [128, 4, 1024], F32, 2)
        XT = rot_tiles(nc, es, "xT", [128, 8, 512], BF16, 2)
        STT = rot_tiles(nc, es, "stT", [128, 8, 512], BF16, 2)
        GST = rot_tiles(nc, es, "gst", [24, 512], F32, 2)
        RC = rot_tiles(nc, es, "rc", [96, 2, 512], F32, 3)
        KST = rot_tiles(nc, es, "kst", [96, 512], BF16, 2)
        RT = rot_tiles(nc, es, "rt", [96, 2, 512], F32, 2)
        VST = rot_tiles(nc, es, "vst", [128, 4, 256], BF16, 2)
        SS = rot_tiles(nc, es, "ss", [128, 4], F32, 2)
        JK = rot_tiles(nc, es, "jk", [128, 256], F32, 2)
        CN = rot_tiles(nc, es, "cn", [128, 384], BF16, 2)
        CT = rot_tiles(nc, es, "cT", [128, 3, 512], BF16, 2)
        QST = rot_tiles(nc, es, "qst", [96, 8, 512], BF16, 2)
        KNST = rot_tiles(nc, es, "knst", [128, 4, 512], BF16, 2)
        VMST = rot_tiles(nc, es, "vmst", [128, 4, 512], BF16, 2)
        TP = rot_tiles(nc, es, "tp", [128, 1024], BF16, 2, psum=True)
        TPQ = rot_tiles(nc, es, "tpq", [128, 3, 128], BF16, 1, psum=True)
        PS = rot_tiles(nc, es, "ps", [128, 512], F32, 5, psum=True)
        ident, ident_b = c.ident, c.ident_b
        xv = din["x"]
        XB = rot_tiles(nc, es, "xbb", [128, 4, 1024], BF16, 2)
        LS = {}

        def L_load(ch):
            t0 = ch * 512
            xf, xf_b = XF.next()
            kb.dma(xf[:], xv[t0:t0 + 512, :].rearrange("(s p) d -> p s d", p=128), writes=[xf_b])
            rc, rc_b = RC.next()
            kb.dma(rc[64:96, 0, :], dc["ropeC"][:, t0:t0 + 512], writes=[rc_b])
            kb.dma(rc[64:96, 1, :], dc["ropeS"][:, t0:t0 + 512], writes=[rc_b])
            LS[ch] = {"xf": (xf, xf_b), "rc": (rc, rc_b)}

        def L_cast(ch):
            xf, xf_b = LS[ch]["xf"]
            xb, xb_b = XB.next()
            kb.op("dve", lambda e: e.tensor_copy(out=xb[:, 0:3, :], in_=xf[:, 0:3, :]), reads=[xf_b], writes=[xb_b])
            kb.op("pool", lambda e: e.tensor_copy(out=xb[:, 3:4, :], in_=xf[:, 3:4, :]), reads=[xf_b], writes=[xb_b])
            LS[ch]["xb"] = (xb, xb_b)

        def L_tr(ch):
            xb, xb_b = LS[ch]["xb"]
            xT, xT_b = XT.next()
            for f2 in range(4):
                tp, tp_b = TP.next()
                for fi in range(2):
                    ft = f2 * 2 + fi
                    for s_ in range(4):
                        last = (fi == 1 and s_ == 3)
                        kb.op("pe", lambda e: e.transpose(out=tp[:, fi * 512 + s_ * 128: fi * 512 + (s_ + 1) * 128],
                                                          in_=xb[:, s_, ft * 128:(ft + 1) * 128], identity=ident[:]),
                              reads=[xb_b, ident_b], writes=[tp_b], inc=last)
                dst = xT[:, 2 * f2:2 * f2 + 2, :].rearrange("p a t -> p (a t)")
                kb.op("act" if f2 % 2 else "dve",
                      (lambda e: e.activation(out=dst, in_=tp[:], func=AF.Copy)) if f2 % 2 else
                      (lambda e: e.tensor_copy(out=dst, in_=tp[:])), reads=[tp_b], writes=[xT_b])
            LS[ch]["xT"] = (xT, xT_b)

        L_load(0)
        L_cast(0)
        L_tr(0)
        if NCH > 1:
            L_load(1)
        for ch in range(NCH):
            t0 = ch * 512
            xT, xT_b = LS[ch]["xT"]
            rc, rc_b = LS[ch]["rc"]
            if ch + 2 < NCH:
                L_load(ch + 2)
            if ch + 1 < NCH:
                L_cast(ch + 1)
            stT, stT_b = STT.next()
            for ct in range(8):
                pt, pt_b = PS.next()
                for ft in range(8):
                    kb.op("pe", lambda e: e.matmul(pt[:, :], lhsT=wtr[:, ft, ct * 128:(ct + 1) * 128], rhs=xT[:, ft, :],
                                                   start=(ft == 0), stop=(ft == 7)),
                          reads=[W, xT_b], writes=[pt_b], inc=(ft == 7))
                if ct < 4:
                    kb.op("act", lambda e: e.activation(out=stT[:, ct, :], in_=pt[:, :], func=AF.Copy, scale=0.125),
                          reads=[pt_b], writes=[stT_b])
                else:
                    kb.op("dve", lambda e: e.tensor_copy(out=stT[:, ct, :], in_=pt[:, :]), reads=[pt_b], writes=[stT_b])
            kb.dma(ds["QN"].rearrange("(c p) t -> p c t", p=128)[:, :, t0:t0 + 512], stT[:, 0:4, :],
                   reads=[stT_b], writes=[dsb["QN"]])
            kb.dma(ds["KX"].rearrange("k p t -> p k t")[:, :, t0:t0 + 512], stT[:, 4:8, :],
                   reads=[stT_b], writes=[dsb["KX"]])
            pt, pt_b = PS.next()
            for ft in range(8):
                kb.op("pe", lambda e: e.matmul(pt[0:24, :], lhsT=wtr[:, ft, 1024:1048], rhs=xT[:, ft, :],
                                               start=(ft == 0), stop=(ft == 7)),
                      reads=[W, xT_b], writes=[pt_b], inc=(ft == 7))
            gst, gst_b = GST.next()
            kb.op("act", lambda e: e.activation(out=gst[:, :], in_=pt[0:24, :], func=AF.Sigmoid), reads=[pt_b], writes=[gst_b])
            kb.dma(ds["G"][:, t0:t0 + 512], gst[:, :], reads=[gst_b], writes=[dsb["G"]])
            pa, pa_b = PS.next()
            for ft in range(8):
                kb.op("pe", lambda e: e.matmul(pa[0:96, :], lhsT=wkpA[:, ft, :], rhs=xT[:, ft, :], start=(ft == 0), stop=(ft == 7)),
                      reads=[W, xT_b], writes=[pa_b], inc=(ft == 7))
            pb, pb_b = PS.next()
            for ft in range(8):
                kb.op("pe", lambda e: e.matmul(pb[0:96, :], lhsT=wkpB[:, ft, :], rhs=xT[:, ft, :], start=(ft == 0), stop=(ft == 7)),
                      reads=[W, xT_b], writes=[pb_b], inc=(ft == 7))
            rt, rt_b = RT.next()
            kst, kst_b = KST.next()
            kb.op("dve", lambda e: e.tensor_tensor(out=rt[64:96, 0, :], in0=pa[64:96, :], in1=rc[64:96, 0, :], op=ALU.mult),
                  reads=[pa_b, rc_b], writes=[rt_b])
            kb.op("dve", lambda e: e.tensor_tensor(out=rt[64:96, 1, :], in0=pb[64:96, :], in1=rc[64:96, 1, :], op=ALU.mult),
                  reads=[pb_b, rc_b], writes=[rt_b])
            kb.op("pool", lambda e: e.tensor_tensor(out=kst[64:96, :], in0=rt[64:96, 0, :], in1=rt[64:96, 1, :], op=ALU.add),
                  reads=[rt_b], writes=[kst_b])
            kb.dma(ds["KPE"][:, t0:t0 + 512], kst[64:96, :], reads=[kst_b], writes=[dsb["KPE"]])
            if ch + 1 < NCH:
                L_tr(ch + 1)
            vst, vst_b = VST.next()
            cT, cT_b = CT.next()
            for s_ in range(4):
                p1, p1_b = PS.next()
                for ft in range(8):
                    kb.op("pe", lambda e: e.matmul(p1[:, 0:384], lhsT=xT[:, ft, s_ * 128:(s_ + 1) * 128], rhs=wtok[:, ft, 0:384],
                                                   start=(ft == 0), stop=(ft == 7)),
                          reads=[W, xT_b], writes=[p1_b], inc=(ft == 7))
                p2, p2_b = PS.next()
                for ft in range(8):
                    kb.op("pe", lambda e: e.matmul(p2[:, 0:256], lhsT=xT[:, ft, s_ * 128:(s_ + 1) * 128], rhs=wtok[:, ft, 384:640],
                                                   start=(ft == 0), stop=(ft == 7)),
                          reads=[W, xT_b], writes=[p2_b], inc=(ft == 7))
                kb.op("dve", lambda e: e.tensor_copy(out=vst[:, s_, :], in_=p2[:, 0:256]), reads=[p2_b], writes=[vst_b])
                ss, ss_b = SS.next()
                jk, jk_b = JK.next()
                kb.op("act", lambda e: e.activation(out=jk[:, 0:256], in_=p1[:, 0:256], func=AF.Square, accum_out=ss[:, 0:1]),
                      reads=[p1_b], writes=[jk_b, ss_b])
                kb.op("act", lambda e: e.activation(out=jk[:, 0:128], in_=p1[:, 256:384], func=AF.Square, accum_out=ss[:, 1:2]),
                      reads=[p1_b], writes=[jk_b, ss_b])
                kb.op("dve", lambda e: e.tensor_scalar(out=ss[:, 0:1], in0=ss[:, 0:1], scalar1=1.0 / 256, scalar2=RMS_EPS,
                                                       op0=ALU.mult, op1=ALU.add), reads=[ss_b], writes=[ss_b])
                kb.op("dve", lambda e: e.tensor_scalar(out=ss[:, 1:2], in0=ss[:, 1:2], scalar1=1.0 / 128, scalar2=RMS_EPS,
                                                       op0=ALU.mult, op1=ALU.add), reads=[ss_b], writes=[ss_b])
                kb.op("act", lambda e: e.activation(out=ss[:, 2:4], in_=ss[:, 0:2], func=AF.Sqrt), reads=[ss_b], writes=[ss_b])
                kb.op("dve", lambda e: e.reciprocal(out=ss[:, 0:2], in_=ss[:, 2:4]), reads=[ss_b], writes=[ss_b])
                cn, cn_b = CN.next()
                kb.op("dve", lambda e: e.tensor_scalar(out=cn[:, 0:256], in0=p1[:, 0:256], scalar1=ss[:, 0:1], scalar2=None,
                                                       op0=ALU.mult), reads=[p1_b, ss_b], writes=[cn_b])
                kb.op("dve", lambda e: e.tensor_scalar(out=cn[:, 256:384], in0=p1[:, 256:384], scalar1=ss[:, 1:2], scalar2=None,
                                                       op0=ALU.mult), reads=[p1_b, ss_b], writes=[cn_b])
                tq, tq_b = TPQ.next()
                for i in range(3):
                    kb.op("pe", lambda e: e.transpose(out=tq[:, i, :], in_=cn[:, i * 128:(i + 1) * 128], identity=ident[:]),
                          reads=[cn_b, ident_b], writes=[tq_b], inc=(i == 2))
                kb.op("act", lambda e: e.activation(out=cT[:, :, s_ * 128:(s_ + 1) * 128], in_=tq[:, :, :], func=AF.Copy),
                      reads=[tq_b], writes=[cT_b])
            kb.dma(ds["VSW"][t0:t0 + 512, :].rearrange("(s p) c -> p s c", p=128), vst[:, :, :], reads=[vst_b], writes=[dsb["VSW"]])
            qst, qst_b = QST.next()
            for h in range(8):
                pa, pa_b = PS.next()
                for kt in range(2):
                    kb.op("pe", lambda e: e.matmul(pa[0:96, :], lhsT=wuqA[:, kt, h, :], rhs=cT[:, kt, :], start=(kt == 0), stop=(kt == 1)),
                          reads=[W, cT_b], writes=[pa_b], inc=(kt == 1))
                pb, pb_b = PS.next()
                for kt in range(2):
                    kb.op("pe", lambda e: e.matmul(pb[0:96, :], lhsT=wuqB[:, kt, h, :], rhs=cT[:, kt, :], start=(kt == 0), stop=(kt == 1)),
                          reads=[W, cT_b], writes=[pb_b], inc=(kt == 1))
                kb.op("act", lambda e: e.activation(out=qst[0:64, h, :], in_=pa[0:64, :], func=AF.Copy), reads=[pa_b], writes=[qst_b])
                rt, rt_b = RT.next()
                kb.op("dve", lambda e: e.tensor_tensor(out=rt[64:96, 0, :], in0=pa[64:96, :], in1=rc[64:96, 0, :], op=ALU.mult),
                      reads=[pa_b, rc_b], writes=[rt_b])
                kb.op("dve", lambda e: e.tensor_tensor(out=rt[64:96, 1, :], in0=pb[64:96, :], in1=rc[64:96, 1, :], op=ALU.mult),
                      reads=[pb_b, rc_b], writes=[rt_b])
                kb.op("pool", lambda e: e.tensor_tensor(out=qst[64:96, h, :], in0=rt[64:96, 0, :], in1=rt[64:96, 1, :], op=ALU.add),
                      reads=[rt_b], writes=[qst_b])
            kb.dma(ds["QM"].rearrange("h r t -> r h t")[:, :, t0:t0 + 512], qst[:, :, :], reads=[qst_b], writes=[dsb["QM"]])
            knst, knst_b = KNST.next()
            for hp in range(4):
                pt, pt_b = PS.next()
                kb.op("pe", lambda e: e.matmul(pt[:, :], lhsT=wkvK[:, 2 * hp:2 * hp + 2, :].rearrange("p a c -> p (a c)"),
                                               rhs=cT[:, 2, :], start=True, stop=True), reads=[W, cT_b], writes=[pt_b])
                kb.op("act" if hp % 2 else "dve",
                      (lambda e: e.activation(out=knst[:, hp, :], in_=pt[:, :], func=AF.Copy)) if hp % 2 else
                      (lambda e: e.tensor_copy(out=knst[:, hp, :], in_=pt[:, :])), reads=[pt_b], writes=[knst_b])
            kb.dma(ds["KN"].rearrange("(c p) t -> p c t", p=128)[:, :, t0:t0 + 512], knst[:, :, :], reads=[knst_b], writes=[dsb["KN"]])
            vmst, vmst_b = VMST.next()
            for s_ in range(4):
                pt, pt_b = PS.next()
                kb.op("pe", lambda e: e.matmul(pt[:, :], lhsT=cT[:, 2, s_ * 128:(s_ + 1) * 128],
                                               rhs=wkvV[:, :, :].rearrange("p a c -> p (a c)"), start=True, stop=True),
                      reads=[W, cT_b], writes=[pt_b])
                kb.op("act" if s_ % 2 else "dve",
                      (lambda e: e.activation(out=vmst[:, s_, :], in_=pt[:, :], func=AF.Copy)) if s_ % 2 else
                      (lambda e: e.tensor_copy(out=vmst[:, s_, :], in_=pt[:, :])), reads=[pt_b], writes=[vmst_b])
            kb.dma(ds["VM"][t0:t0 + 512, :].rearrange("(s p) c -> p s c", p=128), vmst[:, :, :], reads=[vmst_b], writes=[dsb["VM"]])


class TileD:
    __slots__ = ("qk", "scale", "pv", "epi")

    def __init__(self, qk, scale, pv, epi=None):
        self.qk = qk
        self.scale = scale
        self.pv = pv
        self.epi = epi


class Stream:
    def __init__(self, c, R, L=2):
        self.c, self.R, self.L = c, R, L
        self.pending = []

    def push(self, t):
        kb, R = self.c.kb, self.R
        S, S_b = R.SPS.next()
        nq = len(t.qk)
        for a, (lh, rh, rb) in enumerate(t.qk):
            kb.op("pe", lambda e: e.matmul(S[:, :], lhsT=lh, rhs=rh, start=(a == 0), stop=(a == nq - 1)),
                  reads=rb, writes=[S_b], inc=(a == nq - 1))
        P, P_b = R.PSB.next()
        kb.op("act", lambda e: e.activation(out=P[:, :], in_=S[:, :], func=AF.Exp, scale=t.scale), reads=[S_b], writes=[P_b])
        self.pending.append((t, P, P_b))
        if len(self.pending) > self.L:
            self._pop()

    def _pop(self):
        t, P, P_b = self.pending.pop(0)
        t.pv(P, P_b)
        if t.epi is not None:
            t.epi()

    def flush(self):
        while self.pending:
            self._pop()


def run_stream(c, R, tiles, L=2):
    st = c.stream
    for t in tiles:
        st.push(t)


def phase2(c):
    nc, kb = c.nc, c.kb
    din, dc, ds, dsb = c.din, c.dc, c.ds, c.dsb
    ident, ident_b = c.ident, c.ident_b
    with ExitStack() as esA:
        kcmp = sb(nc, esA, "kcmp", [69, 2, 512], BF16)
        kcmp_b = Buf("kcmp")
        vcmp = sb(nc, esA, "vcmp", [128, 4, 2, 128], BF16)
        vcmp_b = Buf("vcmp")
        CB = Buf("consts2")
        cmk = sb(nc, esA, "cmk", [128, 4, 512], BF16)
        wmk = sb(nc, esA, "wmk", [128, 4, 512], BF16)
        pmk = sb(nc, esA, "pmk", [128, 5, 512], BF16)
        eexp = sb(nc, esA, "eexp", [128, 64, 128], BF16)
        impm = sb(nc, esA, "impm", [128, 4, 128], BF16)
        keept = sb(nc, esA, "keept", [128, 256], F32)
        addt = sb(nc, esA, "addt", [128, 256], F32)
        kb.dma(cmk[:], dc["cmk"][:], writes=[CB])
        kb.dma(wmk[:], dc["wmk"][:], writes=[CB])
        kb.dma(pmk[:], dc["pmk"][:], writes=[CB])
        kb.dma(eexp[:], dc["eexp"][:], writes=[CB])
        kb.dma(impm[:], dc["impm"].rearrange("(ct p) b -> p ct b", p=128), writes=[CB])
        kb.dma(keept[:], dc["keept"][:], writes=[CB])
        kb.dma(addt[:], dc["addt"][:], writes=[CB])
        kb.op("pool", lambda e: e.memset(kcmp[:], 0.0), writes=[kcmp_b])
        kb.op("pool", lambda e: e.memset(vcmp[:], 1.0), writes=[vcmp_b])
        for g in range(2):
            kb.dma(kcmp[64:69, g, :], dc["kaugc"][:, :], writes=[kcmp_b])
        with ExitStack() as es:
            kcg = sb(nc, es, "kcg", [64, 4, T], BF16)
            kcg_b = Buf("kcg")
            for kind in range(2):
                for g in range(2):
                    kb.dma(kcg[:, kind * 2 + g, :], ds["KX"][kind, g * 64:(g + 1) * 64, :], reads=[dsb["KX"]], writes=[kcg_b])
            w1s = sb(nc, es, "w1s", [64, 32, 256], F32)
            w1s_b = Buf("w1s")
            w1 = [sb(nc, es, "w1_%d" % k, [64, 32, 256], BF16) for k in range(2)]
            w2s = sb(nc, es, "w2s", [128, 2, 64], F32)
            w2s_b = Buf("w2s")
            w2 = [sb(nc, es, "w2_%d" % k, [128, 2, 64], BF16) for k in range(2)]
            posS = sb(nc, es, "posS", [32, 64], F32)
            posS_b = Buf("posS")
            posB = sb(nc, es, "posB", [32, 64], BF16)
            posB_b = Buf("posB")
            posT = sb(nc, es, "posT", [64, 2, 32], BF16)
            b1 = sb(nc, es, "b1", [128, 4], F32)
            Wc = Buf("wc")
            tpp = ps(nc, es, "tpp", [64, 32], BF16)
            tpp_b = Buf("tpp")
            pb1 = ps(nc, es, "pb1", [128, 4], F32)
            pb1_b = Buf("pb1")
            PSC = rot_tiles(nc, es, "psc", [128, 512], F32, 4, psum=True)
            XG = rot_tiles(nc, es, "xg", [128, 512], F32, 2)
            X2 = rot_tiles(nc, es, "x2", [128, 512], F32, 2)
            GL = rot_tiles(nc, es, "gl", [128, 512], BF16, 4)
            for gl, gl_b in GL.items:
                kb.op("pool", lambda e: e.memset(gl[:], 0.0), writes=[gl_b])
            names = [("w_ck1", "w_ck2", "pos_ck"), ("w_cv1", "w_cv2", "pos_cv")]
            for kind, (n1, n2, npos) in enumerate(names):
                kb.dma(w1s[:], din[n1].rearrange("(l d) n -> d l n", d=64), writes=[w1s_b])
                kb.op("dve", lambda e: e.tensor_copy(out=w1[kind][:, 0:16, :], in_=w1s[:, 0:16, :]), reads=[w1s_b], writes=[Wc])
                kb.op("pool", lambda e: e.tensor_copy(out=w1[kind][:, 16:32, :], in_=w1s[:, 16:32, :]), reads=[w1s_b], writes=[Wc])
                kb.dma(w2s[:], din[n2].rearrange("(c p) n -> p c n", p=128), writes=[w2s_b])
                kb.op("dve", lambda e: e.tensor_copy(out=w2[kind][:], in_=w2s[:]), reads=[w2s_b], writes=[Wc])
                kb.dma(posS[:], din[npos][:, :], writes=[posS_b])
                kb.op("dve", lambda e: e.tensor_copy(out=posB[:], in_=posS[:]), reads=[posS_b], writes=[posB_b])
                kb.op("pe", lambda e: e.transpose(out=tpp[:, :], in_=posB[:, :], identity=ident[0:32, 0:32]),
                      reads=[posB_b, ident_b], writes=[tpp_b])
                kb.op("dve", lambda e: e.tensor_copy(out=posT[:, kind, :], in_=tpp[:, :]), reads=[tpp_b], writes=[Wc])
            for kind in range(2):
                for hc in range(2):
                    idx = kind * 2 + hc
                    for l in range(32):
                        kb.op("pe", lambda e: e.matmul(pb1[:, idx:idx + 1], lhsT=w1[kind][:, l, hc * 128:(hc + 1) * 128],
                                                       rhs=posT[:, kind, l:l + 1], start=(l == 0), stop=(l == 31)),
                              reads=[Wc], writes=[pb1_b], inc=(l == 31))
            kb.op("dve", lambda e: e.tensor_copy(out=b1[:], in_=pb1[:]), reads=[pb1_b], writes=[Wc])
            for kind in range(2):
                for g in range(2):
                    gls = []
                    for hc in range(2):
                        idx = kind * 2 + hc
                        pm, pm_b = PSC.next()
                        for l in range(32):
                            kb.op("pe", lambda e: e.matmul(pm[:, 0:511], lhsT=w1[kind][:, l, hc * 128:(hc + 1) * 128],
                                                           rhs=kcg[:, kind * 2 + g, l:l + 8161:16], start=(l == 0), stop=(l == 31)),
                                  reads=[Wc, kcg_b], writes=[pm_b], inc=(l == 31))
                        xg, xg_b = XG.next()
                        x2, x2_b = X2.next()
                        gl, gl_b = GL.next()
                        kb.op("dve", lambda e: e.tensor_scalar(out=xg[:, 0:511], in0=pm[:, 0:511], scalar1=b1[:, idx:idx + 1],
                                                               scalar2=None, op0=ALU.add), reads=[pm_b, Wc], writes=[xg_b])
                        kb.op("pool", lambda e: e.tensor_tensor(out=x2[:, 0:511], in0=xg[:, 0:511], in1=xg[:, 0:511], op=ALU.mult),
                              reads=[xg_b], writes=[x2_b])
                        kb.op("dve", lambda e: e.tensor_scalar(out=x2[:, 0:511], in0=x2[:, 0:511], scalar1=0.044715, scalar2=1.0,
                                                               op0=ALU.mult, op1=ALU.add), reads=[x2_b], writes=[x2_b])
                        kb.op("pool", lambda e: e.tensor_tensor(out=x2[:, 0:511], in0=x2[:, 0:511], in1=xg[:, 0:511], op=ALU.mult),
                              reads=[x2_b, xg_b], writes=[x2_b])
                        kb.op("act", lambda e: e.activation(out=x2[:, 0:511], in_=x2[:, 0:511], func=AF.Sigmoid, scale=1.5957691216),
                              reads=[x2_b], writes=[x2_b])
                        kb.op("dve", lambda e: e.tensor_tensor(out=gl[:, 0:511], in0=x2[:, 0:511], in1=xg[:, 0:511], op=ALU.mult),
                              reads=[x2_b, xg_b], writes=[gl_b])
                        gls.append((gl, gl_b))
                    if kind == 0:
                        pk, pk_b = PSC.next()
                        for hc in range(2):
                            kb.op("pe", lambda e: e.matmul(pk[0:64, 0:511], lhsT=w2[0][:, hc, :], rhs=gls[hc][0][:, 0:511],
                                                           start=(hc == 0), stop=(hc == 1)),
                                  reads=[Wc, gls[hc][1]], writes=[pk_b], inc=(hc == 1))
                        kb.op("act", lambda e: e.activation(out=kcmp[0:64, g, 0:511], in_=pk[0:64, 0:511], func=AF.Copy),
                              reads=[pk_b], writes=[kcmp_b])
                    else:
                        for ct in range(4):
                            pv, pv_b = PSC.next()
                            for hc in range(2):
                                kb.op("pe", lambda e: e.matmul(pv[:, 0:64], lhsT=gls[hc][0][:, ct * 128:(ct + 1) * 128], rhs=w2[1][:, hc, :],
                                                               start=(hc == 0), stop=(hc == 1)),
                                      reads=[Wc, gls[hc][1]], writes=[pv_b], inc=(hc == 1))
                            kb.op("act", lambda e: e.activation(out=vcmp[:, ct, g, 0:64], in_=pv[:, 0:64], func=AF.Copy),
                                  reads=[pv_b], writes=[vcmp_b])
        kb.barrier()
        if getattr(c, "stop_mid", None) == "compress":
            dbg = c.dbg
            kb.dma(dbg["kcmp"][:], kcmp[:], reads=[kcmp_b])
            kb.dma(dbg["vcmp"][:], vcmp[:], reads=[vcmp_b])
            return
        with ExitStack() as es:
            R = Ctx()
            R.SPS = rot_tiles(nc, es, "sps", [128, 512], F32, 3, psum=True)
            R.OPS = rot_tiles(nc, es, "ops", [128, 512], F32, 2, psum=True)
            R.UPS = rot_tiles(nc, es, "ups", [128, 4, 128], F32, 2, psum=True)
            R.TPN = rot_tiles(nc, es, "tpn", [128, 4, 128], BF16, 1, psum=True)
            R.PSB = rot_tiles(nc, es, "psb", [128, 512], BF16, 4)
            KA = rot_tiles(nc, es, "ka", [96, T], BF16, 2).items
            VA = rot_tiles(nc, es, "va", [128, 64, 128], BF16, 2).items
            for va, va_b in VA:
                kb.op("pool", lambda e: e.memset(va[:], 1.0), writes=[va_b])
            QA = [[(sb(nc, es, "qa%d%d" % (pr, hi), [69, 512], BF16), Buf("qa")) for hi in range(4)] for pr in range(2)]
            OC = [[(sb(nc, es, "oc%d%d" % (pr, hi), [64, 512], F32), Buf("oc")) for hi in range(4)] for pr in range(2)]
            QMT = rot_tiles(nc, es, "qmt", [96, 512], BF16, 3)
            GB = rot_tiles(nc, es, "gb", [64, 3, 512], F32, 4)
            NMT = rot_tiles(nc, es, "nmt", [128, 512], BF16, 2)
            IMP = rot_tiles(nc, es, "imp", [128, 4, 128], F32, 2)
            DEN = rot_tiles(nc, es, "den", [64, 512], F32, 2)
            REC = rot_tiles(nc, es, "rec", [64, 512], F32, 2)
            RG = rot_tiles(nc, es, "rg", [64, 512], F32, 2)
            T1 = rot_tiles(nc, es, "t1", [64, 512], F32, 4)
            ACC = rot_tiles(nc, es, "acc", [64, 512], F32, 2)
            OST = rot_tiles(nc, es, "ost", [64, 512], BF16, 3)
            W1 = rot_tiles(nc, es, "w1k", [128, 128], F32, 2)
            W2 = rot_tiles(nc, es, "w2k", [128, 128], F32, 2)
            M8 = rot_tiles(nc, es, "m8", [128, 16], F32, 2)
            NMS = rot_tiles(nc, es, "nms", [128, 128], BF16, 4)
            RS = rot_tiles(nc, es, "rs", [128, 8], F32, 2)
            Gt = ds["G"].tensor
            c.stream = Stream(c, R)

            def norm_rec(O, O_b):
                den, den_b = DEN.next()
                rec, rec_b = REC.next()
                kb.op("dve", lambda e: e.tensor_scalar_max(out=den[:, :], in0=O[64:128, :], scalar1=1e-30), reads=[O_b], writes=[den_b])
                kb.op("dve", lambda e: e.reciprocal(out=rec[:, :], in_=den[:, :]), reads=[den_b], writes=[rec_b])
                return rec, rec_b

            for g in range(2):
                (ks, ks_b), (kw, kw_b) = KA
                (vs, vs_b), (vw, vw_b) = VA
                kb.dma(ks[0:64, :], ds["KX"][2, g * 64:(g + 1) * 64, :], reads=[dsb["KX"]], writes=[ks_b])
                kb.dma(ks[64:69, :], dc["kaug"][:, :], writes=[ks_b])
                kb.dma(kw[0:64, :], ds["KX"][3, g * 64:(g + 1) * 64, :], reads=[dsb["KX"]], writes=[kw_b])
                kb.dma(kw[64:69, :], dc["kaug"][:, :], writes=[kw_b])
                for half in range(2):
                    hs = slice(half * 32, (half + 1) * 32)
                    rs_ = slice(half * 4096, (half + 1) * 4096)
                    kb.dma(vs[:, hs, 0:64], ds["VSW"][rs_, g * 64:(g + 1) * 64].rearrange("(kt p) d -> p kt d", p=128),
                           reads=[dsb["VSW"]], writes=[vs_b])
                    kb.dma(vw[:, hs, 0:64], ds["VSW"][rs_, 128 + g * 64:128 + (g + 1) * 64].rearrange("(kt p) d -> p kt d", p=128),
                           reads=[dsb["VSW"]], writes=[vw_b])
                CS = {}

                def cmp_begin(qc):
                    CS[qc] = {"imp": IMP.next(), "nmt": NMT.next(), "nms": []}

                def cmp_head(qc, hi, g=g):
                    t0 = qc * 512
                    pr = qc % 2
                    h = 4 * g + hi
                    imp, imp_b = CS[qc]["imp"]
                    qa, qa_b = QA[pr][hi]
                    kb.dma(qa[0:64, :], ds["QN"][h * 64:(h + 1) * 64, t0:t0 + 512], reads=[dsb["QN"]], writes=[qa_b])
                    kb.dma(qa[64:69, :], dc["qaug"][h, :, t0:t0 + 512], writes=[qa_b])
                    O, O_b = R.OPS.next()
                    U, U_b = R.UPS.next()
                    nct = qc // 4 + 1
                    for ct in range(nct):
                        m = qc - 4 * ct
                        qk = [(kcmp[0:69, g, ct * 128:(ct + 1) * 128], qa[0:69, :], [kcmp_b, qa_b])]
                        if m <= 4:
                            qk.append((ident[:, :], pmk[:, m, :], [ident_b, CB]))

                        def pv(P, P_b, ct=ct, nct=nct, O=O, O_b=O_b, U=U, U_b=U_b):
                            kb.op("pe", lambda e: e.matmul(O[:, :], lhsT=vcmp[:, ct, g, :], rhs=P[:, :], start=(ct == 0), stop=(ct == nct - 1)),
                                  reads=[vcmp_b, P_b], writes=[O_b], inc=True)
                            for s_ in range(4):
                                kb.op("pe", lambda e: e.matmul(U[:, s_, :], lhsT=P[:, s_ * 128:(s_ + 1) * 128], rhs=impm[:, ct, :],
                                                               start=(ct == 0), stop=(ct == nct - 1)),
                                      reads=[CB, P_b], writes=[U_b], inc=(s_ == 3))

                        def epi(hi=hi, pr=pr, O=O, O_b=O_b, U=U, U_b=U_b, imp=imp, imp_b=imp_b):
                            rec, rec_b = norm_rec(O, O_b)
                            oc, oc_b = OC[pr][hi]
                            kb.op("dve", lambda e: e.tensor_tensor(out=oc[:, :], in0=O[0:64, :], in1=rec[:, :], op=ALU.mult),
                                  reads=[O_b, rec_b], writes=[oc_b])
                            rs, rs_b = RS.next()
                            kb.op("dve", lambda e: e.tensor_reduce(out=rs[:, 0:4], in_=U[:, :, :], axis=AX.X, op=ALU.add),
                                  reads=[U_b], writes=[rs_b])
                            kb.op("dve", lambda e: e.tensor_scalar(out=rs[:, 0:4], in0=rs[:, 0:4], scalar1=1e-30, scalar2=0.5,
                                                                   op0=ALU.max, op1=ALU.mult), reads=[rs_b], writes=[rs_b])
                            kb.op("dve", lambda e: e.reciprocal(out=rs[:, 4:8], in_=rs[:, 0:4]), reads=[rs_b], writes=[rs_b])
                            for s_ in range(4):
                                if hi == 0:
                                    kb.op("dve", lambda e: e.tensor_scalar(out=imp[:, s_, :], in0=U[:, s_, :], scalar1=rs[:, 4 + s_:5 + s_],
                                                                           scalar2=None, op0=ALU.mult), reads=[U_b, rs_b], writes=[imp_b])
                                else:
                                    kb.op("dve", lambda e: e.scalar_tensor_tensor(out=imp[:, s_, :], in0=U[:, s_, :], scalar=rs[:, 4 + s_:5 + s_],
                                                                                  in1=imp[:, s_, :], op0=ALU.mult, op1=ALU.add),
                                          reads=[U_b, rs_b, imp_b], writes=[imp_b])

                        c.stream.push(TileD(qk, 1.0, pv, epi if ct == nct - 1 else None))

                def topk_dve(qc):
                    imp, imp_b = CS[qc]["imp"]
                    for s_ in range(4):
                        qt = 4 * qc + s_
                        lo = 128 - 2 * qt
                        w, w_b = W1.next()
                        w2, w2_b = W2.next()
                        m8, m8_b = M8.next()
                        nms, nms_b = NMS.next()
                        kb.op("dve", lambda e: e.tensor_tensor(out=w[:, :], in0=imp[:, s_, :], in1=keept[:, lo:lo + 128], op=ALU.mult),
                              reads=[imp_b, CB], writes=[w_b])
                        kb.op("dve", lambda e: e.tensor_tensor(out=w[:, :], in0=w[:, :], in1=addt[:, lo:lo + 128], op=ALU.add),
                              reads=[w_b, CB], writes=[w_b])
                        kb.op("dve", lambda e: e.memset(w[:, 0:1], 1e6), reads=[w_b], writes=[w_b])
                        kb.op("dve", lambda e: e.max(out=m8[:, 0:8], in_=w[:, :]), reads=[w_b], writes=[m8_b])
                        kb.op("dve", lambda e: e.match_replace(out=w2[:, :], in_to_replace=m8[:, 0:8], in_values=w[:, :], imm_value=-2.0),
                              reads=[w_b, m8_b], writes=[w2_b])
                        kb.op("dve", lambda e: e.max(out=m8[:, 8:16], in_=w2[:, :]), reads=[w2_b, m8_b], writes=[m8_b])
                        kb.op("dve", lambda e: e.tensor_scalar(out=nms[:, :], in0=w[:, :], scalar1=m8[:, 15:16], scalar2=-1.0,
                                                               op0=ALU.is_ge, op1=ALU.add), reads=[w_b, m8_b], writes=[nms_b])
                        CS[qc]["nms"].append((nms, nms_b))

                def topk_pe(qc):
                    nmt, nmt_b = CS[qc]["nmt"]
                    tpn, tpn_b = R.TPN.next()
                    for s_ in range(4):
                        nms, nms_b = CS[qc]["nms"][s_]
                        kb.op("pe", lambda e: e.transpose(out=tpn[:, s_, :], in_=nms[:, :], identity=ident[:, :]),
                              reads=[nms_b, ident_b], writes=[tpn_b])
                    kb.op("act", lambda e: e.activation(out=nmt[:, :], in_=tpn[:, :, :].rearrange("p a b -> p (a b)"), func=AF.Copy),
                          reads=[tpn_b], writes=[nmt_b])

                def sw_head(qc, hi, g=g):
                    t0 = qc * 512
                    pr = qc % 2
                    h = 4 * g + hi
                    nmt, nmt_b = CS[qc]["nmt"]
                    qa, qa_b = QA[pr][hi]
                    gb, gb_b = GB.next()
                    kb.dma(gb[:, :, :], bass.AP(Gt, h * 3 * T + t0, [[0, 64], [T, 3], [1, 512]]), reads=[dsb["G"]], writes=[gb_b])
                    st = {}
                    O, O_b = R.OPS.next()
                    nk = 4 * qc + 4
                    k0 = max(0, int((t0 - 127 - ALIBI_CUT * (2.0 ** (h + 1))) // 128) + 1)
                    for kt in range(k0, nk):
                        qk = [(ks[0:69, kt * 128:(kt + 1) * 128], qa[0:69, :], [ks_b, qa_b]),
                              (eexp[:, kt, :], nmt[:, :], [CB, nmt_b])]
                        d = kt - 4 * qc
                        if d >= 0:
                            qk.append((ident[:, :], cmk[:, d, :], [ident_b, CB]))

                        def pv(P, P_b, kt=kt, nk=nk, k0=k0, O=O, O_b=O_b):
                            kb.op("pe", lambda e: e.matmul(O[:, :], lhsT=vs[:, kt, :], rhs=P[:, :], start=(kt == k0), stop=(kt == nk - 1)),
                                  reads=[vs_b, P_b], writes=[O_b], inc=True)

                        def epi(O=O, O_b=O_b, gb=gb, gb_b=gb_b, st=st):
                            rec, rec_b = norm_rec(O, O_b)
                            rg, rg_b = RG.next()
                            t1, t1_b = T1.next()
                            kb.op("pool", lambda e: e.tensor_tensor(out=rg[:, :], in0=rec[:, :], in1=gb[:, 1, :], op=ALU.mult),
                                  reads=[rec_b, gb_b], writes=[rg_b])
                            kb.op("dve", lambda e: e.tensor_tensor(out=t1[:, :], in0=O[0:64, :], in1=rg[:, :], op=ALU.mult),
                                  reads=[O_b, rg_b], writes=[t1_b])
                            st["slc"] = (t1, t1_b)

                        c.stream.push(TileD(qk, 1.0, pv, epi if kt == nk - 1 else None))
                    O, O_b = R.OPS.next()
                    wl = []
                    for d in range(4):
                        wl.append((4 * qc + d, cmk[:, d, :]))
                    for d in range(4):
                        kt = 4 * qc - 4 + d
                        if kt >= 0:
                            wl.append((kt, wmk[:, d, :]))
                    nw = len(wl)
                    for wi, (kt, mk) in enumerate(wl):
                        qk = [(kw[0:69, kt * 128:(kt + 1) * 128], qa[0:69, :], [kw_b, qa_b]),
                              (ident[:, :], mk, [ident_b, CB])]

                        def pv(P, P_b, kt=kt, wi=wi, nw=nw, O=O, O_b=O_b):
                            kb.op("pe", lambda e: e.matmul(O[:, :], lhsT=vw[:, kt, :], rhs=P[:, :], start=(wi == 0), stop=(wi == nw - 1)),
                                  reads=[vw_b, P_b], writes=[O_b], inc=True)

                        def epi(h=h, hi=hi, pr=pr, t0=t0, O=O, O_b=O_b, gb=gb, gb_b=gb_b, st=st):
                            rec, rec_b = norm_rec(O, O_b)
                            rg, rg_b = RG.next()
                            t2, t2_b = T1.next()
                            kb.op("pool", lambda e: e.tensor_tensor(out=rg[:, :], in0=rec[:, :], in1=gb[:, 2, :], op=ALU.mult),
                                  reads=[rec_b, gb_b], writes=[rg_b])
                            kb.op("dve", lambda e: e.tensor_tensor(out=t2[:, :], in0=O[0:64, :], in1=rg[:, :], op=ALU.mult),
                                  reads=[O_b, rg_b], writes=[t2_b])
                            t1, t1_b = st["slc"]
                            oc, oc_b = OC[pr][hi]
                            acc, acc_b = ACC.next()
                            ost, ost_b = OST.next()
                            kb.op("pool", lambda e: e.tensor_tensor(out=acc[:, :], in0=oc[:, :], in1=gb[:, 0, :], op=ALU.mult),
                                  reads=[oc_b, gb_b], writes=[acc_b])
                            kb.op("pool", lambda e: e.tensor_tensor(out=acc[:, :], in0=acc[:, :], in1=t1[:, :], op=ALU.add),
                                  reads=[acc_b, t1_b], writes=[acc_b])
                            kb.op("pool", lambda e: e.tensor_tensor(out=ost[:, :], in0=acc[:, :], in1=t2[:, :], op=ALU.add),
                                  reads=[acc_b, t2_b], writes=[ost_b])
                            kb.dma(ds["ON"][h * 64:(h + 1) * 64, t0:t0 + 512], ost[:, :], reads=[ost_b], writes=[dsb["ON"]], q="pool")

                        c.stream.push(TileD(qk, 1.0, pv, epi if wi == nw - 1 else None))

                cmp_begin(0)
                for hi in range(4):
                    cmp_head(0, hi)
                c.stream.flush()
                topk_dve(0)
                topk_pe(0)
                for qc in range(NCH):
                    nxt = qc + 1 < NCH
                    if nxt:
                        cmp_begin(qc + 1)
                        cmp_head(qc + 1, 0)
                        cmp_head(qc + 1, 1)
                    sw_head(qc, 0)
                    if nxt:
                        cmp_head(qc + 1, 2)
                        cmp_head(qc + 1, 3)
                    sw_head(qc, 1)
                    sw_head(qc, 2)
                    if nxt:
                        topk_dve(qc + 1)
                    sw_head(qc, 3)
                    c.stream.flush()
                    if nxt:
                        topk_pe(qc + 1)
                    CS.pop(qc)
            c.stream.flush()
            if getattr(c, "stop_mid", None) == "nsa":
                return
            for h in range(8):
                ka, ka_b = KA[h % 2]
                va, va_b = VA[h % 2]
                kb.dma(ka[0:64, :], ds["KN"][h * 64:(h + 1) * 64, :], reads=[dsb["KN"]], writes=[ka_b])
                kb.dma(ka[64:96, :], ds["KPE"][:, :], reads=[dsb["KPE"]], writes=[ka_b])
                for half in range(2):
                    hs = slice(half * 32, (half + 1) * 32)
                    rs_ = slice(half * 4096, (half + 1) * 4096)
                    kb.dma(va[:, hs, 0:64], ds["VM"][rs_, h * 64:(h + 1) * 64].rearrange("(kt p) d -> p kt d", p=128),
                           reads=[dsb["VM"]], writes=[va_b])
                qms = {}

                def qload(qc, h=h, qms=qms):
                    qm, qm_b = QMT.next()
                    kb.dma(qm[:, :], ds["QM"][h, :, qc * 512:(qc + 1) * 512], reads=[dsb["QM"]], writes=[qm_b])
                    qms[qc] = (qm, qm_b)

                qload(0)
                for qc in range(NCH):
                    t0 = qc * 512
                    if qc + 1 < NCH:
                        qload(qc + 1)
                    qm, qm_b = qms.pop(qc)
                    O, O_b = R.OPS.next()
                    nk = 4 * qc + 4
                    for kt in range(nk):
                        qk = [(ka[0:96, kt * 128:(kt + 1) * 128], qm[0:96, :], [ka_b, qm_b])]
                        d = kt - 4 * qc
                        if d >= 0:
                            qk.append((ident[:, :], cmk[:, d, :], [ident_b, CB]))

                        def pv(P, P_b, kt=kt, nk=nk, O=O, O_b=O_b, va=va, va_b=va_b):
                            kb.op("pe", lambda e: e.matmul(O[:, :], lhsT=va[:, kt, :], rhs=P[:, :], start=(kt == 0), stop=(kt == nk - 1)),
                                  reads=[va_b, P_b], writes=[O_b], inc=True)

                        def epi(h=h, t0=t0, O=O, O_b=O_b):
                            rec, rec_b = norm_rec(O, O_b)
                            ost, ost_b = OST.next()
                            kb.op("dve", lambda e: e.tensor_tensor(out=ost[:, :], in0=O[0:64, :], in1=rec[:, :], op=ALU.mult),
                                  reads=[O_b, rec_b], writes=[ost_b])
                            kb.dma(ds["OM"][h * 64:(h + 1) * 64, t0:t0 + 512], ost[:, :], reads=[ost_b], writes=[dsb["OM"]], q="pool")

                        c.stream.push(TileD(qk, MLA_SCALE, pv, epi if kt == nk - 1 else None))
            c.stream.flush()

CH3 = 256
NCH3 = T // CH3


def phase3(c):
    nc, kb = c.nc, c.kb
    din, dc, ds, dsb = c.din, c.dc, c.ds, c.dsb
    ident, ident_b = c.ident, c.ident_b
    with ExitStack() as es:
        W = Buf("p3w")
        wo = sb(nc, es, "wo", [128, 8, 1024], BF16)
        wpg = sb(nc, es, "wpg", [128, 8, 1024], BF16)
        wpl = sb(nc, es, "wpl", [128, 2, 1024], BF16)
        wd = sb(nc, es, "wd", [128, 22, 1024], BF16)
        lnv = sb(nc, es, "lnv", [128, 6, 1024], F32)
        for i, nm in enumerate(("ln1_g", "ln1_b", "ln2_g", "ln2_b", "ln3_g", "ln3_b")):
            kb.dma(lnv[:, i, :], din[nm][0:1, :].partition_broadcast(128), writes=[W])
        with ExitStack() as es1:
            STG = rot_tiles(nc, es1, "stg3", [128, 2816], F32, 2)
            STB = rot_tiles(nc, es1, "stb3", [128, 2816], BF16, 2)
            k = 0
            for name, dst in (("w_out", wo), ("w_ple_gate", wpg)):
                for ft in range(8):
                    s_, s_b = STG.next()
                    kb.dma(s_[:, 0:1024], din[name][ft * 128:(ft + 1) * 128, :], writes=[s_b])
                    kb.op("dve" if k % 2 else "pool", lambda e: e.tensor_copy(out=dst[:, ft, :], in_=s_[:, 0:1024]), reads=[s_b], writes=[W])
                    k += 1
            for kt in range(2):
                s_, s_b = STG.next()
                kb.dma(s_[:, 0:1024], din["w_ple"][kt * 128:(kt + 1) * 128, :], writes=[s_b])
                kb.op("dve", lambda e: e.tensor_copy(out=wpl[:, kt, :], in_=s_[:, 0:1024]), reads=[s_b], writes=[W])
            for j in range(22):
                s_, s_b = STG.next()
                kb.dma(s_[:, 0:1024], din["w_down"][j * 128:(j + 1) * 128, :], writes=[s_b])
                kb.op("dve" if j % 2 else "pool", lambda e: e.tensor_copy(out=wd[:, j, :], in_=s_[:, 0:1024]), reads=[s_b], writes=[W])
            for ft in range(8):
                for a in range(2):
                    s_, s_b = STG.next()
                    sb_, sb_b = STB.next()
                    kb.dma(s_[:, :], din["w_up"][ft * 128:(ft + 1) * 128, a * DFF:(a + 1) * DFF], writes=[s_b])
                    kb.op("dve" if a else "pool", lambda e: e.tensor_copy(out=sb_[:, :], in_=s_[:, :]), reads=[s_b], writes=[sb_b])
                    kb.dma(ds["WUP"][:, :, a, ft, :].rearrange("j p c -> p j c"), sb_[:, :].rearrange("p (j c) -> p j c", c=128),
                           reads=[sb_b], writes=[dsb["WUP"]])
        kb.barrier()
        XF = rot_tiles(nc, es, "xf3", [128, 1024], F32, 1)
        OT = rot_tiles(nc, es, "oT3", [128, 8, CH3], BF16, 1)
        PF = rot_tiles(nc, es, "pf3", [128, 2, 256], F32, 1)
        PB = rot_tiles(nc, es, "pb3", [128, 2, 256], BF16, 3)
        PT = rot_tiles(nc, es, "pT3", [128, 2, CH3], BF16, 1)
        Y = rot_tiles(nc, es, "y3", [128, 1024], F32, 2)
        Z = rot_tiles(nc, es, "z3", [128, 1024], F32, 1)
        X1 = rot_tiles(nc, es, "x1r", [128, 2, 1024], F32, 2)
        X2 = rot_tiles(nc, es, "x2r", [128, 2, 1024], F32, 2)
        XBF = rot_tiles(nc, es, "xbf3", [128, 1024], BF16, 2)
        X1T = rot_tiles(nc, es, "x1T", [128, 8, CH3], BF16, 2)
        X2T = rot_tiles(nc, es, "x2T", [128, 8, CH3], BF16, 2)
        AT = rot_tiles(nc, es, "aT", [128, 22, CH3], BF16, 1)
        WU = rot_tiles(nc, es, "wu", [128, 2, 8, 128], BF16, 3)
        SG = rot_tiles(nc, es, "sg", [128, CH3], F32, 2)
        ST = rot_tiles(nc, es, "st3", [128, 8], F32, 4)
        PA = rot_tiles(nc, es, "pa3", [128, 512], F32, 3, psum=True)
        PG = rot_tiles(nc, es, "pg3", [128, 2, CH3], F32, 3, psum=True)
        TP = rot_tiles(nc, es, "tp3", [128, 8, 128], BF16, 2, psum=True)

        def ln_stats(y, y_b):
            st, st_b = ST.next()
            z, z_b = Z.next()
            kb.op("act", lambda e: e.activation(out=z[:, :], in_=y[:, :], func=AF.Copy, accum_out=st[:, 0:1]), reads=[y_b], writes=[z_b, st_b])
            kb.op("act", lambda e: e.activation(out=z[:, :], in_=y[:, :], func=AF.Square, accum_out=st[:, 1:2]), reads=[y_b], writes=[z_b, st_b])
            kb.op("dve", lambda e: e.tensor_scalar(out=st[:, 2:4], in0=st[:, 0:2], scalar1=1.0 / D, scalar2=None, op0=ALU.mult), reads=[st_b], writes=[st_b])
            kb.op("dve", lambda e: e.tensor_tensor(out=st[:, 4:5], in0=st[:, 2:3], in1=st[:, 2:3], op=ALU.mult), reads=[st_b], writes=[st_b])
            kb.op("dve", lambda e: e.tensor_tensor(out=st[:, 5:6], in0=st[:, 3:4], in1=st[:, 4:5], op=ALU.subtract), reads=[st_b], writes=[st_b])
            kb.op("dve", lambda e: e.tensor_scalar_add(out=st[:, 5:6], in0=st[:, 5:6], scalar1=LN_EPS), reads=[st_b], writes=[st_b])
            kb.op("act", lambda e: e.activation(out=st[:, 6:7], in_=st[:, 5:6], func=AF.Sqrt), reads=[st_b], writes=[st_b])
            kb.op("dve", lambda e: e.reciprocal(out=st[:, 7:8], in_=st[:, 6:7]), reads=[st_b], writes=[st_b])
            return st, st_b, z, z_b

        def layer_norm(y, y_b, gi, out_ap, out_b):
            st, st_b, z, z_b = ln_stats(y, y_b)
            kb.op("dve", lambda e: e.scalar_tensor_tensor(out=z[:, :], in0=y[:, :], scalar=st[:, 2:3], in1=lnv[:, gi, :],
                                                          op0=ALU.subtract, op1=ALU.mult), reads=[y_b, st_b, W], writes=[z_b])
            kb.op("dve", lambda e: e.scalar_tensor_tensor(out=out_ap, in0=z[:, :], scalar=st[:, 7:8], in1=lnv[:, gi + 1, :],
                                                           op0=ALU.mult, op1=ALU.add), reads=[z_b, st_b, W], writes=[out_b])

        def cast_bf(src, src_b):
            xb, xb_b = XBF.next()
            kb.op("pool", lambda e: e.tensor_copy(out=xb[:, :], in_=src), reads=[src_b], writes=[xb_b])
            return xb, xb_b

        def transposes(xb, xb_b, s_, dstT, dstT_b):
            tp, tp_b = TP.next()
            for ft in range(8):
                kb.op("pe", lambda e: e.transpose(out=tp[:, ft, :], in_=xb[:, ft * 128:(ft + 1) * 128], identity=ident[:, :]),
                      reads=[xb_b, ident_b], writes=[tp_b], inc=(ft == 7))
            kb.op("act", lambda e: e.activation(out=dstT[:, :, s_ * 128:(s_ + 1) * 128], in_=tp[:, :, :], func=AF.Copy), reads=[tp_b], writes=[dstT_b])

        S = {}

        def A_mm(ch):
            t0 = ch * CH3
            st = S.setdefault(ch, {})
            oT, oT_b = OT.next()
            pf, pf_b = PF.next()
            kb.dma(oT[:, 0:4, :], ds["ON"].rearrange("(c p) t -> p c t", p=128)[:, :, t0:t0 + CH3], reads=[dsb["ON"]], writes=[oT_b])
            kb.dma(oT[:, 4:8, :], ds["OM"].rearrange("(c p) t -> p c t", p=128)[:, :, t0:t0 + CH3], reads=[dsb["OM"]], writes=[oT_b])
            kb.dma(pf[:, :, :], din["p"][t0:t0 + CH3, :].rearrange("(s p) d -> p s d", p=128), writes=[pf_b])
            pb_, pb_b = PB.next()
            kb.op("pool", lambda e: e.tensor_copy(out=pb_[:, :, :], in_=pf[:, :, :]), reads=[pf_b], writes=[pb_b])
            st["pb"] = (pb_, pb_b)
            st["x1"] = X1.next()
            st["x1T"] = X1T.next()
            st["ys"] = []
            for s_ in range(2):
                y, y_b = Y.next()
                xf, xf_b = XF.next()
                kb.dma(xf[:, :], din["x"][t0 + s_ * 128:t0 + (s_ + 1) * 128, :], writes=[xf_b])
                for n in range(2):
                    pa, pa_b = PA.next()
                    for kt in range(8):
                        kb.op("pe", lambda e: e.matmul(pa[:, :], lhsT=oT[:, kt, s_ * 128:(s_ + 1) * 128], rhs=wo[:, kt, n * 512:(n + 1) * 512],
                                                       start=(kt == 0), stop=(kt == 7)), reads=[oT_b, W], writes=[pa_b], inc=(kt == 7))
                    kb.op("dve", lambda e: e.scalar_tensor_tensor(out=y[:, n * 512:(n + 1) * 512], in0=xf[:, n * 512:(n + 1) * 512], scalar=ALPHA,
                                                                  in1=pa[:, :], op0=ALU.mult, op1=ALU.add), reads=[xf_b, pa_b], writes=[y_b])
                st["ys"].append((y, y_b))

        def A_ln(ch, s_):
            st = S[ch]
            x1, x1_b = st["x1"]
            y, y_b = st["ys"][s_]
            layer_norm(y, y_b, 0, x1[:, s_, :], x1_b)
            st["xb%d" % s_] = cast_bf(x1[:, s_, :], x1_b)

        def A_tr(ch, s_):
            st = S[ch]
            x1T, x1T_b = st["x1T"]
            transposes(*st["xb%d" % s_], s_, x1T, x1T_b)

        def B_up(ch, j):
            st = S[ch]
            x1T, x1T_b = st["x1T"]
            if j == 0:
                st["aT"] = AT.next()
            aT, aT_b = st["aT"]
            wu, wu_b = WU.next()
            kb.dma(wu[:, :, :, :], ds["WUP"][j], reads=[dsb["WUP"]], writes=[wu_b])
            pg, pg_b = PG.next()
            for a in range(2):
                for ft in range(8):
                    kb.op("pe", lambda e: e.matmul(pg[:, a, :], lhsT=wu[:, a, ft, :], rhs=x1T[:, ft, :], start=(ft == 0), stop=(ft == 7)),
                          reads=[wu_b, x1T_b], writes=[pg_b], inc=(a == 1 and ft == 7))
            sg, sg_b = SG.next()
            kb.op("act", lambda e: e.activation(out=sg[:, :], in_=pg[:, 0, :], func=AF.Silu), reads=[pg_b], writes=[sg_b])
            kb.op("dve", lambda e: e.tensor_tensor(out=aT[:, j, :], in0=pg[:, 1, :], in1=sg[:, :], op=ALU.mult), reads=[pg_b, sg_b], writes=[aT_b])

        def B_down(ch):
            st = S[ch]
            aT, aT_b = st["aT"]
            x1, x1_b = st["x1"]
            st["x2"] = X2.next()
            st["x2T"] = X2T.next()
            x2, x2_b = st["x2"]
            ys = []
            for s_ in range(2):
                y, y_b = Y.next()
                for n in range(2):
                    pa, pa_b = PA.next()
                    for j in range(22):
                        kb.op("pe", lambda e: e.matmul(pa[:, :], lhsT=aT[:, j, s_ * 128:(s_ + 1) * 128], rhs=wd[:, j, n * 512:(n + 1) * 512],
                                                       start=(j == 0), stop=(j == 21)), reads=[aT_b, W], writes=[pa_b], inc=(j == 21))
                    kb.op("dve", lambda e: e.scalar_tensor_tensor(out=y[:, n * 512:(n + 1) * 512], in0=x1[:, s_, n * 512:(n + 1) * 512], scalar=ALPHA,
                                                                  in1=pa[:, :], op0=ALU.mult, op1=ALU.add), reads=[x1_b, pa_b], writes=[y_b])
                ys.append((y, y_b))
            st["ys2"] = ys

        def B_ln(ch, s_):
            st = S[ch]
            x2, x2_b = st["x2"]
            y, y_b = st["ys2"][s_]
            layer_norm(y, y_b, 2, x2[:, s_, :], x2_b)
            st["x2b%d" % s_] = cast_bf(x2[:, s_, :], x2_b)

        def B_tr(ch, s_):
            st = S[ch]
            x2T, x2T_b = st["x2T"]
            transposes(*st["x2b%d" % s_], s_, x2T, x2T_b)

        def C_pre(ch):
            st = S[ch]
            pb_, pb_b = st["pb"]
            pT, pT_b = PT.next()
            tp, tp_b = TP.next()
            for s_ in range(2):
                for kt in range(2):
                    kb.op("pe", lambda e: e.transpose(out=tp[:, s_ * 2 + kt, :], in_=pb_[:, s_, kt * 128:(kt + 1) * 128], identity=ident[:, :]),
                          reads=[pb_b, ident_b], writes=[tp_b], inc=(s_ == 1 and kt == 1))
            for s_ in range(2):
                kb.op("act", lambda e: e.activation(out=pT[:, :, s_ * 128:(s_ + 1) * 128], in_=tp[:, 2 * s_:2 * s_ + 2, :], func=AF.Copy),
                      reads=[tp_b], writes=[pT_b])
            st["pT"] = (pT, pT_b)

        def C_sub(ch, s_):
            t0 = ch * CH3
            st = S[ch]
            x2, x2_b = st["x2"]
            x2T, x2T_b = st["x2T"]
            pT, pT_b = st["pT"]
            y, y_b = Y.next()
            for n in range(2):
                pa, pa_b = PA.next()
                for kt in range(8):
                    kb.op("pe", lambda e: e.matmul(pa[:, :], lhsT=x2T[:, kt, s_ * 128:(s_ + 1) * 128], rhs=wpg[:, kt, n * 512:(n + 1) * 512],
                                                   start=(kt == 0), stop=(kt == 7)), reads=[x2T_b, W], writes=[pa_b], inc=(kt == 7))
                kb.op("act", lambda e: e.activation(out=y[:, n * 512:(n + 1) * 512], in_=pa[:, :], func=AF.Sigmoid), reads=[pa_b], writes=[y_b])
                pp, pp_b = PA.next()
                for kt in range(2):
                    kb.op("pe", lambda e: e.matmul(pp[:, :], lhsT=pT[:, kt, s_ * 128:(s_ + 1) * 128], rhs=wpl[:, kt, n * 512:(n + 1) * 512],
                                                   start=(kt == 0), stop=(kt == 1)), reads=[pT_b, W], writes=[pp_b], inc=(kt == 1))
                kb.op("dve", lambda e: e.tensor_tensor(out=y[:, n * 512:(n + 1) * 512], in0=pp[:, :], in1=y[:, n * 512:(n + 1) * 512], op=ALU.mult),
                      reads=[pp_b, y_b], writes=[y_b])
                kb.op("dve", lambda e: e.scalar_tensor_tensor(out=y[:, n * 512:(n + 1) * 512], in0=x2[:, s_, n * 512:(n + 1) * 512], scalar=ALPHA,
                                                              in1=y[:, n * 512:(n + 1) * 512], op0=ALU.mult, op1=ALU.add),
                      reads=[x2_b, y_b], writes=[y_b])
            st_, st_b, z, z_b = ln_stats(y, y_b)
            kb.op("dve", lambda e: e.scalar_tensor_tensor(out=z[:, :], in0=y[:, :], scalar=st_[:, 2:3], in1=lnv[:, 4, :],
                                                          op0=ALU.subtract, op1=ALU.mult), reads=[y_b, st_b, W], writes=[z_b])
            kb.op("dve", lambda e: e.scalar_tensor_tensor(out=z[:, :], in0=z[:, :], scalar=st_[:, 7:8], in1=lnv[:, 5, :],
                                                          op0=ALU.mult, op1=ALU.add), reads=[z_b, st_b, W], writes=[z_b])
            kb.dma(c.out[t0 + s_ * 128:t0 + (s_ + 1) * 128, :], z[:, :], reads=[z_b], writes=[c.out_b], q="pool")
            if s_ == 1:
                S.pop(ch)

        A_mm(0)
        for s_ in range(2):
            A_ln(0, s_)
            A_tr(0, s_)
        hooks = {1: ("B_ln", 0), 4: ("B_ln", 1), 6: ("B_tr", 0), 7: ("B_tr", 1), 9: ("A_mm", 0), 11: ("A_ln", 0), 14: ("A_ln", 1),
                 15: ("C_pre", 0), 16: ("C_sub", 0), 18: ("A_tr", 0), 19: ("C_sub", 1), 21: ("A_tr", 1)}

        def run_hook(it, name, s_):
            nxt = it + 1 < NCH3
            prv = it >= 1
            if name == "B_ln" and prv:
                B_ln(it - 1, s_)
            elif name == "B_tr" and prv:
                B_tr(it - 1, s_)
            elif name == "A_mm" and nxt:
                A_mm(it + 1)
            elif name == "A_ln" and nxt:
                A_ln(it + 1, s_)
            elif name == "A_tr" and nxt:
                A_tr(it + 1, s_)
            elif name == "C_pre" and prv:
                C_pre(it - 1)
            elif name == "C_sub" and prv:
                C_sub(it - 1, s_)

        for it in range(NCH3 + 1):
            if it < NCH3:
                for j in range(22):
                    B_up(it, j)
                    if j in hooks:
                        run_hook(it, *hooks[j])
                B_down(it)
            else:
                for j in sorted(hooks):
                    run_hook(it, *hooks[j])


_CACHE = {}


def kernel(**inputs):
    if "nc" not in _CACHE:
        _CACHE["nc"] = build_program()
        _CACHE["consts"] = host_constants()
    nc = _CACHE["nc"]
    consts = _CACHE["consts"]
    B = inputs["x"].shape[0]
    shared = {}
    for k, shp in IN_SPECS.items():
        if k in ("x", "p"):
            continue
        shared[k] = np.ascontiguousarray(np.asarray(inputs[k], dtype=np.float32)[0].reshape(shp))
    for k, v in consts.items():
        shared["c_" + k] = v
    in_maps = []
    for b in range(B):
        m = dict(shared)
        m["x"] = np.ascontiguousarray(np.asarray(inputs["x"], dtype=np.float32)[b])
        m["p"] = np.ascontiguousarray(np.asarray(inputs["p"], dtype=np.float32)[0, b])
        in_maps.append(m)
    res = run_bass_kernel_spmd(nc, in_maps, core_ids=list(range(B)))
    return np.stack([np.asarray(r["out"], dtype=np.float32) for r in res.results], 0)
```

```python
from contextlib import ExitStack
import numpy as np
import ml_dtypes
import concourse.bass as bass
import concourse.mybir as mybir
from concourse.bass_utils import run_bass_kernel_spmd

F32 = mybir.dt.float32
BF16 = mybir.dt.bfloat16
AF = mybir.ActivationFunctionType
ALU = mybir.AluOpType
AX = mybir.AxisListType
bf16 = ml_dtypes.bfloat16

T = 8192
D = 1024
NCH = T // 512
NKT = T // 128
DFF = 2816
BIG = 30000.0
ALPHA = 2.0 ** 0.25
LN_EPS = 1e-5
RMS_EPS = 1e-6
MLA_SCALE = 96.0 ** -0.5
ALIBI_CUT = 200.0


class Buf:
    __slots__ = ("name", "last_w", "readers")

    def __init__(self, name=""):
        self.name = name
        self.last_w = None
        self.readers = {}


class KB:
    NSLOT = 8

    def __init__(self, nc, es):
        self.nc = nc
        self.E = {"pe": nc.tensor, "act": nc.scalar, "dve": nc.vector, "pool": nc.gpsimd, "sp": nc.sync}
        self.sem = {}
        self.seq = {}
        self.known = {}
        for e in self.E:
            self.sem[("e", e)] = es.enter_context(nc.semaphore("s_" + e))
            self.seq[e] = 0
            self.known[e] = {}
        self.slots = {}
        self.slot_rr = {}
        for q in ("sp", "pool"):
            self.slots[q] = []
            self.slot_rr[q] = 0
            for i in range(self.NSLOT):
                key = ("d", q, i)
                self.sem[key] = es.enter_context(nc.semaphore("d_%s%d" % (q, i)))
                self.slots[q].append([key, 0])

    def _collect(self, reads, writes):
        deps = {}
        for b in reads:
            if b.last_w is not None:
                k, v = b.last_w
                o = deps.get(k)
                deps[k] = (max(v, o[0]) if o else v, True)
        for b in writes:
            if b.last_w is not None:
                k, v = b.last_w
                o = deps.get(k)
                if o is None:
                    deps[k] = (v, False)
                elif v > o[0]:
                    deps[k] = (v, o[1])
            for k, v in b.readers.items():
                o = deps.get(k)
                if o is None:
                    deps[k] = (v, False)
                elif v > o[0]:
                    deps[k] = (v, o[1])
        return deps

    def _wait(self, eng, deps):
        kn = self.known[eng]
        for k, (v, raw) in deps.items():
            if k == ("e", eng):
                if eng == "pe" or not raw:
                    continue
            if kn.get(k, 0) >= v:
                continue
            self.E[eng].wait_ge(self.sem[k], v)
            kn[k] = v

    def _record(self, dep, reads, writes):
        k, v = dep
        for b in reads:
            if b.readers.get(k, 0) < v:
                b.readers[k] = v
        for b in writes:
            b.last_w = dep
            b.readers = {}

    def op(self, eng, fn, reads=(), writes=(), inc=True):
        self._wait(eng, self._collect(reads, writes))
        ins = fn(self.E[eng])
        if inc:
            ins.then_inc(self.sem[("e", eng)], 1)
            self.seq[eng] += 1
            dep = (("e", eng), self.seq[eng])
        else:
            dep = (("e", eng), self.seq[eng] + 1)
        self._record(dep, reads, writes)
        return ins

    def dma(self, out, in_, reads=(), writes=(), q="sp", **kw):
        sl = self.slots[q][self.slot_rr[q]]
        self.slot_rr[q] = (self.slot_rr[q] + 1) % self.NSLOT
        deps = self._collect(reads, writes)
        if sl[1] > 0:
            o = deps.get(sl[0])
            if o is None or o[0] < sl[1]:
                deps[sl[0]] = (sl[1], True)
        self._wait(q, deps)
        self.E[q].dma_start(out=out, in_=in_, **kw).then_inc(self.sem[sl[0]], 16)
        sl[1] += 16
        self._record((sl[0], sl[1]), reads, writes)

    def barrier(self):
        cur = {}
        for e in self.E:
            cur[("e", e)] = self.seq[e]
        for q in self.slots:
            for key, cnt in self.slots[q]:
                cur[key] = cnt
        for e in self.E:
            kn = self.known[e]
            for k, v in cur.items():
                if v == 0 or k == ("e", e):
                    continue
                if kn.get(k, 0) >= v:
                    continue
                self.E[e].wait_ge(self.sem[k], v)
                kn[k] = v


class Rot:
    def __init__(self, items):
        self.items = items
        self.i = 0

    def next(self):
        it = self.items[self.i]
        self.i = (self.i + 1) % len(self.items)
        return it


def host_constants():
    c = {}
    c["ident"] = np.eye(128, dtype=np.float32).astype(bf16)
    pos = np.arange(T, dtype=np.float32)
    inv_freq = (10000.0 ** (-np.arange(16, dtype=np.float32) / 16)).astype(np.float32)
    ang = (pos[:, None] * inv_freq[None, :]).astype(np.float32)
    cos = np.cos(ang).astype(np.float32).T
    sin = np.sin(ang).astype(np.float32).T
    c["ropeC"] = np.ascontiguousarray(np.concatenate([cos, cos], 0))
    c["ropeS"] = np.ascontiguousarray(np.concatenate([-sin, sin], 0))
    p = np.arange(128)[:, None]
    j = np.arange(512)[None, :]
    def addmask(valid):
        return np.where(valid, 0.0, -BIG).astype(np.float32).astype(bf16)
    c["cmk"] = addmask(np.stack([(p + 128 * d <= j) for d in range(4)], 1))
    c["wmk"] = addmask(np.stack([(j < 128 * d + p) for d in range(4)], 1))
    c["pmk"] = addmask(np.stack([(16 * p + 31 <= 512 * m + j) for m in range(5)], 1))
    slopes = 2.0 ** (-np.arange(1, 9, dtype=np.float64))
    t = np.arange(T)
    qaug = np.zeros((8, 5, T), np.float32)
    for h in range(8):
        s = slopes[h]
        qaug[h, 0] = s
        qaug[h, 1] = s
        qaug[h, 2] = -s * (t // 128) * 128
        qaug[h, 3] = -s * (t % 128)
        qaug[h, 4] = -s * ((t % 128) - 31)
    c["qaug"] = qaug.astype(bf16)
    ka = np.zeros((5, T), np.float32)
    ka[0] = (t // 128) * 128
    ka[1] = t % 128
    ka[2] = 1
    ka[3] = 1
    c["kaug"] = ka.astype(bf16)
    cc = np.arange(512)
    kc = np.zeros((5, 512), np.float32)
    kc[0] = (cc // 128) * 2048
    kc[1] = (cc % 128) * 16
    kc[2] = 1
    kc[4] = 1
    c["kaugc"] = kc.astype(bf16)
    for k in ("qaug", "kaug", "kaugc"):
        assert np.all(c[k].astype(np.float32) == {"qaug": qaug, "kaug": ka, "kaugc": kc}[k])
    E = np.zeros((128, 64, 128), np.float32)
    for kt in range(64):
        E[2 * kt, kt, :64] = BIG
        E[2 * kt + 1, kt, 64:] = BIG
    c["eexp"] = E.astype(bf16)
    M = np.zeros((512, 128), np.float32)
    for cidx in range(511):
        for s in (cidx, cidx + 1):
            M[cidx, s // 4] += 1
    c["impm"] = M.astype(bf16)
    keep = np.zeros((128, 256), np.float32)
    add = np.zeros((128, 256), np.float32)
    pp = np.arange(128)
    for col in range(256):
        r = col - 128
        if r < -1:
            keep[:, col] = 1
        elif r == -1:
            keep[:, col] = (pp >= 64)
            add[:, col] = 1e6 * (pp < 64)
        elif r == 0:
            add[:, col] = 1e6
        elif r == 1:
            add[:, col] = np.where(pp >= 64, 1e6, -1.0)
        else:
            add[:, col] = -1.0
    c["keept"] = keep
    c["addt"] = add
    return c


CONST_SPECS = {
    "ident": ([128, 128], BF16), "ropeC": ([32, T], F32), "ropeS": ([32, T], F32),
    "cmk": ([128, 4, 512], BF16), "wmk": ([128, 4, 512], BF16), "pmk": ([128, 5, 512], BF16),
    "qaug": ([8, 5, T], BF16), "kaug": ([5, T], BF16), "kaugc": ([5, 512], BF16),
    "eexp": ([128, 64, 128], BF16), "impm": ([512, 128], BF16),
    "keept": ([128, 256], F32), "addt": ([128, 256], F32),
}

IN_SPECS = {
    "x": [T, D], "p": [T, 256], "w_in": [D, 1720], "w_ck1": [2048, 256], "w_ck2": [256, 64],
    "pos_ck": [32, 64], "w_cv1": [2048, 256], "w_cv2": [256, 64], "pos_cv": [32, 64],
    "mla_q_norm": [256, 1], "w_uq": [256, 768], "mla_kv_norm": [128, 1], "w_ukv": [128, 1024],
    "w_out": [D, D], "ln1_g": [1, D], "ln1_b": [1, D], "w_up": [D, 2 * DFF], "w_down": [DFF, D],
    "ln2_g": [1, D], "ln2_b": [1, D], "w_ple_gate": [D, D], "w_ple": [256, D],
    "ln3_g": [1, D], "ln3_b": [1, D],
}

SCRATCH_SPECS = {
    "QN": ([512, T], BF16),
    "KX": ([4, 128, T], BF16),
    "VSW": ([T, 256], BF16),
    "G": ([24, T], F32),
    "KPE": ([32, T], BF16),
    "QM": ([8, 96, T], BF16),
    "KN": ([512, T], BF16),
    "VM": ([T, 512], BF16),
    "ON": ([512, T], BF16),
    "OM": ([512, T], BF16),
    "WUP": ([22, 128, 2, 8, 128], BF16),
}


def sb(nc, es, name, shape, dt):
    return es.enter_context(nc.sbuf_tensor(name, list(shape), dt))


def ps(nc, es, name, shape, dt):
    return es.enter_context(nc.psum_tensor(name, list(shape), dt))


def rot_tiles(nc, es, name, shape, dt, n, psum=False):
    items = []
    for i in range(n):
        t = (ps if psum else sb)(nc, es, "%s%d" % (name, i), shape, dt)
        items.append((t, Buf("%s%d" % (name, i))))
    return Rot(items)


class Ctx:
    pass


def build_program(stop_after=None, debug_scratch=False, stop_mid=None):
    nc = bass.Bass("TRN2", target_bir_lowering=False)
    c = Ctx()
    c.nc = nc
    c.din = {k: nc.dram_tensor(k, v, F32, kind="ExternalInput").ap() for k, v in IN_SPECS.items()}
    c.dc = {k: nc.dram_tensor("c_" + k, v[0], v[1], kind="ExternalInput").ap() for k, v in CONST_SPECS.items()}
    skind = "ExternalOutput" if debug_scratch else "Internal"
    c.ds = {k: nc.dram_tensor("s_" + k, v[0], v[1], kind=skind).ap() for k, v in SCRATCH_SPECS.items()}
    c.dsb = {k: Buf("s_" + k) for k in SCRATCH_SPECS}
    c.out = nc.dram_tensor("out", [T, D], F32, kind="ExternalOutput").ap()
    c.out_b = Buf("out")
    c.stop_mid = stop_mid
    if stop_mid == "compress":
        c.dbg = {"kcmp": nc.dram_tensor("dbg_kcmp", [69, 2, 512], BF16, kind="ExternalOutput").ap(),
                 "vcmp": nc.dram_tensor("dbg_vcmp", [128, 4, 2, 128], BF16, kind="ExternalOutput").ap()}
    with ExitStack() as es0:
        kb = KB(nc, es0)
        c.kb = kb
        c.ident = sb(nc, es0, "ident", [128, 128], BF16)
        c.ident_b = Buf("ident")
        kb.dma(c.ident[:], c.dc["ident"][:], writes=[c.ident_b])
        phases = [phase1, phase2, phase3]
        for ph in phases:
            ph(c)
            kb.barrier()
            if stop_after == ph.__name__:
                break
        kb.barrier()
    return nc


def phase1(c):
    nc, kb = c.nc, c.kb
    din, dc, ds, dsb = c.din, c.dc, c.ds, c.dsb
    with ExitStack() as es:
        wtr = sb(nc, es, "wtr", [128, 8, 1048], BF16)
        wtok = sb(nc, es, "wtok", [128, 8, 640], BF16)
        wkpA = sb(nc, es, "wkpA", [128, 8, 96], BF16)
        wkpB = sb(nc, es, "wkpB", [128, 8, 96], BF16)
        wuqA = sb(nc, es, "wuqA", [128, 2, 8, 96], BF16)
        wuqB = sb(nc, es, "wuqB", [128, 2, 8, 96], BF16)
        wkvK = sb(nc, es, "wkvK", [128, 8, 64], BF16)
        wkvV = sb(nc, es, "wkvV", [128, 8, 64], BF16)
        gq = sb(nc, es, "gq", [128, 2], F32)
        gkv = sb(nc, es, "gkv", [128, 1], F32)
        W = Buf("p1w")
        es_stg = ExitStack()
        stg = rot_tiles(nc, es_stg, "stg", [128, 1720], F32, 2)
        kb.op("pool", lambda e: e.memset(wkpA[:], 0.0), writes=[W])
        kb.op("pool", lambda e: e.memset(wkpB[:], 0.0), writes=[W])
        kb.op("pool", lambda e: e.memset(wuqB[:], 0.0), writes=[W])
        for kt in range(2):
            kb.dma(gq[:, kt:kt + 1], din["mla_q_norm"][kt * 128:(kt + 1) * 128, :], writes=[W])
        kb.dma(gkv[:], din["mla_kv_norm"][:, :], writes=[W])
        for ft in range(8):
            s, sbuf_ = stg.next()
            kb.dma(s[:], din["w_in"][ft * 128:(ft + 1) * 128, :], writes=[sbuf_])
            kb.op("dve", lambda e: e.tensor_copy(out=wtr[:, ft, 0:896], in_=s[:, 0:896]), reads=[sbuf_], writes=[W])
            kb.op("pool", lambda e: e.tensor_copy(out=wtr[:, ft, 896:1024], in_=s[:, 1024:1152]), reads=[sbuf_], writes=[W])
            kb.op("pool", lambda e: e.tensor_copy(out=wtr[:, ft, 1024:1048], in_=s[:, 1280:1304]), reads=[sbuf_], writes=[W])
            kb.op("dve", lambda e: e.tensor_copy(out=wtok[:, ft, 0:384], in_=s[:, 1304:1688]), reads=[sbuf_], writes=[W])
            kb.op("pool", lambda e: e.tensor_copy(out=wtok[:, ft, 384:512], in_=s[:, 896:1024]), reads=[sbuf_], writes=[W])
            kb.op("pool", lambda e: e.tensor_copy(out=wtok[:, ft, 512:640], in_=s[:, 1152:1280]), reads=[sbuf_], writes=[W])
            kb.op("dve", lambda e: e.tensor_copy(out=wkpA[:, ft, 64:96], in_=s[:, 1688:1720]), reads=[sbuf_], writes=[W])
            kb.op("dve", lambda e: e.tensor_copy(out=wkpB[:, ft, 64:80], in_=s[:, 1704:1720]), reads=[sbuf_], writes=[W])
            kb.op("dve", lambda e: e.tensor_copy(out=wkpB[:, ft, 80:96], in_=s[:, 1688:1704]), reads=[sbuf_], writes=[W])
        for kt in range(2):
            s, sbuf_ = stg.next()
            kb.dma(s[:, 0:768], din["w_uq"][kt * 128:(kt + 1) * 128, :], writes=[sbuf_])
            s3 = s[:, 0:768].rearrange("p (h c) -> p h c", c=96)
            kb.op("dve", lambda e: e.tensor_scalar(out=wuqA[:, kt, :, :], in0=s3, scalar1=gq[:, kt:kt + 1], scalar2=None,
                                                   op0=ALU.mult), reads=[sbuf_, W], writes=[W])
            kb.op("dve", lambda e: e.tensor_scalar(out=wuqB[:, kt, :, 64:80], in0=s3[:, :, 80:96], scalar1=gq[:, kt:kt + 1],
                                                   scalar2=None, op0=ALU.mult), reads=[sbuf_, W], writes=[W])
            kb.op("dve", lambda e: e.tensor_scalar(out=wuqB[:, kt, :, 80:96], in0=s3[:, :, 64:80], scalar1=gq[:, kt:kt + 1],
                                                   scalar2=None, op0=ALU.mult), reads=[sbuf_, W], writes=[W])
        s, sbuf_ = stg.next()
        kb.dma(s[:, 0:1024], din["w_ukv"][:, :], writes=[sbuf_])
        s3 = s[:, 0:1024].rearrange("p (h c) -> p h c", c=128)
        kb.op("dve", lambda e: e.tensor_scalar(out=wkvK[:, :, :], in0=s3[:, :, 0:64], scalar1=gkv[:, 0:1], scalar2=None,
                                               op0=ALU.mult), reads=[sbuf_, W], writes=[W])
        kb.op("dve", lambda e: e.tensor_scalar(out=wkvV[:, :, :], in0=s3[:, :, 64:128], scalar1=gkv[:, 0:1], scalar2=None,
                                               op0=ALU.mult), reads=[sbuf_, W], writes=[W])
        kb.barrier()
        es_stg.close()
        XF = rot_tiles(nc, es, "xf", [128, 4, 1024], F32, 2)
        XT = rot_tiles(nc, es, "xT", [128, 8, 512], BF16, 2)
        STT = rot_tiles(nc, es, "stT", [128, 8, 512], BF16, 2)
        GST = rot_tiles(nc, es, "gst", [24, 512], F32, 2)
        RC = rot_tiles(nc, es, "rc", [96, 2, 512], F32, 3)
        KST = rot_tiles(nc, es, "kst", [96, 512], BF16, 2)
        RT = rot_tiles(nc, es, "rt", [96, 2, 512], F32, 2)
        VST = rot_tiles(nc, es, "vst", [128, 4, 256], BF16, 2)
        SS = rot_tiles(nc, es, "ss", [128, 4], F32, 2)
        JK = rot_tiles(nc, es, "jk", [128, 256], F32, 2)
        CN = rot_tiles(nc, es, "cn", [128, 384], BF16, 2)
        CT = rot_tiles(nc, es, "cT", [128, 3, 512], BF16, 2)
        QST = rot_tiles(nc, es, "qst", [96, 8, 512], BF16, 2)
        KNST = rot_tiles(nc, es, "knst", [128, 4, 512], BF16, 2)
        VMST = rot_tiles(nc, es, "vmst", [128, 4, 512], BF16, 2)
        TP = rot_tiles(nc, es, "tp", [128, 1024], BF16, 2, psum=True)
        TPQ = rot_tiles(nc, es, "tpq", [128, 3, 128], BF16, 1, psum=True)
        PS = rot_tiles(nc, es, "ps", [128, 512], F32, 5, psum=True)
        ident, ident_b = c.ident, c.ident_b
        xv = din["x"]
        XB = rot_tiles(nc, es, "xbb", [128, 4, 1024], BF16, 2)
        LS = {}

        def L_load(ch):
            t0 = ch * 512
            xf, xf_b = XF.next()
            kb.dma(xf[:], xv[t0:t0 + 512, :].rearrange("(s p) d -> p s d", p=128), writes=[xf_b])
            rc, rc_b = RC.next()
            kb.dma(rc[64:96, 0, :], dc["ropeC"][:, t0:t0 + 512], writes=[rc_b])
            kb.dma(rc[64:96, 1, :], dc["ropeS"][:, t0:t0 + 512], writes=[rc_b])
            LS[ch] = {"xf": (xf, xf_b), "rc": (rc, rc_b)}

        def L_cast(ch):
            xf, xf_b = LS[ch]["xf"]
            xb, xb_b = XB.next()
            kb.op("dve", lambda e: e.tensor_copy(out=xb[:, 0:3, :], in_=xf[:, 0:3, :]), reads=[xf_b], writes=[xb_b])
            kb.op("pool", lambda e: e.tensor_copy(out=xb[:, 3:4, :], in_=xf[:, 3:4, :]), reads=[xf_b], writes=[xb_b])
            LS[ch]["xb"] = (xb, xb_b)

        def L_tr(ch):
            xb, xb_b = LS[ch]["xb"]
            xT, xT_b = XT.next()
            for f2 in range(4):
                tp, tp_b = TP.next()
                for fi in range(2):
                    ft = f2 * 2 + fi
                    for s_ in range(4):
                        last = (fi == 1 and s_ == 3)
                        kb.op("pe", lambda e: e.transpose(out=tp[:, fi * 512 + s_ * 128: fi * 512 + (s_ + 1) * 128],
                                                          in_=xb[:, s_, ft * 128:(ft + 1) * 128], identity=ident[:]),
                              reads=[xb_b, ident_b], writes=[tp_b], inc=last)
                dst = xT[:, 2 * f2:2 * f2 + 2, :].rearrange("p a t -> p (a t)")
                kb.op("act" if f2 % 2 else "dve",
                      (lambda e: e.activation(out=dst, in_=tp[:], func=AF.Copy)) if f2 % 2 else
                      (lambda e: e.tensor_copy(out=dst, in_=tp[:])), reads=[tp_b], writes=[xT_b])
            LS[ch]["xT"] = (xT, xT_b)

        L_load(0)
        L_cast(0)
        L_tr(0)
        if NCH > 1:
            L_load(1)
        for ch in range(NCH):
            t0 = ch * 512
            xT, xT_b = LS[ch]["xT"]
            rc, rc_b = LS[ch]["rc"]
            if ch + 2 < NCH:
                L_load(ch + 2)
            if ch + 1 < NCH:
                L_cast(ch + 1)
            stT, stT_b = STT.next()
            for ct in range(8):
                pt, pt_b = PS.next()
                for ft in range(8):
                    kb.op("pe", lambda e: e.matmul(pt[:, :], lhsT=wtr[:, ft, ct * 128:(ct + 1) * 128], rhs=xT[:, ft, :],
                                                   start=(ft == 0), stop=(ft == 7)),
                          reads=[W, xT_b], writes=[pt_b], inc=(ft == 7))
                if ct < 4:
                    kb.op("act", lambda e: e.activation(out=stT[:, ct, :], in_=pt[:, :], func=AF.Copy, scale=0.125),
                          reads=[pt_b], writes=[stT_b])
                else:
                    kb.op("dve", lambda e: e.tensor_copy(out=stT[:, ct, :], in_=pt[:, :]), reads=[pt_b], writes=[stT_b])
            kb.dma(ds["QN"].rearrange("(c p) t -> p c t", p=128)[:, :, t0:t0 + 512], stT[:, 0:4, :],
                   reads=[stT_b], writes=[dsb["QN"]])
            kb.dma(ds["KX"].rearrange("k p t -> p k t")[:, :, t0:t0 + 512], stT[:, 4:8, :],
                   reads=[stT_b], writes=[dsb["KX"]])
            pt, pt_b = PS.next()
            for ft in range(8):
                kb.op("pe", lambda e: e.matmul(pt[0:24, :], lhsT=wtr[:, ft, 1024:1048], rhs=xT[:, ft, :],
                                               start=(ft == 0), stop=(ft == 7)),
                      reads=[W, xT_b], writes=[pt_b], inc=(ft == 7))
            gst, gst_b = GST.next()
            kb.op("act", lambda e: e.activation(out=gst[:, :], in_=pt[0:24, :], func=AF.Sigmoid), reads=[pt_b], writes=[gst_b])
            kb.dma(ds["G"][:, t0:t0 + 512], gst[:, :], reads=[gst_b], writes=[dsb["G"]])
            pa, pa_b = PS.next()
            for ft in range(8):
                kb.op("pe", lambda e: e.matmul(pa[0:96, :], lhsT=wkpA[:, ft, :], rhs=xT[:, ft, :], start=(ft == 0), stop=(ft == 7)),
                      reads=[W, xT_b], writes=[pa_b], inc=(ft == 7))
            pb, pb_b = PS.next()
            for ft in range(8):
                kb.op("pe", lambda e: e.matmul(pb[0:96, :], lhsT=wkpB[:, ft, :], rhs=xT[:, ft, :], start=(ft == 0), stop=(ft == 7)),
                      reads=[W, xT_b], writes=[pb_b], inc=(ft == 7))
            rt, rt_b = RT.next()
            kst, kst_b = KST.next()
            kb.op("dve", lambda e: e.tensor_tensor(out=rt[64:96, 0, :], in0=pa[64:96, :], in1=rc[64:96, 0, :], op=ALU.mult),
                  reads=[pa_b, rc_b], writes=[rt_b])
            kb.op("dve", lambda e: e.tensor_tensor(out=rt[64:96, 1, :], in0=pb[64:96, :], in1=rc[64:96, 1, :], op=ALU.mult),
                  reads=[pb_b, rc_b], writes=[rt_b])
            kb.op("pool", lambda e: e.tensor_tensor(out=kst[64:96, :], in0=rt[64:96, 0, :], in1=rt[64:96, 1, :], op=ALU.add),
                  reads=[rt_b], writes=[kst_b])
            kb.dma(ds["KPE"][:, t0:t0 + 512], kst[64:96, :], reads=[kst_b], writes=[dsb["KPE"]])
            if ch + 1 < NCH:
                L_tr(ch + 1)
            vst, vst_b = VST.next()
            cT, cT_b = CT.next()
            for s_ in range(4):
                p1, p1_b = PS.next()
                for ft in range(8):
                    kb.op("pe", lambda e: e.matmul(p1[:, 0:384], lhsT=xT[:, ft, s_ * 128:(s_ + 1) * 128], rhs=wtok[:, ft, 0:384],
                                                   start=(ft == 0), stop=(ft == 7)),
                          reads=[W, xT_b], writes=[p1_b], inc=(ft == 7))
                p2, p2_b = PS.next()
                for ft in range(8):
                    kb.op("pe", lambda e: e.matmul(p2[:, 0:256], lhsT=xT[:, ft, s_ * 128:(s_ + 1) * 128], rhs=wtok[:, ft, 384:640],
                                                   start=(ft == 0), stop=(ft == 7)),
                          reads=[W, xT_b], writes=[p2_b], inc=(ft == 7))
                kb.op("dve", lambda e: e.tensor_copy(out=vst[:, s_, :], in_=p2[:, 0:256]), reads=[p2_b], writes=[vst_b])
                ss, ss_b = SS.next()
                jk, jk_b = JK.next()
                kb.op("act", lambda e: e.activation(out=jk[:, 0:256], in_=p1[:, 0:256], func=AF.Square, accum_out=ss[:, 0:1]),
                      reads=[p1_b], writes=[jk_b, ss_b])
                kb.op("act", lambda e: e.activation(out=jk[:, 0:128], in_=p1[:, 256:384], func=AF.Square, accum_out=ss[:, 1:2]),
                      reads=[p1_b], writes=[jk_b, ss_b])
                kb.op("dve", lambda e: e.tensor_scalar(out=ss[:, 0:1], in0=ss[:, 0:1], scalar1=1.0 / 256, scalar2=RMS_EPS,
                                                       op0=ALU.mult, op1=ALU.add), reads=[ss_b], writes=[ss_b])
                kb.op("dve", lambda e: e.tensor_scalar(out=ss[:, 1:2], in0=ss[:, 1:2], scalar1=1.0 / 128, scalar2=RMS_EPS,
                                                       op0=ALU.mult, op1=ALU.add), reads=[ss_b], writes=[ss_b])
                kb.op("act", lambda e: e.activation(out=ss[:, 2:4], in_=ss[:, 0:2], func=AF.Sqrt), reads=[ss_b], writes=[ss_b])
                kb.op("dve", lambda e: e.reciprocal(out=ss[:, 0:2], in_=ss[:, 2:4]), reads=[ss_b], writes=[ss_b])
                cn, cn_b = CN.next()
                kb.op("dve", lambda e: e.tensor_scalar(out=cn[:, 0:256], in0=p1[:, 0:256], scalar1=ss[:, 0:1], scalar2=None,
                                                       op0=ALU.mult), reads=[p1_b, ss_b], writes=[cn_b])
                kb.op("dve", lambda e: e.tensor_scalar(out=cn[:, 256:384], in0=p1[:, 256:384], scalar1=ss[:, 1:2], scalar2=None,
                                                       op0=ALU.mult), reads=[p1_b, ss_b], writes=[cn_b])
                tq, tq_b = TPQ.next()
                for i in range(3):
                    kb.op("pe", lambda e: e.transpose(out=tq[:, i, :], in_=cn[:, i * 128:(i + 1) * 128], identity=ident[:]),
                          reads=[cn_b, ident_b], writes=[tq_b], inc=(i == 2))
                kb.op("act", lambda e: e.activation(out=cT[:, :, s_ * 128:(s_ + 1) * 128], in_=tq[:, :, :], func=AF.Copy),
                      reads=[tq_b], writes=[cT_b])
            kb.dma(ds["VSW"][t0:t0 + 512, :].rearrange("(s p) c -> p s c", p=128), vst[:, :, :], reads=[vst_b], writes=[dsb["VSW"]])
            qst, qst_b = QST.next()
            for h in range(8):
                pa, pa_b = PS.next()
                for kt in range(2):
                    kb.op("pe", lambda e: e.matmul(pa[0:96, :], lhsT=wuqA[:, kt, h, :], rhs=cT[:, kt, :], start=(kt == 0), stop=(kt == 1)),
                          reads=[W, cT_b], writes=[pa_b], inc=(kt == 1))
                pb, pb_b = PS.next()
                for kt in range(2):
                    kb.op("pe", lambda e: e.matmul(pb[0:96, :], lhsT=wuqB[:, kt, h, :], rhs=cT[:, kt, :], start=(kt == 0), stop=(kt == 1)),
                          reads=[W, cT_b], writes=[pb_b], inc=(kt == 1))
                kb.op("act", lambda e: e.activation(out=qst[0:64, h, :], in_=pa[0:64, :], func=AF.Copy), reads=[pa_b], writes=[qst_b])
                rt, rt_b = RT.next()
                kb.op("dve", lambda e: e.tensor_tensor(out=rt[64:96, 0, :], in0=pa[64:96, :], in1=rc[64:96, 0, :], op=ALU.mult),
                      reads=[pa_b, rc_b], writes=[rt_b])
                kb.op("dve", lambda e: e.tensor_tensor(out=rt[64:96, 1, :], in0=pb[64:96, :], in1=rc[64:96, 1, :], op=ALU.mult),
                      reads=[pb_b, rc_b], writes=[rt_b])
                kb.op("pool", lambda e: e.tensor_tensor(out=qst[64:96, h, :], in0=rt[64:96, 0, :], in1=rt[64:96, 1, :], op=ALU.add),
                      reads=[rt_b], writes=[qst_b])
            kb.dma(ds["QM"].rearrange("h r t -> r h t")[:, :, t0:t0 + 512], qst[:, :, :], reads=[qst_b], writes=[dsb["QM"]])
            knst, knst_b = KNST.next()
            for hp in range(4):
                pt, pt_b = PS.next()
                kb.op("pe", lambda e: e.matmul(pt[:, :], lhsT=wkvK[:, 2 * hp:2 * hp + 2, :].rearrange("p a c -> p (a c)"),
                                               rhs=cT[:, 2, :], start=True, stop=True), reads=[W, cT_b], writes=[pt_b])
                kb.op("act" if hp % 2 else "dve",
                      (lambda e: e.activation(out=knst[:, hp, :], in_=pt[:, :], func=AF.Copy)) if hp % 2 else
                      (lambda e: e.tensor_copy(out=knst[:, hp, :], in_=pt[:, :])), reads=[pt_b], writes=[knst_b])
            kb.dma(ds["KN"].rearrange("(c p) t -> p c t", p=128)[:, :, t0:t0 + 512], knst[:, :, :], reads=[knst_b], writes=[dsb["KN"]])
            vmst, vmst_b = VMST.next()
            for s_ in range(4):
                pt, pt_b = PS.next()
                kb.op("pe", lambda e: e.matmul(pt[:, :], lhsT=cT[:, 2, s_ * 128:(s_ + 1) * 128],
                                               rhs=wkvV[:, :, :].rearrange("p a c -> p (a c)"), start=True, stop=True),
                      reads=[W, cT_b], writes=[pt_b])
                kb.op("act" if s_ % 2 else "dve",
                      (lambda e: e.activation(out=vmst[:, s_, :], in_=pt[:, :], func=AF.Copy)) if s_ % 2 else
                      (lambda e: e.tensor_copy(out=vmst[:, s_, :], in_=pt[:, :])), reads=[pt_b], writes=[vmst_b])
            kb.dma(ds["VM"][t0:t0 + 512, :].rearrange("(s p) c -> p s c", p=128), vmst[:, :, :], reads=[vmst_b], writes=[dsb["VM"]])


class TileD:
    __slots__ = ("qk", "scale", "pv", "epi", "lo", "hi")

    def __init__(self, qk, scale, pv, epi=None, lo=0, hi=512):
        self.qk = qk
        self.scale = scale
        self.pv = pv
        self.epi = epi
        self.lo, self.hi = lo, hi


class Stream:
    def __init__(self, c, R, L=2):
        self.c, self.R, self.L = c, R, L
        self.pending = []

    def push(self, t):
        kb, R = self.c.kb, self.R
        S, S_b = R.SPS.next()
        nq = len(t.qk)
        lo, hi = t.lo, t.hi
        for a, ent in enumerate(t.qk):
            lh, rh, rb = ent[0], ent[1], ent[2]
            clo, chi = (ent[3], ent[4]) if len(ent) > 3 else (lo, hi)
            kb.op("pe", lambda e: e.matmul(S[:, clo:chi], lhsT=lh, rhs=rh, start=(a == 0), stop=(a == nq - 1), skip_group_check=True),
                  reads=rb, writes=[S_b], inc=(a == nq - 1))
        P, P_b = R.PSB.next()
        kb.op("act", lambda e: e.activation(out=P[:, lo:hi], in_=S[:, lo:hi], func=AF.Exp, scale=t.scale), reads=[S_b], writes=[P_b])
        self.pending.append((t, P, P_b))
        if len(self.pending) > self.L:
            self._pop()

    def _pop(self):
        t, P, P_b = self.pending.pop(0)
        t.pv(P, P_b)
        if t.epi is not None:
            t.epi()

    def flush(self):
        while self.pending:
            self._pop()


def run_stream(c, R, tiles, L=2):
    st = c.stream
    for t in tiles:
        st.push(t)


def phase2(c):
    nc, kb = c.nc, c.kb
    din, dc, ds, dsb = c.din, c.dc, c.ds, c.dsb
    ident, ident_b = c.ident, c.ident_b
    with ExitStack() as esA:
        kcmp = sb(nc, esA, "kcmp", [69, 2, 512], BF16)
        kcmp_b = Buf("kcmp")
        vcmp = sb(nc, esA, "vcmp", [128, 4, 2, 128], BF16)
        vcmp_b = Buf("vcmp")
        CB = Buf("consts2")
        cmk = sb(nc, esA, "cmk", [128, 4, 512], BF16)
        wmk = sb(nc, esA, "wmk", [128, 4, 512], BF16)
        pmk = sb(nc, esA, "pmk", [128, 5, 512], BF16)
        eexp = sb(nc, esA, "eexp", [128, 64, 128], BF16)
        impm = sb(nc, esA, "impm", [128, 4, 128], BF16)
        keept = sb(nc, esA, "keept", [128, 256], F32)
        addt = sb(nc, esA, "addt", [128, 256], F32)
        kb.dma(cmk[:], dc["cmk"][:], writes=[CB])
        kb.dma(wmk[:], dc["wmk"][:], writes=[CB])
        kb.dma(pmk[:], dc["pmk"][:], writes=[CB])
        kb.dma(eexp[:], dc["eexp"][:], writes=[CB])
        kb.dma(impm[:], dc["impm"].rearrange("(ct p) b -> p ct b", p=128), writes=[CB])
        kb.dma(keept[:], dc["keept"][:], writes=[CB])
        kb.dma(addt[:], dc["addt"][:], writes=[CB])
        kb.op("pool", lambda e: e.memset(kcmp[:], 0.0), writes=[kcmp_b])
        kb.op("pool", lambda e: e.memset(vcmp[:], 1.0), writes=[vcmp_b])
        for g in range(2):
            kb.dma(kcmp[64:69, g, :], dc["kaugc"][:, :], writes=[kcmp_b])
        with ExitStack() as es:
            kcg = sb(nc, es, "kcg", [64, 4, T], BF16)
            kcg_b = Buf("kcg")
            for kind in range(2):
                for g in range(2):
                    kb.dma(kcg[:, kind * 2 + g, :], ds["KX"][kind, g * 64:(g + 1) * 64, :], reads=[dsb["KX"]], writes=[kcg_b])
            w1s = sb(nc, es, "w1s", [64, 32, 256], F32)
            w1s_b = Buf("w1s")
            w1 = [sb(nc, es, "w1_%d" % k, [64, 32, 256], BF16) for k in range(2)]
            w2s = sb(nc, es, "w2s", [128, 2, 64], F32)
            w2s_b = Buf("w2s")
            w2 = [sb(nc, es, "w2_%d" % k, [128, 2, 64], BF16) for k in range(2)]
            posS = sb(nc, es, "posS", [32, 64], F32)
            posS_b = Buf("posS")
            posB = sb(nc, es, "posB", [32, 64], BF16)
            posB_b = Buf("posB")
            posT = sb(nc, es, "posT", [64, 2, 32], BF16)
            b1 = sb(nc, es, "b1", [128, 4], F32)
            Wc = Buf("wc")
            tpp = ps(nc, es, "tpp", [64, 32], BF16)
            tpp_b = Buf("tpp")
            pb1 = ps(nc, es, "pb1", [128, 4], F32)
            pb1_b = Buf("pb1")
            PSC = rot_tiles(nc, es, "psc", [128, 512], F32, 4, psum=True)
            XG = rot_tiles(nc, es, "xg", [128, 512], F32, 2)
            X2 = rot_tiles(nc, es, "x2", [128, 512], F32, 2)
            GL = rot_tiles(nc, es, "gl", [128, 512], BF16, 4)
            for gl, gl_b in GL.items:
                kb.op("pool", lambda e: e.memset(gl[:], 0.0), writes=[gl_b])
            names = [("w_ck1", "w_ck2", "pos_ck"), ("w_cv1", "w_cv2", "pos_cv")]
            for kind, (n1, n2, npos) in enumerate(names):
                kb.dma(w1s[:], din[n1].rearrange("(l d) n -> d l n", d=64), writes=[w1s_b])
                kb.op("dve", lambda e: e.tensor_copy(out=w1[kind][:, 0:16, :], in_=w1s[:, 0:16, :]), reads=[w1s_b], writes=[Wc])
                kb.op("pool", lambda e: e.tensor_copy(out=w1[kind][:, 16:32, :], in_=w1s[:, 16:32, :]), reads=[w1s_b], writes=[Wc])
                kb.dma(w2s[:], din[n2].rearrange("(c p) n -> p c n", p=128), writes=[w2s_b])
                kb.op("dve", lambda e: e.tensor_copy(out=w2[kind][:], in_=w2s[:]), reads=[w2s_b], writes=[Wc])
                kb.dma(posS[:], din[npos][:, :], writes=[posS_b])
                kb.op("dve", lambda e: e.tensor_copy(out=posB[:], in_=posS[:]), reads=[posS_b], writes=[posB_b])
                kb.op("pe", lambda e: e.transpose(out=tpp[:, :], in_=posB[:, :], identity=ident[0:32, 0:32]),
                      reads=[posB_b, ident_b], writes=[tpp_b])
                kb.op("dve", lambda e: e.tensor_copy(out=posT[:, kind, :], in_=tpp[:, :]), reads=[tpp_b], writes=[Wc])
            for kind in range(2):
                for hc in range(2):
                    idx = kind * 2 + hc
                    for l in range(32):
                        kb.op("pe", lambda e: e.matmul(pb1[:, idx:idx + 1], lhsT=w1[kind][:, l, hc * 128:(hc + 1) * 128],
                                                       rhs=posT[:, kind, l:l + 1], start=(l == 0), stop=(l == 31)),
                              reads=[Wc], writes=[pb1_b], inc=(l == 31))
            kb.op("dve", lambda e: e.tensor_copy(out=b1[:], in_=pb1[:]), reads=[pb1_b], writes=[Wc])
            for kind in range(2):
                for g in range(2):
                    gls = []
                    for hc in range(2):
                        idx = kind * 2 + hc
                        pm, pm_b = PSC.next()
                        for l in range(32):
                            kb.op("pe", lambda e: e.matmul(pm[:, 0:511], lhsT=w1[kind][:, l, hc * 128:(hc + 1) * 128],
                                                           rhs=kcg[:, kind * 2 + g, l:l + 8161:16], start=(l == 0), stop=(l == 31)),
                                  reads=[Wc, kcg_b], writes=[pm_b], inc=(l == 31))
                        xg, xg_b = XG.next()
                        x2, x2_b = X2.next()
                        gl, gl_b = GL.next()
                        kb.op("dve", lambda e: e.tensor_scalar(out=xg[:, 0:511], in0=pm[:, 0:511], scalar1=b1[:, idx:idx + 1],
                                                               scalar2=None, op0=ALU.add), reads=[pm_b, Wc], writes=[xg_b])
                        kb.op("pool", lambda e: e.tensor_tensor(out=x2[:, 0:511], in0=xg[:, 0:511], in1=xg[:, 0:511], op=ALU.mult),
                              reads=[xg_b], writes=[x2_b])
                        kb.op("dve", lambda e: e.tensor_scalar(out=x2[:, 0:511], in0=x2[:, 0:511], scalar1=0.044715, scalar2=1.0,
                                                               op0=ALU.mult, op1=ALU.add), reads=[x2_b], writes=[x2_b])
                        kb.op("pool", lambda e: e.tensor_tensor(out=x2[:, 0:511], in0=x2[:, 0:511], in1=xg[:, 0:511], op=ALU.mult),
                              reads=[x2_b, xg_b], writes=[x2_b])
                        kb.op("act", lambda e: e.activation(out=x2[:, 0:511], in_=x2[:, 0:511], func=AF.Sigmoid, scale=1.5957691216),
                              reads=[x2_b], writes=[x2_b])
                        kb.op("dve", lambda e: e.tensor_tensor(out=gl[:, 0:511], in0=x2[:, 0:511], in1=xg[:, 0:511], op=ALU.mult),
                              reads=[x2_b, xg_b], writes=[gl_b])
                        gls.append((gl, gl_b))
                    if kind == 0:
                        pk, pk_b = PSC.next()
                        for hc in range(2):
                            kb.op("pe", lambda e: e.matmul(pk[0:64, 0:511], lhsT=w2[0][:, hc, :], rhs=gls[hc][0][:, 0:511],
                                                           start=(hc == 0), stop=(hc == 1)),
                                  reads=[Wc, gls[hc][1]], writes=[pk_b], inc=(hc == 1))
                        kb.op("act", lambda e: e.activation(out=kcmp[0:64, g, 0:511], in_=pk[0:64, 0:511], func=AF.Copy),
                              reads=[pk_b], writes=[kcmp_b])
                    else:
                        for ct in range(4):
                            pv, pv_b = PSC.next()
                            for hc in range(2):
                                kb.op("pe", lambda e: e.matmul(pv[:, 0:64], lhsT=gls[hc][0][:, ct * 128:(ct + 1) * 128], rhs=w2[1][:, hc, :],
                                                               start=(hc == 0), stop=(hc == 1)),
                                      reads=[Wc, gls[hc][1]], writes=[pv_b], inc=(hc == 1))
                            kb.op("act", lambda e: e.activation(out=vcmp[:, ct, g, 0:64], in_=pv[:, 0:64], func=AF.Copy),
                                  reads=[pv_b], writes=[vcmp_b])
        kb.barrier()
        if getattr(c, "stop_mid", None) == "compress":
            dbg = c.dbg
            kb.dma(dbg["kcmp"][:], kcmp[:], reads=[kcmp_b])
            kb.dma(dbg["vcmp"][:], vcmp[:], reads=[vcmp_b])
            return
        with ExitStack() as es:
            R = Ctx()
            R.SPS = rot_tiles(nc, es, "sps", [128, 512], F32, 3, psum=True)
            R.OPS = rot_tiles(nc, es, "ops", [128, 512], F32, 2, psum=True)
            R.UPS = rot_tiles(nc, es, "ups", [128, 4, 128], F32, 2, psum=True)
            R.TPN = rot_tiles(nc, es, "tpn", [128, 4, 128], BF16, 1, psum=True)
            R.PSB = rot_tiles(nc, es, "psb", [128, 512], BF16, 4)
            KA = rot_tiles(nc, es, "ka", [96, T], BF16, 2).items
            VA = rot_tiles(nc, es, "va", [128, 64, 128], BF16, 2).items
            for va, va_b in VA:
                kb.op("pool", lambda e: e.memset(va[:], 1.0), writes=[va_b])
            QA = [[(sb(nc, es, "qa%d%d" % (pr, hi), [69, 512], BF16), Buf("qa")) for hi in range(4)] for pr in range(2)]
            OC = [[(sb(nc, es, "oc%d%d" % (pr, hi), [64, 512], F32), Buf("oc")) for hi in range(4)] for pr in range(2)]
            QMT = rot_tiles(nc, es, "qmt", [96, 512], BF16, 3)
            GB = rot_tiles(nc, es, "gb", [64, 3, 512], F32, 4)
            NMT = rot_tiles(nc, es, "nmt", [128, 512], BF16, 2)
            IMP = rot_tiles(nc, es, "imp", [128, 4, 128], F32, 2)
            DEN = rot_tiles(nc, es, "den", [64, 512], F32, 2)
            REC = rot_tiles(nc, es, "rec", [64, 512], F32, 2)
            RG = rot_tiles(nc, es, "rg", [64, 512], F32, 2)
            T1 = rot_tiles(nc, es, "t1", [64, 512], F32, 4)
            ACC = rot_tiles(nc, es, "acc", [64, 512], F32, 2)
            OST = rot_tiles(nc, es, "ost", [64, 512], BF16, 3)
            W1 = rot_tiles(nc, es, "w1k", [128, 128], F32, 2)
            W2 = rot_tiles(nc, es, "w2k", [128, 128], F32, 2)
            M8 = rot_tiles(nc, es, "m8", [128, 16], F32, 2)
            NMS = rot_tiles(nc, es, "nms", [128, 128], BF16, 4)
            RS = rot_tiles(nc, es, "rs", [128, 8], F32, 2)
            Gt = ds["G"].tensor
            c.stream = Stream(c, R)

            def norm_rec(O, O_b):
                den, den_b = DEN.next()
                rec, rec_b = REC.next()
                kb.op("dve", lambda e: e.tensor_scalar_max(out=den[:, :], in0=O[64:128, :], scalar1=1e-30), reads=[O_b], writes=[den_b])
                kb.op("dve", lambda e: e.reciprocal(out=rec[:, :], in_=den[:, :]), reads=[den_b], writes=[rec_b])
                return rec, rec_b

            for g in range(2):
                (ks, ks_b), (kw, kw_b) = KA
                (vs, vs_b), (vw, vw_b) = VA
                kb.dma(ks[0:64, :], ds["KX"][2, g * 64:(g + 1) * 64, :], reads=[dsb["KX"]], writes=[ks_b])
                kb.dma(ks[64:69, :], dc["kaug"][:, :], writes=[ks_b])
                kb.dma(kw[0:64, :], ds["KX"][3, g * 64:(g + 1) * 64, :], reads=[dsb["KX"]], writes=[kw_b])
                kb.dma(kw[64:69, :], dc["kaug"][:, :], writes=[kw_b])
                for half in range(2):
                    hs = slice(half * 32, (half + 1) * 32)
                    rs_ = slice(half * 4096, (half + 1) * 4096)
                    kb.dma(vs[:, hs, 0:64], ds["VSW"][rs_, g * 64:(g + 1) * 64].rearrange("(kt p) d -> p kt d", p=128),
                           reads=[dsb["VSW"]], writes=[vs_b])
                    kb.dma(vw[:, hs, 0:64], ds["VSW"][rs_, 128 + g * 64:128 + (g + 1) * 64].rearrange("(kt p) d -> p kt d", p=128),
                           reads=[dsb["VSW"]], writes=[vw_b])
                CS = {}

                def cmp_begin(qc):
                    CS[qc] = {"imp": IMP.next(), "nmt": NMT.next(), "nms": []}

                def cmp_head(qc, hi, g=g):
                    t0 = qc * 512
                    pr = qc % 2
                    h = 4 * g + hi
                    imp, imp_b = CS[qc]["imp"]
                    qa, qa_b = QA[pr][hi]
                    kb.dma(qa[0:64, :], ds["QN"][h * 64:(h + 1) * 64, t0:t0 + 512], reads=[dsb["QN"]], writes=[qa_b])
                    kb.dma(qa[64:69, :], dc["qaug"][h, :, t0:t0 + 512], writes=[qa_b])
                    O, O_b = R.OPS.next()
                    U, U_b = R.UPS.next()
                    nct = qc // 4 + 1
                    for ct in range(nct):
                        m = qc - 4 * ct
                        qk = [(kcmp[0:69, g, ct * 128:(ct + 1) * 128], qa[0:69, :], [kcmp_b, qa_b])]
                        if m <= 4:
                            qk.append((ident[:, :], pmk[:, m, :], [ident_b, CB]))

                        def pv(P, P_b, ct=ct, nct=nct, O=O, O_b=O_b, U=U, U_b=U_b):
                            kb.op("pe", lambda e: e.matmul(O[:, :], lhsT=vcmp[:, ct, g, :], rhs=P[:, :], start=(ct == 0), stop=(ct == nct - 1)),
                                  reads=[vcmp_b, P_b], writes=[O_b], inc=True)
                            for s_ in range(4):
                                kb.op("pe", lambda e: e.matmul(U[:, s_, :], lhsT=P[:, s_ * 128:(s_ + 1) * 128], rhs=impm[:, ct, :],
                                                               start=(ct == 0), stop=(ct == nct - 1)),
                                      reads=[CB, P_b], writes=[U_b], inc=(s_ == 3))

                        def epi(hi=hi, pr=pr, O=O, O_b=O_b, U=U, U_b=U_b, imp=imp, imp_b=imp_b):
                            rec, rec_b = norm_rec(O, O_b)
                            oc, oc_b = OC[pr][hi]
                            kb.op("dve", lambda e: e.tensor_tensor(out=oc[:, :], in0=O[0:64, :], in1=rec[:, :], op=ALU.mult),
                                  reads=[O_b, rec_b], writes=[oc_b])
                            rs, rs_b = RS.next()
                            kb.op("dve", lambda e: e.tensor_reduce(out=rs[:, 0:4], in_=U[:, :, :], axis=AX.X, op=ALU.add),
                                  reads=[U_b], writes=[rs_b])
                            kb.op("dve", lambda e: e.tensor_scalar(out=rs[:, 0:4], in0=rs[:, 0:4], scalar1=1e-30, scalar2=0.5,
                                                                   op0=ALU.max, op1=ALU.mult), reads=[rs_b], writes=[rs_b])
                            kb.op("dve", lambda e: e.reciprocal(out=rs[:, 4:8], in_=rs[:, 0:4]), reads=[rs_b], writes=[rs_b])
                            for s_ in range(4):
                                if hi == 0:
                                    kb.op("dve", lambda e: e.tensor_scalar(out=imp[:, s_, :], in0=U[:, s_, :], scalar1=rs[:, 4 + s_:5 + s_],
                                                                           scalar2=None, op0=ALU.mult), reads=[U_b, rs_b], writes=[imp_b])
                                else:
                                    kb.op("dve", lambda e: e.scalar_tensor_tensor(out=imp[:, s_, :], in0=U[:, s_, :], scalar=rs[:, 4 + s_:5 + s_],
                                                                                  in1=imp[:, s_, :], op0=ALU.mult, op1=ALU.add),
                                          reads=[U_b, rs_b, imp_b], writes=[imp_b])

                        c.stream.push(TileD(qk, 1.0, pv, epi if ct == nct - 1 else None))

                def topk_dve(qc):
                    imp, imp_b = CS[qc]["imp"]
                    for s_ in range(4):
                        qt = 4 * qc + s_
                        lo = 128 - 2 * qt
                        w, w_b = W1.next()
                        w2, w2_b = W2.next()
                        m8, m8_b = M8.next()
                        nms, nms_b = NMS.next()
                        kb.op("dve", lambda e: e.tensor_tensor(out=w[:, :], in0=imp[:, s_, :], in1=keept[:, lo:lo + 128], op=ALU.mult),
                              reads=[imp_b, CB], writes=[w_b])
                        kb.op("dve", lambda e: e.tensor_tensor(out=w[:, :], in0=w[:, :], in1=addt[:, lo:lo + 128], op=ALU.add),
                              reads=[w_b, CB], writes=[w_b])
                        kb.op("dve", lambda e: e.memset(w[:, 0:1], 1e6), reads=[w_b], writes=[w_b])
                        kb.op("dve", lambda e: e.max(out=m8[:, 0:8], in_=w[:, :]), reads=[w_b], writes=[m8_b])
                        kb.op("dve", lambda e: e.match_replace(out=w2[:, :], in_to_replace=m8[:, 0:8], in_values=w[:, :], imm_value=-2.0),
                              reads=[w_b, m8_b], writes=[w2_b])
                        kb.op("dve", lambda e: e.max(out=m8[:, 8:16], in_=w2[:, :]), reads=[w2_b, m8_b], writes=[m8_b])
                        kb.op("dve", lambda e: e.tensor_scalar(out=nms[:, :], in0=w[:, :], scalar1=m8[:, 15:16], scalar2=-1.0,
                                                               op0=ALU.is_ge, op1=ALU.add), reads=[w_b, m8_b], writes=[nms_b])
                        CS[qc]["nms"].append((nms, nms_b))

                def topk_pe(qc):
                    nmt, nmt_b = CS[qc]["nmt"]
                    tpn, tpn_b = R.TPN.next()
                    for s_ in range(4):
                        nms, nms_b = CS[qc]["nms"][s_]
                        kb.op("pe", lambda e: e.transpose(out=tpn[:, s_, :], in_=nms[:, :], identity=ident[:, :]),
                              reads=[nms_b, ident_b], writes=[tpn_b])
                    kb.op("act", lambda e: e.activation(out=nmt[:, :], in_=tpn[:, :, :].rearrange("p a b -> p (a b)"), func=AF.Copy),
                          reads=[tpn_b], writes=[nmt_b])

                def sw_head(qc, hi, g=g):
                    t0 = qc * 512
                    pr = qc % 2
                    h = 4 * g + hi
                    nmt, nmt_b = CS[qc]["nmt"]
                    qa, qa_b = QA[pr][hi]
                    gb, gb_b = GB.next()
                    kb.dma(gb[:, :, :], bass.AP(Gt, h * 3 * T + t0, [[0, 64], [T, 3], [1, 512]]), reads=[dsb["G"]], writes=[gb_b])
                    st = {}
                    O, O_b = R.OPS.next()
                    nk = 4 * qc + 4
                    k0 = max(0, int((t0 - 127 - ALIBI_CUT * (2.0 ** (h + 1))) // 128) + 1)
                    for kt in range(k0, nk):
                        d = kt - 4 * qc
                        lo = 128 * d if d > 0 else 0
                        qk = [(ks[0:69, kt * 128:(kt + 1) * 128], qa[0:69, lo:512], [ks_b, qa_b]),
                              (eexp[:, kt, :], nmt[:, lo:512], [CB, nmt_b])]
                        if d >= 0:
                            qk.append((ident[:, :], cmk[:, d, lo:lo + 128], [ident_b, CB], lo, lo + 128))

                        def pv(P, P_b, kt=kt, nk=nk, k0=k0, O=O, O_b=O_b, lo=lo):
                            kb.op("pe", lambda e: e.matmul(O[:, lo:512], lhsT=vs[:, kt, :], rhs=P[:, lo:512], start=(kt == k0), stop=(kt == nk - 1),
                                                           skip_group_check=True),
                                  reads=[vs_b, P_b], writes=[O_b], inc=True)

                        def epi(O=O, O_b=O_b, gb=gb, gb_b=gb_b, st=st):
                            rec, rec_b = norm_rec(O, O_b)
                            rg, rg_b = RG.next()
                            t1, t1_b = T1.next()
                            kb.op("pool", lambda e: e.tensor_tensor(out=rg[:, :], in0=rec[:, :], in1=gb[:, 1, :], op=ALU.mult),
                                  reads=[rec_b, gb_b], writes=[rg_b])
                            kb.op("dve", lambda e: e.tensor_tensor(out=t1[:, :], in0=O[0:64, :], in1=rg[:, :], op=ALU.mult),
                                  reads=[O_b, rg_b], writes=[t1_b])
                            st["slc"] = (t1, t1_b)

                        c.stream.push(TileD(qk, 1.0, pv, epi if kt == nk - 1 else None, lo=lo, hi=512))
                    O, O_b = R.OPS.next()
                    wl = []
                    for d in range(4):
                        wl.append((4 * qc + d, cmk[:, d, 128 * d:128 * d + 128], 128 * d, 512, 128 * d))
                    for d in range(4):
                        kt = 4 * qc - 4 + d
                        if kt >= 0:
                            wl.append((kt, wmk[:, d, 128 * d:128 * d + 128], 0, 128 * d + 128, 128 * d))
                    nw = len(wl)
                    for wi, (kt, mk, wlo, whi, mlo) in enumerate(wl):
                        qk = [(kw[0:69, kt * 128:(kt + 1) * 128], qa[0:69, wlo:whi], [kw_b, qa_b]),
                              (ident[:, :], mk, [ident_b, CB], mlo, mlo + 128)]

                        def pv(P, P_b, kt=kt, wi=wi, nw=nw, O=O, O_b=O_b, wlo=wlo, whi=whi):
                            kb.op("pe", lambda e: e.matmul(O[:, wlo:whi], lhsT=vw[:, kt, :], rhs=P[:, wlo:whi], start=(wi == 0), stop=(wi == nw - 1),
                                                           skip_group_check=True),
                                  reads=[vw_b, P_b], writes=[O_b], inc=True)

                        def epi(h=h, hi=hi, pr=pr, t0=t0, O=O, O_b=O_b, gb=gb, gb_b=gb_b, st=st):
                            rec, rec_b = norm_rec(O, O_b)
                            rg, rg_b = RG.next()
                            t2, t2_b = T1.next()
                            kb.op("pool", lambda e: e.tensor_tensor(out=rg[:, :], in0=rec[:, :], in1=gb[:, 2, :], op=ALU.mult),
                                  reads=[rec_b, gb_b], writes=[rg_b])
                            kb.op("dve", lambda e: e.tensor_tensor(out=t2[:, :], in0=O[0:64, :], in1=rg[:, :], op=ALU.mult),
                                  reads=[O_b, rg_b], writes=[t2_b])
                            t1, t1_b = st["slc"]
                            oc, oc_b = OC[pr][hi]
                            acc, acc_b = ACC.next()
                            ost, ost_b = OST.next()
                            kb.op("pool", lambda e: e.tensor_tensor(out=acc[:, :], in0=oc[:, :], in1=gb[:, 0, :], op=ALU.mult),
                                  reads=[oc_b, gb_b], writes=[acc_b])
                            kb.op("pool", lambda e: e.tensor_tensor(out=acc[:, :], in0=acc[:, :], in1=t1[:, :], op=ALU.add),
                                  reads=[acc_b, t1_b], writes=[acc_b])
                            kb.op("pool", lambda e: e.tensor_tensor(out=ost[:, :], in0=acc[:, :], in1=t2[:, :], op=ALU.add),
                                  reads=[acc_b, t2_b], writes=[ost_b])
                            kb.dma(ds["ON"][h * 64:(h + 1) * 64, t0:t0 + 512], ost[:, :], reads=[ost_b], writes=[dsb["ON"]], q="pool")

                        c.stream.push(TileD(qk, 1.0, pv, epi if wi == nw - 1 else None, lo=wlo, hi=whi))

                cmp_begin(0)
                for hi in range(4):
                    cmp_head(0, hi)
                c.stream.flush()
                topk_dve(0)
                topk_pe(0)
                for qc in range(NCH):
                    nxt = qc + 1 < NCH
                    if nxt:
                        cmp_begin(qc + 1)
                        cmp_head(qc + 1, 0)
                        cmp_head(qc + 1, 1)
                    sw_head(qc, 0)
                    if nxt:
                        cmp_head(qc + 1, 2)
                        cmp_head(qc + 1, 3)
                    sw_head(qc, 1)
                    sw_head(qc, 2)
                    if nxt:
                        topk_dve(qc + 1)
                    sw_head(qc, 3)
                    c.stream.flush()
                    if nxt:
                        topk_pe(qc + 1)
                    CS.pop(qc)
            c.stream.flush()
            if getattr(c, "stop_mid", None) == "nsa":
                return
            wst = sb(nc, es, "wst", [128, 1408], F32)
            wst_b = Buf("wst")
            wsb = sb(nc, es, "wsb", [128, 1408], BF16)
            wsb_b = Buf("wsb")
            wpieces = [(ft, a, hf) for ft in range(8) for a in range(2) for hf in range(2)]

            def wup_piece(ft, a, hf):
                c0 = a * DFF + hf * 1408
                kb.dma(wst[:, :], din["w_up"][ft * 128:(ft + 1) * 128, c0:c0 + 1408], writes=[wst_b])
                kb.op("pool", lambda e: e.tensor_copy(out=wsb[:, :], in_=wst[:, :]), reads=[wst_b], writes=[wsb_b])
                kb.dma(ds["WUP"][hf * 11:(hf + 1) * 11, :, a, ft, :].rearrange("j p c -> p j c"),
                       wsb[:, :].rearrange("p (j c) -> p j c", c=128), reads=[wsb_b], writes=[dsb["WUP"]], q="pool")

            for h in range(8):
                ka, ka_b = KA[h % 2]
                va, va_b = VA[h % 2]
                kb.dma(ka[0:64, :], ds["KN"][h * 64:(h + 1) * 64, :], reads=[dsb["KN"]], writes=[ka_b])
                kb.dma(ka[64:96, :], ds["KPE"][:, :], reads=[dsb["KPE"]], writes=[ka_b])
                for half in range(2):
                    hs = slice(half * 32, (half + 1) * 32)
                    rs_ = slice(half * 4096, (half + 1) * 4096)
                    kb.dma(va[:, hs, 0:64], ds["VM"][rs_, h * 64:(h + 1) * 64].rearrange("(kt p) d -> p kt d", p=128),
                           reads=[dsb["VM"]], writes=[va_b])
                qms = {}

                def qload(qc, h=h, qms=qms):
                    qm, qm_b = QMT.next()
                    kb.dma(qm[:, :], ds["QM"][h, :, qc * 512:(qc + 1) * 512], reads=[dsb["QM"]], writes=[qm_b])
                    qms[qc] = (qm, qm_b)

                qload(0)
                for qc in range(NCH):
                    t0 = qc * 512
                    if qc + 1 < NCH:
                        qload(qc + 1)
                    qm, qm_b = qms.pop(qc)
                    O, O_b = R.OPS.next()
                    nk = 4 * qc + 4
                    for kt in range(nk):
                        d = kt - 4 * qc
                        lo = 128 * d if d > 0 else 0
                        qk = [(ka[0:96, kt * 128:(kt + 1) * 128], qm[0:96, lo:512], [ka_b, qm_b])]
                        if d >= 0:
                            qk.append((ident[:, :], cmk[:, d, lo:lo + 128], [ident_b, CB], lo, lo + 128))

                        def pv(P, P_b, kt=kt, nk=nk, O=O, O_b=O_b, va=va, va_b=va_b, lo=lo):
                            kb.op("pe", lambda e: e.matmul(O[:, lo:512], lhsT=va[:, kt, :], rhs=P[:, lo:512], start=(kt == 0), stop=(kt == nk - 1),
                                                           skip_group_check=True),
                                  reads=[va_b, P_b], writes=[O_b], inc=True)

                        def epi(h=h, t0=t0, O=O, O_b=O_b):
                            rec, rec_b = norm_rec(O, O_b)
                            ost, ost_b = OST.next()
                            kb.op("dve", lambda e: e.tensor_tensor(out=ost[:, :], in0=O[0:64, :], in1=rec[:, :], op=ALU.mult),
                                  reads=[O_b, rec_b], writes=[ost_b])
                            kb.dma(ds["OM"][h * 64:(h + 1) * 64, t0:t0 + 512], ost[:, :], reads=[ost_b], writes=[dsb["OM"]], q="pool")

                        c.stream.push(TileD(qk, MLA_SCALE, pv, epi if kt == nk - 1 else None, lo=lo, hi=512))
                    if wpieces:
                        wup_piece(*wpieces.pop(0))
            c.stream.flush()

CH3 = 256
NCH3 = T // CH3


def phase3(c):
    nc, kb = c.nc, c.kb
    din, dc, ds, dsb = c.din, c.dc, c.ds, c.dsb
    ident, ident_b = c.ident, c.ident_b
    with ExitStack() as es:
        W = Buf("p3w")
        wo = sb(nc, es, "wo", [128, 8, 1024], BF16)
        wpg = sb(nc, es, "wpg", [128, 8, 1024], BF16)
        wpl = sb(nc, es, "wpl", [128, 2, 1024], BF16)
        wd = sb(nc, es, "wd", [128, 22, 1024], BF16)
        lnv = sb(nc, es, "lnv", [128, 6, 1024], F32)
        for i, nm in enumerate(("ln1_g", "ln1_b", "ln2_g", "ln2_b", "ln3_g", "ln3_b")):
            kb.dma(lnv[:, i, :], din[nm][0:1, :].partition_broadcast(128), writes=[W])
        with ExitStack() as es1:
            STG = rot_tiles(nc, es1, "stg3", [128, 1024], F32, 4)
            k = 0
            for name, dst in (("w_out", wo), ("w_ple_gate", wpg)):
                for ft in range(8):
                    s_, s_b = STG.next()
                    kb.dma(s_[:, 0:1024], din[name][ft * 128:(ft + 1) * 128, :], writes=[s_b])
                    kb.op("dve" if k % 2 else "pool", lambda e: e.tensor_copy(out=dst[:, ft, :], in_=s_[:, 0:1024]), reads=[s_b], writes=[W])
                    k += 1
            for kt in range(2):
                s_, s_b = STG.next()
                kb.dma(s_[:, 0:1024], din["w_ple"][kt * 128:(kt + 1) * 128, :], writes=[s_b])
                kb.op("dve", lambda e: e.tensor_copy(out=wpl[:, kt, :], in_=s_[:, 0:1024]), reads=[s_b], writes=[W])
            for j in range(22):
                s_, s_b = STG.next()
                kb.dma(s_[:, 0:1024], din["w_down"][j * 128:(j + 1) * 128, :], writes=[s_b])
                kb.op("dve" if j % 2 else "pool", lambda e: e.tensor_copy(out=wd[:, j, :], in_=s_[:, 0:1024]), reads=[s_b], writes=[W])
        kb.barrier()
        XF = rot_tiles(nc, es, "xf3", [128, 1024], F32, 1)
        OT = rot_tiles(nc, es, "oT3", [128, 8, CH3], BF16, 1)
        PF = rot_tiles(nc, es, "pf3", [128, 2, 256], F32, 1)
        PB = rot_tiles(nc, es, "pb3", [128, 2, 256], BF16, 3)
        PT = rot_tiles(nc, es, "pT3", [128, 2, CH3], BF16, 1)
        Y = rot_tiles(nc, es, "y3", [128, 1024], F32, 2)
        Z = rot_tiles(nc, es, "z3", [128, 1024], F32, 1)
        X1 = rot_tiles(nc, es, "x1r", [128, 2, 1024], F32, 2)
        X2 = rot_tiles(nc, es, "x2r", [128, 2, 1024], F32, 2)
        XBF = rot_tiles(nc, es, "xbf3", [128, 1024], BF16, 2)
        X1T = rot_tiles(nc, es, "x1T", [128, 8, CH3], BF16, 2)
        X2T = rot_tiles(nc, es, "x2T", [128, 8, CH3], BF16, 2)
        AT = rot_tiles(nc, es, "aT", [128, 22, CH3], BF16, 1)
        WU = rot_tiles(nc, es, "wu", [128, 2, 8, 128], BF16, 3)
        SG = rot_tiles(nc, es, "sg", [128, CH3], F32, 2)
        ST = rot_tiles(nc, es, "st3", [128, 8], F32, 8)
        PA = rot_tiles(nc, es, "pa3", [128, 512], F32, 3, psum=True)
        PG = rot_tiles(nc, es, "pg3", [128, 2, CH3], F32, 4, psum=True)
        TP = rot_tiles(nc, es, "tp3", [128, 8, 128], BF16, 1, psum=True)

        epsv = sb(nc, es, "epsv", [128, 1], F32)
        kb.op("pool", lambda e: e.memset(epsv[:], LN_EPS), writes=[W])

        def ln_steps(y, y_b, st, st_b, gi, out_ap, out_b, post=None, junk=None):
            def s1():
                j_ap, j_b = junk if junk is not None else (out_ap, out_b)
                kb.op("act", lambda e: e.activation(out=j_ap, in_=y[:, :], func=AF.Square, accum_out=st[:, 2:3]), reads=[y_b], writes=[j_b, st_b])

            def s2():
                kb.op("dve", lambda e: e.tensor_scalar(out=st[:, 3:4], in0=st[:, 0:1], scalar1=st[:, 1:2], scalar2=1.0 / D, op0=ALU.add, op1=ALU.mult),
                      reads=[st_b], writes=[st_b])
                kb.op("dve", lambda e: e.scalar_tensor_tensor(out=st[:, 4:5], in0=st[:, 3:4], scalar=-1.0, in1=st[:, 3:4], op0=ALU.mult, op1=ALU.mult),
                      reads=[st_b], writes=[st_b])
                kb.op("dve", lambda e: e.scalar_tensor_tensor(out=st[:, 5:6], in0=st[:, 2:3], scalar=1.0 / D, in1=st[:, 4:5], op0=ALU.mult, op1=ALU.add),
                      reads=[st_b], writes=[st_b])

            def s3():
                kb.op("act", lambda e: e.activation(out=st[:, 6:7], in_=st[:, 5:6], func=AF.Sqrt, bias=epsv[:, 0:1]), reads=[st_b, W], writes=[st_b])

            def s4():
                z, z_b = Z.next()
                kb.op("dve", lambda e: e.reciprocal(out=st[:, 7:8], in_=st[:, 6:7]), reads=[st_b], writes=[st_b])
                kb.op("dve", lambda e: e.scalar_tensor_tensor(out=z[:, :], in0=y[:, :], scalar=st[:, 3:4], in1=lnv[:, gi, :],
                                                              op0=ALU.subtract, op1=ALU.mult), reads=[y_b, st_b, W], writes=[z_b])
                kb.op("dve", lambda e: e.scalar_tensor_tensor(out=out_ap, in0=z[:, :], scalar=st[:, 7:8], in1=lnv[:, gi + 1, :],
                                                              op0=ALU.mult, op1=ALU.add), reads=[z_b, st_b, W], writes=[out_b])
                if post is not None:
                    post()

            return [s1, s2, s3, s4]

        def cast_bf(src, src_b):
            xb, xb_b = XBF.next()
            kb.op("pool", lambda e: e.tensor_copy(out=xb[:, :], in_=src), reads=[src_b], writes=[xb_b])
            return xb, xb_b

        def transposes(xb, xb_b, s_, dstT, dstT_b):
            tp, tp_b = TP.next()
            for ft in range(8):
                kb.op("pe", lambda e: e.transpose(out=tp[:, ft, :], in_=xb[:, ft * 128:(ft + 1) * 128], identity=ident[:, :]),
                      reads=[xb_b, ident_b], writes=[tp_b], inc=(ft == 7))
            kb.op("act", lambda e: e.activation(out=dstT[:, :, s_ * 128:(s_ + 1) * 128], in_=tp[:, :, :], func=AF.Copy), reads=[tp_b], writes=[dstT_b])

        S = {}

        def A_mm(ch):
            t0 = ch * CH3
            st = S.setdefault(ch, {})
            oT, oT_b = OT.next()
            pf, pf_b = PF.next()
            kb.dma(oT[:, 0:4, :], ds["ON"].rearrange("(c p) t -> p c t", p=128)[:, :, t0:t0 + CH3], reads=[dsb["ON"]], writes=[oT_b])
            kb.dma(oT[:, 4:8, :], ds["OM"].rearrange("(c p) t -> p c t", p=128)[:, :, t0:t0 + CH3], reads=[dsb["OM"]], writes=[oT_b])
            kb.dma(pf[:, :, :], din["p"][t0:t0 + CH3, :].rearrange("(s p) d -> p s d", p=128), writes=[pf_b])
            pb_, pb_b = PB.next()
            kb.op("pool", lambda e: e.tensor_copy(out=pb_[:, :, :], in_=pf[:, :, :]), reads=[pf_b], writes=[pb_b])
            st["pb"] = (pb_, pb_b)
            st["x1"] = X1.next()
            st["x1T"] = X1T.next()
            st["ys"] = []
            for s_ in range(2):
                y, y_b = Y.next()
                sst, sst_b = ST.next()
                xf, xf_b = XF.next()
                kb.dma(xf[:, :], din["x"][t0 + s_ * 128:t0 + (s_ + 1) * 128, :], writes=[xf_b])
                for n in range(2):
                    pa, pa_b = PA.next()
                    for kt in range(8):
                        kb.op("pe", lambda e: e.matmul(pa[:, :], lhsT=oT[:, kt, s_ * 128:(s_ + 1) * 128], rhs=wo[:, kt, n * 512:(n + 1) * 512],
                                                       start=(kt == 0), stop=(kt == 7)), reads=[oT_b, W], writes=[pa_b], inc=(kt == 7))
                    kb.op("dve", lambda e: e.scalar_tensor_tensor(out=y[:, n * 512:(n + 1) * 512], in0=xf[:, n * 512:(n + 1) * 512], scalar=ALPHA,
                                                                  in1=pa[:, :], op0=ALU.mult, op1=ALU.add, accum_out=sst[:, n:n + 1]),
                          reads=[xf_b, pa_b], writes=[y_b, sst_b])
                st["ys"].append((y, y_b, sst, sst_b))

        def A_ln(ch, s_):
            st = S[ch]
            x1, x1_b = st["x1"]
            y, y_b, sst, sst_b = st["ys"][s_]

            def post():
                st["xb%d" % s_] = cast_bf(x1[:, s_, :], x1_b)

            return ln_steps(y, y_b, sst, sst_b, 0, x1[:, s_, :], x1_b, post)

        def A_tr(ch, s_):
            st = S[ch]
            x1T, x1T_b = st["x1T"]
            transposes(*st["xb%d" % s_], s_, x1T, x1T_b)

        def B_up(ch, j):
            st = S[ch]
            x1T, x1T_b = st["x1T"]
            if j == 0:
                st["aT"] = AT.next()
            aT, aT_b = st["aT"]
            wu, wu_b = WU.next()
            kb.dma(wu[:, :, :, :], ds["WUP"][j], reads=[dsb["WUP"]], writes=[wu_b])
            pg, pg_b = PG.next()
            for a in range(2):
                for ft in range(8):
                    kb.op("pe", lambda e: e.matmul(pg[:, a, :], lhsT=wu[:, a, ft, :], rhs=x1T[:, ft, :], start=(ft == 0), stop=(ft == 7)),
                          reads=[wu_b, x1T_b], writes=[pg_b], inc=(a == 1 and ft == 7))
            sg, sg_b = SG.next()
            kb.op("act", lambda e: e.activation(out=sg[:, :], in_=pg[:, 0, :], func=AF.Silu), reads=[pg_b], writes=[sg_b])
            kb.op("dve", lambda e: e.tensor_tensor(out=aT[:, j, :], in0=pg[:, 1, :], in1=sg[:, :], op=ALU.mult), reads=[pg_b, sg_b], writes=[aT_b])

        def B_down(ch):
            st = S[ch]
            aT, aT_b = st["aT"]
            x1, x1_b = st["x1"]
            st["x2"] = X2.next()
            st["x2T"] = X2T.next()
            x2, x2_b = st["x2"]
            ys = []
            for s_ in range(2):
                y, y_b = Y.next()
                sst, sst_b = ST.next()
                for n in range(2):
                    pa, pa_b = PA.next()
                    for j in range(22):
                        kb.op("pe", lambda e: e.matmul(pa[:, :], lhsT=aT[:, j, s_ * 128:(s_ + 1) * 128], rhs=wd[:, j, n * 512:(n + 1) * 512],
                                                       start=(j == 0), stop=(j == 21)), reads=[aT_b, W], writes=[pa_b], inc=(j == 21))
                    kb.op("dve", lambda e: e.scalar_tensor_tensor(out=y[:, n * 512:(n + 1) * 512], in0=x1[:, s_, n * 512:(n + 1) * 512], scalar=ALPHA,
                                                                  in1=pa[:, :], op0=ALU.mult, op1=ALU.add, accum_out=sst[:, n:n + 1]),
                          reads=[x1_b, pa_b], writes=[y_b, sst_b])
                ys.append((y, y_b, sst, sst_b))
            st["ys2"] = ys

        def B_ln(ch, s_):
            st = S[ch]
            x2, x2_b = st["x2"]
            y, y_b, sst, sst_b = st["ys2"][s_]

            def post():
                st["x2b%d" % s_] = cast_bf(x2[:, s_, :], x2_b)

            return ln_steps(y, y_b, sst, sst_b, 2, x2[:, s_, :], x2_b, post)

        def B_tr(ch, s_):
            st = S[ch]
            x2T, x2T_b = st["x2T"]
            transposes(*st["x2b%d" % s_], s_, x2T, x2T_b)

        def C_pre(ch):
            st = S[ch]
            pb_, pb_b = st["pb"]
            pT, pT_b = PT.next()
            tp, tp_b = TP.next()
            for s_ in range(2):
                for kt in range(2):
                    kb.op("pe", lambda e: e.transpose(out=tp[:, s_ * 2 + kt, :], in_=pb_[:, s_, kt * 128:(kt + 1) * 128], identity=ident[:, :]),
                          reads=[pb_b, ident_b], writes=[tp_b], inc=(s_ == 1 and kt == 1))
            for s_ in range(2):
                kb.op("act", lambda e: e.activation(out=pT[:, :, s_ * 128:(s_ + 1) * 128], in_=tp[:, 2 * s_:2 * s_ + 2, :], func=AF.Copy),
                      reads=[tp_b], writes=[pT_b])
            st["pT"] = (pT, pT_b)

        def C_sub(ch, s_):
            t0 = ch * CH3
            st = S[ch]
            x2, x2_b = st["x2"]
            x2T, x2T_b = st["x2T"]
            pT, pT_b = st["pT"]
            y, y_b = Y.next()
            sst, sst_b = ST.next()
            for n in range(2):
                pa, pa_b = PA.next()
                for kt in range(8):
                    kb.op("pe", lambda e: e.matmul(pa[:, :], lhsT=x2T[:, kt, s_ * 128:(s_ + 1) * 128], rhs=wpg[:, kt, n * 512:(n + 1) * 512],
                                                   start=(kt == 0), stop=(kt == 7)), reads=[x2T_b, W], writes=[pa_b], inc=(kt == 7))
                kb.op("act", lambda e: e.activation(out=y[:, n * 512:(n + 1) * 512], in_=pa[:, :], func=AF.Sigmoid), reads=[pa_b], writes=[y_b])
                pp, pp_b = PA.next()
                for kt in range(2):
                    kb.op("pe", lambda e: e.matmul(pp[:, :], lhsT=pT[:, kt, s_ * 128:(s_ + 1) * 128], rhs=wpl[:, kt, n * 512:(n + 1) * 512],
                                                   start=(kt == 0), stop=(kt == 1)), reads=[pT_b, W], writes=[pp_b], inc=(kt == 1))
                kb.op("dve", lambda e: e.tensor_tensor(out=y[:, n * 512:(n + 1) * 512], in0=pp[:, :], in1=y[:, n * 512:(n + 1) * 512], op=ALU.mult),
                      reads=[pp_b, y_b], writes=[y_b])
                kb.op("dve", lambda e: e.scalar_tensor_tensor(out=y[:, n * 512:(n + 1) * 512], in0=x2[:, s_, n * 512:(n + 1) * 512], scalar=ALPHA,
                                                              in1=y[:, n * 512:(n + 1) * 512], op0=ALU.mult, op1=ALU.add, accum_out=sst[:, n:n + 1]),
                      reads=[x2_b, y_b], writes=[y_b, sst_b])
            def post():
                kb.dma(c.out[t0 + s_ * 128:t0 + (s_ + 1) * 128, :], y[:, :], reads=[y_b], writes=[c.out_b], q="pool")
                if s_ == 1:
                    S.pop(ch)

            zj, zj_b = Z.items[0]
            return ln_steps(y, y_b, sst, sst_b, 4, y[:, :], y_b, post, junk=(zj[:, :], zj_b))

        def run_all(steps):
            for f in steps:
                f()

        A_mm(0)
        for s_ in range(2):
            run_all(A_ln(0, s_))
            A_tr(0, s_)
        sched = {
            0: [("Bln", 0, 0)], 1: [("Bln", 0, 1), ("Bln", 1, 0)], 2: [("Bln", 0, 2), ("Bln", 1, 1)], 3: [("Bln", 0, 3), ("Bln", 1, 2)],
            4: [("Bln", 1, 3)], 5: [("Btr", 0, 0)], 6: [("Btr", 1, 0)], 7: [("Amm", 0, 0)],
            8: [("Aln", 0, 0)], 9: [("Aln", 0, 1), ("Aln", 1, 0)], 10: [("Aln", 0, 2), ("Aln", 1, 1)], 11: [("Aln", 0, 3), ("Aln", 1, 2)],
            12: [("Aln", 1, 3)], 13: [("Cpre", 0, 0)], 14: [("Cmm", 0, 0), ("Cln", 0, 0)], 15: [("Cln", 0, 1)],
            16: [("Cln", 0, 2), ("Atr", 0, 0)], 17: [("Cln", 0, 3)], 18: [("Cmm", 1, 0), ("Cln", 1, 0), ("Atr", 1, 0)],
            19: [("Cln", 1, 1)], 20: [("Cln", 1, 2)], 21: [("Cln", 1, 3)],
        }
        chains = {}

        def run_slot(it, j):
            nxt = it + 1 < NCH3
            prv = it >= 1
            for name, s_, k in sched.get(j, []):
                if name == "Bln" and prv:
                    if k == 0:
                        chains[("B", s_)] = B_ln(it - 1, s_)
                    chains[("B", s_)][k]()
                elif name == "Btr" and prv:
                    B_tr(it - 1, s_)
                elif name == "Amm" and nxt:
                    A_mm(it + 1)
                elif name == "Aln" and nxt:
                    if k == 0:
                        chains[("A", s_)] = A_ln(it + 1, s_)
                    chains[("A", s_)][k]()
                elif name == "Atr" and nxt:
                    A_tr(it + 1, s_)
                elif name == "Cpre" and prv:
                    C_pre(it - 1)
                elif name == "Cmm" and prv:
                    chains[("C", s_)] = C_sub(it - 1, s_)
                elif name == "Cln" and prv:
                    chains[("C", s_)][k]()

        for it in range(NCH3 + 1):
            if it < NCH3:
                for j in range(22):
                    B_up(it, j)
                    run_slot(it, j)
                B_down(it)
            else:
                for j in range(22):
                    run_slot(it, j)


_CACHE = {}


def kernel(**inputs):
    if "nc" not in _CACHE:
        _CACHE["nc"] = build_program()
        _CACHE["consts"] = host_constants()
    nc = _CACHE["nc"]
    consts = _CACHE["consts"]
    B = inputs["x"].shape[0]
    shared = {}
    for k, shp in IN_SPECS.items():
        if k in ("x", "p"):
            continue
        shared[k] = np.ascontiguousarray(np.asarray(inputs[k], dtype=np.float32)[0].reshape(shp))
    for k, v in consts.items():
        shared["c_" + k] = v
    in_maps = []
    for b in range(B):
        m = dict(shared)
        m["x"] = np.ascontiguousarray(np.asarray(inputs["x"], dtype=np.float32)[b])
        m["p"] = np.ascontiguousarray(np.asarray(inputs["p"], dtype=np.float32)[0, b])
        in_maps.append(m)
    res = run_bass_kernel_spmd(nc, in_maps, core_ids=list(range(B)))
    return np.stack([np.asarray(r["out"], dtype=np.float32) for r in res.results], 0)
```

```python
from contextlib import ExitStack
import numpy as np
import ml_dtypes
import concourse.bass as bass
import concourse.mybir as mybir
from concourse.bass_utils import run_bass_kernel_spmd

F32 = mybir.dt.float32
BF16 = mybir.dt.bfloat16
AF = mybir.ActivationFunctionType
ALU = mybir.AluOpType
AX = mybir.AxisListType
bf16 = ml_dtypes.bfloat16

T = 8192
D = 1024
NCH = T // 512
NKT = T // 128
DFF = 2816
BIG = 30000.0
ALPHA = 2.0 ** 0.25
LN_EPS = 1e-5
RMS_EPS = 1e-6
MLA_SCALE = 96.0 ** -0.5
ALIBI_CUT = 200.0


class Buf:
    __slots__ = ("name", "last_w", "readers")

    def __init__(self, name=""):
        self.name = name
        self.last_w = None
        self.readers = {}


class KB:
    NSLOT = 8

    def __init__(self, nc, es):
        self.nc = nc
        self.E = {"pe": nc.tensor, "act": nc.scalar, "dve": nc.vector, "pool": nc.gpsimd, "sp": nc.sync}
        self.sem = {}
        self.seq = {}
        self.known = {}
        for e in self.E:
            self.sem[("e", e)] = es.enter_context(nc.semaphore("s_" + e))
            self.seq[e] = 0
            self.known[e] = {}
        self.slots = {}
        self.slot_rr = {}
        for q in ("sp", "pool"):
            self.slots[q] = []
            self.slot_rr[q] = 0
            for i in range(self.NSLOT):
                key = ("d", q, i)
                self.sem[key] = es.enter_context(nc.semaphore("d_%s%d" % (q, i)))
                self.slots[q].append([key, 0])

    def _collect(self, reads, writes):
        deps = {}
        for b in reads:
            if b.last_w is not None:
                k, v = b.last_w
                o = deps.get(k)
                deps[k] = (max(v, o[0]) if o else v, True)
        for b in writes:
            if b.last_w is not None:
                k, v = b.last_w
                o = deps.get(k)
                if o is None:
                    deps[k] = (v, False)
                elif v > o[0]:
                    deps[k] = (v, o[1])
            for k, v in b.readers.items():
                o = deps.get(k)
                if o is None:
                    deps[k] = (v, False)
                elif v > o[0]:
                    deps[k] = (v, o[1])
        return deps

    def _wait(self, eng, deps):
        kn = self.known[eng]
        for k, (v, raw) in deps.items():
            if k == ("e", eng):
                if eng == "pe" or not raw:
                    continue
            if kn.get(k, 0) >= v:
                continue
            self.E[eng].wait_ge(self.sem[k], v)
            kn[k] = v

    def _record(self, dep, reads, writes):
        k, v = dep
        for b in reads:
            if b.readers.get(k, 0) < v:
                b.readers[k] = v
        for b in writes:
            b.last_w = dep
            b.readers = {}

    def op(self, eng, fn, reads=(), writes=(), inc=True):
        self._wait(eng, self._collect(reads, writes))
        ins = fn(self.E[eng])
        if inc:
            ins.then_inc(self.sem[("e", eng)], 1)
            self.seq[eng] += 1
            dep = (("e", eng), self.seq[eng])
        else:
            dep = (("e", eng), self.seq[eng] + 1)
        self._record(dep, reads, writes)
        return ins

    def dma(self, out, in_, reads=(), writes=(), q="sp", **kw):
        sl = self.slots[q][self.slot_rr[q]]
        self.slot_rr[q] = (self.slot_rr[q] + 1) % self.NSLOT
        deps = self._collect(reads, writes)
        if sl[1] > 0:
            o = deps.get(sl[0])
            if o is None or o[0] < sl[1]:
                deps[sl[0]] = (sl[1], True)
        self._wait(q, deps)
        self.E[q].dma_start(out=out, in_=in_, **kw).then_inc(self.sem[sl[0]], 16)
        sl[1] += 16
        self._record((sl[0], sl[1]), reads, writes)

    def barrier(self):
        cur = {}
        for e in self.E:
            cur[("e", e)] = self.seq[e]
        for q in self.slots:
            for key, cnt in self.slots[q]:
                cur[key] = cnt
        for e in self.E:
            kn = self.known[e]
            for k, v in cur.items():
                if v == 0 or k == ("e", e):
                    continue
                if kn.get(k, 0) >= v:
                    continue
                self.E[e].wait_ge(self.sem[k], v)
                kn[k] = v


class Rot:
    def __init__(self, items):
        self.items = items
        self.i = 0

    def next(self):
        it = self.items[self.i]
        self.i = (self.i + 1) % len(self.items)
        return it


def host_constants():
    c = {}
    c["ident"] = np.eye(128, dtype=np.float32).astype(bf16)
    pos = np.arange(T, dtype=np.float32)
    inv_freq = (10000.0 ** (-np.arange(16, dtype=np.float32) / 16)).astype(np.float32)
    ang = (pos[:, None] * inv_freq[None, :]).astype(np.float32)
    cos = np.cos(ang).astype(np.float32).T
    sin = np.sin(ang).astype(np.float32).T
    c["ropeC"] = np.ascontiguousarray(np.concatenate([cos, cos], 0))
    c["ropeS"] = np.ascontiguousarray(np.concatenate([-sin, sin], 0))
    p = np.arange(128)[:, None]
    j = np.arange(512)[None, :]
    def addmask(valid):
        return np.where(valid, 0.0, -BIG).astype(np.float32).astype(bf16)
    c["cmk"] = addmask(np.stack([(p + 128 * d <= j) for d in range(4)], 1))
    c["wmk"] = addmask(np.stack([(j < 128 * d + p) for d in range(4)], 1))
    c["pmk"] = addmask(np.stack([(16 * p + 31 <= 512 * m + j) for m in range(5)], 1))
    slopes = 2.0 ** (-np.arange(1, 9, dtype=np.float64))
    t = np.arange(T)
    qaug = np.zeros((8, 5, T), np.float32)
    for h in range(8):
        s = slopes[h]
        qaug[h, 0] = s
        qaug[h, 1] = s
        qaug[h, 2] = -s * (t // 128) * 128
        qaug[h, 3] = -s * (t % 128)
        qaug[h, 4] = -s * ((t % 128) - 31)
    c["qaug"] = qaug.astype(bf16)
    ka = np.zeros((5, T), np.float32)
    ka[0] = (t // 128) * 128
    ka[1] = t % 128
    ka[2] = 1
    ka[3] = 1
    c["kaug"] = ka.astype(bf16)
    cc = np.arange(512)
    kc = np.zeros((5, 512), np.float32)
    kc[0] = (cc // 128) * 2048
    kc[1] = (cc % 128) * 16
    kc[2] = 1
    kc[4] = 1
    c["kaugc"] = kc.astype(bf16)
    for k in ("qaug", "kaug", "kaugc"):
        assert np.all(c[k].astype(np.float32) == {"qaug": qaug, "kaug": ka, "kaugc": kc}[k])
    E = np.zeros((128, 64, 128), np.float32)
    for kt in range(64):
        E[2 * kt, kt, :64] = BIG
        E[2 * kt + 1, kt, 64:] = BIG
    c["eexp"] = E.astype(bf16)
    M = np.zeros((512, 128), np.float32)
    for cidx in range(511):
        for s in (cidx, cidx + 1):
            M[cidx, s // 4] += 1
    c["impm"] = M.astype(bf16)
    keep = np.zeros((128, 256), np.float32)
    add = np.zeros((128, 256), np.float32)
    pp = np.arange(128)
    for col in range(256):
        r = col - 128
        if r < -1:
            keep[:, col] = 1
        elif r == -1:
            keep[:, col] = (pp >= 64)
            add[:, col] = 1e6 * (pp < 64)
        elif r == 0:
            add[:, col] = 1e6
        elif r == 1:
            add[:, col] = np.where(pp >= 64, 1e6, -1.0)
        else:
            add[:, col] = -1.0
    c["keept"] = keep
    c["addt"] = add
    return c


CONST_SPECS = {
    "ident": ([128, 128], BF16), "ropeC": ([32, T], F32), "ropeS": ([32, T], F32),
    "cmk": ([128, 4, 512], BF16), "wmk": ([128, 4, 512], BF16), "pmk": ([128, 5, 512], BF16),
    "qaug": ([8, 5, T], BF16), "kaug": ([5, T], BF16), "kaugc": ([5, 512], BF16),
    "eexp": ([128, 64, 128], BF16), "impm": ([512, 128], BF16),
    "keept": ([128, 256], F32), "addt": ([128, 256], F32),
}

IN_SPECS = {
    "x": [T, D], "p": [T, 256], "w_in": [D, 1720], "w_ck1": [2048, 256], "w_ck2": [256, 64],
    "pos_ck": [32, 64], "w_cv1": [2048, 256], "w_cv2": [256, 64], "pos_cv": [32, 64],
    "mla_q_norm": [256, 1], "w_uq": [256, 768], "mla_kv_norm": [128, 1], "w_ukv": [128, 1024],
    "w_out": [D, D], "ln1_g": [1, D], "ln1_b": [1, D], "w_up": [D, 2 * DFF], "w_down": [DFF, D],
    "ln2_g": [1, D], "ln2_b": [1, D], "w_ple_gate": [D, D], "w_ple": [256, D],
    "ln3_g": [1, D], "ln3_b": [1, D],
}

SCRATCH_SPECS = {
    "QN": ([512, T], BF16),
    "KX": ([4, 128, T], BF16),
    "VSW": ([T, 256], BF16),
    "G": ([24, T], F32),
    "KPE": ([32, T], BF16),
    "QM": ([8, 96, T], BF16),
    "KN": ([512, T], BF16),
    "VM": ([T, 512], BF16),
    "ON": ([512, T], BF16),
    "OM": ([512, T], BF16),
    "WUP": ([22, 128, 2, 8, 128], BF16),
}


def sb(nc, es, name, shape, dt):
    return es.enter_context(nc.sbuf_tensor(name, list(shape), dt))


def ps(nc, es, name, shape, dt):
    return es.enter_context(nc.psum_tensor(name, list(shape), dt))


def rot_tiles(nc, es, name, shape, dt, n, psum=False):
    items = []
    for i in range(n):
        t = (ps if psum else sb)(nc, es, "%s%d" % (name, i), shape, dt)
        items.append((t, Buf("%s%d" % (name, i))))
    return Rot(items)


class Ctx:
    pass


def build_program(stop_after=None, debug_scratch=False, stop_mid=None):
    nc = bass.Bass("TRN2", target_bir_lowering=False)
    c = Ctx()
    c.nc = nc
    c.din = {k: nc.dram_tensor(k, v, F32, kind="ExternalInput").ap() for k, v in IN_SPECS.items()}
    c.dc = {k: nc.dram_tensor("c_" + k, v[0], v[1], kind="ExternalInput").ap() for k, v in CONST_SPECS.items()}
    skind = "ExternalOutput" if debug_scratch else "Internal"
    c.ds = {k: nc.dram_tensor("s_" + k, v[0], v[1], kind=skind).ap() for k, v in SCRATCH_SPECS.items()}
    c.dsb = {k: Buf("s_" + k) for k in SCRATCH_SPECS}
    c.out = nc.dram_tensor("out", [T, D], F32, kind="ExternalOutput").ap()
    c.out_b = Buf("out")
    c.stop_mid = stop_mid
    if stop_mid == "compress":
        c.dbg = {"kcmp": nc.dram_tensor("dbg_kcmp", [69, 2, 512], BF16, kind="ExternalOutput").ap(),
                 "vcmp": nc.dram_tensor("dbg_vcmp", [128, 4, 2, 128], BF16, kind="ExternalOutput").ap()}
    with ExitStack() as es0:
        kb = KB(nc, es0)
        c.kb = kb
        c.ident = sb(nc, es0, "ident", [128, 128], BF16)
        c.ident_b = Buf("ident")
        kb.dma(c.ident[:], c.dc["ident"][:], writes=[c.ident_b])
        phases = [phase1, phase2, phase3]
        for ph in phases:
            ph(c)
            kb.barrier()
            if stop_after == ph.__name__:
                break
        kb.barrier()
    return nc


def phase1(c):
    nc, kb = c.nc, c.kb
    din, dc, ds, dsb = c.din, c.dc, c.ds, c.dsb
    with ExitStack() as es:
        wtr = sb(nc, es, "wtr", [128, 8, 1048], BF16)
        wtok = sb(nc, es, "wtok", [128, 8, 640], BF16)
        wkpA = sb(nc, es, "wkpA", [128, 8, 96], BF16)
        wkpB = sb(nc, es, "wkpB", [128, 8, 96], BF16)
        wuqA = sb(nc, es, "wuqA", [128, 2, 8, 96], BF16)
        wuqB = sb(nc, es, "wuqB", [128, 2, 8, 96], BF16)
        wkvK = sb(nc, es, "wkvK", [128, 8, 64], BF16)
        wkvV = sb(nc, es, "wkvV", [128, 8, 64], BF16)
        gq = sb(nc, es, "gq", [128, 2], F32)
        gkv = sb(nc, es, "gkv", [128, 1], F32)
        W = Buf("p1w")
        es_stg = ExitStack()
        stg = rot_tiles(nc, es_stg, "stg", [128, 1720], F32, 2)
        kb.op("pool", lambda e: e.memset(wkpA[:], 0.0), writes=[W])
        kb.op("pool", lambda e: e.memset(wkpB[:], 0.0), writes=[W])
        kb.op("pool", lambda e: e.memset(wuqB[:], 0.0), writes=[W])
        for kt in range(2):
            kb.dma(gq[:, kt:kt + 1], din["mla_q_norm"][kt * 128:(kt + 1) * 128, :], writes=[W])
        kb.dma(gkv[:], din["mla_kv_norm"][:, :], writes=[W])
        for ft in range(8):
            s, sbuf_ = stg.next()
            kb.dma(s[:], din["w_in"][ft * 128:(ft + 1) * 128, :], writes=[sbuf_])
            kb.op("dve", lambda e: e.tensor_copy(out=wtr[:, ft, 0:896], in_=s[:, 0:896]), reads=[sbuf_], writes=[W])
            kb.op("pool", lambda e: e.tensor_copy(out=wtr[:, ft, 896:1024], in_=s[:, 1024:1152]), reads=[sbuf_], writes=[W])
            kb.op("pool", lambda e: e.tensor_copy(out=wtr[:, ft, 1024:1048], in_=s[:, 1280:1304]), reads=[sbuf_], writes=[W])
            kb.op("dve", lambda e: e.tensor_copy(out=wtok[:, ft, 0:384], in_=s[:, 1304:1688]), reads=[sbuf_], writes=[W])
            kb.op("pool", lambda e: e.tensor_copy(out=wtok[:, ft, 384:512], in_=s[:, 896:1024]), reads=[sbuf_], writes=[W])
            kb.op("pool", lambda e: e.tensor_copy(out=wtok[:, ft, 512:640], in_=s[:, 1152:1280]), reads=[sbuf_], writes=[W])
            kb.op("dve", lambda e: e.tensor_copy(out=wkpA[:, ft, 64:96], in_=s[:, 1688:1720]), reads=[sbuf_], writes=[W])
            kb.op("dve", lambda e: e.tensor_copy(out=wkpB[:, ft, 64:80], in_=s[:, 1704:1720]), reads=[sbuf_], writes=[W])
            kb.op("dve", lambda e: e.tensor_copy(out=wkpB[:, ft, 80:96], in_=s[:, 1688:1704]), reads=[sbuf_], writes=[W])
        for kt in range(2):
            s, sbuf_ = stg.next()
            kb.dma(s[:, 0:768], din["w_uq"][kt * 128:(kt + 1) * 128, :], writes=[sbuf_])
            s3 = s[:, 0:768].rearrange("p (h c) -> p h c", c=96)
            kb.op("dve", lambda e: e.tensor_scalar(out=wuqA[:, kt, :, :], in0=s3, scalar1=gq[:, kt:kt + 1], scalar2=None,
                                                   op0=ALU.mult), reads=[sbuf_, W], writes=[W])
            kb.op("dve", lambda e: e.tensor_scalar(out=wuqB[:, kt, :, 64:80], in0=s3[:, :, 80:96], scalar1=gq[:, kt:kt + 1],
                                                   scalar2=None, op0=ALU.mult), reads=[sbuf_, W], writes=[W])
            kb.op("dve", lambda e: e.tensor_scalar(out=wuqB[:, kt, :, 80:96], in0=s3[:, :, 64:80], scalar1=gq[:, kt:kt + 1],
                                                   scalar2=None, op0=ALU.mult), reads=[sbuf_, W], writes=[W])
        s, sbuf_ = stg.next()
        kb.dma(s[:, 0:1024], din["w_ukv"][:, :], writes=[sbuf_])
        s3 = s[:, 0:1024].rearrange("p (h c) -> p h c", c=128)
        kb.op("dve", lambda e: e.tensor_scalar(out=wkvK[:, :, :], in0=s3[:, :, 0:64], scalar1=gkv[:, 0:1], scalar2=None,
                                               op0=ALU.mult), reads=[sbuf_, W], writes=[W])
        kb.op("dve", lambda e: e.tensor_scalar(out=wkvV[:, :, :], in0=s3[:, :, 64:128], scalar1=gkv[:, 0:1], scalar2=None,
                                               op0=ALU.mult), reads=[sbuf_, W], writes=[W])
        kb.barrier()
        es_stg.close()
        XF = rot_tiles(nc, es, "xf", [128, 4, 1024], F32, 2)
        XT = rot_tiles(nc, es, "xT", [128, 8, 512], BF16, 2)
        STT = rot_tiles(nc, es, "stT", [128, 8, 512], BF16, 2)
        GST = rot_tiles(nc, es, "gst", [24, 512], F32, 2)
        RC = rot_tiles(nc, es, "rc", [96, 2, 512], F32, 3)
        KST = rot_tiles(nc, es, "kst", [96, 512], BF16, 2)
        RT = rot_tiles(nc, es, "rt", [96, 2, 512], F32, 2)
        VST = rot_tiles(nc, es, "vst", [128, 4, 256], BF16, 2)
        SS = rot_tiles(nc, es, "ss", [128, 4], F32, 2)
        JK = rot_tiles(nc, es, "jk", [128, 256], F32, 2)
        CN = rot_tiles(nc, es, "cn", [128, 384], BF16, 2)
        CT = rot_tiles(nc, es, "cT", [128, 3, 512], BF16, 2)
        QST = rot_tiles(nc, es, "qst", [96, 8, 512], BF16, 2)
        KNST = rot_tiles(nc, es, "knst", [128, 4, 512], BF16, 2)
        VMST = rot_tiles(nc, es, "vmst", [128, 4, 512], BF16, 2)
        TP = rot_tiles(nc, es, "tp", [128, 1024], BF16, 2, psum=True)
        TPQ = rot_tiles(nc, es, "tpq", [128, 3, 128], BF16, 1, psum=True)
        PS = rot_tiles(nc, es, "ps", [128, 512], F32, 5, psum=True)
        ident, ident_b = c.ident, c.ident_b
        xv = din["x"]
        XB = rot_tiles(nc, es, "xbb", [128, 4, 1024], BF16, 2)
        LS = {}

        def L_load(ch):
            t0 = ch * 512
            xf, xf_b = XF.next()
            kb.dma(xf[:], xv[t0:t0 + 512, :].rearrange("(s p) d -> p s d", p=128), writes=[xf_b])
            rc, rc_b = RC.next()
            kb.dma(rc[64:96, 0, :], dc["ropeC"][:, t0:t0 + 512], writes=[rc_b])
            kb.dma(rc[64:96, 1, :], dc["ropeS"][:, t0:t0 + 512], writes=[rc_b])
            LS[ch] = {"xf": (xf, xf_b), "rc": (rc, rc_b)}

        def L_cast(ch):
            xf, xf_b = LS[ch]["xf"]
            xb, xb_b = XB.next()
            kb.op("dve", lambda e: e.tensor_copy(out=xb[:, 0:3, :], in_=xf[:, 0:3, :]), reads=[xf_b], writes=[xb_b])
            kb.op("pool", lambda e: e.tensor_copy(out=xb[:, 3:4, :], in_=xf[:, 3:4, :]), reads=[xf_b], writes=[xb_b])
            LS[ch]["xb"] = (xb, xb_b)

        def L_tr(ch):
            xb, xb_b = LS[ch]["xb"]
            xT, xT_b = XT.next()
            for f2 in range(4):
                tp, tp_b = TP.next()
                for fi in range(2):
                    ft = f2 * 2 + fi
                    for s_ in range(4):
                        last = (fi == 1 and s_ == 3)
                        kb.op("pe", lambda e: e.transpose(out=tp[:, fi * 512 + s_ * 128: fi * 512 + (s_ + 1) * 128],
                                                          in_=xb[:, s_, ft * 128:(ft + 1) * 128], identity=ident[:]),
                              reads=[xb_b, ident_b], writes=[tp_b], inc=last)
                dst = xT[:, 2 * f2:2 * f2 + 2, :].rearrange("p a t -> p (a t)")
                kb.op("act" if f2 % 2 else "dve",
                      (lambda e: e.activation(out=dst, in_=tp[:], func=AF.Copy)) if f2 % 2 else
                      (lambda e: e.tensor_copy(out=dst, in_=tp[:])), reads=[tp_b], writes=[xT_b])
            LS[ch]["xT"] = (xT, xT_b)

        L_load(0)
        L_cast(0)
        L_tr(0)
        if NCH > 1:
            L_load(1)
        for ch in range(NCH):
            t0 = ch * 512
            xT, xT_b = LS[ch]["xT"]
            rc, rc_b = LS[ch]["rc"]
            if ch + 2 < NCH:
                L_load(ch + 2)
            if ch + 1 < NCH:
                L_cast(ch + 1)
            stT, stT_b = STT.next()
            for ct in range(8):
                pt, pt_b = PS.next()
                for ft in range(8):
                    kb.op("pe", lambda e: e.matmul(pt[:, :], lhsT=wtr[:, ft, ct * 128:(ct + 1) * 128], rhs=xT[:, ft, :],
                                                   start=(ft == 0), stop=(ft == 7)),
                          reads=[W, xT_b], writes=[pt_b], inc=(ft == 7))
                if ct < 4:
                    kb.op("act", lambda e: e.activation(out=stT[:, ct, :], in_=pt[:, :], func=AF.Copy, scale=0.125),
                          reads=[pt_b], writes=[stT_b])
                else:
                    kb.op("dve", lambda e: e.tensor_copy(out=stT[:, ct, :], in_=pt[:, :]), reads=[pt_b], writes=[stT_b])
            kb.dma(ds["QN"].rearrange("(c p) t -> p c t", p=128)[:, :, t0:t0 + 512], stT[:, 0:4, :],
                   reads=[stT_b], writes=[dsb["QN"]])
            kb.dma(ds["KX"].rearrange("k p t -> p k t")[:, :, t0:t0 + 512], stT[:, 4:8, :],
                   reads=[stT_b], writes=[dsb["KX"]])
            pt, pt_b = PS.next()
            for ft in range(8):
                kb.op("pe", lambda e: e.matmul(pt[0:24, :], lhsT=wtr[:, ft, 1024:1048], rhs=xT[:, ft, :],
                                               start=(ft == 0), stop=(ft == 7)),
                      reads=[W, xT_b], writes=[pt_b], inc=(ft == 7))
            gst, gst_b = GST.next()
            kb.op("act", lambda e: e.activation(out=gst[:, :], in_=pt[0:24, :], func=AF.Sigmoid), reads=[pt_b], writes=[gst_b])
            kb.dma(ds["G"][:, t0:t0 + 512], gst[:, :], reads=[gst_b], writes=[dsb["G"]])
            pa, pa_b = PS.next()
            for ft in range(8):
                kb.op("pe", lambda e: e.matmul(pa[0:96, :], lhsT=wkpA[:, ft, :], rhs=xT[:, ft, :], start=(ft == 0), stop=(ft == 7)),
                      reads=[W, xT_b], writes=[pa_b], inc=(ft == 7))
            pb, pb_b = PS.next()
            for ft in range(8):
                kb.op("pe", lambda e: e.matmul(pb[0:96, :], lhsT=wkpB[:, ft, :], rhs=xT[:, ft, :], start=(ft == 0), stop=(ft == 7)),
                      reads=[W, xT_b], writes=[pb_b], inc=(ft == 7))
            rt, rt_b = RT.next()
            kst, kst_b = KST.next()
            kb.op("dve", lambda e: e.tensor_tensor(out=rt[64:96, 0, :], in0=pa[64:96, :], in1=rc[64:96, 0, :], op=ALU.mult),
                  reads=[pa_b, rc_b], writes=[rt_b])
            kb.op("dve", lambda e: e.tensor_tensor(out=rt[64:96, 1, :], in0=pb[64:96, :], in1=rc[64:96, 1, :], op=ALU.mult),
                  reads=[pb_b, rc_b], writes=[rt_b])
            kb.op("pool", lambda e: e.tensor_tensor(out=kst[64:96, :], in0=rt[64:96, 0, :], in1=rt[64:96, 1, :], op=ALU.add),
                  reads=[rt_b], writes=[kst_b])
            kb.dma(ds["KPE"][:, t0:t0 + 512], kst[64:96, :], reads=[kst_b], writes=[dsb["KPE"]])
            if ch + 1 < NCH:
                L_tr(ch + 1)
            vst, vst_b = VST.next()
            cT, cT_b = CT.next()
            for s_ in range(4):
                p1, p1_b = PS.next()
                for ft in range(8):
                    kb.op("pe", lambda e: e.matmul(p1[:, 0:384], lhsT=xT[:, ft, s_ * 128:(s_ + 1) * 128], rhs=wtok[:, ft, 0:384],
                                                   start=(ft == 0), stop=(ft == 7)),
                          reads=[W, xT_b], writes=[p1_b], inc=(ft == 7))
                p2, p2_b = PS.next()
                for ft in range(8):
                    kb.op("pe", lambda e: e.matmul(p2[:, 0:256], lhsT=xT[:, ft, s_ * 128:(s_ + 1) * 128], rhs=wtok[:, ft, 384:640],
                                                   start=(ft == 0), stop=(ft == 7)),
                          reads=[W, xT_b], writes=[p2_b], inc=(ft == 7))
                kb.op("dve", lambda e: e.tensor_copy(out=vst[:, s_, :], in_=p2[:, 0:256]), reads=[p2_b], writes=[vst_b])
                ss, ss_b = SS.next()
                jk, jk_b = JK.next()
                kb.op("act", lambda e: e.activation(out=jk[:, 0:256], in_=p1[:, 0:256], func=AF.Square, accum_out=ss[:, 0:1]),
                      reads=[p1_b], writes=[jk_b, ss_b])
                kb.op("act", lambda e: e.activation(out=jk[:, 0:128], in_=p1[:, 256:384], func=AF.Square, accum_out=ss[:, 1:2]),
                      reads=[p1_b], writes=[jk_b, ss_b])
                kb.op("dve", lambda e: e.tensor_scalar(out=ss[:, 0:1], in0=ss[:, 0:1], scalar1=1.0 / 256, scalar2=RMS_EPS,
                                                       op0=ALU.mult, op1=ALU.add), reads=[ss_b], writes=[ss_b])
                kb.op("dve", lambda e: e.tensor_scalar(out=ss[:, 1:2], in0=ss[:, 1:2], scalar1=1.0 / 128, scalar2=RMS_EPS,
                                                       op0=ALU.mult, op1=ALU.add), reads=[ss_b], writes=[ss_b])
                kb.op("act", lambda e: e.activation(out=ss[:, 2:4], in_=ss[:, 0:2], func=AF.Sqrt), reads=[ss_b], writes=[ss_b])
                kb.op("dve", lambda e: e.reciprocal(out=ss[:, 0:2], in_=ss[:, 2:4]), reads=[ss_b], writes=[ss_b])
                cn, cn_b = CN.next()
                kb.op("dve", lambda e: e.tensor_scalar(out=cn[:, 0:256], in0=p1[:, 0:256], scalar1=ss[:, 0:1], scalar2=None,
                                                       op0=ALU.mult), reads=[p1_b, ss_b], writes=[cn_b])
                kb.op("dve", lambda e: e.tensor_scalar(out=cn[:, 256:384], in0=p1[:, 256:384], scalar1=ss[:, 1:2], scalar2=None,
                                                       op0=ALU.mult), reads=[p1_b, ss_b], writes=[cn_b])
                tq, tq_b = TPQ.next()
                for i in range(3):
                    kb.op("pe", lambda e: e.transpose(out=tq[:, i, :], in_=cn[:, i * 128:(i + 1) * 128], identity=ident[:]),
                          reads=[cn_b, ident_b], writes=[tq_b], inc=(i == 2))
                kb.op("act", lambda e: e.activation(out=cT[:, :, s_ * 128:(s_ + 1) * 128], in_=tq[:, :, :], func=AF.Copy),
                      reads=[tq_b], writes=[cT_b])
            kb.dma(ds["VSW"][t0:t0 + 512, :].rearrange("(s p) c -> p s c", p=128), vst[:, :, :], reads=[vst_b], writes=[dsb["VSW"]])
            qst, qst_b = QST.next()
            for h in range(8):
                pa, pa_b = PS.next()
                for kt in range(2):
                    kb.op("pe", lambda e: e.matmul(pa[0:96, :], lhsT=wuqA[:, kt, h, :], rhs=cT[:, kt, :], start=(kt == 0), stop=(kt == 1)),
                          reads=[W, cT_b], writes=[pa_b], inc=(kt == 1))
                pb, pb_b = PS.next()
                for kt in range(2):
                    kb.op("pe", lambda e: e.matmul(pb[0:96, :], lhsT=wuqB[:, kt, h, :], rhs=cT[:, kt, :], start=(kt == 0), stop=(kt == 1)),
                          reads=[W, cT_b], writes=[pb_b], inc=(kt == 1))
                kb.op("act", lambda e: e.activation(out=qst[0:64, h, :], in_=pa[0:64, :], func=AF.Copy), reads=[pa_b], writes=[qst_b])
                rt, rt_b = RT.next()
                kb.op("dve", lambda e: e.tensor_tensor(out=rt[64:96, 0, :], in0=pa[64:96, :], in1=rc[64:96, 0, :], op=ALU.mult),
                      reads=[pa_b, rc_b], writes=[rt_b])
                kb.op("dve", lambda e: e.tensor_tensor(out=rt[64:96, 1, :], in0=pb[64:96, :], in1=rc[64:96, 1, :], op=ALU.mult),
                      reads=[pb_b, rc_b], writes=[rt_b])
                kb.op("pool", lambda e: e.tensor_tensor(out=qst[64:96, h, :], in0=rt[64:96, 0, :], in1=rt[64:96, 1, :], op=ALU.add),
                      reads=[rt_b], writes=[qst_b])
            kb.dma(ds["QM"].rearrange("h r t -> r h t")[:, :, t0:t0 + 512], qst[:, :, :], reads=[qst_b], writes=[dsb["QM"]])
            knst, knst_b = KNST.next()
            for hp in range(4):
                pt, pt_b = PS.next()
                kb.op("pe", lambda e: e.matmul(pt[:, :], lhsT=wkvK[:, 2 * hp:2 * hp + 2, :].rearrange("p a c -> p (a c)"),
                                               rhs=cT[:, 2, :], start=True, stop=True), reads=[W, cT_b], writes=[pt_b])
                kb.op("act" if hp % 2 else "dve",
                      (lambda e: e.activation(out=knst[:, hp, :], in_=pt[:, :], func=AF.Copy)) if hp % 2 else
                      (lambda e: e.tensor_copy(out=knst[:, hp, :], in_=pt[:, :])), reads=[pt_b], writes=[knst_b])
            kb.dma(ds["KN"].rearrange("(c p) t -> p c t", p=128)[:, :, t0:t0 + 512], knst[:, :, :], reads=[knst_b], writes=[dsb["KN"]])
            vmst, vmst_b = VMST.next()
            for s_ in range(4):
                pt, pt_b = PS.next()
                kb.op("pe", lambda e: e.matmul(pt[:, :], lhsT=cT[:, 2, s_ * 128:(s_ + 1) * 128],
                                               rhs=wkvV[:, :, :].rearrange("p a c -> p (a c)"), start=True, stop=True),
                      reads=[W, cT_b], writes=[pt_b])
                kb.op("act" if s_ % 2 else "dve",
                      (lambda e: e.activation(out=vmst[:, s_, :], in_=pt[:, :], func=AF.Copy)) if s_ % 2 else
                      (lambda e: e.tensor_copy(out=vmst[:, s_, :], in_=pt[:, :])), reads=[pt_b], writes=[vmst_b])
            kb.dma(ds["VM"][t0:t0 + 512, :].rearrange("(s p) c -> p s c", p=128), vmst[:, :, :], reads=[vmst_b], writes=[dsb["VM"]])


class TileD:
    __slots__ = ("qk", "scale", "pv", "epi", "lo", "hi")

    def __init__(self, qk, scale, pv, epi=None, lo=0, hi=512):
        self.qk = qk
        self.scale = scale
        self.pv = pv
        self.epi = epi
        self.lo, self.hi = lo, hi


class Stream:
    def __init__(self, c, R, L=2):
        self.c, self.R, self.L = c, R, L
        self.pending = []

    def push(self, t):
        kb, R = self.c.kb, self.R
        S, S_b = R.SPS.next()
        nq = len(t.qk)
        lo, hi = t.lo, t.hi
        for a, ent in enumerate(t.qk):
            lh, rh, rb = ent[0], ent[1], ent[2]
            clo, chi = (ent[3], ent[4]) if len(ent) > 3 else (lo, hi)
            kb.op("pe", lambda e: e.matmul(S[:, clo:chi], lhsT=lh, rhs=rh, start=(a == 0), stop=(a == nq - 1), skip_group_check=True),
                  reads=rb, writes=[S_b], inc=(a == nq - 1))
        P, P_b = R.PSB.next()
        kb.op("act", lambda e: e.activation(out=P[:, lo:hi], in_=S[:, lo:hi], func=AF.Exp, scale=t.scale), reads=[S_b], writes=[P_b])
        self.pending.append((t, P, P_b))
        if len(self.pending) > self.L:
            self._pop()

    def _pop(self):
        t, P, P_b = self.pending.pop(0)
        t.pv(P, P_b)
        if t.epi is not None:
            t.epi()

    def flush(self):
        while self.pending:
            self._pop()


def run_stream(c, R, tiles, L=2):
    st = c.stream
    for t in tiles:
        st.push(t)


def phase2(c):
    nc, kb = c.nc, c.kb
    din, dc, ds, dsb = c.din, c.dc, c.ds, c.dsb
    ident, ident_b = c.ident, c.ident_b
    with ExitStack() as esA:
        kcmp = sb(nc, esA, "kcmp", [69, 2, 512], BF16)
        kcmp_b = Buf("kcmp")
        vcmp = sb(nc, esA, "vcmp", [128, 4, 2, 128], BF16)
        vcmp_b = Buf("vcmp")
        CB = Buf("consts2")
        cmk = sb(nc, esA, "cmk", [128, 4, 512], BF16)
        wmk = sb(nc, esA, "wmk", [128, 4, 512], BF16)
        pmk = sb(nc, esA, "pmk", [128, 5, 512], BF16)
        eexp = sb(nc, esA, "eexp", [128, 64, 128], BF16)
        impm = sb(nc, esA, "impm", [128, 4, 128], BF16)
        keept = sb(nc, esA, "keept", [128, 256], F32)
        addt = sb(nc, esA, "addt", [128, 256], F32)
        kb.dma(cmk[:], dc["cmk"][:], writes=[CB])
        kb.dma(wmk[:], dc["wmk"][:], writes=[CB])
        kb.dma(pmk[:], dc["pmk"][:], writes=[CB])
        kb.dma(eexp[:], dc["eexp"][:], writes=[CB])
        kb.dma(impm[:], dc["impm"].rearrange("(ct p) b -> p ct b", p=128), writes=[CB])
        kb.dma(keept[:], dc["keept"][:], writes=[CB])
        kb.dma(addt[:], dc["addt"][:], writes=[CB])
        kb.op("pool", lambda e: e.memset(kcmp[:], 0.0), writes=[kcmp_b])
        kb.op("pool", lambda e: e.memset(vcmp[:], 1.0), writes=[vcmp_b])
        for g in range(2):
            kb.dma(kcmp[64:69, g, :], dc["kaugc"][:, :], writes=[kcmp_b])
        with ExitStack() as es:
            kcg = sb(nc, es, "kcg", [64, 4, T], BF16)
            kcg_b = Buf("kcg")
            for kind in range(2):
                for g in range(2):
                    kb.dma(kcg[:, kind * 2 + g, :], ds["KX"][kind, g * 64:(g + 1) * 64, :], reads=[dsb["KX"]], writes=[kcg_b])
            w1s = sb(nc, es, "w1s", [64, 32, 256], F32)
            w1s_b = Buf("w1s")
            w1 = [sb(nc, es, "w1_%d" % k, [64, 32, 256], BF16) for k in range(2)]
            w2s = sb(nc, es, "w2s", [128, 2, 64], F32)
            w2s_b = Buf("w2s")
            w2 = [sb(nc, es, "w2_%d" % k, [128, 2, 64], BF16) for k in range(2)]
            posS = sb(nc, es, "posS", [32, 64], F32)
            posS_b = Buf("posS")
            posB = sb(nc, es, "posB", [32, 64], BF16)
            posB_b = Buf("posB")
            posT = sb(nc, es, "posT", [64, 2, 32], BF16)
            b1 = sb(nc, es, "b1", [128, 4], F32)
            Wc = Buf("wc")
            tpp = ps(nc, es, "tpp", [64, 32], BF16)
            tpp_b = Buf("tpp")
            pb1 = ps(nc, es, "pb1", [128, 4], F32)
            pb1_b = Buf("pb1")
            PSC = rot_tiles(nc, es, "psc", [128, 512], F32, 4, psum=True)
            XG = rot_tiles(nc, es, "xg", [128, 512], F32, 2)
            X2 = rot_tiles(nc, es, "x2", [128, 512], F32, 2)
            GL = rot_tiles(nc, es, "gl", [128, 512], BF16, 4)
            for gl, gl_b in GL.items:
                kb.op("pool", lambda e: e.memset(gl[:], 0.0), writes=[gl_b])
            names = [("w_ck1", "w_ck2", "pos_ck"), ("w_cv1", "w_cv2", "pos_cv")]
            for kind, (n1, n2, npos) in enumerate(names):
                kb.dma(w1s[:], din[n1].rearrange("(l d) n -> d l n", d=64), writes=[w1s_b])
                kb.op("dve", lambda e: e.tensor_copy(out=w1[kind][:, 0:16, :], in_=w1s[:, 0:16, :]), reads=[w1s_b], writes=[Wc])
                kb.op("pool", lambda e: e.tensor_copy(out=w1[kind][:, 16:32, :], in_=w1s[:, 16:32, :]), reads=[w1s_b], writes=[Wc])
                kb.dma(w2s[:], din[n2].rearrange("(c p) n -> p c n", p=128), writes=[w2s_b])
                kb.op("dve", lambda e: e.tensor_copy(out=w2[kind][:], in_=w2s[:]), reads=[w2s_b], writes=[Wc])
                kb.dma(posS[:], din[npos][:, :], writes=[posS_b])
                kb.op("dve", lambda e: e.tensor_copy(out=posB[:], in_=posS[:]), reads=[posS_b], writes=[posB_b])
                kb.op("pe", lambda e: e.transpose(out=tpp[:, :], in_=posB[:, :], identity=ident[0:32, 0:32]),
                      reads=[posB_b, ident_b], writes=[tpp_b])
                kb.op("dve", lambda e: e.tensor_copy(out=posT[:, kind, :], in_=tpp[:, :]), reads=[tpp_b], writes=[Wc])
            for kind in range(2):
                for hc in range(2):
                    idx = kind * 2 + hc
                    for l in range(32):
                        kb.op("pe", lambda e: e.matmul(pb1[:, idx:idx + 1], lhsT=w1[kind][:, l, hc * 128:(hc + 1) * 128],
                                                       rhs=posT[:, kind, l:l + 1], start=(l == 0), stop=(l == 31)),
                              reads=[Wc], writes=[pb1_b], inc=(l == 31))
            kb.op("dve", lambda e: e.tensor_copy(out=b1[:], in_=pb1[:]), reads=[pb1_b], writes=[Wc])
            for kind in range(2):
                for g in range(2):
                    gls = []
                    for hc in range(2):
                        idx = kind * 2 + hc
                        pm, pm_b = PSC.next()
                        for l in range(32):
                            kb.op("pe", lambda e: e.matmul(pm[:, 0:511], lhsT=w1[kind][:, l, hc * 128:(hc + 1) * 128],
                                                           rhs=kcg[:, kind * 2 + g, l:l + 8161:16], start=(l == 0), stop=(l == 31)),
                                  reads=[Wc, kcg_b], writes=[pm_b], inc=(l == 31))
                        xg, xg_b = XG.next()
                        x2, x2_b = X2.next()
                        gl, gl_b = GL.next()
                        kb.op("dve", lambda e: e.tensor_scalar(out=xg[:, 0:511], in0=pm[:, 0:511], scalar1=b1[:, idx:idx + 1],
                                                               scalar2=None, op0=ALU.add), reads=[pm_b, Wc], writes=[xg_b])
                        kb.op("pool", lambda e: e.tensor_tensor(out=x2[:, 0:511], in0=xg[:, 0:511], in1=xg[:, 0:511], op=ALU.mult),
                              reads=[xg_b], writes=[x2_b])
                        kb.op("dve", lambda e: e.tensor_scalar(out=x2[:, 0:511], in0=x2[:, 0:511], scalar1=0.044715, scalar2=1.0,
                                                               op0=ALU.mult, op1=ALU.add), reads=[x2_b], writes=[x2_b])
                        kb.op("pool", lambda e: e.tensor_tensor(out=x2[:, 0:511], in0=x2[:, 0:511], in1=xg[:, 0:511], op=ALU.mult),
                              reads=[x2_b, xg_b], writes=[x2_b])
                        kb.op("act", lambda e: e.activation(out=x2[:, 0:511], in_=x2[:, 0:511], func=AF.Sigmoid, scale=1.5957691216),
                              reads=[x2_b], writes=[x2_b])
                        kb.op("dve", lambda e: e.tensor_tensor(out=gl[:, 0:511], in0=x2[:, 0:511], in1=xg[:, 0:511], op=ALU.mult),
                              reads=[x2_b, xg_b], writes=[gl_b])
                        gls.append((gl, gl_b))
                    if kind == 0:
                        pk, pk_b = PSC.next()
                        for hc in range(2):
                            kb.op("pe", lambda e: e.matmul(pk[0:64, 0:511], lhsT=w2[0][:, hc, :], rhs=gls[hc][0][:, 0:511],
                                                           start=(hc == 0), stop=(hc == 1)),
                                  reads=[Wc, gls[hc][1]], writes=[pk_b], inc=(hc == 1))
                        kb.op("act", lambda e: e.activation(out=kcmp[0:64, g, 0:511], in_=pk[0:64, 0:511], func=AF.Copy),
                              reads=[pk_b], writes=[kcmp_b])
                    else:
                        for ct in range(4):
                            pv, pv_b = PSC.next()
                            for hc in range(2):
                                kb.op("pe", lambda e: e.matmul(pv[:, 0:64], lhsT=gls[hc][0][:, ct * 128:(ct + 1) * 128], rhs=w2[1][:, hc, :],
                                                               start=(hc == 0), stop=(hc == 1)),
                                      reads=[Wc, gls[hc][1]], writes=[pv_b], inc=(hc == 1))
                            kb.op("act", lambda e: e.activation(out=vcmp[:, ct, g, 0:64], in_=pv[:, 0:64], func=AF.Copy),
                                  reads=[pv_b], writes=[vcmp_b])
        kb.barrier()
        if getattr(c, "stop_mid", None) == "compress":
            dbg = c.dbg
            kb.dma(dbg["kcmp"][:], kcmp[:], reads=[kcmp_b])
            kb.dma(dbg["vcmp"][:], vcmp[:], reads=[vcmp_b])
            return
        with ExitStack() as es:
            R = Ctx()
            R.SPS = rot_tiles(nc, es, "sps", [128, 512], F32, 3, psum=True)
            R.OPS = rot_tiles(nc, es, "ops", [128, 512], F32, 2, psum=True)
            R.UPS = rot_tiles(nc, es, "ups", [128, 4, 128], F32, 2, psum=True)
            R.TPN = rot_tiles(nc, es, "tpn", [128, 4, 128], BF16, 1, psum=True)
            R.PSB = rot_tiles(nc, es, "psb", [128, 512], BF16, 4)
            KA = rot_tiles(nc, es, "ka", [96, T], BF16, 2).items
            VA = rot_tiles(nc, es, "va", [128, 64, 128], BF16, 2).items
            for va, va_b in VA:
                kb.op("pool", lambda e: e.memset(va[:], 1.0), writes=[va_b])
            QA = [[(sb(nc, es, "qa%d%d" % (pr, hi), [69, 512], BF16), Buf("qa")) for hi in range(4)] for pr in range(2)]
            OC = [[(sb(nc, es, "oc%d%d" % (pr, hi), [64, 512], F32), Buf("oc")) for hi in range(4)] for pr in range(2)]
            QMT = rot_tiles(nc, es, "qmt", [96, 512], BF16, 3)
            GB = rot_tiles(nc, es, "gb", [64, 3, 512], F32, 4)
            NMT = rot_tiles(nc, es, "nmt", [128, 512], BF16, 2)
            IMP = rot_tiles(nc, es, "imp", [128, 4, 128], F32, 2)
            DEN = rot_tiles(nc, es, "den", [64, 512], F32, 2)
            REC = rot_tiles(nc, es, "rec", [64, 512], F32, 2)
            RG = rot_tiles(nc, es, "rg", [64, 512], F32, 2)
            T1 = rot_tiles(nc, es, "t1", [64, 512], F32, 4)
            ACC = rot_tiles(nc, es, "acc", [64, 512], F32, 2)
            OST = rot_tiles(nc, es, "ost", [64, 512], BF16, 3)
            W1 = rot_tiles(nc, es, "w1k", [128, 128], F32, 2)
            W2 = rot_tiles(nc, es, "w2k", [128, 128], F32, 2)
            M8 = rot_tiles(nc, es, "m8", [128, 16], F32, 2)
            NMS = rot_tiles(nc, es, "nms", [128, 128], BF16, 4)
            RS = rot_tiles(nc, es, "rs", [128, 8], F32, 2)
            Gt = ds["G"].tensor
            c.stream = Stream(c, R)

            def norm_rec(O, O_b):
                den, den_b = DEN.next()
                rec, rec_b = REC.next()
                kb.op("dve", lambda e: e.tensor_scalar_max(out=den[:, :], in0=O[64:128, :], scalar1=1e-30), reads=[O_b], writes=[den_b])
                kb.op("dve", lambda e: e.reciprocal(out=rec[:, :], in_=den[:, :]), reads=[den_b], writes=[rec_b])
                return rec, rec_b

            for g in range(2):
                (ks, ks_b), (kw, kw_b) = KA
                (vs, vs_b), (vw, vw_b) = VA
                kb.dma(ks[0:64, :], ds["KX"][2, g * 64:(g + 1) * 64, :], reads=[dsb["KX"]], writes=[ks_b])
                kb.dma(ks[64:69, :], dc["kaug"][:, :], writes=[ks_b])
                kb.dma(kw[0:64, :], ds["KX"][3, g * 64:(g + 1) * 64, :], reads=[dsb["KX"]], writes=[kw_b])
                kb.dma(kw[64:69, :], dc["kaug"][:, :], writes=[kw_b])
                for half in range(2):
                    hs = slice(half * 32, (half + 1) * 32)
                    rs_ = slice(half * 4096, (half + 1) * 4096)
                    kb.dma(vs[:, hs, 0:64], ds["VSW"][rs_, g * 64:(g + 1) * 64].rearrange("(kt p) d -> p kt d", p=128),
                           reads=[dsb["VSW"]], writes=[vs_b])
                    kb.dma(vw[:, hs, 0:64], ds["VSW"][rs_, 128 + g * 64:128 + (g + 1) * 64].rearrange("(kt p) d -> p kt d", p=128),
                           reads=[dsb["VSW"]], writes=[vw_b])
                CS = {}

                def cmp_begin(qc):
                    CS[qc] = {"imp": IMP.next(), "nmt": NMT.next(), "nms": []}

                def cmp_head(qc, hi, g=g):
                    t0 = qc * 512
                    pr = qc % 2
                    h = 4 * g + hi
                    imp, imp_b = CS[qc]["imp"]
                    qa, qa_b = QA[pr][hi]
                    kb.dma(qa[0:64, :], ds["QN"][h * 64:(h + 1) * 64, t0:t0 + 512], reads=[dsb["QN"]], writes=[qa_b])
                    kb.dma(qa[64:69, :], dc["qaug"][h, :, t0:t0 + 512], writes=[qa_b])
                    O, O_b = R.OPS.next()
                    U, U_b = R.UPS.next()
                    nct = qc // 4 + 1
                    for ct in range(nct):
                        m = qc - 4 * ct
                        qk = [(kcmp[0:69, g, ct * 128:(ct + 1) * 128], qa[0:69, :], [kcmp_b, qa_b])]
                        if m <= 4:
                            qk.append((ident[:, :], pmk[:, m, :], [ident_b, CB]))

                        def pv(P, P_b, ct=ct, nct=nct, O=O, O_b=O_b, U=U, U_b=U_b):
                            kb.op("pe", lambda e: e.matmul(O[:, :], lhsT=vcmp[:, ct, g, :], rhs=P[:, :], start=(ct == 0), stop=(ct == nct - 1)),
                                  reads=[vcmp_b, P_b], writes=[O_b], inc=True)
                            for s_ in range(4):
                                kb.op("pe", lambda e: e.matmul(U[:, s_, :], lhsT=P[:, s_ * 128:(s_ + 1) * 128], rhs=impm[:, ct, :],
                                                               start=(ct == 0), stop=(ct == nct - 1)),
                                      reads=[CB, P_b], writes=[U_b], inc=(s_ == 3))

                        def epi(hi=hi, pr=pr, O=O, O_b=O_b, U=U, U_b=U_b, imp=imp, imp_b=imp_b):
                            rec, rec_b = norm_rec(O, O_b)
                            oc, oc_b = OC[pr][hi]
                            kb.op("dve", lambda e: e.tensor_tensor(out=oc[:, :], in0=O[0:64, :], in1=rec[:, :], op=ALU.mult),
                                  reads=[O_b, rec_b], writes=[oc_b])
                            rs, rs_b = RS.next()
                            kb.op("dve", lambda e: e.tensor_reduce(out=rs[:, 0:4], in_=U[:, :, :], axis=AX.X, op=ALU.add),
                                  reads=[U_b], writes=[rs_b])
                            kb.op("dve", lambda e: e.tensor_scalar(out=rs[:, 0:4], in0=rs[:, 0:4], scalar1=1e-30, scalar2=0.5,
                                                                   op0=ALU.max, op1=ALU.mult), reads=[rs_b], writes=[rs_b])
                            kb.op("dve", lambda e: e.reciprocal(out=rs[:, 4:8], in_=rs[:, 0:4]), reads=[rs_b], writes=[rs_b])
                            for s_ in range(4):
                                if hi == 0:
                                    kb.op("dve", lambda e: e.tensor_scalar(out=imp[:, s_, :], in0=U[:, s_, :], scalar1=rs[:, 4 + s_:5 + s_],
                                                                           scalar2=None, op0=ALU.mult), reads=[U_b, rs_b], writes=[imp_b])
                                else:
                                    kb.op("dve", lambda e: e.scalar_tensor_tensor(out=imp[:, s_, :], in0=U[:, s_, :], scalar=rs[:, 4 + s_:5 + s_],
                                                                                  in1=imp[:, s_, :], op0=ALU.mult, op1=ALU.add),
                                          reads=[U_b, rs_b, imp_b], writes=[imp_b])

                        c.stream.push(TileD(qk, 1.0, pv, epi if ct == nct - 1 else None))

                def topk_dve(qc):
                    imp, imp_b = CS[qc]["imp"]
                    for s_ in range(4):
                        qt = 4 * qc + s_
                        lo = 128 - 2 * qt
                        w, w_b = W1.next()
                        w2, w2_b = W2.next()
                        m8, m8_b = M8.next()
                        nms, nms_b = NMS.next()
                        kb.op("dve", lambda e: e.tensor_tensor(out=w[:, :], in0=imp[:, s_, :], in1=keept[:, lo:lo + 128], op=ALU.mult),
                              reads=[imp_b, CB], writes=[w_b])
                        kb.op("dve", lambda e: e.tensor_tensor(out=w[:, :], in0=w[:, :], in1=addt[:, lo:lo + 128], op=ALU.add),
                              reads=[w_b, CB], writes=[w_b])
                        kb.op("dve", lambda e: e.memset(w[:, 0:1], 1e6), reads=[w_b], writes=[w_b])
                        kb.op("dve", lambda e: e.max(out=m8[:, 0:8], in_=w[:, :]), reads=[w_b], writes=[m8_b])
                        kb.op("dve", lambda e: e.match_replace(out=w2[:, :], in_to_replace=m8[:, 0:8], in_values=w[:, :], imm_value=-2.0),
                              reads=[w_b, m8_b], writes=[w2_b])
                        kb.op("dve", lambda e: e.max(out=m8[:, 8:16], in_=w2[:, :]), reads=[w2_b, m8_b], writes=[m8_b])
                        kb.op("dve", lambda e: e.tensor_scalar(out=nms[:, :], in0=w[:, :], scalar1=m8[:, 15:16], scalar2=-1.0,
                                                               op0=ALU.is_ge, op1=ALU.add), reads=[w_b, m8_b], writes=[nms_b])
                        CS[qc]["nms"].append((nms, nms_b))

                def topk_pe(qc):
                    nmt, nmt_b = CS[qc]["nmt"]
                    tpn, tpn_b = R.TPN.next()
                    for s_ in range(4):
                        nms, nms_b = CS[qc]["nms"][s_]
                        kb.op("pe", lambda e: e.transpose(out=tpn[:, s_, :], in_=nms[:, :], identity=ident[:, :]),
                              reads=[nms_b, ident_b], writes=[tpn_b])
                    kb.op("act", lambda e: e.activation(out=nmt[:, :], in_=tpn[:, :, :].rearrange("p a b -> p (a b)"), func=AF.Copy),
                          reads=[tpn_b], writes=[nmt_b])

                def sw_head(qc, hi, g=g):
                    t0 = qc * 512
                    pr = qc % 2
                    h = 4 * g + hi
                    nmt, nmt_b = CS[qc]["nmt"]
                    qa, qa_b = QA[pr][hi]
                    gb, gb_b = GB.next()
                    kb.dma(gb[:, :, :], bass.AP(Gt, h * 3 * T + t0, [[0, 64], [T, 3], [1, 512]]), reads=[dsb["G"]], writes=[gb_b])
                    st = {}
                    O, O_b = R.OPS.next()
                    nk = 4 * qc + 4
                    k0 = max(0, int((t0 - 127 - ALIBI_CUT * (2.0 ** (h + 1))) // 128) + 1)
                    for kt in range(k0, nk):
                        d = kt - 4 * qc
                        lo = 128 * d if d > 0 else 0
                        qk = [(ks[0:69, kt * 128:(kt + 1) * 128], qa[0:69, lo:512], [ks_b, qa_b]),
                              (eexp[:, kt, :], nmt[:, lo:512], [CB, nmt_b])]
                        if d >= 0:
                            qk.append((ident[:, :], cmk[:, d, lo:lo + 128], [ident_b, CB], lo, lo + 128))

                        def pv(P, P_b, kt=kt, nk=nk, k0=k0, O=O, O_b=O_b, lo=lo):
                            kb.op("pe", lambda e: e.matmul(O[:, lo:512], lhsT=vs[:, kt, :], rhs=P[:, lo:512], start=(kt == k0), stop=(kt == nk - 1),
                                                           skip_group_check=True),
                                  reads=[vs_b, P_b], writes=[O_b], inc=True)

                        def epi(O=O, O_b=O_b, gb=gb, gb_b=gb_b, st=st):
                            rec, rec_b = norm_rec(O, O_b)
                            rg, rg_b = RG.next()
                            t1, t1_b = T1.next()
                            kb.op("pool", lambda e: e.tensor_tensor(out=rg[:, :], in0=rec[:, :], in1=gb[:, 1, :], op=ALU.mult),
                                  reads=[rec_b, gb_b], writes=[rg_b])
                            kb.op("dve", lambda e: e.tensor_tensor(out=t1[:, :], in0=O[0:64, :], in1=rg[:, :], op=ALU.mult),
                                  reads=[O_b, rg_b], writes=[t1_b])
                            st["slc"] = (t1, t1_b)

                        c.stream.push(TileD(qk, 1.0, pv, epi if kt == nk - 1 else None, lo=lo, hi=512))
                    O, O_b = R.OPS.next()
                    wl = []
                    for d in range(4):
                        wl.append((4 * qc + d, cmk[:, d, 128 * d:128 * d + 128], 128 * d, 512, 128 * d))
                    for d in range(4):
                        kt = 4 * qc - 4 + d
                        if kt >= 0:
                            wl.append((kt, wmk[:, d, 128 * d:128 * d + 128], 0, 128 * d + 128, 128 * d))
                    nw = len(wl)
                    for wi, (kt, mk, wlo, whi, mlo) in enumerate(wl):
                        qk = [(kw[0:69, kt * 128:(kt + 1) * 128], qa[0:69, wlo:whi], [kw_b, qa_b]),
                              (ident[:, :], mk, [ident_b, CB], mlo, mlo + 128)]

                        def pv(P, P_b, kt=kt, wi=wi, nw=nw, O=O, O_b=O_b, wlo=wlo, whi=whi):
                            kb.op("pe", lambda e: e.matmul(O[:, wlo:whi], lhsT=vw[:, kt, :], rhs=P[:, wlo:whi], start=(wi == 0), stop=(wi == nw - 1),
                                                           skip_group_check=True),
                                  reads=[vw_b, P_b], writes=[O_b], inc=True)

                        def epi(h=h, hi=hi, pr=pr, t0=t0, O=O, O_b=O_b, gb=gb, gb_b=gb_b, st=st):
                            rec, rec_b = norm_rec(O, O_b)
                            rg, rg_b = RG.next()
                            t2, t2_b = T1.next()
                            kb.op("pool", lambda e: e.tensor_tensor(out=rg[:, :], in0=rec[:, :], in1=gb[:, 2, :], op=ALU.mult),
                                  reads=[rec_b, gb_b], writes=[rg_b])
                            kb.op("dve", lambda e: e.tensor_tensor(out=t2[:, :], in0=O[0:64, :], in1=rg[:, :], op=ALU.mult),
                                  reads=[O_b, rg_b], writes=[t2_b])
                            t1, t1_b = st["slc"]
                            oc, oc_b = OC[pr][hi]
                            acc, acc_b = ACC.next()
                            ost, ost_b = OST.next()
                            kb.op("pool", lambda e: e.tensor_tensor(out=acc[:, :], in0=oc[:, :], in1=gb[:, 0, :], op=ALU.mult),
                                  reads=[oc_b, gb_b], writes=[acc_b])
                            kb.op("pool", lambda e: e.tensor_tensor(out=acc[:, :], in0=acc[:, :], in1=t1[:, :], op=ALU.add),
                                  reads=[acc_b, t1_b], writes=[acc_b])
                            kb.op("pool", lambda e: e.tensor_tensor(out=ost[:, :], in0=acc[:, :], in1=t2[:, :], op=ALU.add),
                                  reads=[acc_b, t2_b], writes=[ost_b])
                            kb.dma(ds["ON"][h * 64:(h + 1) * 64, t0:t0 + 512], ost[:, :], reads=[ost_b], writes=[dsb["ON"]], q="pool")

                        c.stream.push(TileD(qk, 1.0, pv, epi if wi == nw - 1 else None, lo=wlo, hi=whi))

                cmp_begin(0)
                for hi in range(4):
                    cmp_head(0, hi)
                c.stream.flush()
                topk_dve(0)
                topk_pe(0)
                for qc in range(NCH):
                    nxt = qc + 1 < NCH
                    if nxt:
                        cmp_begin(qc + 1)
                        cmp_head(qc + 1, 0)
                        cmp_head(qc + 1, 1)
                    sw_head(qc, 0)
                    if nxt:
                        cmp_head(qc + 1, 2)
                        cmp_head(qc + 1, 3)
                    sw_head(qc, 1)
                    sw_head(qc, 2)
                    if nxt:
                        topk_dve(qc + 1)
                    sw_head(qc, 3)
                    c.stream.flush()
                    if nxt:
                        topk_pe(qc + 1)
                    CS.pop(qc)
            c.stream.flush()
            if getattr(c, "stop_mid", None) == "nsa":
                return
            wst = sb(nc, es, "wst", [128, 1408], F32)
            wst_b = Buf("wst")
            wsb = sb(nc, es, "wsb", [128, 1408], BF16)
            wsb_b = Buf("wsb")
            wpieces = [(ft, a, hf) for ft in range(8) for a in range(2) for hf in range(2)]

            def wup_piece(ft, a, hf):
                c0 = a * DFF + hf * 1408
                kb.dma(wst[:, :], din["w_up"][ft * 128:(ft + 1) * 128, c0:c0 + 1408], writes=[wst_b])
                kb.op("pool", lambda e: e.tensor_copy(out=wsb[:, :], in_=wst[:, :]), reads=[wst_b], writes=[wsb_b])
                kb.dma(ds["WUP"][hf * 11:(hf + 1) * 11, :, a, ft, :].rearrange("j p c -> p j c"),
                       wsb[:, :].rearrange("p (j c) -> p j c", c=128), reads=[wsb_b], writes=[dsb["WUP"]], q="pool")

            for h in range(8):
                ka, ka_b = KA[h % 2]
                va, va_b = VA[h % 2]
                kb.dma(ka[0:64, :], ds["KN"][h * 64:(h + 1) * 64, :], reads=[dsb["KN"]], writes=[ka_b])
                kb.dma(ka[64:96, :], ds["KPE"][:, :], reads=[dsb["KPE"]], writes=[ka_b])
                for half in range(2):
                    hs = slice(half * 32, (half + 1) * 32)
                    rs_ = slice(half * 4096, (half + 1) * 4096)
                    kb.dma(va[:, hs, 0:64], ds["VM"][rs_, h * 64:(h + 1) * 64].rearrange("(kt p) d -> p kt d", p=128),
                           reads=[dsb["VM"]], writes=[va_b])
                qms = {}

                def qload(qc, h=h, qms=qms):
                    qm, qm_b = QMT.next()
                    kb.dma(qm[:, :], ds["QM"][h, :, qc * 512:(qc + 1) * 512], reads=[dsb["QM"]], writes=[qm_b])
                    qms[qc] = (qm, qm_b)

                qload(0)
                for qc in range(NCH):
                    t0 = qc * 512
                    if qc + 1 < NCH:
                        qload(qc + 1)
                    qm, qm_b = qms.pop(qc)
                    O, O_b = R.OPS.next()
                    nk = 4 * qc + 4
                    for kt in range(nk):
                        d = kt - 4 * qc
                        lo = 128 * d if d > 0 else 0
                        qk = [(ka[0:96, kt * 128:(kt + 1) * 128], qm[0:96, lo:512], [ka_b, qm_b])]
                        if d >= 0:
                            qk.append((ident[:, :], cmk[:, d, lo:lo + 128], [ident_b, CB], lo, lo + 128))

                        def pv(P, P_b, kt=kt, nk=nk, O=O, O_b=O_b, va=va, va_b=va_b, lo=lo):
                            kb.op("pe", lambda e: e.matmul(O[:, lo:512], lhsT=va[:, kt, :], rhs=P[:, lo:512], start=(kt == 0), stop=(kt == nk - 1),
                                                           skip_group_check=True),
                                  reads=[va_b, P_b], writes=[O_b], inc=True)

                        def epi(h=h, t0=t0, O=O, O_b=O_b):
                            rec, rec_b = norm_rec(O, O_b)
                            ost, ost_b = OST.next()
                            kb.op("dve", lambda e: e.tensor_tensor(out=ost[:, :], in0=O[0:64, :], in1=rec[:, :], op=ALU.mult),
                                  reads=[O_b, rec_b], writes=[ost_b])
                            kb.dma(ds["OM"][h * 64:(h + 1) * 64, t0:t0 + 512], ost[:, :], reads=[ost_b], writes=[dsb["OM"]], q="pool")

                        c.stream.push(TileD(qk, MLA_SCALE, pv, epi if kt == nk - 1 else None, lo=lo, hi=512))
                    if wpieces:
                        wup_piece(*wpieces.pop(0))
            c.stream.flush()

CH3 = 256
NCH3 = T // CH3


def phase3(c):
    nc, kb = c.nc, c.kb
    din, dc, ds, dsb = c.din, c.dc, c.ds, c.dsb
    ident, ident_b = c.ident, c.ident_b
    with ExitStack() as es:
        W = Buf("p3w")
        wo = sb(nc, es, "wo", [128, 8, 1024], BF16)
        wpg = sb(nc, es, "wpg", [128, 8, 1024], BF16)
        wpl = sb(nc, es, "wpl", [128, 2, 1024], BF16)
        wd = sb(nc, es, "wd", [128, 22, 1024], BF16)
        lnv = sb(nc, es, "lnv", [128, 6, 1024], F32)
        for i, nm in enumerate(("ln1_g", "ln1_b", "ln2_g", "ln2_b", "ln3_g", "ln3_b")):
            kb.dma(lnv[:, i, :], din[nm][0:1, :].partition_broadcast(128), writes=[W])
        with ExitStack() as es1:
            STG = rot_tiles(nc, es1, "stg3", [128, 1024], F32, 4)
            k = 0
            for name, dst in (("w_out", wo), ("w_ple_gate", wpg)):
                for ft in range(8):
                    s_, s_b = STG.next()
                    kb.dma(s_[:, 0:1024], din[name][ft * 128:(ft + 1) * 128, :], writes=[s_b])
                    kb.op("dve" if k % 2 else "pool", lambda e: e.tensor_copy(out=dst[:, ft, :], in_=s_[:, 0:1024]), reads=[s_b], writes=[W])
                    k += 1
            for kt in range(2):
                s_, s_b = STG.next()
                kb.dma(s_[:, 0:1024], din["w_ple"][kt * 128:(kt + 1) * 128, :], writes=[s_b])
                kb.op("dve", lambda e: e.tensor_copy(out=wpl[:, kt, :], in_=s_[:, 0:1024]), reads=[s_b], writes=[W])
            for j in range(22):
                s_, s_b = STG.next()
                kb.dma(s_[:, 0:1024], din["w_down"][j * 128:(j + 1) * 128, :], writes=[s_b])
                kb.op("dve" if j % 2 else "pool", lambda e: e.tensor_copy(out=wd[:, j, :], in_=s_[:, 0:1024]), reads=[s_b], writes=[W])
        kb.barrier()
        XF = rot_tiles(nc, es, "xf3", [128, 1024], F32, 1)
        OT = rot_tiles(nc, es, "oT3", [128, 8, CH3], BF16, 1)
        PF = rot_tiles(nc, es, "pf3", [128, 2, 256], F32, 1)
        PB = rot_tiles(nc, es, "pb3", [128, 2, 256], BF16, 3)
        PT = rot_tiles(nc, es, "pT3", [128, 2, CH3], BF16, 1)
        Y = rot_tiles(nc, es, "y3", [128, 1024], F32, 2)
        Z = rot_tiles(nc, es, "z3", [128, 1024], F32, 1)
        X1 = rot_tiles(nc, es, "x1r", [128, 2, 1024], F32, 2)
        X2 = rot_tiles(nc, es, "x2r", [128, 2, 1024], F32, 2)
        XBF = rot_tiles(nc, es, "xbf3", [128, 1024], BF16, 2)
        X1T = rot_tiles(nc, es, "x1T", [128, 8, CH3], BF16, 2)
        X2T = rot_tiles(nc, es, "x2T", [128, 8, CH3], BF16, 2)
        AT = rot_tiles(nc, es, "aT", [128, 22, CH3], BF16, 1)
        WU = rot_tiles(nc, es, "wu", [128, 2, 8, 128], BF16, 3)
        SG = rot_tiles(nc, es, "sg", [128, CH3], F32, 2)
        ST = rot_tiles(nc, es, "st3", [128, 16], F32, 6)
        PA = rot_tiles(nc, es, "pa3", [128, 512], F32, 3, psum=True)
        PG = rot_tiles(nc, es, "pg3", [128, 2, CH3], F32, 4, psum=True)
        TP = rot_tiles(nc, es, "tp3", [128, 8, 128], BF16, 1, psum=True)

        epsv = sb(nc, es, "epsv", [128, 2], F32)
        kb.op("pool", lambda e: e.memset(epsv[:, 0:1], LN_EPS), writes=[W])
        kb.op("pool", lambda e: e.memset(epsv[:, 1:2], 4.0 * LN_EPS), writes=[W])

        def ln_pair_steps(items, st, st_b, gi, eps_col=0):
            def s1():
                for k, it_ in enumerate(items):
                    j_ap, j_b = it_.get("junk") or (it_["out_ap"], it_["out_b"])
                    y, y_b = it_["y"], it_["y_b"]
                    kb.op("act", lambda e: e.activation(out=j_ap, in_=y[:, :], func=AF.Square, accum_out=st[:, 8 * k + 2:8 * k + 3]),
                          reads=[y_b], writes=[j_b, st_b])

            def s2():
                for k in range(len(items)):
                    b = 8 * k
                    kb.op("dve", lambda e: e.tensor_scalar(out=st[:, b + 3:b + 4], in0=st[:, b:b + 1], scalar1=st[:, b + 1:b + 2], scalar2=1.0 / D,
                                                           op0=ALU.add, op1=ALU.mult), reads=[st_b], writes=[st_b])
                    kb.op("dve", lambda e: e.scalar_tensor_tensor(out=st[:, b + 4:b + 5], in0=st[:, b + 3:b + 4], scalar=-1.0, in1=st[:, b + 3:b + 4],
                                                                  op0=ALU.mult, op1=ALU.mult), reads=[st_b], writes=[st_b])
                    kb.op("dve", lambda e: e.scalar_tensor_tensor(out=st[:, b + 5:b + 6], in0=st[:, b + 2:b + 3], scalar=1.0 / D, in1=st[:, b + 4:b + 5],
                                                                  op0=ALU.mult, op1=ALU.add), reads=[st_b], writes=[st_b])

            def s3():
                n = len(items)
                src = st[:, 0:8 * n].rearrange("p (k c) -> p k c", c=8)[:, :, 5:6]
                dst = st[:, 0:8 * n].rearrange("p (k c) -> p k c", c=8)[:, :, 6:7]
                kb.op("act", lambda e: e.activation(out=dst, in_=src, func=AF.Sqrt, bias=epsv[:, eps_col:eps_col + 1]), reads=[st_b, W], writes=[st_b])
                src2 = st[:, 0:8 * n].rearrange("p (k c) -> p k c", c=8)[:, :, 6:7]
                dst2 = st[:, 0:8 * n].rearrange("p (k c) -> p k c", c=8)[:, :, 7:8]
                kb.op("dve", lambda e: e.reciprocal(out=dst2, in_=src2), reads=[st_b], writes=[st_b])

            def s4(k):
                def f():
                    it_ = items[k]
                    b = 8 * k
                    y, y_b = it_["y"], it_["y_b"]
                    z, z_b = Z.next()
                    kb.op("dve", lambda e: e.scalar_tensor_tensor(out=z[:, :], in0=y[:, :], scalar=st[:, b + 3:b + 4], in1=lnv[:, gi, :],
                                                                  op0=ALU.subtract, op1=ALU.mult), reads=[y_b, st_b, W], writes=[z_b])
                    kb.op("dve", lambda e: e.scalar_tensor_tensor(out=it_["out_ap"], in0=z[:, :], scalar=st[:, b + 7:b + 8], in1=lnv[:, gi + 1, :],
                                                                  op0=ALU.mult, op1=ALU.add), reads=[z_b, st_b, W], writes=[it_["out_b"]])
                    if it_.get("post") is not None:
                        it_["post"]()
                return f

            return [s1, s2, s3, s4(0), s4(1)]

        def cast_bf(src, src_b):
            xb, xb_b = XBF.next()
            kb.op("pool", lambda e: e.tensor_copy(out=xb[:, :], in_=src), reads=[src_b], writes=[xb_b])
            return xb, xb_b

        def transposes(xb, xb_b, s_, dstT, dstT_b):
            tp, tp_b = TP.next()
            for ft in range(8):
                kb.op("pe", lambda e: e.transpose(out=tp[:, ft, :], in_=xb[:, ft * 128:(ft + 1) * 128], identity=ident[:, :]),
                      reads=[xb_b, ident_b], writes=[tp_b], inc=(ft == 7))
            kb.op("act", lambda e: e.activation(out=dstT[:, :, s_ * 128:(s_ + 1) * 128], in_=tp[:, :, :], func=AF.Copy), reads=[tp_b], writes=[dstT_b])

        S = {}

        def A_mm(ch):
            t0 = ch * CH3
            st = S.setdefault(ch, {})
            oT, oT_b = OT.next()
            pf, pf_b = PF.next()
            kb.dma(oT[:, 0:4, :], ds["ON"].rearrange("(c p) t -> p c t", p=128)[:, :, t0:t0 + CH3], reads=[dsb["ON"]], writes=[oT_b])
            kb.dma(oT[:, 4:8, :], ds["OM"].rearrange("(c p) t -> p c t", p=128)[:, :, t0:t0 + CH3], reads=[dsb["OM"]], writes=[oT_b])
            kb.dma(pf[:, :, :], din["p"][t0:t0 + CH3, :].rearrange("(s p) d -> p s d", p=128), writes=[pf_b])
            pb_, pb_b = PB.next()
            kb.op("pool", lambda e: e.tensor_copy(out=pb_[:, :, :], in_=pf[:, :, :]), reads=[pf_b], writes=[pb_b])
            st["pb"] = (pb_, pb_b)
            st["x1"] = X1.next()
            st["x1T"] = X1T.next()
            st["ys"] = []
            sst, sst_b = ST.next()
            st["sst"] = (sst, sst_b)
            for s_ in range(2):
                y, y_b = Y.next()
                xf, xf_b = XF.next()
                kb.dma(xf[:, :], din["x"][t0 + s_ * 128:t0 + (s_ + 1) * 128, :], writes=[xf_b])
                for n in range(2):
                    pa, pa_b = PA.next()
                    for kt in range(8):
                        kb.op("pe", lambda e: e.matmul(pa[:, :], lhsT=oT[:, kt, s_ * 128:(s_ + 1) * 128], rhs=wo[:, kt, n * 512:(n + 1) * 512],
                                                       start=(kt == 0), stop=(kt == 7)), reads=[oT_b, W], writes=[pa_b], inc=(kt == 7))
                    kb.op("dve", lambda e: e.scalar_tensor_tensor(out=y[:, n * 512:(n + 1) * 512], in0=xf[:, n * 512:(n + 1) * 512], scalar=ALPHA,
                                                                  in1=pa[:, :], op0=ALU.mult, op1=ALU.add, accum_out=sst[:, 8 * s_ + n:8 * s_ + n + 1]),
                          reads=[xf_b, pa_b], writes=[y_b, sst_b])
                st["ys"].append((y, y_b))

        def A_ln(ch):
            st = S[ch]
            x1, x1_b = st["x1"]
            sst, sst_b = st["sst"]
            items = []
            for s_ in range(2):
                y, y_b = st["ys"][s_]

                def post(s_=s_):
                    st["xb%d" % s_] = cast_bf(x1[:, s_, :], x1_b)

                items.append(dict(y=y, y_b=y_b, out_ap=x1[:, s_, :], out_b=x1_b, post=post))
            return ln_pair_steps(items, sst, sst_b, 0)

        def A_tr(ch, s_):
            st = S[ch]
            x1T, x1T_b = st["x1T"]
            transposes(*st["xb%d" % s_], s_, x1T, x1T_b)

        def B_up(ch, j):
            st = S[ch]
            x1T, x1T_b = st["x1T"]
            if j == 0:
                st["aT"] = AT.next()
            aT, aT_b = st["aT"]
            wu, wu_b = WU.next()
            kb.dma(wu[:, :, :, :], ds["WUP"][j], reads=[dsb["WUP"]], writes=[wu_b])
            pg, pg_b = PG.next()
            for a in range(2):
                for ft in range(8):
                    kb.op("pe", lambda e: e.matmul(pg[:, a, :], lhsT=wu[:, a, ft, :], rhs=x1T[:, ft, :], start=(ft == 0), stop=(ft == 7)),
                          reads=[wu_b, x1T_b], writes=[pg_b], inc=(a == 1 and ft == 7))
            sg, sg_b = SG.next()
            kb.op("act", lambda e: e.activation(out=sg[:, :], in_=pg[:, 0, :], func=AF.Silu), reads=[pg_b], writes=[sg_b])
            kb.op("dve", lambda e: e.tensor_tensor(out=aT[:, j, :], in0=pg[:, 1, :], in1=sg[:, :], op=ALU.mult), reads=[pg_b, sg_b], writes=[aT_b])

        def B_down(ch):
            st = S[ch]
            aT, aT_b = st["aT"]
            x1, x1_b = st["x1"]
            st["x2"] = X2.next()
            st["x2T"] = X2T.next()
            x2, x2_b = st["x2"]
            ys = []
            sst, sst_b = ST.next()
            st["sst2"] = (sst, sst_b)
            for s_ in range(2):
                y, y_b = Y.next()
                for n in range(2):
                    pa, pa_b = PA.next()
                    for j in range(22):
                        kb.op("pe", lambda e: e.matmul(pa[:, :], lhsT=aT[:, j, s_ * 128:(s_ + 1) * 128], rhs=wd[:, j, n * 512:(n + 1) * 512],
                                                       start=(j == 0), stop=(j == 21)), reads=[aT_b, W], writes=[pa_b], inc=(j == 21))
                    kb.op("dve", lambda e: e.scalar_tensor_tensor(out=y[:, n * 512:(n + 1) * 512], in0=x1[:, s_, n * 512:(n + 1) * 512], scalar=ALPHA,
                                                                  in1=pa[:, :], op0=ALU.mult, op1=ALU.add, accum_out=sst[:, 8 * s_ + n:8 * s_ + n + 1]),
                          reads=[x1_b, pa_b], writes=[y_b, sst_b])
                ys.append((y, y_b))
            st["ys2"] = ys

        def B_ln(ch):
            st = S[ch]
            x2, x2_b = st["x2"]
            sst, sst_b = st["sst2"]
            items = []
            for s_ in range(2):
                y, y_b = st["ys2"][s_]

                def post(s_=s_):
                    st["x2b%d" % s_] = cast_bf(x2[:, s_, :], x2_b)

                items.append(dict(y=y, y_b=y_b, out_ap=x2[:, s_, :], out_b=x2_b, post=post))
            return ln_pair_steps(items, sst, sst_b, 2)

        def B_tr(ch, s_):
            st = S[ch]
            x2T, x2T_b = st["x2T"]
            transposes(*st["x2b%d" % s_], s_, x2T, x2T_b)

        def C_pre(ch):
            st = S[ch]
            pb_, pb_b = st["pb"]
            pT, pT_b = PT.next()
            tp, tp_b = TP.next()
            for s_ in range(2):
                for kt in range(2):
                    kb.op("pe", lambda e: e.transpose(out=tp[:, s_ * 2 + kt, :], in_=pb_[:, s_, kt * 128:(kt + 1) * 128], identity=ident[:, :]),
                          reads=[pb_b, ident_b], writes=[tp_b], inc=(s_ == 1 and kt == 1))
            for s_ in range(2):
                kb.op("act", lambda e: e.activation(out=pT[:, :, s_ * 128:(s_ + 1) * 128], in_=tp[:, 2 * s_:2 * s_ + 2, :], func=AF.Copy),
                      reads=[tp_b], writes=[pT_b])
            st["pT"] = (pT, pT_b)

        def C_mm(ch, s_):
            st = S[ch]
            x2, x2_b = st["x2"]
            x2T, x2T_b = st["x2T"]
            pT, pT_b = st["pT"]
            if s_ == 0:
                st["sst3"] = ST.next()
                st["ys3"] = []
            sst, sst_b = st["sst3"]
            y, y_b = Y.next()
            for n in range(2):
                pa, pa_b = PA.next()
                for kt in range(8):
                    kb.op("pe", lambda e: e.matmul(pa[:, :], lhsT=x2T[:, kt, s_ * 128:(s_ + 1) * 128], rhs=wpg[:, kt, n * 512:(n + 1) * 512],
                                                   start=(kt == 0), stop=(kt == 7)), reads=[x2T_b, W], writes=[pa_b], inc=(kt == 7))
                kb.op("act", lambda e: e.activation(out=y[:, n * 512:(n + 1) * 512], in_=pa[:, :], func=AF.Tanh, scale=0.5), reads=[pa_b], writes=[y_b])
                pp, pp_b = PA.next()
                for kt in range(2):
                    kb.op("pe", lambda e: e.matmul(pp[:, :], lhsT=pT[:, kt, s_ * 128:(s_ + 1) * 128], rhs=wpl[:, kt, n * 512:(n + 1) * 512],
                                                   start=(kt == 0), stop=(kt == 1)), reads=[pT_b, W], writes=[pp_b], inc=(kt == 1))
                kb.op("dve", lambda e: e.scalar_tensor_tensor(out=y[:, n * 512:(n + 1) * 512], in0=y[:, n * 512:(n + 1) * 512], scalar=1.0, in1=pp[:, :],
                                                              op0=ALU.add, op1=ALU.mult), reads=[pp_b, y_b], writes=[y_b])
                kb.op("dve", lambda e: e.scalar_tensor_tensor(out=y[:, n * 512:(n + 1) * 512], in0=x2[:, s_, n * 512:(n + 1) * 512], scalar=2.0 * ALPHA,
                                                              in1=y[:, n * 512:(n + 1) * 512], op0=ALU.mult, op1=ALU.add,
                                                              accum_out=sst[:, 8 * s_ + n:8 * s_ + n + 1]),
                      reads=[x2_b, y_b], writes=[y_b, sst_b])
            st["ys3"].append((y, y_b))

        def C_ln(ch):
            t0 = ch * CH3
            st = S[ch]
            sst, sst_b = st["sst3"]
            zj, zj_b = Z.items[0]
            items = []
            for s_ in range(2):
                y, y_b = st["ys3"][s_]

                def post(s_=s_, y=y, y_b=y_b):
                    kb.dma(c.out[t0 + s_ * 128:t0 + (s_ + 1) * 128, :], y[:, :], reads=[y_b], writes=[c.out_b], q="pool")
                    if s_ == 1:
                        S.pop(ch)

                items.append(dict(y=y, y_b=y_b, out_ap=y[:, :], out_b=y_b, post=post, junk=(zj[:, :], zj_b)))
            return ln_pair_steps(items, sst, sst_b, 4, eps_col=1)

        def run_all(steps):
            for f in steps:
                f()

        A_mm(0)
        run_all(A_ln(0))
        A_tr(0, 0)
        A_tr(0, 1)
        sched = {
            0: [("Bln", 0)], 1: [("Bln", 1)], 2: [("Bln", 2)], 3: [("Bln", 3)], 4: [("Bln", 4)], 5: [("Btr", 0)], 6: [("Btr", 1), ("Amm", 0)],
            7: [("Aln", 0)], 8: [("Aln", 1), ("Cpre", 0)], 9: [("Aln", 2)], 10: [("Aln", 3), ("Cmm", 0)], 11: [("Aln", 4)], 12: [("Cmm", 1)],
            13: [("Cln", 0), ("Atr", 0)], 14: [("Cln", 1)], 15: [("Cln", 2), ("Atr", 1)], 16: [("Cln", 3)], 17: [("Cln", 4)],
        }
        chains = {}

        def run_slot(it, j):
            nxt = it + 1 < NCH3
            prv = it >= 1
            for name, k in sched.get(j, []):
                if name == "Bln" and prv:
                    if k == 0:
                        chains["B"] = B_ln(it - 1)
                    chains["B"][k]()
                elif name == "Btr" and prv:
                    B_tr(it - 1, k)
                elif name == "Amm" and nxt:
                    A_mm(it + 1)
                elif name == "Aln" and nxt:
                    if k == 0:
                        chains["A"] = A_ln(it + 1)
                    chains["A"][k]()
                elif name == "Atr" and nxt:
                    A_tr(it + 1, k)
                elif name == "Cpre" and prv:
                    C_pre(it - 1)
                elif name == "Cmm" and prv:
                    C_mm(it - 1, k)
                elif name == "Cln" and prv:
                    if k == 0:
                        chains["C"] = C_ln(it - 1)
                    chains["C"][k]()

        for it in range(NCH3 + 1):
            if it < NCH3:
                for j in range(22):
                    B_up(it, j)
                    run_slot(it, j)
                B_down(it)
            else:
                for j in range(22):
                    run_slot(it, j)


_CACHE = {}


def kernel(**inputs):
    if "nc" not in _CACHE:
        _CACHE["nc"] = build_program()
        _CACHE["consts"] = host_constants()
    nc = _CACHE["nc"]
    consts = _CACHE["consts"]
    B = inputs["x"].shape[0]
    shared = {}
    for k, shp in IN_SPECS.items():
        if k in ("x", "p"):
            continue
        shared[k] = np.ascontiguousarray(np.asarray(inputs[k], dtype=np.float32)[0].reshape(shp))
    for k, v in consts.items():
        shared["c_" + k] = v
    in_maps = []
    for b in range(B):
        m = dict(shared)
        m["x"] = np.ascontiguousarray(np.asarray(inputs["x"], dtype=np.float32)[b])
        m["p"] = np.ascontiguousarray(np.asarray(inputs["p"], dtype=np.float32)[0, b])
        in_maps.append(m)
    res = run_bass_kernel_spmd(nc, in_maps, core_ids=list(range(B)))
    return np.stack([np.asarray(r["out"], dtype=np.float32) for r in res.results], 0)
```

```python
from contextlib import ExitStack
import numpy as np
import ml_dtypes
import concourse.bass as bass
import concourse.mybir as mybir
from concourse.bass_utils import run_bass_kernel_spmd

F32 = mybir.dt.float32
BF16 = mybir.dt.bfloat16
AF = mybir.ActivationFunctionType
ALU = mybir.AluOpType
AX = mybir.AxisListType
bf16 = ml_dtypes.bfloat16

T = 8192
D = 1024
NCH = T // 512
NKT = T // 128
DFF = 2816
BIG = 30000.0
ALPHA = 2.0 ** 0.25
LN_EPS = 1e-5
RMS_EPS = 1e-6
MLA_SCALE = 96.0 ** -0.5
ALIBI_CUT = 200.0


class Buf:
    __slots__ = ("name", "last_w", "readers")

    def __init__(self, name=""):
        self.name = name
        self.last_w = None
        self.readers = {}


class KB:
    NSLOT = 8

    def __init__(self, nc, es):
        self.nc = nc
        self.E = {"pe": nc.tensor, "act": nc.scalar, "dve": nc.vector, "pool": nc.gpsimd, "sp": nc.sync}
        self.sem = {}
        self.seq = {}
        self.known = {}
        for e in self.E:
            self.sem[("e", e)] = es.enter_context(nc.semaphore("s_" + e))
            self.seq[e] = 0
            self.known[e] = {}
        self.slots = {}
        self.slot_rr = {}
        for q in ("sp", "pool"):
            self.slots[q] = []
            self.slot_rr[q] = 0
            for i in range(self.NSLOT):
                key = ("d", q, i)
                self.sem[key] = es.enter_context(nc.semaphore("d_%s%d" % (q, i)))
                self.slots[q].append([key, 0])

    def _collect(self, reads, writes):
        deps = {}
        for b in reads:
            if b.last_w is not None:
                k, v = b.last_w
                o = deps.get(k)
                deps[k] = (max(v, o[0]) if o else v, True)
        for b in writes:
            if b.last_w is not None:
                k, v = b.last_w
                o = deps.get(k)
                if o is None:
                    deps[k] = (v, False)
                elif v > o[0]:
                    deps[k] = (v, o[1])
            for k, v in b.readers.items():
                o = deps.get(k)
                if o is None:
                    deps[k] = (v, False)
                elif v > o[0]:
                    deps[k] = (v, o[1])
        return deps

    def _wait(self, eng, deps):
        kn = self.known[eng]
        for k, (v, raw) in deps.items():
            if k == ("e", eng):
                if eng == "pe" or not raw:
                    continue
            if kn.get(k, 0) >= v:
                continue
            self.E[eng].wait_ge(self.sem[k], v)
            kn[k] = v

    def _record(self, dep, reads, writes):
        k, v = dep
        for b in reads:
            if b.readers.get(k, 0) < v:
                b.readers[k] = v
        for b in writes:
            b.last_w = dep
            b.readers = {}

    def op(self, eng, fn, reads=(), writes=(), inc=True):
        self._wait(eng, self._collect(reads, writes))
        ins = fn(self.E[eng])
        if inc:
            ins.then_inc(self.sem[("e", eng)], 1)
            self.seq[eng] += 1
            dep = (("e", eng), self.seq[eng])
        else:
            dep = (("e", eng), self.seq[eng] + 1)
        self._record(dep, reads, writes)
        return ins

    def dma(self, out, in_, reads=(), writes=(), q="sp", **kw):
        sl = self.slots[q][self.slot_rr[q]]
        self.slot_rr[q] = (self.slot_rr[q] + 1) % self.NSLOT
        deps = self._collect(reads, writes)
        if sl[1] > 0:
            o = deps.get(sl[0])
            if o is None or o[0] < sl[1]:
                deps[sl[0]] = (sl[1], True)
        self._wait(q, deps)
        self.E[q].dma_start(out=out, in_=in_, **kw).then_inc(self.sem[sl[0]], 16)
        sl[1] += 16
        self._record((sl[0], sl[1]), reads, writes)

    def barrier(self):
        cur = {}
        for e in self.E:
            cur[("e", e)] = self.seq[e]
        for q in self.slots:
            for key, cnt in self.slots[q]:
                cur[key] = cnt
        for e in self.E:
            kn = self.known[e]
            for k, v in cur.items():
                if v == 0 or k == ("e", e):
                    continue
                if kn.get(k, 0) >= v:
                    continue
                self.E[e].wait_ge(self.sem[k], v)
                kn[k] = v


class Rot:
    def __init__(self, items):
        self.items = items
        self.i = 0

    def next(self):
        it = self.items[self.i]
        self.i = (self.i + 1) % len(self.items)
        return it


def host_constants():
    c = {}
    c["ident"] = np.eye(128, dtype=np.float32).astype(bf16)
    pos = np.arange(T, dtype=np.float32)
    inv_freq = (10000.0 ** (-np.arange(16, dtype=np.float32) / 16)).astype(np.float32)
    ang = (pos[:, None] * inv_freq[None, :]).astype(np.float32)
    cos = np.cos(ang).astype(np.float32).T
    sin = np.sin(ang).astype(np.float32).T
    c["ropeC"] = np.ascontiguousarray(np.concatenate([cos, cos], 0))
    c["ropeS"] = np.ascontiguousarray(np.concatenate([-sin, sin], 0))
    p = np.arange(128)[:, None]
    j = np.arange(512)[None, :]
    def addmask(valid):
        return np.where(valid, 0.0, -BIG).astype(np.float32).astype(bf16)
    c["cmk"] = addmask(np.stack([(p + 128 * d <= j) for d in range(4)], 1))
    c["wmk"] = addmask(np.stack([(j < 128 * d + p) for d in range(4)], 1))
    c["pmk"] = addmask(np.stack([(16 * p + 31 <= 512 * m + j) for m in range(5)], 1))
    slopes = 2.0 ** (-np.arange(1, 9, dtype=np.float64))
    t = np.arange(T)
    qaug = np.zeros((8, 5, T), np.float32)
    for h in range(8):
        s = slopes[h]
        qaug[h, 0] = s
        qaug[h, 1] = s
        qaug[h, 2] = -s * (t // 128) * 128
        qaug[h, 3] = -s * (t % 128)
        qaug[h, 4] = -s * ((t % 128) - 31)
    c["qaug"] = qaug.astype(bf16)
    ka = np.zeros((5, T), np.float32)
    ka[0] = (t // 128) * 128
    ka[1] = t % 128
    ka[2] = 1
    ka[3] = 1
    c["kaug"] = ka.astype(bf16)
    cc = np.arange(512)
    kc = np.zeros((5, 512), np.float32)
    kc[0] = (cc // 128) * 2048
    kc[1] = (cc % 128) * 16
    kc[2] = 1
    kc[4] = 1
    c["kaugc"] = kc.astype(bf16)
    for k in ("qaug", "kaug", "kaugc"):
        assert np.all(c[k].astype(np.float32) == {"qaug": qaug, "kaug": ka, "kaugc": kc}[k])
    E = np.zeros((128, 64, 128), np.float32)
    for kt in range(64):
        E[2 * kt, kt, :64] = BIG
        E[2 * kt + 1, kt, 64:] = BIG
    c["eexp"] = E.astype(bf16)
    M = np.zeros((512, 128), np.float32)
    for cidx in range(511):
        for s in (cidx, cidx + 1):
            M[cidx, s // 4] += 1
    c["impm"] = M.astype(bf16)
    keep = np.zeros((128, 256), np.float32)
    add = np.zeros((128, 256), np.float32)
    pp = np.arange(128)
    for col in range(256):
        r = col - 128
        if r < -1:
            keep[:, col] = 1
        elif r == -1:
            keep[:, col] = (pp >= 64)
            add[:, col] = 1e6 * (pp < 64)
        elif r == 0:
            add[:, col] = 1e6
        elif r == 1:
            add[:, col] = np.where(pp >= 64, 1e6, -1.0)
        else:
            add[:, col] = -1.0
    c["keept"] = keep
    c["addt"] = add
    return c


CONST_SPECS = {
    "ident": ([128, 128], BF16), "ropeC": ([32, T], F32), "ropeS": ([32, T], F32),
    "cmk": ([128, 4, 512], BF16), "wmk": ([128, 4, 512], BF16), "pmk": ([128, 5, 512], BF16),
    "qaug": ([8, 5, T], BF16), "kaug": ([5, T], BF16), "kaugc": ([5, 512], BF16),
    "eexp": ([128, 64, 128], BF16), "impm": ([512, 128], BF16),
    "keept": ([128, 256], F32), "addt": ([128, 256], F32),
}

IN_SPECS = {
    "x": [T, D], "p": [T, 256], "w_in": [D, 1720], "w_ck1": [2048, 256], "w_ck2": [256, 64],
    "pos_ck": [32, 64], "w_cv1": [2048, 256], "w_cv2": [256, 64], "pos_cv": [32, 64],
    "mla_q_norm": [256, 1], "w_uq": [256, 768], "mla_kv_norm": [128, 1], "w_ukv": [128, 1024],
    "w_out": [D, D], "ln1_g": [1, D], "ln1_b": [1, D], "w_up": [D, 2 * DFF], "w_down": [DFF, D],
    "ln2_g": [1, D], "ln2_b": [1, D], "w_ple_gate": [D, D], "w_ple": [256, D],
    "ln3_g": [1, D], "ln3_b": [1, D],
}

SCRATCH_SPECS = {
    "QN": ([512, T], BF16),
    "KX": ([4, 128, T], BF16),
    "VSW": ([T, 256], BF16),
    "G": ([24, T], F32),
    "KPE": ([32, T], BF16),
    "QM": ([8, 96, T], BF16),
    "KN": ([512, T], BF16),
    "VM": ([T, 512], BF16),
    "ON": ([512, T], BF16),
    "OM": ([512, T], BF16),
    "WUP": ([22, 128, 2, 8, 128], BF16),
}


def sb(nc, es, name, shape, dt):
    return es.enter_context(nc.sbuf_tensor(name, list(shape), dt))


def ps(nc, es, name, shape, dt):
    return es.enter_context(nc.psum_tensor(name, list(shape), dt))


def rot_tiles(nc, es, name, shape, dt, n, psum=False):
    items = []
    for i in range(n):
        t = (ps if psum else sb)(nc, es, "%s%d" % (name, i), shape, dt)
        items.append((t, Buf("%s%d" % (name, i))))
    return Rot(items)


class Ctx:
    pass


def build_program(stop_after=None, debug_scratch=False, stop_mid=None):
    nc = bass.Bass("TRN2", target_bir_lowering=False)
    c = Ctx()
    c.nc = nc
    c.din = {k: nc.dram_tensor(k, v, F32, kind="ExternalInput").ap() for k, v in IN_SPECS.items()}
    c.dc = {k: nc.dram_tensor("c_" + k, v[0], v[1], kind="ExternalInput").ap() for k, v in CONST_SPECS.items()}
    skind = "ExternalOutput" if debug_scratch else "Internal"
    c.ds = {k: nc.dram_tensor("s_" + k, v[0], v[1], kind=skind).ap() for k, v in SCRATCH_SPECS.items()}
    c.dsb = {k: Buf("s_" + k) for k in SCRATCH_SPECS}
    c.out = nc.dram_tensor("out", [T, D], F32, kind="ExternalOutput").ap()
    c.out_b = Buf("out")
    c.stop_mid = stop_mid
    if stop_mid == "compress":
        c.dbg = {"kcmp": nc.dram_tensor("dbg_kcmp", [69, 2, 512], BF16, kind="ExternalOutput").ap(),
                 "vcmp": nc.dram_tensor("dbg_vcmp", [128, 4, 2, 128], BF16, kind="ExternalOutput").ap()}
    with ExitStack() as es0:
        kb = KB(nc, es0)
        c.kb = kb
        c.ident = sb(nc, es0, "ident", [128, 128], BF16)
        c.ident_b = Buf("ident")
        kb.dma(c.ident[:], c.dc["ident"][:], writes=[c.ident_b])
        phases = [phase1, phase2, phase3]
        for ph in phases:
            ph(c)
            kb.barrier()
            if stop_after == ph.__name__:
                break
        kb.barrier()
    return nc


def phase1(c):
    nc, kb = c.nc, c.kb
    din, dc, ds, dsb = c.din, c.dc, c.ds, c.dsb
    with ExitStack() as es:
        wtr = sb(nc, es, "wtr", [128, 8, 1048], BF16)
        wtok = sb(nc, es, "wtok", [128, 8, 640], BF16)
        wkpA = sb(nc, es, "wkpA", [128, 8, 96], BF16)
        wkpB = sb(nc, es, "wkpB", [128, 8, 96], BF16)
        wuqA = sb(nc, es, "wuqA", [128, 2, 8, 96], BF16)
        wuqB = sb(nc, es, "wuqB", [128, 2, 8, 96], BF16)
        wkvK = sb(nc, es, "wkvK", [128, 8, 64], BF16)
        wkvV = sb(nc, es, "wkvV", [128, 8, 64], BF16)
        gq = sb(nc, es, "gq", [128, 2], F32)
        gkv = sb(nc, es, "gkv", [128, 1], F32)
        W = Buf("p1w")
        es_stg = ExitStack()
        stg = rot_tiles(nc, es_stg, "stg", [128, 1720], F32, 2)
        kb.op("pool", lambda e: e.memset(wkpA[:], 0.0), writes=[W])
        kb.op("pool", lambda e: e.memset(wkpB[:], 0.0), writes=[W])
        kb.op("pool", lambda e: e.memset(wuqB[:], 0.0), writes=[W])
        for kt in range(2):
            kb.dma(gq[:, kt:kt + 1], din["mla_q_norm"][kt * 128:(kt + 1) * 128, :], writes=[W])
        kb.dma(gkv[:], din["mla_kv_norm"][:, :], writes=[W])
        for ft in range(8):
            s, sbuf_ = stg.next()
            kb.dma(s[:], din["w_in"][ft * 128:(ft + 1) * 128, :], writes=[sbuf_])
            kb.op("dve", lambda e: e.tensor_copy(out=wtr[:, ft, 0:896], in_=s[:, 0:896]), reads=[sbuf_], writes=[W])
            kb.op("pool", lambda e: e.tensor_copy(out=wtr[:, ft, 896:1024], in_=s[:, 1024:1152]), reads=[sbuf_], writes=[W])
            kb.op("pool", lambda e: e.tensor_copy(out=wtr[:, ft, 1024:1048], in_=s[:, 1280:1304]), reads=[sbuf_], writes=[W])
            kb.op("dve", lambda e: e.tensor_copy(out=wtok[:, ft, 0:384], in_=s[:, 1304:1688]), reads=[sbuf_], writes=[W])
            kb.op("pool", lambda e: e.tensor_copy(out=wtok[:, ft, 384:512], in_=s[:, 896:1024]), reads=[sbuf_], writes=[W])
            kb.op("pool", lambda e: e.tensor_copy(out=wtok[:, ft, 512:640], in_=s[:, 1152:1280]), reads=[sbuf_], writes=[W])
            kb.op("dve", lambda e: e.tensor_copy(out=wkpA[:, ft, 64:96], in_=s[:, 1688:1720]), reads=[sbuf_], writes=[W])
            kb.op("dve", lambda e: e.tensor_copy(out=wkpB[:, ft, 64:80], in_=s[:, 1704:1720]), reads=[sbuf_], writes=[W])
            kb.op("dve", lambda e: e.tensor_copy(out=wkpB[:, ft, 80:96], in_=s[:, 1688:1704]), reads=[sbuf_], writes=[W])
        for kt in range(2):
            s, sbuf_ = stg.next()
            kb.dma(s[:, 0:768], din["w_uq"][kt * 128:(kt + 1) * 128, :], writes=[sbuf_])
            s3 = s[:, 0:768].rearrange("p (h c) -> p h c", c=96)
            kb.op("dve", lambda e: e.tensor_scalar(out=wuqA[:, kt, :, :], in0=s3, scalar1=gq[:, kt:kt + 1], scalar2=None,
                                                   op0=ALU.mult), reads=[sbuf_, W], writes=[W])
            kb.op("dve", lambda e: e.tensor_scalar(out=wuqB[:, kt, :, 64:80], in0=s3[:, :, 80:96], scalar1=gq[:, kt:kt + 1],
                                                   scalar2=None, op0=ALU.mult), reads=[sbuf_, W], writes=[W])
            kb.op("dve", lambda e: e.tensor_scalar(out=wuqB[:, kt, :, 80:96], in0=s3[:, :, 64:80], scalar1=gq[:, kt:kt + 1],
                                                   scalar2=None, op0=ALU.mult), reads=[sbuf_, W], writes=[W])
        s, sbuf_ = stg.next()
        kb.dma(s[:, 0:1024], din["w_ukv"][:, :], writes=[sbuf_])
        s3 = s[:, 0:1024].rearrange("p (h c) -> p h c", c=128)
        kb.op("dve", lambda e: e.tensor_scalar(out=wkvK[:, :, :], in0=s3[:, :, 0:64], scalar1=gkv[:, 0:1], scalar2=None,
                                               op0=ALU.mult), reads=[sbuf_, W], writes=[W])
        kb.op("dve", lambda e: e.tensor_scalar(out=wkvV[:, :, :], in0=s3[:, :, 64:128], scalar1=gkv[:, 0:1], scalar2=None,
                                               op0=ALU.mult), reads=[sbuf_, W], writes=[W])
        kb.barrier()
        es_stg.close()
        XF = rot_tiles(nc, es, "xf", [128, 4, 1024], F32, 2)
        XT = rot_tiles(nc, es, "xT", [128, 8, 512], BF16, 2)
        STT = rot_tiles(nc, es, "stT", [128, 8, 512], BF16, 2)
        GST = rot_tiles(nc, es, "gst", [24, 512], F32, 2)
        RC = rot_tiles(nc, es, "rc", [96, 2, 512], F32, 3)
        KST = rot_tiles(nc, es, "kst", [96, 512], BF16, 2)
        RT = rot_tiles(nc, es, "rt", [96, 2, 512], F32, 2)
        VST = rot_tiles(nc, es, "vst", [128, 4, 256], BF16, 2)
        SS = rot_tiles(nc, es, "ss", [128, 4], F32, 2)
        JK = rot_tiles(nc, es, "jk", [128, 256], F32, 2)
        CN = rot_tiles(nc, es, "cn", [128, 384], BF16, 2)
        CT = rot_tiles(nc, es, "cT", [128, 3, 512], BF16, 2)
        QST = rot_tiles(nc, es, "qst", [96, 8, 512], BF16, 2)
        KNST = rot_tiles(nc, es, "knst", [128, 4, 512], BF16, 2)
        VMST = rot_tiles(nc, es, "vmst", [128, 4, 512], BF16, 2)
        TP = rot_tiles(nc, es, "tp", [128, 1024], BF16, 2, psum=True)
        TPQ = rot_tiles(nc, es, "tpq", [128, 3, 128], BF16, 1, psum=True)
        PS = rot_tiles(nc, es, "ps", [128, 512], F32, 5, psum=True)
        ident, ident_b = c.ident, c.ident_b
        xv = din["x"]
        XB = rot_tiles(nc, es, "xbb", [128, 4, 1024], BF16, 2)
        LS = {}

        def L_load(ch):
            t0 = ch * 512
            xf, xf_b = XF.next()
            kb.dma(xf[:], xv[t0:t0 + 512, :].rearrange("(s p) d -> p s d", p=128), writes=[xf_b])
            rc, rc_b = RC.next()
            kb.dma(rc[64:96, 0, :], dc["ropeC"][:, t0:t0 + 512], writes=[rc_b])
            kb.dma(rc[64:96, 1, :], dc["ropeS"][:, t0:t0 + 512], writes=[rc_b])
            LS[ch] = {"xf": (xf, xf_b), "rc": (rc, rc_b)}

        def L_cast(ch):
            xf, xf_b = LS[ch]["xf"]
            xb, xb_b = XB.next()
            kb.op("dve", lambda e: e.tensor_copy(out=xb[:, 0:3, :], in_=xf[:, 0:3, :]), reads=[xf_b], writes=[xb_b])
            kb.op("pool", lambda e: e.tensor_copy(out=xb[:, 3:4, :], in_=xf[:, 3:4, :]), reads=[xf_b], writes=[xb_b])
            LS[ch]["xb"] = (xb, xb_b)

        def L_tr(ch):
            xb, xb_b = LS[ch]["xb"]
            xT, xT_b = XT.next()
            for f2 in range(4):
                tp, tp_b = TP.next()
                for fi in range(2):
                    ft = f2 * 2 + fi
                    for s_ in range(4):
                        last = (fi == 1 and s_ == 3)
                        kb.op("pe", lambda e: e.transpose(out=tp[:, fi * 512 + s_ * 128: fi * 512 + (s_ + 1) * 128],
                                                          in_=xb[:, s_, ft * 128:(ft + 1) * 128], identity=ident[:]),
                              reads=[xb_b, ident_b], writes=[tp_b], inc=last)
                dst = xT[:, 2 * f2:2 * f2 + 2, :].rearrange("p a t -> p (a t)")
                kb.op("act" if f2 % 2 else "dve",
                      (lambda e: e.activation(out=dst, in_=tp[:], func=AF.Copy)) if f2 % 2 else
                      (lambda e: e.tensor_copy(out=dst, in_=tp[:])), reads=[tp_b], writes=[xT_b])
            LS[ch]["xT"] = (xT, xT_b)

        L_load(0)
        L_cast(0)
        L_tr(0)
        if NCH > 1:
            L_load(1)
        for ch in range(NCH):
            t0 = ch * 512
            xT, xT_b = LS[ch]["xT"]
            rc, rc_b = LS[ch]["rc"]
            if ch + 2 < NCH:
                L_load(ch + 2)
            if ch + 1 < NCH:
                L_cast(ch + 1)
            stT, stT_b = STT.next()
            for ct in range(8):
                pt, pt_b = PS.next()
                for ft in range(8):
                    kb.op("pe", lambda e: e.matmul(pt[:, :], lhsT=wtr[:, ft, ct * 128:(ct + 1) * 128], rhs=xT[:, ft, :],
                                                   start=(ft == 0), stop=(ft == 7)),
                          reads=[W, xT_b], writes=[pt_b], inc=(ft == 7))
                if ct < 4:
                    kb.op("act", lambda e: e.activation(out=stT[:, ct, :], in_=pt[:, :], func=AF.Copy, scale=0.125),
                          reads=[pt_b], writes=[stT_b])
                else:
                    kb.op("dve", lambda e: e.tensor_copy(out=stT[:, ct, :], in_=pt[:, :]), reads=[pt_b], writes=[stT_b])
            kb.dma(ds["QN"].rearrange("(c p) t -> p c t", p=128)[:, :, t0:t0 + 512], stT[:, 0:4, :],
                   reads=[stT_b], writes=[dsb["QN"]])
            kb.dma(ds["KX"].rearrange("k p t -> p k t")[:, :, t0:t0 + 512], stT[:, 4:8, :],
                   reads=[stT_b], writes=[dsb["KX"]])
            pt, pt_b = PS.next()
            for ft in range(8):
                kb.op("pe", lambda e: e.matmul(pt[0:24, :], lhsT=wtr[:, ft, 1024:1048], rhs=xT[:, ft, :],
                                               start=(ft == 0), stop=(ft == 7)),
                      reads=[W, xT_b], writes=[pt_b], inc=(ft == 7))
            gst, gst_b = GST.next()
            kb.op("act", lambda e: e.activation(out=gst[:, :], in_=pt[0:24, :], func=AF.Sigmoid), reads=[pt_b], writes=[gst_b])
            kb.dma(ds["G"][:, t0:t0 + 512], gst[:, :], reads=[gst_b], writes=[dsb["G"]])
            pa, pa_b = PS.next()
            for ft in range(8):
                kb.op("pe", lambda e: e.matmul(pa[0:96, :], lhsT=wkpA[:, ft, :], rhs=xT[:, ft, :], start=(ft == 0), stop=(ft == 7)),
                      reads=[W, xT_b], writes=[pa_b], inc=(ft == 7))
            pb, pb_b = PS.next()
            for ft in range(8):
                kb.op("pe", lambda e: e.matmul(pb[0:96, :], lhsT=wkpB[:, ft, :], rhs=xT[:, ft, :], start=(ft == 0), stop=(ft == 7)),
                      reads=[W, xT_b], writes=[pb_b], inc=(ft == 7))
            rt, rt_b = RT.next()
            kst, kst_b = KST.next()
            kb.op("dve", lambda e: e.tensor_tensor(out=rt[64:96, 0, :], in0=pa[64:96, :], in1=rc[64:96, 0, :], op=ALU.mult),
                  reads=[pa_b, rc_b], writes=[rt_b])
            kb.op("dve", lambda e: e.tensor_tensor(out=rt[64:96, 1, :], in0=pb[64:96, :], in1=rc[64:96, 1, :], op=ALU.mult),
                  reads=[pb_b, rc_b], writes=[rt_b])
            kb.op("pool", lambda e: e.tensor_tensor(out=kst[64:96, :], in0=rt[64:96, 0, :], in1=rt[64:96, 1, :], op=ALU.add),
                  reads=[rt_b], writes=[kst_b])
            kb.dma(ds["KPE"][:, t0:t0 + 512], kst[64:96, :], reads=[kst_b], writes=[dsb["KPE"]])
            if ch + 1 < NCH:
                L_tr(ch + 1)
            vst, vst_b = VST.next()
            cT, cT_b = CT.next()
            for s_ in range(4):
                p1, p1_b = PS.next()
                for ft in range(8):
                    kb.op("pe", lambda e: e.matmul(p1[:, 0:384], lhsT=xT[:, ft, s_ * 128:(s_ + 1) * 128], rhs=wtok[:, ft, 0:384],
                                                   start=(ft == 0), stop=(ft == 7)),
                          reads=[W, xT_b], writes=[p1_b], inc=(ft == 7))
                p2, p2_b = PS.next()
                for ft in range(8):
                    kb.op("pe", lambda e: e.matmul(p2[:, 0:256], lhsT=xT[:, ft, s_ * 128:(s_ + 1) * 128], rhs=wtok[:, ft, 384:640],
                                                   start=(ft == 0), stop=(ft == 7)),
                          reads=[W, xT_b], writes=[p2_b], inc=(ft == 7))
                kb.op("dve", lambda e: e.tensor_copy(out=vst[:, s_, :], in_=p2[:, 0:256]), reads=[p2_b], writes=[vst_b])
                ss, ss_b = SS.next()
                jk, jk_b = JK.next()
                kb.op("act", lambda e: e.activation(out=jk[:, 0:256], in_=p1[:, 0:256], func=AF.Square, accum_out=ss[:, 0:1]),
                      reads=[p1_b], writes=[jk_b, ss_b])
                kb.op("act", lambda e: e.activation(out=jk[:, 0:128], in_=p1[:, 256:384], func=AF.Square, accum_out=ss[:, 1:2]),
                      reads=[p1_b], writes=[jk_b, ss_b])
                kb.op("dve", lambda e: e.tensor_scalar(out=ss[:, 0:1], in0=ss[:, 0:1], scalar1=1.0 / 256, scalar2=RMS_EPS,
                                                       op0=ALU.mult, op1=ALU.add), reads=[ss_b], writes=[ss_b])
                kb.op("dve", lambda e: e.tensor_scalar(out=ss[:, 1:2], in0=ss[:, 1:2], scalar1=1.0 / 128, scalar2=RMS_EPS,
                                                       op0=ALU.mult, op1=ALU.add), reads=[ss_b], writes=[ss_b])
                kb.op("act", lambda e: e.activation(out=ss[:, 2:4], in_=ss[:, 0:2], func=AF.Sqrt), reads=[ss_b], writes=[ss_b])
                kb.op("dve", lambda e: e.reciprocal(out=ss[:, 0:2], in_=ss[:, 2:4]), reads=[ss_b], writes=[ss_b])
                cn, cn_b = CN.next()
                kb.op("dve", lambda e: e.tensor_scalar(out=cn[:, 0:256], in0=p1[:, 0:256], scalar1=ss[:, 0:1], scalar2=None,
                                                       op0=ALU.mult), reads=[p1_b, ss_b], writes=[cn_b])
                kb.op("dve", lambda e: e.tensor_scalar(out=cn[:, 256:384], in0=p1[:, 256:384], scalar1=ss[:, 1:2], scalar2=None,
                                                       op0=ALU.mult), reads=[p1_b, ss_b], writes=[cn_b])
                tq, tq_b = TPQ.next()
                for i in range(3):
                    kb.op("pe", lambda e: e.transpose(out=tq[:, i, :], in_=cn[:, i * 128:(i + 1) * 128], identity=ident[:]),
                          reads=[cn_b, ident_b], writes=[tq_b], inc=(i == 2))
                kb.op("act", lambda e: e.activation(out=cT[:, :, s_ * 128:(s_ + 1) * 128], in_=tq[:, :, :], func=AF.Copy),
                      reads=[tq_b], writes=[cT_b])
            kb.dma(ds["VSW"][t0:t0 + 512, :].rearrange("(s p) c -> p s c", p=128), vst[:, :, :], reads=[vst_b], writes=[dsb["VSW"]])
            qst, qst_b = QST.next()
            for h in range(8):
                pa, pa_b = PS.next()
                for kt in range(2):
                    kb.op("pe", lambda e: e.matmul(pa[0:96, :], lhsT=wuqA[:, kt, h, :], rhs=cT[:, kt, :], start=(kt == 0), stop=(kt == 1)),
                          reads=[W, cT_b], writes=[pa_b], inc=(kt == 1))
                pb, pb_b = PS.next()
                for kt in range(2):
                    kb.op("pe", lambda e: e.matmul(pb[0:96, :], lhsT=wuqB[:, kt, h, :], rhs=cT[:, kt, :], start=(kt == 0), stop=(kt == 1)),
                          reads=[W, cT_b], writes=[pb_b], inc=(kt == 1))
                kb.op("act", lambda e: e.activation(out=qst[0:64, h, :], in_=pa[0:64, :], func=AF.Copy), reads=[pa_b], writes=[qst_b])
                rt, rt_b = RT.next()
                kb.op("dve", lambda e: e.tensor_tensor(out=rt[64:96, 0, :], in0=pa[64:96, :], in1=rc[64:96, 0, :], op=ALU.mult),
                      reads=[pa_b, rc_b], writes=[rt_b])
                kb.op("dve", lambda e: e.tensor_tensor(out=rt[64:96, 1, :], in0=pb[64:96, :], in1=rc[64:96, 1, :], op=ALU.mult),
                      reads=[pb_b, rc_b], writes=[rt_b])
                kb.op("pool", lambda e: e.tensor_tensor(out=qst[64:96, h, :], in0=rt[64:96, 0, :], in1=rt[64:96, 1, :], op=ALU.add),
                      reads=[rt_b], writes=[qst_b])
            kb.dma(ds["QM"].rearrange("h r t -> r h t")[:, :, t0:t0 + 512], qst[:, :, :], reads=[qst_b], writes=[dsb["QM"]])
            knst, knst_b = KNST.next()
            for hp in range(4):
                pt, pt_b = PS.next()
                kb.op("pe", lambda e: e.matmul(pt[:, :], lhsT=wkvK[:, 2 * hp:2 * hp + 2, :].rearrange("p a c -> p (a c)"),
                                               rhs=cT[:, 2, :], start=True, stop=True), reads=[W, cT_b], writes=[pt_b])
                kb.op("act" if hp % 2 else "dve",
                      (lambda e: e.activation(out=knst[:, hp, :], in_=pt[:, :], func=AF.Copy)) if hp % 2 else
                      (lambda e: e.tensor_copy(out=knst[:, hp, :], in_=pt[:, :])), reads=[pt_b], writes=[knst_b])
            kb.dma(ds["KN"].rearrange("(c p) t -> p c t", p=128)[:, :, t0:t0 + 512], knst[:, :, :], reads=[knst_b], writes=[dsb["KN"]])
            vmst, vmst_b = VMST.next()
            for s_ in range(4):
                pt, pt_b = PS.next()
                kb.op("pe", lambda e: e.matmul(pt[:, :], lhsT=cT[:, 2, s_ * 128:(s_ + 1) * 128],
                                               rhs=wkvV[:, :, :].rearrange("p a c -> p (a c)"), start=True, stop=True),
                      reads=[W, cT_b], writes=[pt_b])
                kb.op("act" if s_ % 2 else "dve",
                      (lambda e: e.activation(out=vmst[:, s_, :], in_=pt[:, :], func=AF.Copy)) if s_ % 2 else
                      (lambda e: e.tensor_copy(out=vmst[:, s_, :], in_=pt[:, :])), reads=[pt_b], writes=[vmst_b])
            kb.dma(ds["VM"][t0:t0 + 512, :].rearrange("(s p) c -> p s c", p=128), vmst[:, :, :], reads=[vmst_b], writes=[dsb["VM"]])


class TileD:
    __slots__ = ("qk", "scale", "pv", "epi", "lo", "hi")

    def __init__(self, qk, scale, pv, epi=None, lo=0, hi=512):
        self.qk = qk
        self.scale = scale
        self.pv = pv
        self.epi = epi
        self.lo, self.hi = lo, hi


class Stream:
    def __init__(self, c, R, L=2):
        self.c, self.R, self.L = c, R, L
        self.pending = []

    def push(self, t):
        kb, R = self.c.kb, self.R
        S, S_b = R.SPS.next()
        nq = len(t.qk)
        lo, hi = t.lo, t.hi
        for a, ent in enumerate(t.qk):
            lh, rh, rb = ent[0], ent[1], ent[2]
            clo, chi = (ent[3], ent[4]) if len(ent) > 3 else (lo, hi)
            kb.op("pe", lambda e: e.matmul(S[:, clo:chi], lhsT=lh, rhs=rh, start=(a == 0), stop=(a == nq - 1), skip_group_check=True),
                  reads=rb, writes=[S_b], inc=(a == nq - 1))
        P, P_b = R.PSB.next()
        kb.op("act", lambda e: e.activation(out=P[:, lo:hi], in_=S[:, lo:hi], func=AF.Exp, scale=t.scale), reads=[S_b], writes=[P_b])
        self.pending.append((t, P, P_b))
        if len(self.pending) > self.L:
            self._pop()

    def _pop(self):
        t, P, P_b = self.pending.pop(0)
        t.pv(P, P_b)
        if t.epi is not None:
            t.epi()

    def flush(self):
        while self.pending:
            self._pop()


def run_stream(c, R, tiles, L=2):
    st = c.stream
    for t in tiles:
        st.push(t)


def phase2(c):
    nc, kb = c.nc, c.kb
    din, dc, ds, dsb = c.din, c.dc, c.ds, c.dsb
    ident, ident_b = c.ident, c.ident_b
    with ExitStack() as esA:
        kcmp = sb(nc, esA, "kcmp", [69, 2, 512], BF16)
        kcmp_b = Buf("kcmp")
        vcmp = sb(nc, esA, "vcmp", [128, 4, 2, 128], BF16)
        vcmp_b = Buf("vcmp")
        CB = Buf("consts2")
        cmk = sb(nc, esA, "cmk", [128, 4, 512], BF16)
        wmk = sb(nc, esA, "wmk", [128, 4, 512], BF16)
        pmk = sb(nc, esA, "pmk", [128, 5, 512], BF16)
        eexp = sb(nc, esA, "eexp", [128, 64, 128], BF16)
        impm = sb(nc, esA, "impm", [128, 4, 128], BF16)
        keept = sb(nc, esA, "keept", [128, 256], F32)
        addt = sb(nc, esA, "addt", [128, 256], F32)
        kb.dma(cmk[:], dc["cmk"][:], writes=[CB])
        kb.dma(wmk[:], dc["wmk"][:], writes=[CB])
        kb.dma(pmk[:], dc["pmk"][:], writes=[CB])
        kb.dma(eexp[:], dc["eexp"][:], writes=[CB])
        kb.dma(impm[:], dc["impm"].rearrange("(ct p) b -> p ct b", p=128), writes=[CB])
        kb.dma(keept[:], dc["keept"][:], writes=[CB])
        kb.dma(addt[:], dc["addt"][:], writes=[CB])
        kb.op("pool", lambda e: e.memset(kcmp[:], 0.0), writes=[kcmp_b])
        kb.op("pool", lambda e: e.memset(vcmp[:], 1.0), writes=[vcmp_b])
        for g in range(2):
            kb.dma(kcmp[64:69, g, :], dc["kaugc"][:, :], writes=[kcmp_b])
        with ExitStack() as es:
            kcg = sb(nc, es, "kcg", [64, 4, T], BF16)
            kcg_b = Buf("kcg")
            for kind in range(2):
                for g in range(2):
                    kb.dma(kcg[:, kind * 2 + g, :], ds["KX"][kind, g * 64:(g + 1) * 64, :], reads=[dsb["KX"]], writes=[kcg_b])
            w1s = sb(nc, es, "w1s", [64, 32, 256], F32)
            w1s_b = Buf("w1s")
            w1 = [sb(nc, es, "w1_%d" % k, [64, 32, 256], BF16) for k in range(2)]
            w2s = sb(nc, es, "w2s", [128, 2, 64], F32)
            w2s_b = Buf("w2s")
            w2 = [sb(nc, es, "w2_%d" % k, [128, 2, 64], BF16) for k in range(2)]
            posS = sb(nc, es, "posS", [32, 64], F32)
            posS_b = Buf("posS")
            posB = sb(nc, es, "posB", [32, 64], BF16)
            posB_b = Buf("posB")
            posT = sb(nc, es, "posT", [64, 2, 32], BF16)
            b1 = sb(nc, es, "b1", [128, 4], F32)
            Wc = Buf("wc")
            tpp = ps(nc, es, "tpp", [64, 32], BF16)
            tpp_b = Buf("tpp")
            pb1 = ps(nc, es, "pb1", [128, 4], F32)
            pb1_b = Buf("pb1")
            PSC = rot_tiles(nc, es, "psc", [128, 512], F32, 4, psum=True)
            XG = rot_tiles(nc, es, "xg", [128, 512], F32, 2)
            X2 = rot_tiles(nc, es, "x2", [128, 512], F32, 2)
            GL = rot_tiles(nc, es, "gl", [128, 512], BF16, 4)
            for gl, gl_b in GL.items:
                kb.op("pool", lambda e: e.memset(gl[:], 0.0), writes=[gl_b])
            names = [("w_ck1", "w_ck2", "pos_ck"), ("w_cv1", "w_cv2", "pos_cv")]
            for kind, (n1, n2, npos) in enumerate(names):
                kb.dma(w1s[:], din[n1].rearrange("(l d) n -> d l n", d=64), writes=[w1s_b])
                kb.op("dve", lambda e: e.tensor_copy(out=w1[kind][:, 0:16, :], in_=w1s[:, 0:16, :]), reads=[w1s_b], writes=[Wc])
                kb.op("pool", lambda e: e.tensor_copy(out=w1[kind][:, 16:32, :], in_=w1s[:, 16:32, :]), reads=[w1s_b], writes=[Wc])
                kb.dma(w2s[:], din[n2].rearrange("(c p) n -> p c n", p=128), writes=[w2s_b])
                kb.op("dve", lambda e: e.tensor_copy(out=w2[kind][:], in_=w2s[:]), reads=[w2s_b], writes=[Wc])
                kb.dma(posS[:], din[npos][:, :], writes=[posS_b])
                kb.op("dve", lambda e: e.tensor_copy(out=posB[:], in_=posS[:]), reads=[posS_b], writes=[posB_b])
                kb.op("pe", lambda e: e.transpose(out=tpp[:, :], in_=posB[:, :], identity=ident[0:32, 0:32]),
                      reads=[posB_b, ident_b], writes=[tpp_b])
                kb.op("dve", lambda e: e.tensor_copy(out=posT[:, kind, :], in_=tpp[:, :]), reads=[tpp_b], writes=[Wc])
            for kind in range(2):
                for hc in range(2):
                    idx = kind * 2 + hc
                    for l in range(32):
                        kb.op("pe", lambda e: e.matmul(pb1[:, idx:idx + 1], lhsT=w1[kind][:, l, hc * 128:(hc + 1) * 128],
                                                       rhs=posT[:, kind, l:l + 1], start=(l == 0), stop=(l == 31)),
                              reads=[Wc], writes=[pb1_b], inc=(l == 31))
            kb.op("dve", lambda e: e.tensor_copy(out=b1[:], in_=pb1[:]), reads=[pb1_b], writes=[Wc])
            for kind in range(2):
                for g in range(2):
                    gls = []
                    for hc in range(2):
                        idx = kind * 2 + hc
                        pm, pm_b = PSC.next()
                        for l in range(32):
                            kb.op("pe", lambda e: e.matmul(pm[:, 0:511], lhsT=w1[kind][:, l, hc * 128:(hc + 1) * 128],
                                                           rhs=kcg[:, kind * 2 + g, l:l + 8161:16], start=(l == 0), stop=(l == 31)),
                                  reads=[Wc, kcg_b], writes=[pm_b], inc=(l == 31))
                        xg, xg_b = XG.next()
                        x2, x2_b = X2.next()
                        gl, gl_b = GL.next()
                        kb.op("dve", lambda e: e.tensor_scalar(out=xg[:, 0:511], in0=pm[:, 0:511], scalar1=b1[:, idx:idx + 1],
                                                               scalar2=None, op0=ALU.add), reads=[pm_b, Wc], writes=[xg_b])
                        kb.op("pool", lambda e: e.tensor_tensor(out=x2[:, 0:511], in0=xg[:, 0:511], in1=xg[:, 0:511], op=ALU.mult),
                              reads=[xg_b], writes=[x2_b])
                        kb.op("dve", lambda e: e.tensor_scalar(out=x2[:, 0:511], in0=x2[:, 0:511], scalar1=0.044715, scalar2=1.0,
                                                               op0=ALU.mult, op1=ALU.add), reads=[x2_b], writes=[x2_b])
                        kb.op("pool", lambda e: e.tensor_tensor(out=x2[:, 0:511], in0=x2[:, 0:511], in1=xg[:, 0:511], op=ALU.mult),
                              reads=[x2_b, xg_b], writes=[x2_b])
                        kb.op("act", lambda e: e.activation(out=x2[:, 0:511], in_=x2[:, 0:511], func=AF.Sigmoid, scale=1.5957691216),
                              reads=[x2_b], writes=[x2_b])
                        kb.op("dve", lambda e: e.tensor_tensor(out=gl[:, 0:511], in0=x2[:, 0:511], in1=xg[:, 0:511], op=ALU.mult),
                              reads=[x2_b, xg_b], writes=[gl_b])
                        gls.append((gl, gl_b))
                    if kind == 0:
                        pk, pk_b = PSC.next()
                        for hc in range(2):
                            kb.op("pe", lambda e: e.matmul(pk[0:64, 0:511], lhsT=w2[0][:, hc, :], rhs=gls[hc][0][:, 0:511],
                                                           start=(hc == 0), stop=(hc == 1)),
                                  reads=[Wc, gls[hc][1]], writes=[pk_b], inc=(hc == 1))
                        kb.op("act", lambda e: e.activation(out=kcmp[0:64, g, 0:511], in_=pk[0:64, 0:511], func=AF.Copy),
                              reads=[pk_b], writes=[kcmp_b])
                    else:
                        for ct in range(4):
                            pv, pv_b = PSC.next()
                            for hc in range(2):
                                kb.op("pe", lambda e: e.matmul(pv[:, 0:64], lhsT=gls[hc][0][:, ct * 128:(ct + 1) * 128], rhs=w2[1][:, hc, :],
                                                               start=(hc == 0), stop=(hc == 1)),
                                      reads=[Wc, gls[hc][1]], writes=[pv_b], inc=(hc == 1))
                            kb.op("act", lambda e: e.activation(out=vcmp[:, ct, g, 0:64], in_=pv[:, 0:64], func=AF.Copy),
                                  reads=[pv_b], writes=[vcmp_b])
        kb.barrier()
        if getattr(c, "stop_mid", None) == "compress":
            dbg = c.dbg
            kb.dma(dbg["kcmp"][:], kcmp[:], reads=[kcmp_b])
            kb.dma(dbg["vcmp"][:], vcmp[:], reads=[vcmp_b])
            return
        with ExitStack() as es:
            R = Ctx()
            R.SPS = rot_tiles(nc, es, "sps", [128, 512], F32, 3, psum=True)
            R.OPS = rot_tiles(nc, es, "ops", [128, 512], F32, 2, psum=True)
            R.UPS = rot_tiles(nc, es, "ups", [128, 4, 128], F32, 2, psum=True)
            R.TPN = rot_tiles(nc, es, "tpn", [128, 4, 128], BF16, 1, psum=True)
            R.PSB = rot_tiles(nc, es, "psb", [128, 512], BF16, 4)
            KA = rot_tiles(nc, es, "ka", [96, T], BF16, 2).items
            VA = rot_tiles(nc, es, "va", [128, 64, 128], BF16, 2).items
            for va, va_b in VA:
                kb.op("pool", lambda e: e.memset(va[:], 1.0), writes=[va_b])
            QA = [[(sb(nc, es, "qa%d%d" % (pr, hi), [69, 512], BF16), Buf("qa")) for hi in range(4)] for pr in range(2)]
            OC = [[(sb(nc, es, "oc%d%d" % (pr, hi), [64, 512], F32), Buf("oc")) for hi in range(4)] for pr in range(2)]
            QMT = rot_tiles(nc, es, "qmt", [96, 512], BF16, 3)
            GB = rot_tiles(nc, es, "gb", [64, 3, 512], F32, 4)
            NMT = rot_tiles(nc, es, "nmt", [128, 512], BF16, 2)
            IMP = rot_tiles(nc, es, "imp", [128, 4, 128], F32, 2)
            DEN = rot_tiles(nc, es, "den", [128, 256], F32, 2)
            REC = rot_tiles(nc, es, "rec", [128, 256], F32, 2)
            RG = rot_tiles(nc, es, "rg", [64, 512], F32, 2)
            T1 = rot_tiles(nc, es, "t1", [64, 512], F32, 4)
            ACC = rot_tiles(nc, es, "acc", [64, 512], F32, 2)
            OST = rot_tiles(nc, es, "ost", [64, 512], BF16, 3)
            W1 = rot_tiles(nc, es, "w1k", [128, 128], F32, 2)
            W2 = rot_tiles(nc, es, "w2k", [128, 128], F32, 2)
            M8 = rot_tiles(nc, es, "m8", [128, 16], F32, 2)
            NMS = rot_tiles(nc, es, "nms", [128, 128], BF16, 4)
            RS = rot_tiles(nc, es, "rs", [128, 8], F32, 2)
            Gt = ds["G"].tensor
            c.stream = Stream(c, R)

            def norm_rec(O, O_b):
                den, den_b = DEN.next()
                rec, rec_b = REC.next()
                kb.op("dve", lambda e: e.tensor_scalar_max(out=den[0:64, :], in0=O[64:128, 0:256], scalar1=1e-30), reads=[O_b], writes=[den_b])
                kb.op("dve", lambda e: e.tensor_scalar_max(out=den[64:128, :], in0=O[64:128, 256:512], scalar1=1e-30), reads=[O_b], writes=[den_b])
                kb.op("dve", lambda e: e.reciprocal(out=rec[:, :], in_=den[:, :]), reads=[den_b], writes=[rec_b])
                return rec, rec_b

            def mul_rec(eng, out_t, in_t, rec, reads, writes):
                kb.op(eng, lambda e: e.tensor_tensor(out=out_t[0:64, 0:256], in0=in_t[0:64, 0:256], in1=rec[0:64, :], op=ALU.mult), reads=reads, writes=writes)
                kb.op(eng, lambda e: e.tensor_tensor(out=out_t[0:64, 256:512], in0=in_t[0:64, 256:512], in1=rec[64:128, :], op=ALU.mult), reads=reads, writes=writes)

            for g in range(2):
                (ks, ks_b), (kw, kw_b) = KA
                (vs, vs_b), (vw, vw_b) = VA
                kb.dma(ks[0:64, :], ds["KX"][2, g * 64:(g + 1) * 64, :], reads=[dsb["KX"]], writes=[ks_b])
                kb.dma(ks[64:69, :], dc["kaug"][:, :], writes=[ks_b])
                kb.dma(kw[0:64, :], ds["KX"][3, g * 64:(g + 1) * 64, :], reads=[dsb["KX"]], writes=[kw_b])
                kb.dma(kw[64:69, :], dc["kaug"][:, :], writes=[kw_b])
                for half in range(2):
                    hs = slice(half * 32, (half + 1) * 32)
                    rs_ = slice(half * 4096, (half + 1) * 4096)
                    kb.dma(vs[:, hs, 0:64], ds["VSW"][rs_, g * 64:(g + 1) * 64].rearrange("(kt p) d -> p kt d", p=128),
                           reads=[dsb["VSW"]], writes=[vs_b])
                    kb.dma(vw[:, hs, 0:64], ds["VSW"][rs_, 128 + g * 64:128 + (g + 1) * 64].rearrange("(kt p) d -> p kt d", p=128),
                           reads=[dsb["VSW"]], writes=[vw_b])
                CS = {}

                def cmp_begin(qc):
                    CS[qc] = {"imp": IMP.next(), "nmt": NMT.next(), "nms": []}

                def cmp_head(qc, hi, g=g):
                    t0 = qc * 512
                    pr = qc % 2
                    h = 4 * g + hi
                    imp, imp_b = CS[qc]["imp"]
                    qa, qa_b = QA[pr][hi]
                    kb.dma(qa[0:64, :], ds["QN"][h * 64:(h + 1) * 64, t0:t0 + 512], reads=[dsb["QN"]], writes=[qa_b])
                    kb.dma(qa[64:69, :], dc["qaug"][h, :, t0:t0 + 512], writes=[qa_b])
                    O, O_b = R.OPS.next()
                    U, U_b = R.UPS.next()
                    nct = qc // 4 + 1
                    for ct in range(nct):
                        m = qc - 4 * ct
                        qk = [(kcmp[0:69, g, ct * 128:(ct + 1) * 128], qa[0:69, :], [kcmp_b, qa_b])]
                        if m <= 4:
                            qk.append((ident[:, :], pmk[:, m, :], [ident_b, CB]))

                        def pv(P, P_b, ct=ct, nct=nct, O=O, O_b=O_b, U=U, U_b=U_b):
                            kb.op("pe", lambda e: e.matmul(O[:, :], lhsT=vcmp[:, ct, g, :], rhs=P[:, :], start=(ct == 0), stop=(ct == nct - 1)),
                                  reads=[vcmp_b, P_b], writes=[O_b], inc=True)
                            for s_ in range(4):
                                kb.op("pe", lambda e: e.matmul(U[:, s_, :], lhsT=P[:, s_ * 128:(s_ + 1) * 128], rhs=impm[:, ct, :],
                                                               start=(ct == 0), stop=(ct == nct - 1)),
                                      reads=[CB, P_b], writes=[U_b], inc=(s_ == 3))

                        def epi(hi=hi, pr=pr, O=O, O_b=O_b, U=U, U_b=U_b, imp=imp, imp_b=imp_b):
                            rec, rec_b = norm_rec(O, O_b)
                            oc, oc_b = OC[pr][hi]
                            mul_rec("dve", oc, O, rec, [O_b, rec_b], [oc_b])
                            rs, rs_b = RS.next()
                            kb.op("dve", lambda e: e.tensor_reduce(out=rs[:, 0:4], in_=U[:, :, :], axis=AX.X, op=ALU.add),
                                  reads=[U_b], writes=[rs_b])
                            kb.op("dve", lambda e: e.tensor_scalar(out=rs[:, 0:4], in0=rs[:, 0:4], scalar1=1e-30, scalar2=0.5,
                                                                   op0=ALU.max, op1=ALU.mult), reads=[rs_b], writes=[rs_b])
                            kb.op("dve", lambda e: e.reciprocal(out=rs[:, 4:8], in_=rs[:, 0:4]), reads=[rs_b], writes=[rs_b])
                            for s_ in range(4):
                                if hi == 0:
                                    kb.op("dve", lambda e: e.tensor_scalar(out=imp[:, s_, :], in0=U[:, s_, :], scalar1=rs[:, 4 + s_:5 + s_],
                                                                           scalar2=None, op0=ALU.mult), reads=[U_b, rs_b], writes=[imp_b])
                                else:
                                    kb.op("dve", lambda e: e.scalar_tensor_tensor(out=imp[:, s_, :], in0=U[:, s_, :], scalar=rs[:, 4 + s_:5 + s_],
                                                                                  in1=imp[:, s_, :], op0=ALU.mult, op1=ALU.add),
                                          reads=[U_b, rs_b, imp_b], writes=[imp_b])

                        c.stream.push(TileD(qk, 1.0, pv, epi if ct == nct - 1 else None))

                def topk_dve(qc):
                    imp, imp_b = CS[qc]["imp"]
                    for s_ in range(4):
                        qt = 4 * qc + s_
                        lo = 128 - 2 * qt
                        w, w_b = W1.next()
                        w2, w2_b = W2.next()
                        m8, m8_b = M8.next()
                        nms, nms_b = NMS.next()
                        kb.op("pool", lambda e: e.tensor_tensor(out=w[:, :], in0=imp[:, s_, :], in1=keept[:, lo:lo + 128], op=ALU.mult),
                              reads=[imp_b, CB], writes=[w_b])
                        kb.op("pool", lambda e: e.tensor_tensor(out=w[:, :], in0=w[:, :], in1=addt[:, lo:lo + 128], op=ALU.add),
                              reads=[w_b, CB], writes=[w_b])
                        kb.op("pool", lambda e: e.memset(w[:, 0:1], 1e6), reads=[w_b], writes=[w_b])
                        kb.op("dve", lambda e: e.max(out=m8[:, 0:8], in_=w[:, :]), reads=[w_b], writes=[m8_b])
                        kb.op("dve", lambda e: e.match_replace(out=w2[:, :], in_to_replace=m8[:, 0:8], in_values=w[:, :], imm_value=-2.0),
                              reads=[w_b, m8_b], writes=[w2_b])
                        kb.op("dve", lambda e: e.max(out=m8[:, 8:16], in_=w2[:, :]), reads=[w2_b, m8_b], writes=[m8_b])
                        kb.op("dve", lambda e: e.tensor_scalar(out=nms[:, :], in0=w[:, :], scalar1=m8[:, 15:16], scalar2=-1.0,
                                                               op0=ALU.is_ge, op1=ALU.add), reads=[w_b, m8_b], writes=[nms_b])
                        CS[qc]["nms"].append((nms, nms_b))

                def topk_pe(qc):
                    nmt, nmt_b = CS[qc]["nmt"]
                    tpn, tpn_b = R.TPN.next()
                    for s_ in range(4):
                        nms, nms_b = CS[qc]["nms"][s_]
                        kb.op("pe", lambda e: e.transpose(out=tpn[:, s_, :], in_=nms[:, :], identity=ident[:, :]),
                              reads=[nms_b, ident_b], writes=[tpn_b])
                    kb.op("act", lambda e: e.activation(out=nmt[:, :], in_=tpn[:, :, :].rearrange("p a b -> p (a b)"), func=AF.Copy),
                          reads=[tpn_b], writes=[nmt_b])

                def sw_head(qc, hi, g=g):
                    t0 = qc * 512
                    pr = qc % 2
                    h = 4 * g + hi
                    nmt, nmt_b = CS[qc]["nmt"]
                    qa, qa_b = QA[pr][hi]
                    gb, gb_b = GB.next()
                    kb.dma(gb[:, :, :], bass.AP(Gt, h * 3 * T + t0, [[0, 64], [T, 3], [1, 512]]), reads=[dsb["G"]], writes=[gb_b])
                    st = {}
                    O, O_b = R.OPS.next()
                    nk = 4 * qc + 4
                    k0 = max(0, int((t0 - 127 - ALIBI_CUT * (2.0 ** (h + 1))) // 128) + 1)
                    for kt in range(k0, nk):
                        d = kt - 4 * qc
                        lo = 128 * d if d > 0 else 0
                        qk = [(ks[0:69, kt * 128:(kt + 1) * 128], qa[0:69, lo:512], [ks_b, qa_b]),
                              (eexp[:, kt, :], nmt[:, lo:512], [CB, nmt_b])]
                        if d >= 0:
                            qk.append((ident[:, :], cmk[:, d, lo:lo + 128], [ident_b, CB], lo, lo + 128))

                        def pv(P, P_b, kt=kt, nk=nk, k0=k0, O=O, O_b=O_b, lo=lo):
                            kb.op("pe", lambda e: e.matmul(O[:, lo:512], lhsT=vs[:, kt, :], rhs=P[:, lo:512], start=(kt == k0), stop=(kt == nk - 1),
                                                           skip_group_check=True),
                                  reads=[vs_b, P_b], writes=[O_b], inc=True)

                        def epi(O=O, O_b=O_b, gb=gb, gb_b=gb_b, st=st):
                            rec, rec_b = norm_rec(O, O_b)
                            t1, t1_b = T1.next()
                            mul_rec("dve", t1, O, rec, [O_b, rec_b], [t1_b])
                            kb.op("pool", lambda e: e.tensor_tensor(out=t1[:, :], in0=t1[:, :], in1=gb[:, 1, :], op=ALU.mult),
                                  reads=[t1_b, gb_b], writes=[t1_b])
                            st["slc"] = (t1, t1_b)

                        c.stream.push(TileD(qk, 1.0, pv, epi if kt == nk - 1 else None, lo=lo, hi=512))
                    O, O_b = R.OPS.next()
                    wl = []
                    for d in range(4):
                        wl.append((4 * qc + d, cmk[:, d, 128 * d:128 * d + 128], 128 * d, 512, 128 * d))
                    for d in range(4):
                        kt = 4 * qc - 4 + d
                        if kt >= 0:
                            wl.append((kt, wmk[:, d, 128 * d:128 * d + 128], 0, 128 * d + 128, 128 * d))
                    nw = len(wl)
                    for wi, (kt, mk, wlo, whi, mlo) in enumerate(wl):
                        qk = [(kw[0:69, kt * 128:(kt + 1) * 128], qa[0:69, wlo:whi], [kw_b, qa_b]),
                              (ident[:, :], mk, [ident_b, CB], mlo, mlo + 128)]

                        def pv(P, P_b, kt=kt, wi=wi, nw=nw, O=O, O_b=O_b, wlo=wlo, whi=whi):
                            kb.op("pe", lambda e: e.matmul(O[:, wlo:whi], lhsT=vw[:, kt, :], rhs=P[:, wlo:whi], start=(wi == 0), stop=(wi == nw - 1),
                                                           skip_group_check=True),
                                  reads=[vw_b, P_b], writes=[O_b], inc=True)

                        def epi(h=h, hi=hi, pr=pr, t0=t0, O=O, O_b=O_b, gb=gb, gb_b=gb_b, st=st):
                            rec, rec_b = norm_rec(O, O_b)
                            t2, t2_b = T1.next()
                            mul_rec("dve", t2, O, rec, [O_b, rec_b], [t2_b])
                            kb.op("pool", lambda e: e.tensor_tensor(out=t2[:, :], in0=t2[:, :], in1=gb[:, 2, :], op=ALU.mult),
                                  reads=[t2_b, gb_b], writes=[t2_b])
                            t1, t1_b = st["slc"]
                            oc, oc_b = OC[pr][hi]
                            acc, acc_b = ACC.next()
                            ost, ost_b = OST.next()
                            kb.op("pool", lambda e: e.tensor_tensor(out=acc[:, :], in0=oc[:, :], in1=gb[:, 0, :], op=ALU.mult),
                                  reads=[oc_b, gb_b], writes=[acc_b])
                            kb.op("pool", lambda e: e.tensor_tensor(out=acc[:, :], in0=acc[:, :], in1=t1[:, :], op=ALU.add),
                                  reads=[acc_b, t1_b], writes=[acc_b])
                            kb.op("pool", lambda e: e.tensor_tensor(out=ost[:, :], in0=acc[:, :], in1=t2[:, :], op=ALU.add),
                                  reads=[acc_b, t2_b], writes=[ost_b])
                            kb.dma(ds["ON"][h * 64:(h + 1) * 64, t0:t0 + 512], ost[:, :], reads=[ost_b], writes=[dsb["ON"]], q="pool")

                        c.stream.push(TileD(qk, 1.0, pv, epi if wi == nw - 1 else None, lo=wlo, hi=whi))

                cmp_begin(0)
                for hi in range(4):
                    cmp_head(0, hi)
                c.stream.flush()
                topk_dve(0)
                topk_pe(0)
                for qc in range(NCH):
                    nxt = qc + 1 < NCH
                    if nxt:
                        cmp_begin(qc + 1)
                        cmp_head(qc + 1, 0)
                        cmp_head(qc + 1, 1)
                    sw_head(qc, 0)
                    if nxt:
                        cmp_head(qc + 1, 2)
                        cmp_head(qc + 1, 3)
                    sw_head(qc, 1)
                    sw_head(qc, 2)
                    if nxt:
                        topk_dve(qc + 1)
                    sw_head(qc, 3)
                    c.stream.flush()
                    if nxt:
                        topk_pe(qc + 1)
                    CS.pop(qc)
            c.stream.flush()
            if getattr(c, "stop_mid", None) == "nsa":
                return
            wst = sb(nc, es, "wst", [128, 1408], F32)
            wst_b = Buf("wst")
            wsb = sb(nc, es, "wsb", [128, 1408], BF16)
            wsb_b = Buf("wsb")
            wpieces = [(ft, a, hf) for ft in range(8) for a in range(2) for hf in range(2)]

            def wup_piece(ft, a, hf):
                c0 = a * DFF + hf * 1408
                kb.dma(wst[:, :], din["w_up"][ft * 128:(ft + 1) * 128, c0:c0 + 1408], writes=[wst_b])
                kb.op("pool", lambda e: e.tensor_copy(out=wsb[:, :], in_=wst[:, :]), reads=[wst_b], writes=[wsb_b])
                kb.dma(ds["WUP"][hf * 11:(hf + 1) * 11, :, a, ft, :].rearrange("j p c -> p j c"),
                       wsb[:, :].rearrange("p (j c) -> p j c", c=128), reads=[wsb_b], writes=[dsb["WUP"]], q="pool")

            for h in range(8):
                ka, ka_b = KA[h % 2]
                va, va_b = VA[h % 2]
                kb.dma(ka[0:64, :], ds["KN"][h * 64:(h + 1) * 64, :], reads=[dsb["KN"]], writes=[ka_b])
                kb.dma(ka[64:96, :], ds["KPE"][:, :], reads=[dsb["KPE"]], writes=[ka_b])
                for half in range(2):
                    hs = slice(half * 32, (half + 1) * 32)
                    rs_ = slice(half * 4096, (half + 1) * 4096)
                    kb.dma(va[:, hs, 0:64], ds["VM"][rs_, h * 64:(h + 1) * 64].rearrange("(kt p) d -> p kt d", p=128),
                           reads=[dsb["VM"]], writes=[va_b])
                qms = {}

                def qload(qc, h=h, qms=qms):
                    qm, qm_b = QMT.next()
                    kb.dma(qm[:, :], ds["QM"][h, :, qc * 512:(qc + 1) * 512], reads=[dsb["QM"]], writes=[qm_b])
                    qms[qc] = (qm, qm_b)

                qload(0)
                for qc in range(NCH):
                    t0 = qc * 512
                    if qc + 1 < NCH:
                        qload(qc + 1)
                    qm, qm_b = qms.pop(qc)
                    O, O_b = R.OPS.next()
                    nk = 4 * qc + 4
                    for kt in range(nk):
                        d = kt - 4 * qc
                        lo = 128 * d if d > 0 else 0
                        qk = [(ka[0:96, kt * 128:(kt + 1) * 128], qm[0:96, lo:512], [ka_b, qm_b])]
                        if d >= 0:
                            qk.append((ident[:, :], cmk[:, d, lo:lo + 128], [ident_b, CB], lo, lo + 128))

                        def pv(P, P_b, kt=kt, nk=nk, O=O, O_b=O_b, va=va, va_b=va_b, lo=lo):
                            kb.op("pe", lambda e: e.matmul(O[:, lo:512], lhsT=va[:, kt, :], rhs=P[:, lo:512], start=(kt == 0), stop=(kt == nk - 1),
                                                           skip_group_check=True),
                                  reads=[va_b, P_b], writes=[O_b], inc=True)

                        def epi(h=h, t0=t0, O=O, O_b=O_b):
                            rec, rec_b = norm_rec(O, O_b)
                            ost, ost_b = OST.next()
                            mul_rec("dve", ost, O, rec, [O_b, rec_b], [ost_b])
                            kb.dma(ds["OM"][h * 64:(h + 1) * 64, t0:t0 + 512], ost[:, :], reads=[ost_b], writes=[dsb["OM"]], q="pool")

                        c.stream.push(TileD(qk, MLA_SCALE, pv, epi if kt == nk - 1 else None, lo=lo, hi=512))
                    if wpieces:
                        wup_piece(*wpieces.pop(0))
            c.stream.flush()

CH3 = 256
NCH3 = T // CH3


def phase3(c):
    nc, kb = c.nc, c.kb
    din, dc, ds, dsb = c.din, c.dc, c.ds, c.dsb
    ident, ident_b = c.ident, c.ident_b
    with ExitStack() as es:
        W = Buf("p3w")
        wo = sb(nc, es, "wo", [128, 8, 1024], BF16)
        wpg = sb(nc, es, "wpg", [128, 8, 1024], BF16)
        wpl = sb(nc, es, "wpl", [128, 2, 1024], BF16)
        wd = sb(nc, es, "wd", [128, 22, 1024], BF16)
        lnv = sb(nc, es, "lnv", [128, 6, 1024], F32)
        for i, nm in enumerate(("ln1_g", "ln1_b", "ln2_g", "ln2_b", "ln3_g", "ln3_b")):
            kb.dma(lnv[:, i, :], din[nm][0:1, :].partition_broadcast(128), writes=[W])
        with ExitStack() as es1:
            STG = rot_tiles(nc, es1, "stg3", [128, 1024], F32, 4)
            k = 0
            for name, dst in (("w_out", wo), ("w_ple_gate", wpg)):
                for ft in range(8):
                    s_, s_b = STG.next()
                    kb.dma(s_[:, 0:1024], din[name][ft * 128:(ft + 1) * 128, :], writes=[s_b])
                    kb.op("dve" if k % 2 else "pool", lambda e: e.tensor_copy(out=dst[:, ft, :], in_=s_[:, 0:1024]), reads=[s_b], writes=[W])
                    k += 1
            for kt in range(2):
                s_, s_b = STG.next()
                kb.dma(s_[:, 0:1024], din["w_ple"][kt * 128:(kt + 1) * 128, :], writes=[s_b])
                kb.op("dve", lambda e: e.tensor_copy(out=wpl[:, kt, :], in_=s_[:, 0:1024]), reads=[s_b], writes=[W])
            for j in range(22):
                s_, s_b = STG.next()
                kb.dma(s_[:, 0:1024], din["w_down"][j * 128:(j + 1) * 128, :], writes=[s_b])
                kb.op("dve" if j % 2 else "pool", lambda e: e.tensor_copy(out=wd[:, j, :], in_=s_[:, 0:1024]), reads=[s_b], writes=[W])
        kb.barrier()
        XF = rot_tiles(nc, es, "xf3", [128, 1024], F32, 1)
        OT = rot_tiles(nc, es, "oT3", [128, 8, CH3], BF16, 1)
        PF = rot_tiles(nc, es, "pf3", [128, 2, 256], F32, 1)
        PB = rot_tiles(nc, es, "pb3", [128, 2, 256], BF16, 3)
        PT = rot_tiles(nc, es, "pT3", [128, 2, CH3], BF16, 1)
        Y = rot_tiles(nc, es, "y3", [128, 1024], F32, 2)
        Z = rot_tiles(nc, es, "z3", [128, 1024], F32, 1)
        X1 = rot_tiles(nc, es, "x1r", [128, 2, 1024], F32, 2)
        X2 = rot_tiles(nc, es, "x2r", [128, 2, 1024], F32, 2)
        XBF = rot_tiles(nc, es, "xbf3", [128, 1024], BF16, 2)
        X1T = rot_tiles(nc, es, "x1T", [128, 8, CH3], BF16, 2)
        X2T = rot_tiles(nc, es, "x2T", [128, 8, CH3], BF16, 2)
        AT = rot_tiles(nc, es, "aT", [128, 22, CH3], BF16, 1)
        WU = rot_tiles(nc, es, "wu", [128, 2, 8, 128], BF16, 3)
        SG = rot_tiles(nc, es, "sg", [128, CH3], F32, 2)
        ST = rot_tiles(nc, es, "st3", [128, 16], F32, 6)
        PA = rot_tiles(nc, es, "pa3", [128, 512], F32, 3, psum=True)
        PG = rot_tiles(nc, es, "pg3", [128, 2, CH3], F32, 4, psum=True)
        TP = rot_tiles(nc, es, "tp3", [128, 8, 128], BF16, 1, psum=True)

        epsv = sb(nc, es, "epsv", [128, 2], F32)
        kb.op("pool", lambda e: e.memset(epsv[:, 0:1], LN_EPS), writes=[W])
        kb.op("pool", lambda e: e.memset(epsv[:, 1:2], 4.0 * LN_EPS), writes=[W])

        def ln_pair_steps(items, st, st_b, gi, eps_col=0):
            def s1():
                for k, it_ in enumerate(items):
                    j_ap, j_b = it_.get("junk") or (it_["out_ap"], it_["out_b"])
                    y, y_b = it_["y"], it_["y_b"]
                    kb.op("act", lambda e: e.activation(out=j_ap, in_=y[:, :], func=AF.Square, accum_out=st[:, 8 * k + 2:8 * k + 3]),
                          reads=[y_b], writes=[j_b, st_b])

            def s2():
                for k in range(len(items)):
                    b = 8 * k
                    kb.op("dve", lambda e: e.tensor_scalar(out=st[:, b + 3:b + 4], in0=st[:, b:b + 1], scalar1=st[:, b + 1:b + 2], scalar2=1.0 / D,
                                                           op0=ALU.add, op1=ALU.mult), reads=[st_b], writes=[st_b])
                    kb.op("dve", lambda e: e.scalar_tensor_tensor(out=st[:, b + 4:b + 5], in0=st[:, b + 3:b + 4], scalar=-1.0, in1=st[:, b + 3:b + 4],
                                                                  op0=ALU.mult, op1=ALU.mult), reads=[st_b], writes=[st_b])
                    kb.op("dve", lambda e: e.scalar_tensor_tensor(out=st[:, b + 5:b + 6], in0=st[:, b + 2:b + 3], scalar=1.0 / D, in1=st[:, b + 4:b + 5],
                                                                  op0=ALU.mult, op1=ALU.add), reads=[st_b], writes=[st_b])

            def s3():
                n = len(items)
                src = st[:, 0:8 * n].rearrange("p (k c) -> p k c", c=8)[:, :, 5:6]
                dst = st[:, 0:8 * n].rearrange("p (k c) -> p k c", c=8)[:, :, 6:7]
                kb.op("act", lambda e: e.activation(out=dst, in_=src, func=AF.Sqrt, bias=epsv[:, eps_col:eps_col + 1]), reads=[st_b, W], writes=[st_b])
                src2 = st[:, 0:8 * n].rearrange("p (k c) -> p k c", c=8)[:, :, 6:7]
                dst2 = st[:, 0:8 * n].rearrange("p (k c) -> p k c", c=8)[:, :, 7:8]
                kb.op("dve", lambda e: e.reciprocal(out=dst2, in_=src2), reads=[st_b], writes=[st_b])

            def s4(k):
                def f():
                    it_ = items[k]
                    b = 8 * k
                    y, y_b = it_["y"], it_["y_b"]
                    z, z_b = Z.next()
                    kb.op("dve", lambda e: e.scalar_tensor_tensor(out=z[:, :], in0=y[:, :], scalar=st[:, b + 3:b + 4], in1=lnv[:, gi, :],
                                                                  op0=ALU.subtract, op1=ALU.mult), reads=[y_b, st_b, W], writes=[z_b])
                    kb.op("dve", lambda e: e.scalar_tensor_tensor(out=it_["out_ap"], in0=z[:, :], scalar=st[:, b + 7:b + 8], in1=lnv[:, gi + 1, :],
                                                                  op0=ALU.mult, op1=ALU.add), reads=[z_b, st_b, W], writes=[it_["out_b"]])
                    if it_.get("post") is not None:
                        it_["post"]()
                return f

            return [s1, s2, s3, s4(0), s4(1)]

        def cast_bf(src, src_b):
            xb, xb_b = XBF.next()
            kb.op("pool", lambda e: e.tensor_copy(out=xb[:, :], in_=src), reads=[src_b], writes=[xb_b])
            return xb, xb_b

        def transposes(xb, xb_b, s_, dstT, dstT_b):
            tp, tp_b = TP.next()
            for ft in range(8):
                kb.op("pe", lambda e: e.transpose(out=tp[:, ft, :], in_=xb[:, ft * 128:(ft + 1) * 128], identity=ident[:, :]),
                      reads=[xb_b, ident_b], writes=[tp_b], inc=(ft == 7))
            kb.op("act", lambda e: e.activation(out=dstT[:, :, s_ * 128:(s_ + 1) * 128], in_=tp[:, :, :], func=AF.Copy), reads=[tp_b], writes=[dstT_b])

        S = {}

        def A_mm(ch):
            t0 = ch * CH3
            st = S.setdefault(ch, {})
            oT, oT_b = OT.next()
            pf, pf_b = PF.next()
            kb.dma(oT[:, 0:4, :], ds["ON"].rearrange("(c p) t -> p c t", p=128)[:, :, t0:t0 + CH3], reads=[dsb["ON"]], writes=[oT_b])
            kb.dma(oT[:, 4:8, :], ds["OM"].rearrange("(c p) t -> p c t", p=128)[:, :, t0:t0 + CH3], reads=[dsb["OM"]], writes=[oT_b])
            kb.dma(pf[:, :, :], din["p"][t0:t0 + CH3, :].rearrange("(s p) d -> p s d", p=128), writes=[pf_b])
            pb_, pb_b = PB.next()
            kb.op("pool", lambda e: e.tensor_copy(out=pb_[:, :, :], in_=pf[:, :, :]), reads=[pf_b], writes=[pb_b])
            st["pb"] = (pb_, pb_b)
            st["x1"] = X1.next()
            st["x1T"] = X1T.next()
            st["ys"] = []
            sst, sst_b = ST.next()
            st["sst"] = (sst, sst_b)
            for s_ in range(2):
                y, y_b = Y.next()
                xf, xf_b = XF.next()
                kb.dma(xf[:, :], din["x"][t0 + s_ * 128:t0 + (s_ + 1) * 128, :], writes=[xf_b])
                for n in range(2):
                    pa, pa_b = PA.next()
                    for kt in range(8):
                        kb.op("pe", lambda e: e.matmul(pa[:, :], lhsT=oT[:, kt, s_ * 128:(s_ + 1) * 128], rhs=wo[:, kt, n * 512:(n + 1) * 512],
                                                       start=(kt == 0), stop=(kt == 7)), reads=[oT_b, W], writes=[pa_b], inc=(kt == 7))
                    kb.op("dve", lambda e: e.scalar_tensor_tensor(out=y[:, n * 512:(n + 1) * 512], in0=xf[:, n * 512:(n + 1) * 512], scalar=ALPHA,
                                                                  in1=pa[:, :], op0=ALU.mult, op1=ALU.add, accum_out=sst[:, 8 * s_ + n:8 * s_ + n + 1]),
                          reads=[xf_b, pa_b], writes=[y_b, sst_b])
                st["ys"].append((y, y_b))

        def A_ln(ch):
            st = S[ch]
            x1, x1_b = st["x1"]
            sst, sst_b = st["sst"]
            items = []
            for s_ in range(2):
                y, y_b = st["ys"][s_]

                def post(s_=s_):
                    st["xb%d" % s_] = cast_bf(x1[:, s_, :], x1_b)

                items.append(dict(y=y, y_b=y_b, out_ap=x1[:, s_, :], out_b=x1_b, post=post))
            return ln_pair_steps(items, sst, sst_b, 0)

        def A_tr(ch, s_):
            st = S[ch]
            x1T, x1T_b = st["x1T"]
            transposes(*st["xb%d" % s_], s_, x1T, x1T_b)

        def B_up(ch, j):
            st = S[ch]
            x1T, x1T_b = st["x1T"]
            if j == 0:
                st["aT"] = AT.next()
            aT, aT_b = st["aT"]
            wu, wu_b = WU.next()
            kb.dma(wu[:, :, :, :], ds["WUP"][j], reads=[dsb["WUP"]], writes=[wu_b])
            pg, pg_b = PG.next()
            for a in range(2):
                for ft in range(8):
                    kb.op("pe", lambda e: e.matmul(pg[:, a, :], lhsT=wu[:, a, ft, :], rhs=x1T[:, ft, :], start=(ft == 0), stop=(ft == 7)),
                          reads=[wu_b, x1T_b], writes=[pg_b], inc=(a == 1 and ft == 7))
            sg, sg_b = SG.next()
            kb.op("act", lambda e: e.activation(out=sg[:, :], in_=pg[:, 0, :], func=AF.Silu), reads=[pg_b], writes=[sg_b])
            kb.op("dve", lambda e: e.tensor_tensor(out=aT[:, j, :], in0=pg[:, 1, :], in1=sg[:, :], op=ALU.mult), reads=[pg_b, sg_b], writes=[aT_b])

        def B_down(ch):
            st = S[ch]
            aT, aT_b = st["aT"]
            x1, x1_b = st["x1"]
            st["x2"] = X2.next()
            st["x2T"] = X2T.next()
            x2, x2_b = st["x2"]
            ys = []
            sst, sst_b = ST.next()
            st["sst2"] = (sst, sst_b)
            for s_ in range(2):
                y, y_b = Y.next()
                for n in range(2):
                    pa, pa_b = PA.next()
                    for j in range(22):
                        kb.op("pe", lambda e: e.matmul(pa[:, :], lhsT=aT[:, j, s_ * 128:(s_ + 1) * 128], rhs=wd[:, j, n * 512:(n + 1) * 512],
                                                       start=(j == 0), stop=(j == 21)), reads=[aT_b, W], writes=[pa_b], inc=(j == 21))
                    kb.op("dve", lambda e: e.scalar_tensor_tensor(out=y[:, n * 512:(n + 1) * 512], in0=x1[:, s_, n * 512:(n + 1) * 512], scalar=ALPHA,
                                                                  in1=pa[:, :], op0=ALU.mult, op1=ALU.add, accum_out=sst[:, 8 * s_ + n:8 * s_ + n + 1]),
                          reads=[x1_b, pa_b], writes=[y_b, sst_b])
                ys.append((y, y_b))
            st["ys2"] = ys

        def B_ln(ch):
            st = S[ch]
            x2, x2_b = st["x2"]
            sst, sst_b = st["sst2"]
            items = []
            for s_ in range(2):
                y, y_b = st["ys2"][s_]

                def post(s_=s_):
                    st["x2b%d" % s_] = cast_bf(x2[:, s_, :], x2_b)

                items.append(dict(y=y, y_b=y_b, out_ap=x2[:, s_, :], out_b=x2_b, post=post))
            return ln_pair_steps(items, sst, sst_b, 2)

        def B_tr(ch, s_):
            st = S[ch]
            x2T, x2T_b = st["x2T"]
            transposes(*st["x2b%d" % s_], s_, x2T, x2T_b)

        def C_pre(ch):
            st = S[ch]
            pb_, pb_b = st["pb"]
            pT, pT_b = PT.next()
            tp, tp_b = TP.next()
            for s_ in range(2):
                for kt in range(2):
                    kb.op("pe", lambda e: e.transpose(out=tp[:, s_ * 2 + kt, :], in_=pb_[:, s_, kt * 128:(kt + 1) * 128], identity=ident[:, :]),
                          reads=[pb_b, ident_b], writes=[tp_b], inc=(s_ == 1 and kt == 1))
            for s_ in range(2):
                kb.op("act", lambda e: e.activation(out=pT[:, :, s_ * 128:(s_ + 1) * 128], in_=tp[:, 2 * s_:2 * s_ + 2, :], func=AF.Copy),
                      reads=[tp_b], writes=[pT_b])
            st["pT"] = (pT, pT_b)

        def C_mm(ch, s_):
            st = S[ch]
            x2, x2_b = st["x2"]
            x2T, x2T_b = st["x2T"]
            pT, pT_b = st["pT"]
            if s_ == 0:
                st["sst3"] = ST.next()
                st["ys3"] = []
            sst, sst_b = st["sst3"]
            y, y_b = Y.next()
            for n in range(2):
                pa, pa_b = PA.next()
                for kt in range(8):
                    kb.op("pe", lambda e: e.matmul(pa[:, :], lhsT=x2T[:, kt, s_ * 128:(s_ + 1) * 128], rhs=wpg[:, kt, n * 512:(n + 1) * 512],
                                                   start=(kt == 0), stop=(kt == 7)), reads=[x2T_b, W], writes=[pa_b], inc=(kt == 7))
                kb.op("act", lambda e: e.activation(out=y[:, n * 512:(n + 1) * 512], in_=pa[:, :], func=AF.Tanh, scale=0.5), reads=[pa_b], writes=[y_b])
                pp, pp_b = PA.next()
                for kt in range(2):
                    kb.op("pe", lambda e: e.matmul(pp[:, :], lhsT=pT[:, kt, s_ * 128:(s_ + 1) * 128], rhs=wpl[:, kt, n * 512:(n + 1) * 512],
                                                   start=(kt == 0), stop=(kt == 1)), reads=[pT_b, W], writes=[pp_b], inc=(kt == 1))
                kb.op("dve", lambda e: e.scalar_tensor_tensor(out=y[:, n * 512:(n + 1) * 512], in0=y[:, n * 512:(n + 1) * 512], scalar=1.0, in1=pp[:, :],
                                                              op0=ALU.add, op1=ALU.mult), reads=[pp_b, y_b], writes=[y_b])
                kb.op("dve", lambda e: e.scalar_tensor_tensor(out=y[:, n * 512:(n + 1) * 512], in0=x2[:, s_, n * 512:(n + 1) * 512], scalar=2.0 * ALPHA,
                                                              in1=y[:, n * 512:(n + 1) * 512], op0=ALU.mult, op1=ALU.add,
                                                              accum_out=sst[:, 8 * s_ + n:8 * s_ + n + 1]),
                      reads=[x2_b, y_b], writes=[y_b, sst_b])
            st["ys3"].append((y, y_b))

        def C_ln(ch):
            t0 = ch * CH3
            st = S[ch]
            sst, sst_b = st["sst3"]
            zj, zj_b = Z.items[0]
            items = []
            for s_ in range(2):
                y, y_b = st["ys3"][s_]

                def post(s_=s_, y=y, y_b=y_b):
                    kb.dma(c.out[t0 + s_ * 128:t0 + (s_ + 1) * 128, :], y[:, :], reads=[y_b], writes=[c.out_b], q="pool")
                    if s_ == 1:
                        S.pop(ch)

                items.append(dict(y=y, y_b=y_b, out_ap=y[:, :], out_b=y_b, post=post, junk=(zj[:, :], zj_b)))
            return ln_pair_steps(items, sst, sst_b, 4, eps_col=1)

        def run_all(steps):
            for f in steps:
                f()

        A_mm(0)
        run_all(A_ln(0))
        A_tr(0, 0)
        A_tr(0, 1)
        sched = {
            0: [("Bln", 0)], 1: [("Bln", 1)], 2: [("Bln", 2)], 3: [("Bln", 3)], 4: [("Bln", 4)], 5: [("Btr", 0)], 6: [("Btr", 1), ("Amm", 0)],
            7: [("Aln", 0)], 8: [("Aln", 1), ("Cpre", 0)], 9: [("Aln", 2)], 10: [("Aln", 3), ("Cmm", 0)], 11: [("Aln", 4)], 12: [("Cmm", 1)],
            13: [("Cln", 0), ("Atr", 0)], 14: [("Cln", 1)], 15: [("Cln", 2), ("Atr", 1)], 16: [("Cln", 3)], 17: [("Cln", 4)],
        }
        chains = {}

        def run_slot(it, j):
            nxt = it + 1 < NCH3
            prv = it >= 1
            for name, k in sched.get(j, []):
                if name == "Bln" and prv:
                    if k == 0:
                        chains["B"] = B_ln(it - 1)
                    chains["B"][k]()
                elif name == "Btr" and prv:
                    B_tr(it - 1, k)
                elif name == "Amm" and nxt:
                    A_mm(it + 1)
                elif name == "Aln" and nxt:
                    if k == 0:
                        chains["A"] = A_ln(it + 1)
                    chains["A"][k]()
                elif name == "Atr" and nxt:
                    A_tr(it + 1, k)
                elif name == "Cpre" and prv:
                    C_pre(it - 1)
                elif name == "Cmm" and prv:
                    C_mm(it - 1, k)
                elif name == "Cln" and prv:
                    if k == 0:
                        chains["C"] = C_ln(it - 1)
                    chains["C"][k]()

        for it in range(NCH3 + 1):
            if it < NCH3:
                for j in range(22):
                    B_up(it, j)
                    run_slot(it, j)
                B_down(it)
            else:
                for j in range(22):
                    run_slot(it, j)


_CACHE = {}


def kernel(**inputs):
    if "nc" not in _CACHE:
        _CACHE["nc"] = build_program()
        _CACHE["consts"] = host_constants()
    nc = _CACHE["nc"]
    consts = _CACHE["consts"]
    B = inputs["x"].shape[0]
    shared = {}
    for k, shp in IN_SPECS.items():
        if k in ("x", "p"):
            continue
        shared[k] = np.ascontiguousarray(np.asarray(inputs[k], dtype=np.float32)[0].reshape(shp))
    for k, v in consts.items():
        shared["c_" + k] = v
    in_maps = []
    for b in range(B):
        m = dict(shared)
        m["x"] = np.ascontiguousarray(np.asarray(inputs["x"], dtype=np.float32)[b])
        m["p"] = np.ascontiguousarray(np.asarray(inputs["p"], dtype=np.float32)[0, b])
        in_maps.append(m)
    res = run_bass_kernel_spmd(nc, in_maps, core_ids=list(range(B)))
    return np.stack([np.asarray(r["out"], dtype=np.float32) for r in res.results], 0)
```

```python
from contextlib import ExitStack
import numpy as np
import ml_dtypes
import concourse.bass as bass
import concourse.mybir as mybir
from concourse.bass_utils import run_bass_kernel_spmd

F32 = mybir.dt.float32
BF16 = mybir.dt.bfloat16
AF = mybir.ActivationFunctionType
ALU = mybir.AluOpType
AX = mybir.AxisListType
bf16 = ml_dtypes.bfloat16

T = 8192
D = 1024
NCH = T // 512
NKT = T // 128
DFF = 2816
BIG = 30000.0
ALPHA = 2.0 ** 0.25
LN_EPS = 1e-5
RMS_EPS = 1e-6
MLA_SCALE = 96.0 ** -0.5
ALIBI_CUT = 200.0


class Buf:
    __slots__ = ("name", "last_w", "readers")

    def __init__(self, name=""):
        self.name = name
        self.last_w = None
        self.readers = {}


class KB:
    NSLOT = 8

    def __init__(self, nc, es):
        self.nc = nc
        self.E = {"pe": nc.tensor, "act": nc.scalar, "dve": nc.vector, "pool": nc.gpsimd, "sp": nc.sync}
        self.sem = {}
        self.seq = {}
        self.known = {}
        for e in self.E:
            self.sem[("e", e)] = es.enter_context(nc.semaphore("s_" + e))
            self.seq[e] = 0
            self.known[e] = {}
        self.slots = {}
        self.slot_rr = {}
        for q in ("sp", "pool"):
            self.slots[q] = []
            self.slot_rr[q] = 0
            for i in range(self.NSLOT):
                key = ("d", q, i)
                self.sem[key] = es.enter_context(nc.semaphore("d_%s%d" % (q, i)))
                self.slots[q].append([key, 0])

    def _collect(self, reads, writes):
        deps = {}
        for b in reads:
            if b.last_w is not None:
                k, v = b.last_w
                o = deps.get(k)
                deps[k] = (max(v, o[0]) if o else v, True)
        for b in writes:
            if b.last_w is not None:
                k, v = b.last_w
                o = deps.get(k)
                if o is None:
                    deps[k] = (v, False)
                elif v > o[0]:
                    deps[k] = (v, o[1])
            for k, v in b.readers.items():
                o = deps.get(k)
                if o is None:
                    deps[k] = (v, False)
                elif v > o[0]:
                    deps[k] = (v, o[1])
        return deps

    def _wait(self, eng, deps):
        kn = self.known[eng]
        for k, (v, raw) in deps.items():
            if k == ("e", eng):
                if eng == "pe" or not raw:
                    continue
            if kn.get(k, 0) >= v:
                continue
            self.E[eng].wait_ge(self.sem[k], v)
            kn[k] = v

    def _record(self, dep, reads, writes):
        k, v = dep
        for b in reads:
            if b.readers.get(k, 0) < v:
                b.readers[k] = v
        for b in writes:
            b.last_w = dep
            b.readers = {}

    def op(self, eng, fn, reads=(), writes=(), inc=True):
        self._wait(eng, self._collect(reads, writes))
        ins = fn(self.E[eng])
        if inc:
            ins.then_inc(self.sem[("e", eng)], 1)
            self.seq[eng] += 1
            dep = (("e", eng), self.seq[eng])
        else:
            dep = (("e", eng), self.seq[eng] + 1)
        self._record(dep, reads, writes)
        return ins

    def dma(self, out, in_, reads=(), writes=(), q="sp", **kw):
        sl = self.slots[q][self.slot_rr[q]]
        self.slot_rr[q] = (self.slot_rr[q] + 1) % self.NSLOT
        deps = self._collect(reads, writes)
        if sl[1] > 0:
            o = deps.get(sl[0])
            if o is None or o[0] < sl[1]:
                deps[sl[0]] = (sl[1], True)
        self._wait(q, deps)
        self.E[q].dma_start(out=out, in_=in_, **kw).then_inc(self.sem[sl[0]], 16)
        sl[1] += 16
        self._record((sl[0], sl[1]), reads, writes)

    def barrier(self):
        cur = {}
        for e in self.E:
            cur[("e", e)] = self.seq[e]
        for q in self.slots:
            for key, cnt in self.slots[q]:
                cur[key] = cnt
        for e in self.E:
            kn = self.known[e]
            for k, v in cur.items():
                if v == 0 or k == ("e", e):
                    continue
                if kn.get(k, 0) >= v:
                    continue
                self.E[e].wait_ge(self.sem[k], v)
                kn[k] = v


class Rot:
    def __init__(self, items):
        self.items = items
        self.i = 0

    def next(self):
        it = self.items[self.i]
        self.i = (self.i + 1) % len(self.items)
        return it


def host_constants():
    c = {}
    c["ident"] = np.eye(128, dtype=np.float32).astype(bf16)
    pos = np.arange(T, dtype=np.float32)
    inv_freq = (10000.0 ** (-np.arange(16, dtype=np.float32) / 16)).astype(np.float32)
    ang = (pos[:, None] * inv_freq[None, :]).astype(np.float32)
    cos = np.cos(ang).astype(np.float32).T
    sin = np.sin(ang).astype(np.float32).T
    c["ropeC"] = np.ascontiguousarray(np.concatenate([cos, cos], 0))
    c["ropeS"] = np.ascontiguousarray(np.concatenate([-sin, sin], 0))
    p = np.arange(128)[:, None]
    j = np.arange(512)[None, :]
    def addmask(valid):
        return np.where(valid, 0.0, -BIG).astype(np.float32).astype(bf16)
    c["cmk"] = addmask(np.stack([(p + 128 * d <= j) for d in range(4)], 1))
    c["wmk"] = addmask(np.stack([(j < 128 * d + p) for d in range(4)], 1))
    c["pmk"] = addmask(np.stack([(16 * p + 31 <= 512 * m + j) for m in range(5)], 1))
    slopes = 2.0 ** (-np.arange(1, 9, dtype=np.float64))
    t = np.arange(T)
    qaug = np.zeros((8, 5, T), np.float32)
    for h in range(8):
        s = slopes[h]
        qaug[h, 0] = s
        qaug[h, 1] = s
        qaug[h, 2] = -s * (t // 128) * 128
        qaug[h, 3] = -s * (t % 128)
        qaug[h, 4] = -s * ((t % 128) - 31)
    c["qaug"] = qaug.astype(bf16)
    ka = np.zeros((5, T), np.float32)
    ka[0] = (t // 128) * 128
    ka[1] = t % 128
    ka[2] = 1
    ka[3] = 1
    c["kaug"] = ka.astype(bf16)
    cc = np.arange(512)
    kc = np.zeros((5, 512), np.float32)
    kc[0] = (cc // 128) * 2048
    kc[1] = (cc % 128) * 16
    kc[2] = 1
    kc[4] = 1
    c["kaugc"] = kc.astype(bf16)
    for k in ("qaug", "kaug", "kaugc"):
        assert np.all(c[k].astype(np.float32) == {"qaug": qaug, "kaug": ka, "kaugc": kc}[k])
    E = np.zeros((128, 64, 128), np.float32)
    for kt in range(64):
        E[2 * kt, kt, :64] = BIG
        E[2 * kt + 1, kt, 64:] = BIG
    c["eexp"] = E.astype(bf16)
    M = np.zeros((512, 128), np.float32)
    for cidx in range(511):
        for s in (cidx, cidx + 1):
            M[cidx, s // 4] += 1
    c["impm"] = M.astype(bf16)
    keep = np.zeros((128, 256), np.float32)
    add = np.zeros((128, 256), np.float32)
    pp = np.arange(128)
    for col in range(256):
        r = col - 128
        if r < -1:
            keep[:, col] = 1
        elif r == -1:
            keep[:, col] = (pp >= 64)
            add[:, col] = 1e6 * (pp < 64)
        elif r == 0:
            add[:, col] = 1e6
        elif r == 1:
            add[:, col] = np.where(pp >= 64, 1e6, -1.0)
        else:
            add[:, col] = -1.0
    c["keept"] = keep
    c["addt"] = add
    return c


CONST_SPECS = {
    "ident": ([128, 128], BF16), "ropeC": ([32, T], F32), "ropeS": ([32, T], F32),
    "cmk": ([128, 4, 512], BF16), "wmk": ([128, 4, 512], BF16), "pmk": ([128, 5, 512], BF16),
    "qaug": ([8, 5, T], BF16), "kaug": ([5, T], BF16), "kaugc": ([5, 512], BF16),
    "eexp": ([128, 64, 128], BF16), "impm": ([512, 128], BF16),
    "keept": ([128, 256], F32), "addt": ([128, 256], F32),
}

IN_SPECS = {
    "x": [T, D], "p": [T, 256], "w_in": [D, 1720], "w_ck1": [2048, 256], "w_ck2": [256, 64],
    "pos_ck": [32, 64], "w_cv1": [2048, 256], "w_cv2": [256, 64], "pos_cv": [32, 64],
    "mla_q_norm": [256, 1], "w_uq": [256, 768], "mla_kv_norm": [128, 1], "w_ukv": [128, 1024],
    "w_out": [D, D], "ln1_g": [1, D], "ln1_b": [1, D], "w_up": [D, 2 * DFF], "w_down": [DFF, D],
    "ln2_g": [1, D], "ln2_b": [1, D], "w_ple_gate": [D, D], "w_ple": [256, D],
    "ln3_g": [1, D], "ln3_b": [1, D],
}

SCRATCH_SPECS = {
    "QN": ([512, T], BF16),
    "KX": ([4, 128, T], BF16),
    "VSW": ([T, 256], BF16),
    "G": ([24, T], F32),
    "KPE": ([32, T], BF16),
    "QM": ([8, 96, T], BF16),
    "KN": ([512, T], BF16),
    "VM": ([T, 512], BF16),
    "ON": ([512, T], BF16),
    "OM": ([512, T], BF16),
    "WUP": ([22, 128, 2, 8, 128], BF16),
}


def sb(nc, es, name, shape, dt):
    return es.enter_context(nc.sbuf_tensor(name, list(shape), dt))


def ps(nc, es, name, shape, dt):
    return es.enter_context(nc.psum_tensor(name, list(shape), dt))


def rot_tiles(nc, es, name, shape, dt, n, psum=False):
    items = []
    for i in range(n):
        t = (ps if psum else sb)(nc, es, "%s%d" % (name, i), shape, dt)
        items.append((t, Buf("%s%d" % (name, i))))
    return Rot(items)


class Ctx:
    pass


def build_program(stop_after=None, debug_scratch=False, stop_mid=None):
    nc = bass.Bass("TRN2", target_bir_lowering=False)
    c = Ctx()
    c.nc = nc
    c.din = {k: nc.dram_tensor(k, v, F32, kind="ExternalInput").ap() for k, v in IN_SPECS.items()}
    c.dc = {k: nc.dram_tensor("c_" + k, v[0], v[1], kind="ExternalInput").ap() for k, v in CONST_SPECS.items()}
    skind = "ExternalOutput" if debug_scratch else "Internal"
    c.ds = {k: nc.dram_tensor("s_" + k, v[0], v[1], kind=skind).ap() for k, v in SCRATCH_SPECS.items()}
    c.dsb = {k: Buf("s_" + k) for k in SCRATCH_SPECS}
    c.out = nc.dram_tensor("out", [T, D], F32, kind="ExternalOutput").ap()
    c.out_b = Buf("out")
    c.stop_mid = stop_mid
    if stop_mid == "compress":
        c.dbg = {"kcmp": nc.dram_tensor("dbg_kcmp", [69, 2, 512], BF16, kind="ExternalOutput").ap(),
                 "vcmp": nc.dram_tensor("dbg_vcmp", [128, 4, 2, 128], BF16, kind="ExternalOutput").ap()}
    with ExitStack() as es0:
        kb = KB(nc, es0)
        c.kb = kb
        c.ident = sb(nc, es0, "ident", [128, 128], BF16)
        c.ident_b = Buf("ident")
        kb.dma(c.ident[:], c.dc["ident"][:], writes=[c.ident_b])
        phases = [phase1, phase2, phase3]
        for ph in phases:
            ph(c)
            kb.barrier()
            if stop_after == ph.__name__:
                break
        kb.barrier()
    return nc


def phase1(c):
    nc, kb = c.nc, c.kb
    din, dc, ds, dsb = c.din, c.dc, c.ds, c.dsb
    with ExitStack() as es:
        wtr = sb(nc, es, "wtr", [128, 8, 1048], BF16)
        wtok = sb(nc, es, "wtok", [128, 8, 640], BF16)
        wkpA = sb(nc, es, "wkpA", [128, 8, 96], BF16)
        wkpB = sb(nc, es, "wkpB", [128, 8, 96], BF16)
        wuqA = sb(nc, es, "wuqA", [128, 2, 8, 96], BF16)
        wuqB = sb(nc, es, "wuqB", [128, 2, 8, 96], BF16)
        wkvK = sb(nc, es, "wkvK", [128, 8, 64], BF16)
        wkvV = sb(nc, es, "wkvV", [128, 8, 64], BF16)
        gq = sb(nc, es, "gq", [128, 2], F32)
        gkv = sb(nc, es, "gkv", [128, 1], F32)
        W = Buf("p1w")
        es_stg = ExitStack()
        stg = rot_tiles(nc, es_stg, "stg", [128, 1720], F32, 2)
        kb.op("pool", lambda e: e.memset(wkpA[:], 0.0), writes=[W])
        kb.op("pool", lambda e: e.memset(wkpB[:], 0.0), writes=[W])
        kb.op("pool", lambda e: e.memset(wuqB[:], 0.0), writes=[W])
        for kt in range(2):
            kb.dma(gq[:, kt:kt + 1], din["mla_q_norm"][kt * 128:(kt + 1) * 128, :], writes=[W])
        kb.dma(gkv[:], din["mla_kv_norm"][:, :], writes=[W])
        for ft in range(8):
            s, sbuf_ = stg.next()
            kb.dma(s[:], din["w_in"][ft * 128:(ft + 1) * 128, :], writes=[sbuf_])
            kb.op("dve", lambda e: e.tensor_copy(out=wtr[:, ft, 0:896], in_=s[:, 0:896]), reads=[sbuf_], writes=[W])
            kb.op("pool", lambda e: e.tensor_copy(out=wtr[:, ft, 896:1024], in_=s[:, 1024:1152]), reads=[sbuf_], writes=[W])
            kb.op("pool", lambda e: e.tensor_copy(out=wtr[:, ft, 1024:1048], in_=s[:, 1280:1304]), reads=[sbuf_], writes=[W])
            kb.op("dve", lambda e: e.tensor_copy(out=wtok[:, ft, 0:384], in_=s[:, 1304:1688]), reads=[sbuf_], writes=[W])
            kb.op("pool", lambda e: e.tensor_copy(out=wtok[:, ft, 384:512], in_=s[:, 896:1024]), reads=[sbuf_], writes=[W])
            kb.op("pool", lambda e: e.tensor_copy(out=wtok[:, ft, 512:640], in_=s[:, 1152:1280]), reads=[sbuf_], writes=[W])
            kb.op("dve", lambda e: e.tensor_copy(out=wkpA[:, ft, 64:96], in_=s[:, 1688:1720]), reads=[sbuf_], writes=[W])
            kb.op("dve", lambda e: e.tensor_copy(out=wkpB[:, ft, 64:80], in_=s[:, 1704:1720]), reads=[sbuf_], writes=[W])
            kb.op("dve", lambda e: e.tensor_copy(out=wkpB[:, ft, 80:96], in_=s[:, 1688:1704]), reads=[sbuf_], writes=[W])
        for kt in range(2):
            s, sbuf_ = stg.next()
            kb.dma(s[:, 0:768], din["w_uq"][kt * 128:(kt + 1) * 128, :], writes=[sbuf_])
            s3 = s[:, 0:768].rearrange("p (h c) -> p h c", c=96)
            kb.op("dve", lambda e: e.tensor_scalar(out=wuqA[:, kt, :, :], in0=s3, scalar1=gq[:, kt:kt + 1], scalar2=None,
                                                   op0=ALU.mult), reads=[sbuf_, W], writes=[W])
            kb.op("dve", lambda e: e.tensor_scalar(out=wuqB[:, kt, :, 64:80], in0=s3[:, :, 80:96], scalar1=gq[:, kt:kt + 1],
                                                   scalar2=None, op0=ALU.mult), reads=[sbuf_, W], writes=[W])
            kb.op("dve", lambda e: e.tensor_scalar(out=wuqB[:, kt, :, 80:96], in0=s3[:, :, 64:80], scalar1=gq[:, kt:kt + 1],
                                                   scalar2=None, op0=ALU.mult), reads=[sbuf_, W], writes=[W])
        s, sbuf_ = stg.next()
        kb.dma(s[:, 0:1024], din["w_ukv"][:, :], writes=[sbuf_])
        s3 = s[:, 0:1024].rearrange("p (h c) -> p h c", c=128)
        kb.op("dve", lambda e: e.tensor_scalar(out=wkvK[:, :, :], in0=s3[:, :, 0:64], scalar1=gkv[:, 0:1], scalar2=None,
                                               op0=ALU.mult), reads=[sbuf_, W], writes=[W])
        kb.op("dve", lambda e: e.tensor_scalar(out=wkvV[:, :, :], in0=s3[:, :, 64:128], scalar1=gkv[:, 0:1], scalar2=None,
                                               op0=ALU.mult), reads=[sbuf_, W], writes=[W])
        kb.barrier()
        es_stg.close()
        XF = rot_tiles(nc, es, "xf", [128, 4, 1024], F32, 2)
        XT = rot_tiles(nc, es, "xT", [128, 8, 512], BF16, 2)
        STT = rot_tiles(nc, es, "stT", [128, 8, 512], BF16, 2)
        GST = rot_tiles(nc, es, "gst", [24, 512], F32, 2)
        RC = rot_tiles(nc, es, "rc", [96, 2, 512], F32, 3)
        KST = rot_tiles(nc, es, "kst", [96, 512], BF16, 2)
        RT = rot_tiles(nc, es, "rt", [96, 2, 512], F32, 2)
        VST = rot_tiles(nc, es, "vst", [128, 4, 256], BF16, 2)
        SS = rot_tiles(nc, es, "ss", [128, 4], F32, 2)
        JK = rot_tiles(nc, es, "jk", [128, 256], F32, 2)
        CN = rot_tiles(nc, es, "cn", [128, 384], BF16, 2)
        CT = rot_tiles(nc, es, "cT", [128, 3, 512], BF16, 2)
        QST = rot_tiles(nc, es, "qst", [96, 8, 512], BF16, 2)
        KNST = rot_tiles(nc, es, "knst", [128, 4, 512], BF16, 2)
        VMST = rot_tiles(nc, es, "vmst", [128, 4, 512], BF16, 2)
        TP = rot_tiles(nc, es, "tp", [128, 1024], BF16, 2, psum=True)
        TPQ = rot_tiles(nc, es, "tpq", [128, 3, 128], BF16, 1, psum=True)
        PS = rot_tiles(nc, es, "ps", [128, 512], F32, 5, psum=True)
        ident, ident_b = c.ident, c.ident_b
        xv = din["x"]
        XB = rot_tiles(nc, es, "xbb", [128, 4, 1024], BF16, 2)
        LS = {}

        def L_load(ch):
            t0 = ch * 512
            xf, xf_b = XF.next()
            kb.dma(xf[:], xv[t0:t0 + 512, :].rearrange("(s p) d -> p s d", p=128), writes=[xf_b])
            rc, rc_b = RC.next()
            kb.dma(rc[64:96, 0, :], dc["ropeC"][:, t0:t0 + 512], writes=[rc_b])
            kb.dma(rc[64:96, 1, :], dc["ropeS"][:, t0:t0 + 512], writes=[rc_b])
            LS[ch] = {"xf": (xf, xf_b), "rc": (rc, rc_b)}

        def L_cast(ch):
            xf, xf_b = LS[ch]["xf"]
            xb, xb_b = XB.next()
            kb.op("dve", lambda e: e.tensor_copy(out=xb[:, 0:3, :], in_=xf[:, 0:3, :]), reads=[xf_b], writes=[xb_b])
            kb.op("pool", lambda e: e.tensor_copy(out=xb[:, 3:4, :], in_=xf[:, 3:4, :]), reads=[xf_b], writes=[xb_b])
            LS[ch]["xb"] = (xb, xb_b)

        def L_tr(ch):
            xb, xb_b = LS[ch]["xb"]
            xT, xT_b = XT.next()
            for f2 in range(4):
                tp, tp_b = TP.next()
                for fi in range(2):
                    ft = f2 * 2 + fi
                    for s_ in range(4):
                        last = (fi == 1 and s_ == 3)
                        kb.op("pe", lambda e: e.transpose(out=tp[:, fi * 512 + s_ * 128: fi * 512 + (s_ + 1) * 128],
                                                          in_=xb[:, s_, ft * 128:(ft + 1) * 128], identity=ident[:]),
                              reads=[xb_b, ident_b], writes=[tp_b], inc=last)
                dst = xT[:, 2 * f2:2 * f2 + 2, :].rearrange("p a t -> p (a t)")
                kb.op("act" if f2 % 2 else "dve",
                      (lambda e: e.activation(out=dst, in_=tp[:], func=AF.Copy)) if f2 % 2 else
                      (lambda e: e.tensor_copy(out=dst, in_=tp[:])), reads=[tp_b], writes=[xT_b])
            LS[ch]["xT"] = (xT, xT_b)

        L_load(0)
        L_cast(0)
        L_tr(0)
        if NCH > 1:
            L_load(1)
        for ch in range(NCH):
            t0 = ch * 512
            xT, xT_b = LS[ch]["xT"]
            rc, rc_b = LS[ch]["rc"]
            if ch + 2 < NCH:
                L_load(ch + 2)
            if ch + 1 < NCH:
                L_cast(ch + 1)
            stT, stT_b = STT.next()
            for ct in range(8):
                pt, pt_b = PS.next()
                for ft in range(8):
                    kb.op("pe", lambda e: e.matmul(pt[:, :], lhsT=wtr[:, ft, ct * 128:(ct + 1) * 128], rhs=xT[:, ft, :],
                                                   start=(ft == 0), stop=(ft == 7)),
                          reads=[W, xT_b], writes=[pt_b], inc=(ft == 7))
                if ct < 4:
                    kb.op("act", lambda e: e.activation(out=stT[:, ct, :], in_=pt[:, :], func=AF.Copy, scale=0.125),
                          reads=[pt_b], writes=[stT_b])
                else:
                    kb.op("dve", lambda e: e.tensor_copy(out=stT[:, ct, :], in_=pt[:, :]), reads=[pt_b], writes=[stT_b])
            kb.dma(ds["QN"].rearrange("(c p) t -> p c t", p=128)[:, :, t0:t0 + 512], stT[:, 0:4, :],
                   reads=[stT_b], writes=[dsb["QN"]])
            kb.dma(ds["KX"].rearrange("k p t -> p k t")[:, :, t0:t0 + 512], stT[:, 4:8, :],
                   reads=[stT_b], writes=[dsb["KX"]])
            pt, pt_b = PS.next()
            for ft in range(8):
                kb.op("pe", lambda e: e.matmul(pt[0:24, :], lhsT=wtr[:, ft, 1024:1048], rhs=xT[:, ft, :],
                                               start=(ft == 0), stop=(ft == 7)),
                      reads=[W, xT_b], writes=[pt_b], inc=(ft == 7))
            gst, gst_b = GST.next()
            kb.op("act", lambda e: e.activation(out=gst[:, :], in_=pt[0:24, :], func=AF.Sigmoid), reads=[pt_b], writes=[gst_b])
            kb.dma(ds["G"][:, t0:t0 + 512], gst[:, :], reads=[gst_b], writes=[dsb["G"]])
            pa, pa_b = PS.next()
            for ft in range(8):
                kb.op("pe", lambda e: e.matmul(pa[0:96, :], lhsT=wkpA[:, ft, :], rhs=xT[:, ft, :], start=(ft == 0), stop=(ft == 7)),
                      reads=[W, xT_b], writes=[pa_b], inc=(ft == 7))
            pb, pb_b = PS.next()
            for ft in range(8):
                kb.op("pe", lambda e: e.matmul(pb[0:96, :], lhsT=wkpB[:, ft, :], rhs=xT[:, ft, :], start=(ft == 0), stop=(ft == 7)),
                      reads=[W, xT_b], writes=[pb_b], inc=(ft == 7))
            rt, rt_b = RT.next()
            kst, kst_b = KST.next()
            kb.op("dve", lambda e: e.tensor_tensor(out=rt[64:96, 0, :], in0=pa[64:96, :], in1=rc[64:96, 0, :], op=ALU.mult),
                  reads=[pa_b, rc_b], writes=[rt_b])
            kb.op("dve", lambda e: e.tensor_tensor(out=rt[64:96, 1, :], in0=pb[64:96, :], in1=rc[64:96, 1, :], op=ALU.mult),
                  reads=[pb_b, rc_b], writes=[rt_b])
            kb.op("pool", lambda e: e.tensor_tensor(out=kst[64:96, :], in0=rt[64:96, 0, :], in1=rt[64:96, 1, :], op=ALU.add),
                  reads=[rt_b], writes=[kst_b])
            kb.dma(ds["KPE"][:, t0:t0 + 512], kst[64:96, :], reads=[kst_b], writes=[dsb["KPE"]])
            if ch + 1 < NCH:
                L_tr(ch + 1)
            vst, vst_b = VST.next()
            cT, cT_b = CT.next()
            P1S = {}

            def tm_mm(s_):
                p1, p1_b = PS.next()
                for ft in range(8):
                    kb.op("pe", lambda e: e.matmul(p1[:, 0:384], lhsT=xT[:, ft, s_ * 128:(s_ + 1) * 128], rhs=wtok[:, ft, 0:384],
                                                   start=(ft == 0), stop=(ft == 7)),
                          reads=[W, xT_b], writes=[p1_b], inc=(ft == 7))
                p2, p2_b = PS.next()
                for ft in range(8):
                    kb.op("pe", lambda e: e.matmul(p2[:, 0:256], lhsT=xT[:, ft, s_ * 128:(s_ + 1) * 128], rhs=wtok[:, ft, 384:640],
                                                   start=(ft == 0), stop=(ft == 7)),
                          reads=[W, xT_b], writes=[p2_b], inc=(ft == 7))
                kb.op("dve", lambda e: e.tensor_copy(out=vst[:, s_, :], in_=p2[:, 0:256]), reads=[p2_b], writes=[vst_b])
                P1S[s_] = (p1, p1_b)

            def tm_chain(s_):
                p1, p1_b = P1S.pop(s_)
                ss, ss_b = SS.next()
                jk, jk_b = JK.next()
                kb.op("act", lambda e: e.activation(out=jk[:, 0:256], in_=p1[:, 0:256], func=AF.Square, accum_out=ss[:, 0:1]),
                      reads=[p1_b], writes=[jk_b, ss_b])
                kb.op("act", lambda e: e.activation(out=jk[:, 0:128], in_=p1[:, 256:384], func=AF.Square, accum_out=ss[:, 1:2]),
                      reads=[p1_b], writes=[jk_b, ss_b])
                kb.op("dve", lambda e: e.tensor_scalar(out=ss[:, 0:1], in0=ss[:, 0:1], scalar1=1.0 / 256, scalar2=RMS_EPS,
                                                       op0=ALU.mult, op1=ALU.add), reads=[ss_b], writes=[ss_b])
                kb.op("dve", lambda e: e.tensor_scalar(out=ss[:, 1:2], in0=ss[:, 1:2], scalar1=1.0 / 128, scalar2=RMS_EPS,
                                                       op0=ALU.mult, op1=ALU.add), reads=[ss_b], writes=[ss_b])
                kb.op("act", lambda e: e.activation(out=ss[:, 2:4], in_=ss[:, 0:2], func=AF.Sqrt), reads=[ss_b], writes=[ss_b])
                kb.op("dve", lambda e: e.reciprocal(out=ss[:, 0:2], in_=ss[:, 2:4]), reads=[ss_b], writes=[ss_b])
                cn, cn_b = CN.next()
                kb.op("dve", lambda e: e.tensor_scalar(out=cn[:, 0:256], in0=p1[:, 0:256], scalar1=ss[:, 0:1], scalar2=None,
                                                       op0=ALU.mult), reads=[p1_b, ss_b], writes=[cn_b])
                kb.op("dve", lambda e: e.tensor_scalar(out=cn[:, 256:384], in0=p1[:, 256:384], scalar1=ss[:, 1:2], scalar2=None,
                                                       op0=ALU.mult), reads=[p1_b, ss_b], writes=[cn_b])
                tq, tq_b = TPQ.next()
                for i in range(3):
                    kb.op("pe", lambda e: e.transpose(out=tq[:, i, :], in_=cn[:, i * 128:(i + 1) * 128], identity=ident[:]),
                          reads=[cn_b, ident_b], writes=[tq_b], inc=(i == 2))
                kb.op("act", lambda e: e.activation(out=cT[:, :, s_ * 128:(s_ + 1) * 128], in_=tq[:, :, :], func=AF.Copy),
                      reads=[tq_b], writes=[cT_b])

            tm_mm(0)
            for s_ in range(4):
                if s_ + 1 < 4:
                    tm_mm(s_ + 1)
                tm_chain(s_)
            kb.dma(ds["VSW"][t0:t0 + 512, :].rearrange("(s p) c -> p s c", p=128), vst[:, :, :], reads=[vst_b], writes=[dsb["VSW"]])
            qst, qst_b = QST.next()
            for h in range(8):
                pa, pa_b = PS.next()
                for kt in range(2):
                    kb.op("pe", lambda e: e.matmul(pa[0:96, :], lhsT=wuqA[:, kt, h, :], rhs=cT[:, kt, :], start=(kt == 0), stop=(kt == 1)),
                          reads=[W, cT_b], writes=[pa_b], inc=(kt == 1))
                pb, pb_b = PS.next()
                for kt in range(2):
                    kb.op("pe", lambda e: e.matmul(pb[0:96, :], lhsT=wuqB[:, kt, h, :], rhs=cT[:, kt, :], start=(kt == 0), stop=(kt == 1)),
                          reads=[W, cT_b], writes=[pb_b], inc=(kt == 1))
                kb.op("act", lambda e: e.activation(out=qst[0:64, h, :], in_=pa[0:64, :], func=AF.Copy), reads=[pa_b], writes=[qst_b])
                rt, rt_b = RT.next()
                kb.op("dve", lambda e: e.tensor_tensor(out=rt[64:96, 0, :], in0=pa[64:96, :], in1=rc[64:96, 0, :], op=ALU.mult),
                      reads=[pa_b, rc_b], writes=[rt_b])
                kb.op("dve", lambda e: e.tensor_tensor(out=rt[64:96, 1, :], in0=pb[64:96, :], in1=rc[64:96, 1, :], op=ALU.mult),
                      reads=[pb_b, rc_b], writes=[rt_b])
                kb.op("pool", lambda e: e.tensor_tensor(out=qst[64:96, h, :], in0=rt[64:96, 0, :], in1=rt[64:96, 1, :], op=ALU.add),
                      reads=[rt_b], writes=[qst_b])
            kb.dma(ds["QM"].rearrange("h r t -> r h t")[:, :, t0:t0 + 512], qst[:, :, :], reads=[qst_b], writes=[dsb["QM"]])
            knst, knst_b = KNST.next()
            for hp in range(4):
                pt, pt_b = PS.next()
                kb.op("pe", lambda e: e.matmul(pt[:, :], lhsT=wkvK[:, 2 * hp:2 * hp + 2, :].rearrange("p a c -> p (a c)"),
                                               rhs=cT[:, 2, :], start=True, stop=True), reads=[W, cT_b], writes=[pt_b])
                kb.op("act" if hp % 2 else "dve",
                      (lambda e: e.activation(out=knst[:, hp, :], in_=pt[:, :], func=AF.Copy)) if hp % 2 else
                      (lambda e: e.tensor_copy(out=knst[:, hp, :], in_=pt[:, :])), reads=[pt_b], writes=[knst_b])
            kb.dma(ds["KN"].rearrange("(c p) t -> p c t", p=128)[:, :, t0:t0 + 512], knst[:, :, :], reads=[knst_b], writes=[dsb["KN"]])
            vmst, vmst_b = VMST.next()
            for s_ in range(4):
                pt, pt_b = PS.next()
                kb.op("pe", lambda e: e.matmul(pt[:, :], lhsT=cT[:, 2, s_ * 128:(s_ + 1) * 128],
                                               rhs=wkvV[:, :, :].rearrange("p a c -> p (a c)"), start=True, stop=True),
                      reads=[W, cT_b], writes=[pt_b])
                kb.op("act" if s_ % 2 else "dve",
                      (lambda e: e.activation(out=vmst[:, s_, :], in_=pt[:, :], func=AF.Copy)) if s_ % 2 else
                      (lambda e: e.tensor_copy(out=vmst[:, s_, :], in_=pt[:, :])), reads=[pt_b], writes=[vmst_b])
            kb.dma(ds["VM"][t0:t0 + 512, :].rearrange("(s p) c -> p s c", p=128), vmst[:, :, :], reads=[vmst_b], writes=[dsb["VM"]])


class TileD:
    __slots__ = ("qk", "scale", "pv", "epi", "lo", "hi")

    def __init__(self, qk, scale, pv, epi=None, lo=0, hi=512):
        self.qk = qk
        self.scale = scale
        self.pv = pv
        self.epi = epi
        self.lo, self.hi = lo, hi


class Stream:
    def __init__(self, c, R, L=2):
        self.c, self.R, self.L = c, R, L
        self.pending = []

    def push(self, t):
        kb, R = self.c.kb, self.R
        S, S_b = R.SPS.next()
        nq = len(t.qk)
        lo, hi = t.lo, t.hi
        for a, ent in enumerate(t.qk):
            lh, rh, rb = ent[0], ent[1], ent[2]
            clo, chi = (ent[3], ent[4]) if len(ent) > 3 else (lo, hi)
            kb.op("pe", lambda e: e.matmul(S[:, clo:chi], lhsT=lh, rhs=rh, start=(a == 0), stop=(a == nq - 1), skip_group_check=True),
                  reads=rb, writes=[S_b], inc=(a == nq - 1))
        P, P_b = R.PSB.next()
        kb.op("act", lambda e: e.activation(out=P[:, lo:hi], in_=S[:, lo:hi], func=AF.Exp, scale=t.scale), reads=[S_b], writes=[P_b])
        self.pending.append((t, P, P_b))
        if len(self.pending) > self.L:
            self._pop()

    def _pop(self):
        t, P, P_b = self.pending.pop(0)
        t.pv(P, P_b)
        if t.epi is not None:
            t.epi()

    def flush(self):
        while self.pending:
            self._pop()


def run_stream(c, R, tiles, L=2):
    st = c.stream
    for t in tiles:
        st.push(t)


def phase2(c):
    nc, kb = c.nc, c.kb
    din, dc, ds, dsb = c.din, c.dc, c.ds, c.dsb
    ident, ident_b = c.ident, c.ident_b
    with ExitStack() as esA:
        kcmp = sb(nc, esA, "kcmp", [69, 2, 512], BF16)
        kcmp_b = Buf("kcmp")
        vcmp = sb(nc, esA, "vcmp", [128, 4, 2, 128], BF16)
        vcmp_b = Buf("vcmp")
        CB = Buf("consts2")
        cmk = sb(nc, esA, "cmk", [128, 4, 512], BF16)
        wmk = sb(nc, esA, "wmk", [128, 4, 512], BF16)
        pmk = sb(nc, esA, "pmk", [128, 5, 512], BF16)
        eexp = sb(nc, esA, "eexp", [128, 64, 128], BF16)
        impm = sb(nc, esA, "impm", [128, 4, 128], BF16)
        keept = sb(nc, esA, "keept", [128, 256], F32)
        addt = sb(nc, esA, "addt", [128, 256], F32)
        kb.dma(cmk[:], dc["cmk"][:], writes=[CB])
        kb.dma(wmk[:], dc["wmk"][:], writes=[CB])
        kb.dma(pmk[:], dc["pmk"][:], writes=[CB])
        kb.dma(eexp[:], dc["eexp"][:], writes=[CB])
        kb.dma(impm[:], dc["impm"].rearrange("(ct p) b -> p ct b", p=128), writes=[CB])
        kb.dma(keept[:], dc["keept"][:], writes=[CB])
        kb.dma(addt[:], dc["addt"][:], writes=[CB])
        kb.op("pool", lambda e: e.memset(kcmp[:], 0.0), writes=[kcmp_b])
        kb.op("pool", lambda e: e.memset(vcmp[:], 1.0), writes=[vcmp_b])
        for g in range(2):
            kb.dma(kcmp[64:69, g, :], dc["kaugc"][:, :], writes=[kcmp_b])
        with ExitStack() as es:
            kcg = sb(nc, es, "kcg", [64, 4, T], BF16)
            kcg_b = Buf("kcg")
            for kind in range(2):
                for g in range(2):
                    kb.dma(kcg[:, kind * 2 + g, :], ds["KX"][kind, g * 64:(g + 1) * 64, :], reads=[dsb["KX"]], writes=[kcg_b])
            w1s = sb(nc, es, "w1s", [64, 32, 256], F32)
            w1s_b = Buf("w1s")
            w1 = [sb(nc, es, "w1_%d" % k, [64, 32, 256], BF16) for k in range(2)]
            w2s = sb(nc, es, "w2s", [128, 2, 64], F32)
            w2s_b = Buf("w2s")
            w2 = [sb(nc, es, "w2_%d" % k, [128, 2, 64], BF16) for k in range(2)]
            posS = sb(nc, es, "posS", [32, 64], F32)
            posS_b = Buf("posS")
            posB = sb(nc, es, "posB", [32, 64], BF16)
            posB_b = Buf("posB")
            posT = sb(nc, es, "posT", [64, 2, 32], BF16)
            b1 = sb(nc, es, "b1", [128, 4], F32)
            Wc = Buf("wc")
            tpp = ps(nc, es, "tpp", [64, 32], BF16)
            tpp_b = Buf("tpp")
            pb1 = ps(nc, es, "pb1", [128, 4], F32)
            pb1_b = Buf("pb1")
            PSC = rot_tiles(nc, es, "psc", [128, 512], F32, 4, psum=True)
            XG = rot_tiles(nc, es, "xg", [128, 512], F32, 2)
            X2 = rot_tiles(nc, es, "x2", [128, 512], F32, 2)
            GL = rot_tiles(nc, es, "gl", [128, 512], BF16, 4)
            for gl, gl_b in GL.items:
                kb.op("pool", lambda e: e.memset(gl[:], 0.0), writes=[gl_b])
            names = [("w_ck1", "w_ck2", "pos_ck"), ("w_cv1", "w_cv2", "pos_cv")]
            for kind, (n1, n2, npos) in enumerate(names):
                kb.dma(w1s[:], din[n1].rearrange("(l d) n -> d l n", d=64), writes=[w1s_b])
                kb.op("dve", lambda e: e.tensor_copy(out=w1[kind][:, 0:16, :], in_=w1s[:, 0:16, :]), reads=[w1s_b], writes=[Wc])
                kb.op("pool", lambda e: e.tensor_copy(out=w1[kind][:, 16:32, :], in_=w1s[:, 16:32, :]), reads=[w1s_b], writes=[Wc])
                kb.dma(w2s[:], din[n2].rearrange("(c p) n -> p c n", p=128), writes=[w2s_b])
                kb.op("dve", lambda e: e.tensor_copy(out=w2[kind][:], in_=w2s[:]), reads=[w2s_b], writes=[Wc])
                kb.dma(posS[:], din[npos][:, :], writes=[posS_b])
                kb.op("dve", lambda e: e.tensor_copy(out=posB[:], in_=posS[:]), reads=[posS_b], writes=[posB_b])
                kb.op("pe", lambda e: e.transpose(out=tpp[:, :], in_=posB[:, :], identity=ident[0:32, 0:32]),
                      reads=[posB_b, ident_b], writes=[tpp_b])
                kb.op("dve", lambda e: e.tensor_copy(out=posT[:, kind, :], in_=tpp[:, :]), reads=[tpp_b], writes=[Wc])
            for kind in range(2):
                for hc in range(2):
                    idx = kind * 2 + hc
                    for l in range(32):
                        kb.op("pe", lambda e: e.matmul(pb1[:, idx:idx + 1], lhsT=w1[kind][:, l, hc * 128:(hc + 1) * 128],
                                                       rhs=posT[:, kind, l:l + 1], start=(l == 0), stop=(l == 31)),
                              reads=[Wc], writes=[pb1_b], inc=(l == 31))
            kb.op("dve", lambda e: e.tensor_copy(out=b1[:], in_=pb1[:]), reads=[pb1_b], writes=[Wc])
            for kind in range(2):
                for g in range(2):
                    gls = []
                    for hc in range(2):
                        idx = kind * 2 + hc
                        pm, pm_b = PSC.next()
                        for l in range(32):
                            kb.op("pe", lambda e: e.matmul(pm[:, 0:511], lhsT=w1[kind][:, l, hc * 128:(hc + 1) * 128],
                                                           rhs=kcg[:, kind * 2 + g, l:l + 8161:16], start=(l == 0), stop=(l == 31)),
                                  reads=[Wc, kcg_b], writes=[pm_b], inc=(l == 31))
                        xg, xg_b = XG.next()
                        x2, x2_b = X2.next()
                        gl, gl_b = GL.next()
                        kb.op("dve", lambda e: e.tensor_scalar(out=xg[:, 0:511], in0=pm[:, 0:511], scalar1=b1[:, idx:idx + 1],
                                                               scalar2=None, op0=ALU.add), reads=[pm_b, Wc], writes=[xg_b])
                        kb.op("pool", lambda e: e.tensor_tensor(out=x2[:, 0:511], in0=xg[:, 0:511], in1=xg[:, 0:511], op=ALU.mult),
                              reads=[xg_b], writes=[x2_b])
                        kb.op("dve", lambda e: e.tensor_scalar(out=x2[:, 0:511], in0=x2[:, 0:511], scalar1=0.044715, scalar2=1.0,
                                                               op0=ALU.mult, op1=ALU.add), reads=[x2_b], writes=[x2_b])
                        kb.op("pool", lambda e: e.tensor_tensor(out=x2[:, 0:511], in0=x2[:, 0:511], in1=xg[:, 0:511], op=ALU.mult),
                              reads=[x2_b, xg_b], writes=[x2_b])
                        kb.op("act", lambda e: e.activation(out=x2[:, 0:511], in_=x2[:, 0:511], func=AF.Sigmoid, scale=1.5957691216),
                              reads=[x2_b], writes=[x2_b])
                        kb.op("dve", lambda e: e.tensor_tensor(out=gl[:, 0:511], in0=x2[:, 0:511], in1=xg[:, 0:511], op=ALU.mult),
                              reads=[x2_b, xg_b], writes=[gl_b])
                        gls.append((gl, gl_b))
                    if kind == 0:
                        pk, pk_b = PSC.next()
                        for hc in range(2):
                            kb.op("pe", lambda e: e.matmul(pk[0:64, 0:511], lhsT=w2[0][:, hc, :], rhs=gls[hc][0][:, 0:511],
                                                           start=(hc == 0), stop=(hc == 1)),
                                  reads=[Wc, gls[hc][1]], writes=[pk_b], inc=(hc == 1))
                        kb.op("act", lambda e: e.activation(out=kcmp[0:64, g, 0:511], in_=pk[0:64, 0:511], func=AF.Copy),
                              reads=[pk_b], writes=[kcmp_b])
                    else:
                        for ct in range(4):
                            pv, pv_b = PSC.next()
                            for hc in range(2):
                                kb.op("pe", lambda e: e.matmul(pv[:, 0:64], lhsT=gls[hc][0][:, ct * 128:(ct + 1) * 128], rhs=w2[1][:, hc, :],
                                                               start=(hc == 0), stop=(hc == 1)),
                                      reads=[Wc, gls[hc][1]], writes=[pv_b], inc=(hc == 1))
                            kb.op("act", lambda e: e.activation(out=vcmp[:, ct, g, 0:64], in_=pv[:, 0:64], func=AF.Copy),
                                  reads=[pv_b], writes=[vcmp_b])
        kb.barrier()
        if getattr(c, "stop_mid", None) == "compress":
            dbg = c.dbg
            kb.dma(dbg["kcmp"][:], kcmp[:], reads=[kcmp_b])
            kb.dma(dbg["vcmp"][:], vcmp[:], reads=[vcmp_b])
            return
        with ExitStack() as es:
            R = Ctx()
            R.SPS = rot_tiles(nc, es, "sps", [128, 512], F32, 3, psum=True)
            R.OPS = rot_tiles(nc, es, "ops", [128, 512], F32, 2, psum=True)
            R.UPS = rot_tiles(nc, es, "ups", [128, 4, 128], F32, 2, psum=True)
            R.TPN = rot_tiles(nc, es, "tpn", [128, 4, 128], BF16, 1, psum=True)
            R.PSB = rot_tiles(nc, es, "psb", [128, 512], BF16, 4)
            KA = rot_tiles(nc, es, "ka", [96, T], BF16, 2).items
            VA = rot_tiles(nc, es, "va", [128, 64, 128], BF16, 2).items
            for va, va_b in VA:
                kb.op("pool", lambda e: e.memset(va[:], 1.0), writes=[va_b])
            QA = [[(sb(nc, es, "qa%d%d" % (pr, hi), [69, 512], BF16), Buf("qa")) for hi in range(4)] for pr in range(2)]
            OC = [[(sb(nc, es, "oc%d%d" % (pr, hi), [64, 512], F32), Buf("oc")) for hi in range(4)] for pr in range(2)]
            QMT = rot_tiles(nc, es, "qmt", [96, 512], BF16, 3)
            GB = rot_tiles(nc, es, "gb", [64, 3, 512], F32, 4)
            NMT = rot_tiles(nc, es, "nmt", [128, 512], BF16, 2)
            IMP = rot_tiles(nc, es, "imp", [128, 4, 128], F32, 2)
            DEN = rot_tiles(nc, es, "den", [128, 256], F32, 2)
            REC = rot_tiles(nc, es, "rec", [128, 256], F32, 2)
            RG = rot_tiles(nc, es, "rg", [64, 512], F32, 2)
            T1 = rot_tiles(nc, es, "t1", [64, 512], F32, 4)
            ACC = rot_tiles(nc, es, "acc", [64, 512], F32, 2)
            OST = rot_tiles(nc, es, "ost", [64, 512], BF16, 3)
            W1 = rot_tiles(nc, es, "w1k", [128, 128], F32, 2)
            W2 = rot_tiles(nc, es, "w2k", [128, 128], F32, 2)
            M8 = rot_tiles(nc, es, "m8", [128, 16], F32, 2)
            NMS = rot_tiles(nc, es, "nms", [128, 128], BF16, 4)
            RS = rot_tiles(nc, es, "rs", [128, 8], F32, 2)
            Gt = ds["G"].tensor
            c.stream = Stream(c, R)

            def norm_rec(O, O_b):
                den, den_b = DEN.next()
                rec, rec_b = REC.next()
                kb.op("dve", lambda e: e.tensor_scalar_max(out=den[0:64, :], in0=O[64:128, 0:256], scalar1=1e-30), reads=[O_b], writes=[den_b])
                kb.op("dve", lambda e: e.tensor_scalar_max(out=den[64:128, :], in0=O[64:128, 256:512], scalar1=1e-30), reads=[O_b], writes=[den_b])
                kb.op("dve", lambda e: e.reciprocal(out=rec[:, :], in_=den[:, :]), reads=[den_b], writes=[rec_b])
                return rec, rec_b

            def mul_rec(eng, out_t, in_t, rec, reads, writes):
                kb.op(eng, lambda e: e.tensor_tensor(out=out_t[0:64, 0:256], in0=in_t[0:64, 0:256], in1=rec[0:64, :], op=ALU.mult), reads=reads, writes=writes)
                kb.op(eng, lambda e: e.tensor_tensor(out=out_t[0:64, 256:512], in0=in_t[0:64, 256:512], in1=rec[64:128, :], op=ALU.mult), reads=reads, writes=writes)

            for g in range(2):
                (ks, ks_b), (kw, kw_b) = KA
                (vs, vs_b), (vw, vw_b) = VA
                kb.dma(ks[0:64, :], ds["KX"][2, g * 64:(g + 1) * 64, :], reads=[dsb["KX"]], writes=[ks_b])
                kb.dma(ks[64:69, :], dc["kaug"][:, :], writes=[ks_b])
                kb.dma(kw[0:64, :], ds["KX"][3, g * 64:(g + 1) * 64, :], reads=[dsb["KX"]], writes=[kw_b])
                kb.dma(kw[64:69, :], dc["kaug"][:, :], writes=[kw_b])
                for half in range(2):
                    hs = slice(half * 32, (half + 1) * 32)
                    rs_ = slice(half * 4096, (half + 1) * 4096)
                    kb.dma(vs[:, hs, 0:64], ds["VSW"][rs_, g * 64:(g + 1) * 64].rearrange("(kt p) d -> p kt d", p=128),
                           reads=[dsb["VSW"]], writes=[vs_b])
                    kb.dma(vw[:, hs, 0:64], ds["VSW"][rs_, 128 + g * 64:128 + (g + 1) * 64].rearrange("(kt p) d -> p kt d", p=128),
                           reads=[dsb["VSW"]], writes=[vw_b])
                CS = {}

                def cmp_begin(qc):
                    CS[qc] = {"imp": IMP.next(), "nmt": NMT.next(), "nms": []}

                def cmp_head(qc, hi, g=g):
                    t0 = qc * 512
                    pr = qc % 2
                    h = 4 * g + hi
                    imp, imp_b = CS[qc]["imp"]
                    qa, qa_b = QA[pr][hi]
                    kb.dma(qa[0:64, :], ds["QN"][h * 64:(h + 1) * 64, t0:t0 + 512], reads=[dsb["QN"]], writes=[qa_b])
                    kb.dma(qa[64:69, :], dc["qaug"][h, :, t0:t0 + 512], writes=[qa_b])
                    O, O_b = R.OPS.next()
                    U, U_b = R.UPS.next()
                    nct = qc // 4 + 1
                    for ct in range(nct):
                        m = qc - 4 * ct
                        qk = [(kcmp[0:69, g, ct * 128:(ct + 1) * 128], qa[0:69, :], [kcmp_b, qa_b])]
                        if m <= 4:
                            qk.append((ident[:, :], pmk[:, m, :], [ident_b, CB]))

                        def pv(P, P_b, ct=ct, nct=nct, O=O, O_b=O_b, U=U, U_b=U_b):
                            kb.op("pe", lambda e: e.matmul(O[:, :], lhsT=vcmp[:, ct, g, :], rhs=P[:, :], start=(ct == 0), stop=(ct == nct - 1)),
                                  reads=[vcmp_b, P_b], writes=[O_b], inc=True)
                            for s_ in range(4):
                                kb.op("pe", lambda e: e.matmul(U[:, s_, :], lhsT=P[:, s_ * 128:(s_ + 1) * 128], rhs=impm[:, ct, :],
                                                               start=(ct == 0), stop=(ct == nct - 1)),
                                      reads=[CB, P_b], writes=[U_b], inc=(s_ == 3))

                        def epi(hi=hi, pr=pr, O=O, O_b=O_b, U=U, U_b=U_b, imp=imp, imp_b=imp_b):
                            rec, rec_b = norm_rec(O, O_b)
                            oc, oc_b = OC[pr][hi]
                            mul_rec("dve", oc, O, rec, [O_b, rec_b], [oc_b])
                            rs, rs_b = RS.next()
                            kb.op("dve", lambda e: e.tensor_reduce(out=rs[:, 0:4], in_=U[:, :, :], axis=AX.X, op=ALU.add),
                                  reads=[U_b], writes=[rs_b])
                            kb.op("dve", lambda e: e.tensor_scalar(out=rs[:, 0:4], in0=rs[:, 0:4], scalar1=1e-30, scalar2=0.5,
                                                                   op0=ALU.max, op1=ALU.mult), reads=[rs_b], writes=[rs_b])
                            kb.op("dve", lambda e: e.reciprocal(out=rs[:, 4:8], in_=rs[:, 0:4]), reads=[rs_b], writes=[rs_b])
                            for s_ in range(4):
                                if hi == 0:
                                    kb.op("dve", lambda e: e.tensor_scalar(out=imp[:, s_, :], in0=U[:, s_, :], scalar1=rs[:, 4 + s_:5 + s_],
                                                                           scalar2=None, op0=ALU.mult), reads=[U_b, rs_b], writes=[imp_b])
                                else:
                                    kb.op("dve", lambda e: e.scalar_tensor_tensor(out=imp[:, s_, :], in0=U[:, s_, :], scalar=rs[:, 4 + s_:5 + s_],
                                                                                  in1=imp[:, s_, :], op0=ALU.mult, op1=ALU.add),
                                          reads=[U_b, rs_b, imp_b], writes=[imp_b])

                        c.stream.push(TileD(qk, 1.0, pv, epi if ct == nct - 1 else None))

                def topk_dve(qc):
                    imp, imp_b = CS[qc]["imp"]
                    for s_ in range(4):
                        qt = 4 * qc + s_
                        lo = 128 - 2 * qt
                        w, w_b = W1.next()
                        w2, w2_b = W2.next()
                        m8, m8_b = M8.next()
                        nms, nms_b = NMS.next()
                        kb.op("pool", lambda e: e.tensor_tensor(out=w[:, :], in0=imp[:, s_, :], in1=keept[:, lo:lo + 128], op=ALU.mult),
                              reads=[imp_b, CB], writes=[w_b])
                        kb.op("pool", lambda e: e.tensor_tensor(out=w[:, :], in0=w[:, :], in1=addt[:, lo:lo + 128], op=ALU.add),
                              reads=[w_b, CB], writes=[w_b])
                        kb.op("pool", lambda e: e.memset(w[:, 0:1], 1e6), reads=[w_b], writes=[w_b])
                        kb.op("dve", lambda e: e.max(out=m8[:, 0:8], in_=w[:, :]), reads=[w_b], writes=[m8_b])
                        kb.op("dve", lambda e: e.match_replace(out=w2[:, :], in_to_replace=m8[:, 0:8], in_values=w[:, :], imm_value=-2.0),
                              reads=[w_b, m8_b], writes=[w2_b])
                        kb.op("dve", lambda e: e.max(out=m8[:, 8:16], in_=w2[:, :]), reads=[w2_b, m8_b], writes=[m8_b])
                        kb.op("dve", lambda e: e.tensor_scalar(out=nms[:, :], in0=w[:, :], scalar1=m8[:, 15:16], scalar2=-1.0,
                                                               op0=ALU.is_ge, op1=ALU.add), reads=[w_b, m8_b], writes=[nms_b])
                        CS[qc]["nms"].append((nms, nms_b))

                def topk_pe(qc):
                    nmt, nmt_b = CS[qc]["nmt"]
                    tpn, tpn_b = R.TPN.next()
                    for s_ in range(4):
                        nms, nms_b = CS[qc]["nms"][s_]
                        kb.op("pe", lambda e: e.transpose(out=tpn[:, s_, :], in_=nms[:, :], identity=ident[:, :]),
                              reads=[nms_b, ident_b], writes=[tpn_b])
                    kb.op("act", lambda e: e.activation(out=nmt[:, :], in_=tpn[:, :, :].rearrange("p a b -> p (a b)"), func=AF.Copy),
                          reads=[tpn_b], writes=[nmt_b])

                def sw_head(qc, hi, g=g):
                    t0 = qc * 512
                    pr = qc % 2
                    h = 4 * g + hi
                    nmt, nmt_b = CS[qc]["nmt"]
                    qa, qa_b = QA[pr][hi]
                    gb, gb_b = GB.next()
                    kb.dma(gb[:, :, :], bass.AP(Gt, h * 3 * T + t0, [[0, 64], [T, 3], [1, 512]]), reads=[dsb["G"]], writes=[gb_b])
                    st = {}
                    O, O_b = R.OPS.next()
                    nk = 4 * qc + 4
                    k0 = max(0, int((t0 - 127 - ALIBI_CUT * (2.0 ** (h + 1))) // 128) + 1)
                    for kt in range(k0, nk):
                        d = kt - 4 * qc
                        lo = 128 * d if d > 0 else 0
                        qk = [(ks[0:69, kt * 128:(kt + 1) * 128], qa[0:69, lo:512], [ks_b, qa_b]),
                              (eexp[:, kt, :], nmt[:, lo:512], [CB, nmt_b])]
                        if d >= 0:
                            qk.append((ident[:, :], cmk[:, d, lo:lo + 128], [ident_b, CB], lo, lo + 128))

                        def pv(P, P_b, kt=kt, nk=nk, k0=k0, O=O, O_b=O_b, lo=lo):
                            kb.op("pe", lambda e: e.matmul(O[:, lo:512], lhsT=vs[:, kt, :], rhs=P[:, lo:512], start=(kt == k0), stop=(kt == nk - 1),
                                                           skip_group_check=True),
                                  reads=[vs_b, P_b], writes=[O_b], inc=True)

                        def epi(O=O, O_b=O_b, gb=gb, gb_b=gb_b, st=st):
                            rec, rec_b = norm_rec(O, O_b)
                            t1, t1_b = T1.next()
                            mul_rec("dve", t1, O, rec, [O_b, rec_b], [t1_b])
                            kb.op("pool", lambda e: e.tensor_tensor(out=t1[:, :], in0=t1[:, :], in1=gb[:, 1, :], op=ALU.mult),
                                  reads=[t1_b, gb_b], writes=[t1_b])
                            st["slc"] = (t1, t1_b)

                        c.stream.push(TileD(qk, 1.0, pv, epi if kt == nk - 1 else None, lo=lo, hi=512))
                    O, O_b = R.OPS.next()
                    wl = []
                    for d in range(4):
                        wl.append((4 * qc + d, cmk[:, d, 128 * d:128 * d + 128], 128 * d, 512, 128 * d))
                    for d in range(4):
                        kt = 4 * qc - 4 + d
                        if kt >= 0:
                            wl.append((kt, wmk[:, d, 128 * d:128 * d + 128], 0, 128 * d + 128, 128 * d))
                    nw = len(wl)
                    for wi, (kt, mk, wlo, whi, mlo) in enumerate(wl):
                        qk = [(kw[0:69, kt * 128:(kt + 1) * 128], qa[0:69, wlo:whi], [kw_b, qa_b]),
                              (ident[:, :], mk, [ident_b, CB], mlo, mlo + 128)]

                        def pv(P, P_b, kt=kt, wi=wi, nw=nw, O=O, O_b=O_b, wlo=wlo, whi=whi):
                            kb.op("pe", lambda e: e.matmul(O[:, wlo:whi], lhsT=vw[:, kt, :], rhs=P[:, wlo:whi], start=(wi == 0), stop=(wi == nw - 1),
                                                           skip_group_check=True),
                                  reads=[vw_b, P_b], writes=[O_b], inc=True)

                        def epi(h=h, hi=hi, pr=pr, t0=t0, O=O, O_b=O_b, gb=gb, gb_b=gb_b, st=st):
                            rec, rec_b = norm_rec(O, O_b)
                            t2, t2_b = T1.next()
                            mul_rec("dve", t2, O, rec, [O_b, rec_b], [t2_b])
                            kb.op("pool", lambda e: e.tensor_tensor(out=t2[:, :], in0=t2[:, :], in1=gb[:, 2, :], op=ALU.mult),
                                  reads=[t2_b, gb_b], writes=[t2_b])
                            t1, t1_b = st["slc"]
                            oc, oc_b = OC[pr][hi]
                            acc, acc_b = ACC.next()
                            ost, ost_b = OST.next()
                            kb.op("pool", lambda e: e.tensor_tensor(out=acc[:, :], in0=oc[:, :], in1=gb[:, 0, :], op=ALU.mult),
                                  reads=[oc_b, gb_b], writes=[acc_b])
                            kb.op("pool", lambda e: e.tensor_tensor(out=acc[:, :], in0=acc[:, :], in1=t1[:, :], op=ALU.add),
                                  reads=[acc_b, t1_b], writes=[acc_b])
                            kb.op("pool", lambda e: e.tensor_tensor(out=ost[:, :], in0=acc[:, :], in1=t2[:, :], op=ALU.add),
                                  reads=[acc_b, t2_b], writes=[ost_b])
                            kb.dma(ds["ON"][h * 64:(h + 1) * 64, t0:t0 + 512], ost[:, :], reads=[ost_b], writes=[dsb["ON"]], q="pool")

                        c.stream.push(TileD(qk, 1.0, pv, epi if wi == nw - 1 else None, lo=wlo, hi=whi))

                cmp_begin(0)
                for hi in range(4):
                    cmp_head(0, hi)
                c.stream.flush()
                topk_dve(0)
                topk_pe(0)
                for qc in range(NCH):
                    nxt = qc + 1 < NCH
                    if nxt:
                        cmp_begin(qc + 1)
                        cmp_head(qc + 1, 0)
                        cmp_head(qc + 1, 1)
                    sw_head(qc, 0)
                    if nxt:
                        cmp_head(qc + 1, 2)
                        cmp_head(qc + 1, 3)
                    sw_head(qc, 1)
                    sw_head(qc, 2)
                    if nxt:
                        topk_dve(qc + 1)
                    sw_head(qc, 3)
                    c.stream.flush()
                    if nxt:
                        topk_pe(qc + 1)
                    CS.pop(qc)
            c.stream.flush()
            if getattr(c, "stop_mid", None) == "nsa":
                return
            wst = sb(nc, es, "wst", [128, 1408], F32)
            wst_b = Buf("wst")
            wsb = sb(nc, es, "wsb", [128, 1408], BF16)
            wsb_b = Buf("wsb")
            wpieces = [(ft, a, hf) for ft in range(8) for a in range(2) for hf in range(2)]

            def wup_piece(ft, a, hf):
                c0 = a * DFF + hf * 1408
                kb.dma(wst[:, :], din["w_up"][ft * 128:(ft + 1) * 128, c0:c0 + 1408], writes=[wst_b])
                kb.op("pool", lambda e: e.tensor_copy(out=wsb[:, :], in_=wst[:, :]), reads=[wst_b], writes=[wsb_b])
                kb.dma(ds["WUP"][hf * 11:(hf + 1) * 11, :, a, ft, :].rearrange("j p c -> p j c"),
                       wsb[:, :].rearrange("p (j c) -> p j c", c=128), reads=[wsb_b], writes=[dsb["WUP"]], q="pool")

            for h in range(8):
                ka, ka_b = KA[h % 2]
                va, va_b = VA[h % 2]
                kb.dma(ka[0:64, :], ds["KN"][h * 64:(h + 1) * 64, :], reads=[dsb["KN"]], writes=[ka_b])
                kb.dma(ka[64:96, :], ds["KPE"][:, :], reads=[dsb["KPE"]], writes=[ka_b])
                for half in range(2):
                    hs = slice(half * 32, (half + 1) * 32)
                    rs_ = slice(half * 4096, (half + 1) * 4096)
                    kb.dma(va[:, hs, 0:64], ds["VM"][rs_, h * 64:(h + 1) * 64].rearrange("(kt p) d -> p kt d", p=128),
                           reads=[dsb["VM"]], writes=[va_b])
                qms = {}

                def qload(qc, h=h, qms=qms):
                    qm, qm_b = QMT.next()
                    kb.dma(qm[:, :], ds["QM"][h, :, qc * 512:(qc + 1) * 512], reads=[dsb["QM"]], writes=[qm_b])
                    qms[qc] = (qm, qm_b)

                qload(0)
                for qc in range(NCH):
                    t0 = qc * 512
                    if qc + 1 < NCH:
                        qload(qc + 1)
                    qm, qm_b = qms.pop(qc)
                    O, O_b = R.OPS.next()
                    nk = 4 * qc + 4
                    for kt in range(nk):
                        d = kt - 4 * qc
                        lo = 128 * d if d > 0 else 0
                        qk = [(ka[0:96, kt * 128:(kt + 1) * 128], qm[0:96, lo:512], [ka_b, qm_b])]
                        if d >= 0:
                            qk.append((ident[:, :], cmk[:, d, lo:lo + 128], [ident_b, CB], lo, lo + 128))

                        def pv(P, P_b, kt=kt, nk=nk, O=O, O_b=O_b, va=va, va_b=va_b, lo=lo):
                            kb.op("pe", lambda e: e.matmul(O[:, lo:512], lhsT=va[:, kt, :], rhs=P[:, lo:512], start=(kt == 0), stop=(kt == nk - 1),
                                                           skip_group_check=True),
                                  reads=[va_b, P_b], writes=[O_b], inc=True)

                        def epi(h=h, t0=t0, O=O, O_b=O_b):
                            rec, rec_b = norm_rec(O, O_b)
                            ost, ost_b = OST.next()
                            mul_rec("dve", ost, O, rec, [O_b, rec_b], [ost_b])
                            kb.dma(ds["OM"][h * 64:(h + 1) * 64, t0:t0 + 512], ost[:, :], reads=[ost_b], writes=[dsb["OM"]], q="pool")

                        c.stream.push(TileD(qk, MLA_SCALE, pv, epi if kt == nk - 1 else None, lo=lo, hi=512))
                    if wpieces:
                        wup_piece(*wpieces.pop(0))
            c.stream.flush()

CH3 = 256
NCH3 = T // CH3


def phase3(c):
    nc, kb = c.nc, c.kb
    din, dc, ds, dsb = c.din, c.dc, c.ds, c.dsb
    ident, ident_b = c.ident, c.ident_b
    with ExitStack() as es:
        W = Buf("p3w")
        wo = sb(nc, es, "wo", [128, 8, 1024], BF16)
        wpg = sb(nc, es, "wpg", [128, 8, 1024], BF16)
        wpl = sb(nc, es, "wpl", [128, 2, 1024], BF16)
        wd = sb(nc, es, "wd", [128, 22, 1024], BF16)
        lnv = sb(nc, es, "lnv", [128, 6, 1024], F32)
        for i, nm in enumerate(("ln1_g", "ln1_b", "ln2_g", "ln2_b", "ln3_g", "ln3_b")):
            kb.dma(lnv[:, i, :], din[nm][0:1, :].partition_broadcast(128), writes=[W])
        with ExitStack() as es1:
            STG = rot_tiles(nc, es1, "stg3", [128, 1024], F32, 4)
            k = 0
            for name, dst in (("w_out", wo), ("w_ple_gate", wpg)):
                for ft in range(8):
                    s_, s_b = STG.next()
                    kb.dma(s_[:, 0:1024], din[name][ft * 128:(ft + 1) * 128, :], writes=[s_b])
                    kb.op("dve" if k % 2 else "pool", lambda e: e.tensor_copy(out=dst[:, ft, :], in_=s_[:, 0:1024]), reads=[s_b], writes=[W])
                    k += 1
            for kt in range(2):
                s_, s_b = STG.next()
                kb.dma(s_[:, 0:1024], din["w_ple"][kt * 128:(kt + 1) * 128, :], writes=[s_b])
                kb.op("dve", lambda e: e.tensor_copy(out=wpl[:, kt, :], in_=s_[:, 0:1024]), reads=[s_b], writes=[W])
            for j in range(22):
                s_, s_b = STG.next()
                kb.dma(s_[:, 0:1024], din["w_down"][j * 128:(j + 1) * 128, :], writes=[s_b])
                kb.op("dve" if j % 2 else "pool", lambda e: e.tensor_copy(out=wd[:, j, :], in_=s_[:, 0:1024]), reads=[s_b], writes=[W])
        kb.barrier()
        XF = rot_tiles(nc, es, "xf3", [128, 1024], F32, 1)
        OT = rot_tiles(nc, es, "oT3", [128, 8, CH3], BF16, 1)
        PF = rot_tiles(nc, es, "pf3", [128, 2, 256], F32, 1)
        PB = rot_tiles(nc, es, "pb3", [128, 2, 256], BF16, 3)
        PT = rot_tiles(nc, es, "pT3", [128, 2, CH3], BF16, 1)
        Y = rot_tiles(nc, es, "y3", [128, 1024], F32, 2)
        Z = rot_tiles(nc, es, "z3", [128, 1024], F32, 1)
        X1 = rot_tiles(nc, es, "x1r", [128, 2, 1024], F32, 2)
        X2 = rot_tiles(nc, es, "x2r", [128, 2, 1024], F32, 2)
        XBF = rot_tiles(nc, es, "xbf3", [128, 1024], BF16, 2)
        X1T = rot_tiles(nc, es, "x1T", [128, 8, CH3], BF16, 2)
        X2T = rot_tiles(nc, es, "x2T", [128, 8, CH3], BF16, 2)
        AT = rot_tiles(nc, es, "aT", [128, 22, CH3], BF16, 1)
        WU = rot_tiles(nc, es, "wu", [128, 2, 8, 128], BF16, 3)
        SG = rot_tiles(nc, es, "sg", [128, CH3], F32, 2)
        ST = rot_tiles(nc, es, "st3", [128, 16], F32, 6)
        PA = rot_tiles(nc, es, "pa3", [128, 512], F32, 3, psum=True)
        PG = rot_tiles(nc, es, "pg3", [128, 2, CH3], F32, 4, psum=True)
        TP = rot_tiles(nc, es, "tp3", [128, 8, 128], BF16, 1, psum=True)

        epsv = sb(nc, es, "epsv", [128, 2], F32)
        kb.op("pool", lambda e: e.memset(epsv[:, 0:1], LN_EPS), writes=[W])
        kb.op("pool", lambda e: e.memset(epsv[:, 1:2], 4.0 * LN_EPS), writes=[W])

        def ln_pair_steps(items, st, st_b, gi, eps_col=0):
            def s1():
                for k, it_ in enumerate(items):
                    j_ap, j_b = it_.get("junk") or (it_["out_ap"], it_["out_b"])
                    y, y_b = it_["y"], it_["y_b"]
                    kb.op("act", lambda e: e.activation(out=j_ap, in_=y[:, :], func=AF.Square, accum_out=st[:, 8 * k + 2:8 * k + 3]),
                          reads=[y_b], writes=[j_b, st_b])

            def s2():
                for k in range(len(items)):
                    b = 8 * k
                    kb.op("dve", lambda e: e.tensor_scalar(out=st[:, b + 3:b + 4], in0=st[:, b:b + 1], scalar1=st[:, b + 1:b + 2], scalar2=1.0 / D,
                                                           op0=ALU.add, op1=ALU.mult), reads=[st_b], writes=[st_b])
                    kb.op("dve", lambda e: e.scalar_tensor_tensor(out=st[:, b + 4:b + 5], in0=st[:, b + 3:b + 4], scalar=-1.0, in1=st[:, b + 3:b + 4],
                                                                  op0=ALU.mult, op1=ALU.mult), reads=[st_b], writes=[st_b])
                    kb.op("dve", lambda e: e.scalar_tensor_tensor(out=st[:, b + 5:b + 6], in0=st[:, b + 2:b + 3], scalar=1.0 / D, in1=st[:, b + 4:b + 5],
                                                                  op0=ALU.mult, op1=ALU.add), reads=[st_b], writes=[st_b])

            def s3():
                n = len(items)
                src = st[:, 0:8 * n].rearrange("p (k c) -> p k c", c=8)[:, :, 5:6]
                dst = st[:, 0:8 * n].rearrange("p (k c) -> p k c", c=8)[:, :, 6:7]
                kb.op("act", lambda e: e.activation(out=dst, in_=src, func=AF.Sqrt, bias=epsv[:, eps_col:eps_col + 1]), reads=[st_b, W], writes=[st_b])
                src2 = st[:, 0:8 * n].rearrange("p (k c) -> p k c", c=8)[:, :, 6:7]
                dst2 = st[:, 0:8 * n].rearrange("p (k c) -> p k c", c=8)[:, :, 7:8]
                kb.op("dve", lambda e: e.reciprocal(out=dst2, in_=src2), reads=[st_b], writes=[st_b])

            def s4(k):
                def f():
                    it_ = items[k]
                    b = 8 * k
                    y, y_b = it_["y"], it_["y_b"]
                    z, z_b = Z.next()
                    kb.op("dve", lambda e: e.scalar_tensor_tensor(out=z[:, :], in0=y[:, :], scalar=st[:, b + 3:b + 4], in1=lnv[:, gi, :],
                                                                  op0=ALU.subtract, op1=ALU.mult), reads=[y_b, st_b, W], writes=[z_b])
                    kb.op("dve", lambda e: e.scalar_tensor_tensor(out=it_["out_ap"], in0=z[:, :], scalar=st[:, b + 7:b + 8], in1=lnv[:, gi + 1, :],
                                                                  op0=ALU.mult, op1=ALU.add), reads=[z_b, st_b, W], writes=[it_["out_b"]])
                    if it_.get("post") is not None:
                        it_["post"]()
                return f

            return [s1, s2, s3, s4(0), s4(1)]

        def cast_bf(src, src_b):
            xb, xb_b = XBF.next()
            kb.op("pool", lambda e: e.tensor_copy(out=xb[:, :], in_=src), reads=[src_b], writes=[xb_b])
            return xb, xb_b

        def transposes(xb, xb_b, s_, dstT, dstT_b):
            tp, tp_b = TP.next()
            for ft in range(8):
                kb.op("pe", lambda e: e.transpose(out=tp[:, ft, :], in_=xb[:, ft * 128:(ft + 1) * 128], identity=ident[:, :]),
                      reads=[xb_b, ident_b], writes=[tp_b], inc=(ft == 7))
            kb.op("act", lambda e: e.activation(out=dstT[:, :, s_ * 128:(s_ + 1) * 128], in_=tp[:, :, :], func=AF.Copy), reads=[tp_b], writes=[dstT_b])

        S = {}

        def A_mm(ch):
            t0 = ch * CH3
            st = S.setdefault(ch, {})
            oT, oT_b = OT.next()
            pf, pf_b = PF.next()
            kb.dma(oT[:, 0:4, :], ds["ON"].rearrange("(c p) t -> p c t", p=128)[:, :, t0:t0 + CH3], reads=[dsb["ON"]], writes=[oT_b])
            kb.dma(oT[:, 4:8, :], ds["OM"].rearrange("(c p) t -> p c t", p=128)[:, :, t0:t0 + CH3], reads=[dsb["OM"]], writes=[oT_b])
            kb.dma(pf[:, :, :], din["p"][t0:t0 + CH3, :].rearrange("(s p) d -> p s d", p=128), writes=[pf_b])
            pb_, pb_b = PB.next()
            kb.op("pool", lambda e: e.tensor_copy(out=pb_[:, :, :], in_=pf[:, :, :]), reads=[pf_b], writes=[pb_b])
            st["pb"] = (pb_, pb_b)
            st["x1"] = X1.next()
            st["x1T"] = X1T.next()
            st["ys"] = []
            sst, sst_b = ST.next()
            st["sst"] = (sst, sst_b)
            for s_ in range(2):
                y, y_b = Y.next()
                xf, xf_b = XF.next()
                kb.dma(xf[:, :], din["x"][t0 + s_ * 128:t0 + (s_ + 1) * 128, :], writes=[xf_b])
                for n in range(2):
                    pa, pa_b = PA.next()
                    for kt in range(8):
                        kb.op("pe", lambda e: e.matmul(pa[:, :], lhsT=oT[:, kt, s_ * 128:(s_ + 1) * 128], rhs=wo[:, kt, n * 512:(n + 1) * 512],
                                                       start=(kt == 0), stop=(kt == 7)), reads=[oT_b, W], writes=[pa_b], inc=(kt == 7))
                    kb.op("dve", lambda e: e.scalar_tensor_tensor(out=y[:, n * 512:(n + 1) * 512], in0=xf[:, n * 512:(n + 1) * 512], scalar=ALPHA,
                                                                  in1=pa[:, :], op0=ALU.mult, op1=ALU.add, accum_out=sst[:, 8 * s_ + n:8 * s_ + n + 1]),
                          reads=[xf_b, pa_b], writes=[y_b, sst_b])
                st["ys"].append((y, y_b))

        def A_ln(ch):
            st = S[ch]
            x1, x1_b = st["x1"]
            sst, sst_b = st["sst"]
            items = []
            for s_ in range(2):
                y, y_b = st["ys"][s_]

                def post(s_=s_):
                    st["xb%d" % s_] = cast_bf(x1[:, s_, :], x1_b)

                items.append(dict(y=y, y_b=y_b, out_ap=x1[:, s_, :], out_b=x1_b, post=post))
            return ln_pair_steps(items, sst, sst_b, 0)

        def A_tr(ch, s_):
            st = S[ch]
            x1T, x1T_b = st["x1T"]
            transposes(*st["xb%d" % s_], s_, x1T, x1T_b)

        def B_up(ch, j):
            st = S[ch]
            x1T, x1T_b = st["x1T"]
            if j == 0:
                st["aT"] = AT.next()
            aT, aT_b = st["aT"]
            wu, wu_b = WU.next()
            kb.dma(wu[:, :, :, :], ds["WUP"][j], reads=[dsb["WUP"]], writes=[wu_b])
            pg, pg_b = PG.next()
            for a in range(2):
                for ft in range(8):
                    kb.op("pe", lambda e: e.matmul(pg[:, a, :], lhsT=wu[:, a, ft, :], rhs=x1T[:, ft, :], start=(ft == 0), stop=(ft == 7)),
                          reads=[wu_b, x1T_b], writes=[pg_b], inc=(a == 1 and ft == 7))
            sg, sg_b = SG.next()
            kb.op("act", lambda e: e.activation(out=sg[:, :], in_=pg[:, 0, :], func=AF.Silu), reads=[pg_b], writes=[sg_b])
            kb.op("dve", lambda e: e.tensor_tensor(out=aT[:, j, :], in0=pg[:, 1, :], in1=sg[:, :], op=ALU.mult), reads=[pg_b, sg_b], writes=[aT_b])

        def B_down(ch):
            st = S[ch]
            aT, aT_b = st["aT"]
            x1, x1_b = st["x1"]
            st["x2"] = X2.next()
            st["x2T"] = X2T.next()
            x2, x2_b = st["x2"]
            ys = []
            sst, sst_b = ST.next()
            st["sst2"] = (sst, sst_b)
            for s_ in range(2):
                y, y_b = Y.next()
                for n in range(2):
                    pa, pa_b = PA.next()
                    for j in range(22):
                        kb.op("pe", lambda e: e.matmul(pa[:, :], lhsT=aT[:, j, s_ * 128:(s_ + 1) * 128], rhs=wd[:, j, n * 512:(n + 1) * 512],
                                                       start=(j == 0), stop=(j == 21)), reads=[aT_b, W], writes=[pa_b], inc=(j == 21))
                    kb.op("dve", lambda e: e.scalar_tensor_tensor(out=y[:, n * 512:(n + 1) * 512], in0=x1[:, s_, n * 512:(n + 1) * 512], scalar=ALPHA,
                                                                  in1=pa[:, :], op0=ALU.mult, op1=ALU.add, accum_out=sst[:, 8 * s_ + n:8 * s_ + n + 1]),
                          reads=[x1_b, pa_b], writes=[y_b, sst_b])
                ys.append((y, y_b))
            st["ys2"] = ys

        def B_ln(ch):
            st = S[ch]
            x2, x2_b = st["x2"]
            sst, sst_b = st["sst2"]
            items = []
            for s_ in range(2):
                y, y_b = st["ys2"][s_]

                def post(s_=s_):
                    st["x2b%d" % s_] = cast_bf(x2[:, s_, :], x2_b)

                items.append(dict(y=y, y_b=y_b, out_ap=x2[:, s_, :], out_b=x2_b, post=post))
            return ln_pair_steps(items, sst, sst_b, 2)

        def B_tr(ch, s_):
            st = S[ch]
            x2T, x2T_b = st["x2T"]
            transposes(*st["x2b%d" % s_], s_, x2T, x2T_b)

        def C_pre(ch):
            st = S[ch]
            pb_, pb_b = st["pb"]
            pT, pT_b = PT.next()
            tp, tp_b = TP.next()
            for s_ in range(2):
                for kt in range(2):
                    kb.op("pe", lambda e: e.transpose(out=tp[:, s_ * 2 + kt, :], in_=pb_[:, s_, kt * 128:(kt + 1) * 128], identity=ident[:, :]),
                          reads=[pb_b, ident_b], writes=[tp_b], inc=(s_ == 1 and kt == 1))
            for s_ in range(2):
                kb.op("act", lambda e: e.activation(out=pT[:, :, s_ * 128:(s_ + 1) * 128], in_=tp[:, 2 * s_:2 * s_ + 2, :], func=AF.Copy),
                      reads=[tp_b], writes=[pT_b])
            st["pT"] = (pT, pT_b)

        def C_mm(ch, s_):
            st = S[ch]
            x2, x2_b = st["x2"]
            x2T, x2T_b = st["x2T"]
            pT, pT_b = st["pT"]
            if s_ == 0:
                st["sst3"] = ST.next()
                st["ys3"] = []
            sst, sst_b = st["sst3"]
            y, y_b = Y.next()
            for n in range(2):
                pa, pa_b = PA.next()
                for kt in range(8):
                    kb.op("pe", lambda e: e.matmul(pa[:, :], lhsT=x2T[:, kt, s_ * 128:(s_ + 1) * 128], rhs=wpg[:, kt, n * 512:(n + 1) * 512],
                                                   start=(kt == 0), stop=(kt == 7)), reads=[x2T_b, W], writes=[pa_b], inc=(kt == 7))
                kb.op("act", lambda e: e.activation(out=y[:, n * 512:(n + 1) * 512], in_=pa[:, :], func=AF.Tanh, scale=0.5), reads=[pa_b], writes=[y_b])
                pp, pp_b = PA.next()
                for kt in range(2):
                    kb.op("pe", lambda e: e.matmul(pp[:, :], lhsT=pT[:, kt, s_ * 128:(s_ + 1) * 128], rhs=wpl[:, kt, n * 512:(n + 1) * 512],
                                                   start=(kt == 0), stop=(kt == 1)), reads=[pT_b, W], writes=[pp_b], inc=(kt == 1))
                kb.op("dve", lambda e: e.scalar_tensor_tensor(out=y[:, n * 512:(n + 1) * 512], in0=y[:, n * 512:(n + 1) * 512], scalar=1.0, in1=pp[:, :],
                                                              op0=ALU.add, op1=ALU.mult), reads=[pp_b, y_b], writes=[y_b])
                kb.op("dve", lambda e: e.scalar_tensor_tensor(out=y[:, n * 512:(n + 1) * 512], in0=x2[:, s_, n * 512:(n + 1) * 512], scalar=2.0 * ALPHA,
                                                              in1=y[:, n * 512:(n + 1) * 512], op0=ALU.mult, op1=ALU.add,
                                                              accum_out=sst[:, 8 * s_ + n:8 * s_ + n + 1]),
                      reads=[x2_b, y_b], writes=[y_b, sst_b])
            st["ys3"].append((y, y_b))

        def C_ln(ch):
            t0 = ch * CH3
            st = S[ch]
            sst, sst_b = st["sst3"]
            zj, zj_b = Z.items[0]
            items = []
            for s_ in range(2):
                y, y_b = st["ys3"][s_]

                def post(s_=s_, y=y, y_b=y_b):
                    kb.dma(c.out[t0 + s_ * 128:t0 + (s_ + 1) * 128, :], y[:, :], reads=[y_b], writes=[c.out_b], q="pool")
                    if s_ == 1:
                        S.pop(ch)

                items.append(dict(y=y, y_b=y_b, out_ap=y[:, :], out_b=y_b, post=post, junk=(zj[:, :], zj_b)))
            return ln_pair_steps(items, sst, sst_b, 4, eps_col=1)

        def run_all(steps):
            for f in steps:
                f()

        A_mm(0)
        run_all(A_ln(0))
        A_tr(0, 0)
        A_tr(0, 1)
        sched = {
            0: [("Bln", 0)], 1: [("Bln", 1)], 2: [("Bln", 2)], 3: [("Bln", 3)], 4: [("Bln", 4)], 5: [("Btr", 0)], 6: [("Btr", 1), ("Amm", 0)],
            7: [("Aln", 0)], 8: [("Aln", 1), ("Cpre", 0)], 9: [("Aln", 2)], 10: [("Aln", 3), ("Cmm", 0)], 11: [("Aln", 4)], 12: [("Cmm", 1)],
            13: [("Cln", 0), ("Atr", 0)], 14: [("Cln", 1)], 15: [("Cln", 2), ("Atr", 1)], 16: [("Cln", 3)], 17: [("Cln", 4)],
        }
        chains = {}

        def run_slot(it, j):
            nxt = it + 1 < NCH3
            prv = it >= 1
            for name, k in sched.get(j, []):
                if name == "Bln" and prv:
                    if k == 0:
                        chains["B"] = B_ln(it - 1)
                    chains["B"][k]()
                elif name == "Btr" and prv:
                    B_tr(it - 1, k)
                elif name == "Amm" and nxt:
                    A_mm(it + 1)
                elif name == "Aln" and nxt:
                    if k == 0:
                        chains["A"] = A_ln(it + 1)
                    chains["A"][k]()
                elif name == "Atr" and nxt:
                    A_tr(it + 1, k)
                elif name == "Cpre" and prv:
                    C_pre(it - 1)
                elif name == "Cmm" and prv:
                    C_mm(it - 1, k)
                elif name == "Cln" and prv:
                    if k == 0:
                        chains["C"] = C_ln(it - 1)
                    chains["C"][k]()

        for it in range(NCH3 + 1):
            if it < NCH3:
                for j in range(22):
                    B_up(it, j)
                    run_slot(it, j)
                B_down(it)
            else:
                for j in range(22):
                    run_slot(it, j)


_CACHE = {}


def kernel(**inputs):
    if "nc" not in _CACHE:
        _CACHE["nc"] = build_program()
        _CACHE["consts"] = host_constants()
    nc = _CACHE["nc"]
    consts = _CACHE["consts"]
    B = inputs["x"].shape[0]
    shared = {}
    for k, shp in IN_SPECS.items():
        if k in ("x", "p"):
            continue
        shared[k] = np.ascontiguousarray(np.asarray(inputs[k], dtype=np.float32)[0].reshape(shp))
    for k, v in consts.items():
        shared["c_" + k] = v
    in_maps = []
    for b in range(B):
        m = dict(shared)
        m["x"] = np.ascontiguousarray(np.asarray(inputs["x"], dtype=np.float32)[b])
        m["p"] = np.ascontiguousarray(np.asarray(inputs["p"], dtype=np.float32)[0, b])
        in_maps.append(m)
    res = run_bass_kernel_spmd(nc, in_maps, core_ids=list(range(B)))
    return np.stack([np.asarray(r["out"], dtype=np.float32) for r in res.results], 0)
```

```python
from contextlib import ExitStack
import numpy as np
import ml_dtypes
import concourse.bass as bass
import concourse.mybir as mybir
from concourse.bass_utils import run_bass_kernel_spmd

F32 = mybir.dt.float32
BF16 = mybir.dt.bfloat16
AF = mybir.ActivationFunctionType
ALU = mybir.AluOpType
AX = mybir.AxisListType
bf16 = ml_dtypes.bfloat16

T = 8192
D = 1024
NCH = T // 512
NKT = T // 128
DFF = 2816
BIG = 30000.0
ALPHA = 2.0 ** 0.25
LN_EPS = 1e-5
RMS_EPS = 1e-6
MLA_SCALE = 96.0 ** -0.5
ALIBI_CUT = 160.0


class Buf:
    __slots__ = ("name", "last_w", "readers")

    def __init__(self, name=""):
        self.name = name
        self.last_w = None
        self.readers = {}


class KB:
    NSLOT = 8

    def __init__(self, nc, es):
        self.nc = nc
        self.E = {"pe": nc.tensor, "act": nc.scalar, "dve": nc.vector, "pool": nc.gpsimd, "sp": nc.sync}
        self.sem = {}
        self.seq = {}
        self.known = {}
        for e in self.E:
            self.sem[("e", e)] = es.enter_context(nc.semaphore("s_" + e))
            self.seq[e] = 0
            self.known[e] = {}
        self.slots = {}
        self.slot_rr = {}
        for q in ("sp", "pool"):
            self.slots[q] = []
            self.slot_rr[q] = 0
            for i in range(self.NSLOT):
                key = ("d", q, i)
                self.sem[key] = es.enter_context(nc.semaphore("d_%s%d" % (q, i)))
                self.slots[q].append([key, 0])

    def _collect(self, reads, writes):
        deps = {}
        for b in reads:
            if b.last_w is not None:
                k, v = b.last_w
                o = deps.get(k)
                deps[k] = (max(v, o[0]) if o else v, True)
        for b in writes:
            if b.last_w is not None:
                k, v = b.last_w
                o = deps.get(k)
                if o is None:
                    deps[k] = (v, False)
                elif v > o[0]:
                    deps[k] = (v, o[1])
            for k, v in b.readers.items():
                o = deps.get(k)
                if o is None:
                    deps[k] = (v, False)
                elif v > o[0]:
                    deps[k] = (v, o[1])
        return deps

    def _wait(self, eng, deps):
        kn = self.known[eng]
        for k, (v, raw) in deps.items():
            if k == ("e", eng):
                if eng == "pe" or not raw:
                    continue
            if kn.get(k, 0) >= v:
                continue
            self.E[eng].wait_ge(self.sem[k], v)
            kn[k] = v

    def _record(self, dep, reads, writes):
        k, v = dep
        for b in reads:
            if b.readers.get(k, 0) < v:
                b.readers[k] = v
        for b in writes:
            b.last_w = dep
            b.readers = {}

    def op(self, eng, fn, reads=(), writes=(), inc=True):
        self._wait(eng, self._collect(reads, writes))
        ins = fn(self.E[eng])
        if inc:
            ins.then_inc(self.sem[("e", eng)], 1)
            self.seq[eng] += 1
            dep = (("e", eng), self.seq[eng])
        else:
            dep = (("e", eng), self.seq[eng] + 1)
        self._record(dep, reads, writes)
        return ins

    def dma(self, out, in_, reads=(), writes=(), q="sp", **kw):
        sl = self.slots[q][self.slot_rr[q]]
        self.slot_rr[q] = (self.slot_rr[q] + 1) % self.NSLOT
        deps = self._collect(reads, writes)
        if sl[1] > 0:
            o = deps.get(sl[0])
            if o is None or o[0] < sl[1]:
                deps[sl[0]] = (sl[1], True)
        self._wait(q, deps)
        self.E[q].dma_start(out=out, in_=in_, **kw).then_inc(self.sem[sl[0]], 16)
        sl[1] += 16
        self._record((sl[0], sl[1]), reads, writes)

    def barrier(self):
        cur = {}
        for e in self.E:
            cur[("e", e)] = self.seq[e]
        for q in self.slots:
            for key, cnt in self.slots[q]:
                cur[key] = cnt
        for e in self.E:
            kn = self.known[e]
            for k, v in cur.items():
                if v == 0 or k == ("e", e):
                    continue
                if kn.get(k, 0) >= v:
                    continue
                self.E[e].wait_ge(self.sem[k], v)
                kn[k] = v


class Rot:
    def __init__(self, items):
        self.items = items
        self.i = 0

    def next(self):
        it = self.items[self.i]
        self.i = (self.i + 1) % len(self.items)
        return it


def host_constants():
    c = {}
    c["ident"] = np.eye(128, dtype=np.float32).astype(bf16)
    pos = np.arange(T, dtype=np.float32)
    inv_freq = (10000.0 ** (-np.arange(16, dtype=np.float32) / 16)).astype(np.float32)
    ang = (pos[:, None] * inv_freq[None, :]).astype(np.float32)
    cos = np.cos(ang).astype(np.float32).T
    sin = np.sin(ang).astype(np.float32).T
    c["ropeC"] = np.ascontiguousarray(np.concatenate([cos, cos], 0))
    c["ropeS"] = np.ascontiguousarray(np.concatenate([-sin, sin], 0))
    p = np.arange(128)[:, None]
    j = np.arange(512)[None, :]
    def addmask(valid):
        return np.where(valid, 0.0, -BIG).astype(np.float32).astype(bf16)
    c["cmk"] = addmask(np.stack([(p + 128 * d <= j) for d in range(4)], 1))
    c["wmk"] = addmask(np.stack([(j < 128 * d + p) for d in range(4)], 1))
    c["pmk"] = addmask(np.stack([(16 * p + 31 <= 512 * m + j) for m in range(5)], 1))
    slopes = 2.0 ** (-np.arange(1, 9, dtype=np.float64))
    t = np.arange(T)
    qaug = np.zeros((8, 5, T), np.float32)
    for h in range(8):
        s = slopes[h]
        qaug[h, 0] = s
        qaug[h, 1] = s
        qaug[h, 2] = -s * (t // 128) * 128
        qaug[h, 3] = -s * (t % 128)
        qaug[h, 4] = -s * ((t % 128) - 31)
    c["qaug"] = qaug.astype(bf16)
    ka = np.zeros((5, T), np.float32)
    ka[0] = (t // 128) * 128
    ka[1] = t % 128
    ka[2] = 1
    ka[3] = 1
    c["kaug"] = ka.astype(bf16)
    cc = np.arange(512)
    kc = np.zeros((5, 512), np.float32)
    kc[0] = (cc // 128) * 2048
    kc[1] = (cc % 128) * 16
    kc[2] = 1
    kc[4] = 1
    c["kaugc"] = kc.astype(bf16)
    for k in ("qaug", "kaug", "kaugc"):
        assert np.all(c[k].astype(np.float32) == {"qaug": qaug, "kaug": ka, "kaugc": kc}[k])
    E = np.zeros((128, 64, 128), np.float32)
    for kt in range(64):
        E[2 * kt, kt, :64] = BIG
        E[2 * kt + 1, kt, 64:] = BIG
    c["eexp"] = E.astype(bf16)
    M = np.zeros((512, 128), np.float32)
    for cidx in range(511):
        for s in (cidx, cidx + 1):
            M[cidx, s // 4] += 1
    c["impm"] = M.astype(bf16)
    keep = np.zeros((128, 256), np.float32)
    add = np.zeros((128, 256), np.float32)
    pp = np.arange(128)
    for col in range(256):
        r = col - 128
        if r < -1:
            keep[:, col] = 1
        elif r == -1:
            keep[:, col] = (pp >= 64)
            add[:, col] = 1e6 * (pp < 64)
        elif r == 0:
            add[:, col] = 1e6
        elif r == 1:
            add[:, col] = np.where(pp >= 64, 1e6, -1.0)
        else:
            add[:, col] = -1.0
    c["keept"] = keep
    c["addt"] = add
    return c


CONST_SPECS = {
    "ident": ([128, 128], BF16), "ropeC": ([32, T], F32), "ropeS": ([32, T], F32),
    "cmk": ([128, 4, 512], BF16), "wmk": ([128, 4, 512], BF16), "pmk": ([128, 5, 512], BF16),
    "qaug": ([8, 5, T], BF16), "kaug": ([5, T], BF16), "kaugc": ([5, 512], BF16),
    "eexp": ([128, 64, 128], BF16), "impm": ([512, 128], BF16),
    "keept": ([128, 256], F32), "addt": ([128, 256], F32),
}

IN_SPECS = {
    "x": [T, D], "p": [T, 256], "w_in": [D, 1720], "w_ck1": [2048, 256], "w_ck2": [256, 64],
    "pos_ck": [32, 64], "w_cv1": [2048, 256], "w_cv2": [256, 64], "pos_cv": [32, 64],
    "mla_q_norm": [256, 1], "w_uq": [256, 768], "mla_kv_norm": [128, 1], "w_ukv": [128, 1024],
    "w_out": [D, D], "ln1_g": [1, D], "ln1_b": [1, D], "w_up": [D, 2 * DFF], "w_down": [DFF, D],
    "ln2_g": [1, D], "ln2_b": [1, D], "w_ple_gate": [D, D], "w_ple": [256, D],
    "ln3_g": [1, D], "ln3_b": [1, D],
}

SCRATCH_SPECS = {
    "QN": ([512, T], BF16),
    "KX": ([4, 128, T], BF16),
    "VSW": ([T, 256], BF16),
    "G": ([24, T], F32),
    "KPE": ([32, T], BF16),
    "QM": ([8, 96, T], BF16),
    "KN": ([512, T], BF16),
    "VM": ([T, 512], BF16),
    "ON": ([512, T], BF16),
    "OM": ([512, T], BF16),
    "WUP": ([22, 128, 2, 8, 128], BF16),
}


def sb(nc, es, name, shape, dt):
    return es.enter_context(nc.sbuf_tensor(name, list(shape), dt))


def ps(nc, es, name, shape, dt):
    return es.enter_context(nc.psum_tensor(name, list(shape), dt))


def rot_tiles(nc, es, name, shape, dt, n, psum=False):
    items = []
    for i in range(n):
        t = (ps if psum else sb)(nc, es, "%s%d" % (name, i), shape, dt)
        items.append((t, Buf("%s%d" % (name, i))))
    return Rot(items)


class Ctx:
    pass


def build_program(stop_after=None, debug_scratch=False, stop_mid=None):
    nc = bass.Bass("TRN2", target_bir_lowering=False)
    c = Ctx()
    c.nc = nc
    c.din = {k: nc.dram_tensor(k, v, F32, kind="ExternalInput").ap() for k, v in IN_SPECS.items()}
    c.dc = {k: nc.dram_tensor("c_" + k, v[0], v[1], kind="ExternalInput").ap() for k, v in CONST_SPECS.items()}
    skind = "ExternalOutput" if debug_scratch else "Internal"
    c.ds = {k: nc.dram_tensor("s_" + k, v[0], v[1], kind=skind).ap() for k, v in SCRATCH_SPECS.items()}
    c.dsb = {k: Buf("s_" + k) for k in SCRATCH_SPECS}
    c.out = nc.dram_tensor("out", [T, D], F32, kind="ExternalOutput").ap()
    c.out_b = Buf("out")
    c.stop_mid = stop_mid
    if stop_mid == "compress":
        c.dbg = {"kcmp": nc.dram_tensor("dbg_kcmp", [69, 2, 512], BF16, kind="ExternalOutput").ap(),
                 "vcmp": nc.dram_tensor("dbg_vcmp", [128, 4, 2, 128], BF16, kind="ExternalOutput").ap()}
    with ExitStack() as es0:
        kb = KB(nc, es0)
        c.kb = kb
        c.ident = sb(nc, es0, "ident", [128, 128], BF16)
        c.ident_b = Buf("ident")
        kb.dma(c.ident[:], c.dc["ident"][:], writes=[c.ident_b])
        phases = [phase1, phase2, phase3]
        for ph in phases:
            ph(c)
            kb.barrier()
            if stop_after == ph.__name__:
                break
        kb.barrier()
    return nc


def phase1(c):
    nc, kb = c.nc, c.kb
    din, dc, ds, dsb = c.din, c.dc, c.ds, c.dsb
    with ExitStack() as es:
        wtr = sb(nc, es, "wtr", [128, 8, 1048], BF16)
        wtok = sb(nc, es, "wtok", [128, 8, 640], BF16)
        wkpA = sb(nc, es, "wkpA", [128, 8, 96], BF16)
        wkpB = sb(nc, es, "wkpB", [128, 8, 96], BF16)
        wuqA = sb(nc, es, "wuqA", [128, 2, 8, 96], BF16)
        wuqB = sb(nc, es, "wuqB", [128, 2, 8, 96], BF16)
        wkvK = sb(nc, es, "wkvK", [128, 8, 64], BF16)
        wkvV = sb(nc, es, "wkvV", [128, 8, 64], BF16)
        gq = sb(nc, es, "gq", [128, 2], F32)
        gkv = sb(nc, es, "gkv", [128, 1], F32)
        W = Buf("p1w")
        es_stg = ExitStack()
        stg = rot_tiles(nc, es_stg, "stg", [128, 1720], F32, 2)
        kb.op("pool", lambda e: e.memset(wkpA[:], 0.0), writes=[W])
        kb.op("pool", lambda e: e.memset(wkpB[:], 0.0), writes=[W])
        kb.op("pool", lambda e: e.memset(wuqB[:], 0.0), writes=[W])
        for kt in range(2):
            kb.dma(gq[:, kt:kt + 1], din["mla_q_norm"][kt * 128:(kt + 1) * 128, :], writes=[W])
        kb.dma(gkv[:], din["mla_kv_norm"][:, :], writes=[W])
        for ft in range(8):
            s, sbuf_ = stg.next()
            kb.dma(s[:], din["w_in"][ft * 128:(ft + 1) * 128, :], writes=[sbuf_])
            kb.op("dve", lambda e: e.tensor_copy(out=wtr[:, ft, 0:896], in_=s[:, 0:896]), reads=[sbuf_], writes=[W])
            kb.op("pool", lambda e: e.tensor_copy(out=wtr[:, ft, 896:1024], in_=s[:, 1024:1152]), reads=[sbuf_], writes=[W])
            kb.op("pool", lambda e: e.tensor_copy(out=wtr[:, ft, 1024:1048], in_=s[:, 1280:1304]), reads=[sbuf_], writes=[W])
            kb.op("dve", lambda e: e.tensor_copy(out=wtok[:, ft, 0:384], in_=s[:, 1304:1688]), reads=[sbuf_], writes=[W])
            kb.op("pool", lambda e: e.tensor_copy(out=wtok[:, ft, 384:512], in_=s[:, 896:1024]), reads=[sbuf_], writes=[W])
            kb.op("pool", lambda e: e.tensor_copy(out=wtok[:, ft, 512:640], in_=s[:, 1152:1280]), reads=[sbuf_], writes=[W])
            kb.op("dve", lambda e: e.tensor_copy(out=wkpA[:, ft, 64:96], in_=s[:, 1688:1720]), reads=[sbuf_], writes=[W])
            kb.op("dve", lambda e: e.tensor_copy(out=wkpB[:, ft, 64:80], in_=s[:, 1704:1720]), reads=[sbuf_], writes=[W])
            kb.op("dve", lambda e: e.tensor_copy(out=wkpB[:, ft, 80:96], in_=s[:, 1688:1704]), reads=[sbuf_], writes=[W])
        for kt in range(2):
            s, sbuf_ = stg.next()
            kb.dma(s[:, 0:768], din["w_uq"][kt * 128:(kt + 1) * 128, :], writes=[sbuf_])
            s3 = s[:, 0:768].rearrange("p (h c) -> p h c", c=96)
            kb.op("dve", lambda e: e.tensor_scalar(out=wuqA[:, kt, :, :], in0=s3, scalar1=gq[:, kt:kt + 1], scalar2=None,
                                                   op0=ALU.mult), reads=[sbuf_, W], writes=[W])
            kb.op("dve", lambda e: e.tensor_scalar(out=wuqB[:, kt, :, 64:80], in0=s3[:, :, 80:96], scalar1=gq[:, kt:kt + 1],
                                                   scalar2=None, op0=ALU.mult), reads=[sbuf_, W], writes=[W])
            kb.op("dve", lambda e: e.tensor_scalar(out=wuqB[:, kt, :, 80:96], in0=s3[:, :, 64:80], scalar1=gq[:, kt:kt + 1],
                                                   scalar2=None, op0=ALU.mult), reads=[sbuf_, W], writes=[W])
        s, sbuf_ = stg.next()
        kb.dma(s[:, 0:1024], din["w_ukv"][:, :], writes=[sbuf_])
        s3 = s[:, 0:1024].rearrange("p (h c) -> p h c", c=128)
        kb.op("dve", lambda e: e.tensor_scalar(out=wkvK[:, :, :], in0=s3[:, :, 0:64], scalar1=gkv[:, 0:1], scalar2=None,
                                               op0=ALU.mult), reads=[sbuf_, W], writes=[W])
        kb.op("dve", lambda e: e.tensor_scalar(out=wkvV[:, :, :], in0=s3[:, :, 64:128], scalar1=gkv[:, 0:1], scalar2=None,
                                               op0=ALU.mult), reads=[sbuf_, W], writes=[W])
        kb.barrier()
        es_stg.close()
        XF = rot_tiles(nc, es, "xf", [128, 4, 1024], F32, 2)
        XT = rot_tiles(nc, es, "xT", [128, 8, 512], BF16, 2)
        STT = rot_tiles(nc, es, "stT", [128, 8, 512], BF16, 2)
        GST = rot_tiles(nc, es, "gst", [24, 512], F32, 2)
        RC = rot_tiles(nc, es, "rc", [96, 2, 512], F32, 3)
        KST = rot_tiles(nc, es, "kst", [96, 512], BF16, 2)
        RT = rot_tiles(nc, es, "rt", [96, 2, 512], F32, 2)
        VST = rot_tiles(nc, es, "vst", [128, 4, 256], BF16, 2)
        SS = rot_tiles(nc, es, "ss", [128, 4], F32, 2)
        JK = rot_tiles(nc, es, "jk", [128, 256], F32, 2)
        CN = rot_tiles(nc, es, "cn", [128, 384], BF16, 2)
        CT = rot_tiles(nc, es, "cT", [128, 3, 512], BF16, 2)
        QST = rot_tiles(nc, es, "qst", [96, 8, 512], BF16, 2)
        KNST = rot_tiles(nc, es, "knst", [128, 4, 512], BF16, 2)
        VMST = rot_tiles(nc, es, "vmst", [128, 4, 512], BF16, 2)
        TP = rot_tiles(nc, es, "tp", [128, 1024], BF16, 2, psum=True)
        TPQ = rot_tiles(nc, es, "tpq", [128, 3, 128], BF16, 1, psum=True)
        PS = rot_tiles(nc, es, "ps", [128, 512], F32, 5, psum=True)
        ident, ident_b = c.ident, c.ident_b
        xv = din["x"]
        XB = rot_tiles(nc, es, "xbb", [128, 4, 1024], BF16, 2)
        LS = {}

        def L_load(ch):
            t0 = ch * 512
            xf, xf_b = XF.next()
            kb.dma(xf[:], xv[t0:t0 + 512, :].rearrange("(s p) d -> p s d", p=128), writes=[xf_b])
            rc, rc_b = RC.next()
            kb.dma(rc[64:96, 0, :], dc["ropeC"][:, t0:t0 + 512], writes=[rc_b])
            kb.dma(rc[64:96, 1, :], dc["ropeS"][:, t0:t0 + 512], writes=[rc_b])
            LS[ch] = {"xf": (xf, xf_b), "rc": (rc, rc_b)}

        def L_cast(ch):
            xf, xf_b = LS[ch]["xf"]
            xb, xb_b = XB.next()
            kb.op("dve", lambda e: e.tensor_copy(out=xb[:, 0:3, :], in_=xf[:, 0:3, :]), reads=[xf_b], writes=[xb_b])
            kb.op("pool", lambda e: e.tensor_copy(out=xb[:, 3:4, :], in_=xf[:, 3:4, :]), reads=[xf_b], writes=[xb_b])
            LS[ch]["xb"] = (xb, xb_b)

        def L_tr(ch):
            xb, xb_b = LS[ch]["xb"]
            xT, xT_b = XT.next()
            for f2 in range(4):
                tp, tp_b = TP.next()
                for fi in range(2):
                    ft = f2 * 2 + fi
                    for s_ in range(4):
                        last = (fi == 1 and s_ == 3)
                        kb.op("pe", lambda e: e.transpose(out=tp[:, fi * 512 + s_ * 128: fi * 512 + (s_ + 1) * 128],
                                                          in_=xb[:, s_, ft * 128:(ft + 1) * 128], identity=ident[:]),
                              reads=[xb_b, ident_b], writes=[tp_b], inc=last)
                dst = xT[:, 2 * f2:2 * f2 + 2, :].rearrange("p a t -> p (a t)")
                kb.op("act" if f2 % 2 else "dve",
                      (lambda e: e.activation(out=dst, in_=tp[:], func=AF.Copy)) if f2 % 2 else
                      (lambda e: e.tensor_copy(out=dst, in_=tp[:])), reads=[tp_b], writes=[xT_b])
            LS[ch]["xT"] = (xT, xT_b)

        L_load(0)
        L_cast(0)
        L_tr(0)
        if NCH > 1:
            L_load(1)
        for ch in range(NCH):
            t0 = ch * 512
            xT, xT_b = LS[ch]["xT"]
            rc, rc_b = LS[ch]["rc"]
            if ch + 2 < NCH:
                L_load(ch + 2)
            if ch + 1 < NCH:
                L_cast(ch + 1)
            stT, stT_b = STT.next()
            for ct in range(8):
                pt, pt_b = PS.next()
                for ft in range(8):
                    kb.op("pe", lambda e: e.matmul(pt[:, :], lhsT=wtr[:, ft, ct * 128:(ct + 1) * 128], rhs=xT[:, ft, :],
                                                   start=(ft == 0), stop=(ft == 7)),
                          reads=[W, xT_b], writes=[pt_b], inc=(ft == 7))
                if ct < 4:
                    kb.op("act", lambda e: e.activation(out=stT[:, ct, :], in_=pt[:, :], func=AF.Copy, scale=0.125),
                          reads=[pt_b], writes=[stT_b])
                else:
                    kb.op("dve", lambda e: e.tensor_copy(out=stT[:, ct, :], in_=pt[:, :]), reads=[pt_b], writes=[stT_b])
            kb.dma(ds["QN"].rearrange("(c p) t -> p c t", p=128)[:, :, t0:t0 + 512], stT[:, 0:4, :],
                   reads=[stT_b], writes=[dsb["QN"]])
            kb.dma(ds["KX"].rearrange("k p t -> p k t")[:, :, t0:t0 + 512], stT[:, 4:8, :],
                   reads=[stT_b], writes=[dsb["KX"]])
            pt, pt_b = PS.next()
            for ft in range(8):
                kb.op("pe", lambda e: e.matmul(pt[0:24, :], lhsT=wtr[:, ft, 1024:1048], rhs=xT[:, ft, :],
                                               start=(ft == 0), stop=(ft == 7)),
                      reads=[W, xT_b], writes=[pt_b], inc=(ft == 7))
            gst, gst_b = GST.next()
            kb.op("act", lambda e: e.activation(out=gst[:, :], in_=pt[0:24, :], func=AF.Sigmoid), reads=[pt_b], writes=[gst_b])
            kb.dma(ds["G"][:, t0:t0 + 512], gst[:, :], reads=[gst_b], writes=[dsb["G"]])
            pa, pa_b = PS.next()
            for ft in range(8):
                kb.op("pe", lambda e: e.matmul(pa[0:96, :], lhsT=wkpA[:, ft, :], rhs=xT[:, ft, :], start=(ft == 0), stop=(ft == 7)),
                      reads=[W, xT_b], writes=[pa_b], inc=(ft == 7))
            pb, pb_b = PS.next()
            for ft in range(8):
                kb.op("pe", lambda e: e.matmul(pb[0:96, :], lhsT=wkpB[:, ft, :], rhs=xT[:, ft, :], start=(ft == 0), stop=(ft == 7)),
                      reads=[W, xT_b], writes=[pb_b], inc=(ft == 7))
            rt, rt_b = RT.next()
            kst, kst_b = KST.next()
            kb.op("dve", lambda e: e.tensor_tensor(out=rt[64:96, 0, :], in0=pa[64:96, :], in1=rc[64:96, 0, :], op=ALU.mult),
                  reads=[pa_b, rc_b], writes=[rt_b])
            kb.op("dve", lambda e: e.tensor_tensor(out=rt[64:96, 1, :], in0=pb[64:96, :], in1=rc[64:96, 1, :], op=ALU.mult),
                  reads=[pb_b, rc_b], writes=[rt_b])
            kb.op("pool", lambda e: e.tensor_tensor(out=kst[64:96, :], in0=rt[64:96, 0, :], in1=rt[64:96, 1, :], op=ALU.add),
                  reads=[rt_b], writes=[kst_b])
            kb.dma(ds["KPE"][:, t0:t0 + 512], kst[64:96, :], reads=[kst_b], writes=[dsb["KPE"]])
            if ch + 1 < NCH:
                L_tr(ch + 1)
            vst, vst_b = VST.next()
            cT, cT_b = CT.next()
            P1S = {}

            def tm_mm(s_):
                p1, p1_b = PS.next()
                for ft in range(8):
                    kb.op("pe", lambda e: e.matmul(p1[:, 0:384], lhsT=xT[:, ft, s_ * 128:(s_ + 1) * 128], rhs=wtok[:, ft, 0:384],
                                                   start=(ft == 0), stop=(ft == 7)),
                          reads=[W, xT_b], writes=[p1_b], inc=(ft == 7))
                p2, p2_b = PS.next()
                for ft in range(8):
                    kb.op("pe", lambda e: e.matmul(p2[:, 0:256], lhsT=xT[:, ft, s_ * 128:(s_ + 1) * 128], rhs=wtok[:, ft, 384:640],
                                                   start=(ft == 0), stop=(ft == 7)),
                          reads=[W, xT_b], writes=[p2_b], inc=(ft == 7))
                kb.op("dve", lambda e: e.tensor_copy(out=vst[:, s_, :], in_=p2[:, 0:256]), reads=[p2_b], writes=[vst_b])
                P1S[s_] = (p1, p1_b)

            def tm_chain(s_):
                p1, p1_b = P1S.pop(s_)
                ss, ss_b = SS.next()
                jk, jk_b = JK.next()
                kb.op("act", lambda e: e.activation(out=jk[:, 0:256], in_=p1[:, 0:256], func=AF.Square, accum_out=ss[:, 0:1]),
                      reads=[p1_b], writes=[jk_b, ss_b])
                kb.op("act", lambda e: e.activation(out=jk[:, 0:128], in_=p1[:, 256:384], func=AF.Square, accum_out=ss[:, 1:2]),
                      reads=[p1_b], writes=[jk_b, ss_b])
                kb.op("dve", lambda e: e.tensor_scalar(out=ss[:, 0:1], in0=ss[:, 0:1], scalar1=1.0 / 256, scalar2=RMS_EPS,
                                                       op0=ALU.mult, op1=ALU.add), reads=[ss_b], writes=[ss_b])
                kb.op("dve", lambda e: e.tensor_scalar(out=ss[:, 1:2], in0=ss[:, 1:2], scalar1=1.0 / 128, scalar2=RMS_EPS,
                                                       op0=ALU.mult, op1=ALU.add), reads=[ss_b], writes=[ss_b])
                kb.op("act", lambda e: e.activation(out=ss[:, 2:4], in_=ss[:, 0:2], func=AF.Sqrt), reads=[ss_b], writes=[ss_b])
                kb.op("dve", lambda e: e.reciprocal(out=ss[:, 0:2], in_=ss[:, 2:4]), reads=[ss_b], writes=[ss_b])
                cn, cn_b = CN.next()
                kb.op("dve", lambda e: e.tensor_scalar(out=cn[:, 0:256], in0=p1[:, 0:256], scalar1=ss[:, 0:1], scalar2=None,
                                                       op0=ALU.mult), reads=[p1_b, ss_b], writes=[cn_b])
                kb.op("dve", lambda e: e.tensor_scalar(out=cn[:, 256:384], in0=p1[:, 256:384], scalar1=ss[:, 1:2], scalar2=None,
                                                       op0=ALU.mult), reads=[p1_b, ss_b], writes=[cn_b])
                tq, tq_b = TPQ.next()
                for i in range(3):
                    kb.op("pe", lambda e: e.transpose(out=tq[:, i, :], in_=cn[:, i * 128:(i + 1) * 128], identity=ident[:]),
                          reads=[cn_b, ident_b], writes=[tq_b], inc=(i == 2))
                kb.op("act", lambda e: e.activation(out=cT[:, :, s_ * 128:(s_ + 1) * 128], in_=tq[:, :, :], func=AF.Copy),
                      reads=[tq_b], writes=[cT_b])

            tm_mm(0)
            for s_ in range(4):
                if s_ + 1 < 4:
                    tm_mm(s_ + 1)
                tm_chain(s_)
            kb.dma(ds["VSW"][t0:t0 + 512, :].rearrange("(s p) c -> p s c", p=128), vst[:, :, :], reads=[vst_b], writes=[dsb["VSW"]])
            qst, qst_b = QST.next()
            for h in range(8):
                pa, pa_b = PS.next()
                for kt in range(2):
                    kb.op("pe", lambda e: e.matmul(pa[0:96, :], lhsT=wuqA[:, kt, h, :], rhs=cT[:, kt, :], start=(kt == 0), stop=(kt == 1)),
                          reads=[W, cT_b], writes=[pa_b], inc=(kt == 1))
                pb, pb_b = PS.next()
                for kt in range(2):
                    kb.op("pe", lambda e: e.matmul(pb[0:96, :], lhsT=wuqB[:, kt, h, :], rhs=cT[:, kt, :], start=(kt == 0), stop=(kt == 1)),
                          reads=[W, cT_b], writes=[pb_b], inc=(kt == 1))
                kb.op("act", lambda e: e.activation(out=qst[0:64, h, :], in_=pa[0:64, :], func=AF.Copy), reads=[pa_b], writes=[qst_b])
                rt, rt_b = RT.next()
                kb.op("dve", lambda e: e.tensor_tensor(out=rt[64:96, 0, :], in0=pa[64:96, :], in1=rc[64:96, 0, :], op=ALU.mult),
                      reads=[pa_b, rc_b], writes=[rt_b])
                kb.op("dve", lambda e: e.tensor_tensor(out=rt[64:96, 1, :], in0=pb[64:96, :], in1=rc[64:96, 1, :], op=ALU.mult),
                      reads=[pb_b, rc_b], writes=[rt_b])
                kb.op("pool", lambda e: e.tensor_tensor(out=qst[64:96, h, :], in0=rt[64:96, 0, :], in1=rt[64:96, 1, :], op=ALU.add),
                      reads=[rt_b], writes=[qst_b])
            kb.dma(ds["QM"].rearrange("h r t -> r h t")[:, :, t0:t0 + 512], qst[:, :, :], reads=[qst_b], writes=[dsb["QM"]])
            knst, knst_b = KNST.next()
            for hp in range(4):
                pt, pt_b = PS.next()
                kb.op("pe", lambda e: e.matmul(pt[:, :], lhsT=wkvK[:, 2 * hp:2 * hp + 2, :].rearrange("p a c -> p (a c)"),
                                               rhs=cT[:, 2, :], start=True, stop=True), reads=[W, cT_b], writes=[pt_b])
                kb.op("act" if hp % 2 else "dve",
                      (lambda e: e.activation(out=knst[:, hp, :], in_=pt[:, :], func=AF.Copy)) if hp % 2 else
                      (lambda e: e.tensor_copy(out=knst[:, hp, :], in_=pt[:, :])), reads=[pt_b], writes=[knst_b])
            kb.dma(ds["KN"].rearrange("(c p) t -> p c t", p=128)[:, :, t0:t0 + 512], knst[:, :, :], reads=[knst_b], writes=[dsb["KN"]])
            vmst, vmst_b = VMST.next()
            for s_ in range(4):
                pt, pt_b = PS.next()
                kb.op("pe", lambda e: e.matmul(pt[:, :], lhsT=cT[:, 2, s_ * 128:(s_ + 1) * 128],
                                               rhs=wkvV[:, :, :].rearrange("p a c -> p (a c)"), start=True, stop=True),
                      reads=[W, cT_b], writes=[pt_b])
                kb.op("act" if s_ % 2 else "dve",
                      (lambda e: e.activation(out=vmst[:, s_, :], in_=pt[:, :], func=AF.Copy)) if s_ % 2 else
                      (lambda e: e.tensor_copy(out=vmst[:, s_, :], in_=pt[:, :])), reads=[pt_b], writes=[vmst_b])
            kb.dma(ds["VM"][t0:t0 + 512, :].rearrange("(s p) c -> p s c", p=128), vmst[:, :, :], reads=[vmst_b], writes=[dsb["VM"]])


class TileD:
    __slots__ = ("qk", "scale", "pv", "epi", "lo", "hi")

    def __init__(self, qk, scale, pv, epi=None, lo=0, hi=512):
        self.qk = qk
        self.scale = scale
        self.pv = pv
        self.epi = epi
        self.lo, self.hi = lo, hi


class Stream:
    def __init__(self, c, R, L=2):
        self.c, self.R, self.L = c, R, L
        self.pending = []

    def push(self, t):
        kb, R = self.c.kb, self.R
        S, S_b = R.SPS.next()
        nq = len(t.qk)
        lo, hi = t.lo, t.hi
        for a, ent in enumerate(t.qk):
            lh, rh, rb = ent[0], ent[1], ent[2]
            clo, chi = (ent[3], ent[4]) if len(ent) > 3 else (lo, hi)
            kb.op("pe", lambda e: e.matmul(S[:, clo:chi], lhsT=lh, rhs=rh, start=(a == 0), stop=(a == nq - 1), skip_group_check=True),
                  reads=rb, writes=[S_b], inc=(a == nq - 1))
        P, P_b = R.PSB.next()
        kb.op("act", lambda e: e.activation(out=P[:, lo:hi], in_=S[:, lo:hi], func=AF.Exp, scale=t.scale), reads=[S_b], writes=[P_b])
        self.pending.append((t, P, P_b))
        if len(self.pending) > self.L:
            self._pop()

    def _pop(self):
        t, P, P_b = self.pending.pop(0)
        t.pv(P, P_b)
        if t.epi is not None:
            t.epi()

    def flush(self):
        while self.pending:
            self._pop()


def run_stream(c, R, tiles, L=2):
    st = c.stream
    for t in tiles:
        st.push(t)


def phase2(c):
    nc, kb = c.nc, c.kb
    din, dc, ds, dsb = c.din, c.dc, c.ds, c.dsb
    ident, ident_b = c.ident, c.ident_b
    with ExitStack() as esA:
        kcmp = sb(nc, esA, "kcmp", [69, 2, 512], BF16)
        kcmp_b = Buf("kcmp")
        vcmp = sb(nc, esA, "vcmp", [128, 4, 2, 128], BF16)
        vcmp_b = Buf("vcmp")
        CB = Buf("consts2")
        cmk = sb(nc, esA, "cmk", [128, 4, 512], BF16)
        wmk = sb(nc, esA, "wmk", [128, 4, 512], BF16)
        pmk = sb(nc, esA, "pmk", [128, 5, 512], BF16)
        eexp = sb(nc, esA, "eexp", [128, 64, 128], BF16)
        impm = sb(nc, esA, "impm", [128, 4, 128], BF16)
        keept = sb(nc, esA, "keept", [128, 256], F32)
        addt = sb(nc, esA, "addt", [128, 256], F32)
        kb.dma(cmk[:], dc["cmk"][:], writes=[CB])
        kb.dma(wmk[:], dc["wmk"][:], writes=[CB])
        kb.dma(pmk[:], dc["pmk"][:], writes=[CB])
        kb.dma(eexp[:], dc["eexp"][:], writes=[CB])
        kb.dma(impm[:], dc["impm"].rearrange("(ct p) b -> p ct b", p=128), writes=[CB])
        kb.dma(keept[:], dc["keept"][:], writes=[CB])
        kb.dma(addt[:], dc["addt"][:], writes=[CB])
        kb.op("pool", lambda e: e.memset(kcmp[:], 0.0), writes=[kcmp_b])
        kb.op("pool", lambda e: e.memset(vcmp[:], 1.0), writes=[vcmp_b])
        for g in range(2):
            kb.dma(kcmp[64:69, g, :], dc["kaugc"][:, :], writes=[kcmp_b])
        with ExitStack() as es:
            kcg = sb(nc, es, "kcg", [64, 4, T], BF16)
            kcg_b = Buf("kcg")
            for kind in range(2):
                for g in range(2):
                    kb.dma(kcg[:, kind * 2 + g, :], ds["KX"][kind, g * 64:(g + 1) * 64, :], reads=[dsb["KX"]], writes=[kcg_b])
            w1s = sb(nc, es, "w1s", [64, 32, 256], F32)
            w1s_b = Buf("w1s")
            w1 = [sb(nc, es, "w1_%d" % k, [64, 32, 256], BF16) for k in range(2)]
            w2s = sb(nc, es, "w2s", [128, 2, 64], F32)
            w2s_b = Buf("w2s")
            w2 = [sb(nc, es, "w2_%d" % k, [128, 2, 64], BF16) for k in range(2)]
            posS = sb(nc, es, "posS", [32, 64], F32)
            posS_b = Buf("posS")
            posB = sb(nc, es, "posB", [32, 64], BF16)
            posB_b = Buf("posB")
            posT = sb(nc, es, "posT", [64, 2, 32], BF16)
            b1 = sb(nc, es, "b1", [128, 4], F32)
            Wc = Buf("wc")
            tpp = ps(nc, es, "tpp", [64, 32], BF16)
            tpp_b = Buf("tpp")
            pb1 = ps(nc, es, "pb1", [128, 4], F32)
            pb1_b = Buf("pb1")
            PSC = rot_tiles(nc, es, "psc", [128, 512], F32, 4, psum=True)
            XG = rot_tiles(nc, es, "xg", [128, 512], F32, 2)
            X2 = rot_tiles(nc, es, "x2", [128, 512], F32, 2)
            GL = rot_tiles(nc, es, "gl", [128, 512], BF16, 4)
            for gl, gl_b in GL.items:
                kb.op("pool", lambda e: e.memset(gl[:], 0.0), writes=[gl_b])
            names = [("w_ck1", "w_ck2", "pos_ck"), ("w_cv1", "w_cv2", "pos_cv")]
            for kind, (n1, n2, npos) in enumerate(names):
                kb.dma(w1s[:], din[n1].rearrange("(l d) n -> d l n", d=64), writes=[w1s_b])
                kb.op("dve", lambda e: e.tensor_copy(out=w1[kind][:, 0:16, :], in_=w1s[:, 0:16, :]), reads=[w1s_b], writes=[Wc])
                kb.op("pool", lambda e: e.tensor_copy(out=w1[kind][:, 16:32, :], in_=w1s[:, 16:32, :]), reads=[w1s_b], writes=[Wc])
                kb.dma(w2s[:], din[n2].rearrange("(c p) n -> p c n", p=128), writes=[w2s_b])
                kb.op("dve", lambda e: e.tensor_copy(out=w2[kind][:], in_=w2s[:]), reads=[w2s_b], writes=[Wc])
                kb.dma(posS[:], din[npos][:, :], writes=[posS_b])
                kb.op("dve", lambda e: e.tensor_copy(out=posB[:], in_=posS[:]), reads=[posS_b], writes=[posB_b])
                kb.op("pe", lambda e: e.transpose(out=tpp[:, :], in_=posB[:, :], identity=ident[0:32, 0:32]),
                      reads=[posB_b, ident_b], writes=[tpp_b])
                kb.op("dve", lambda e: e.tensor_copy(out=posT[:, kind, :], in_=tpp[:, :]), reads=[tpp_b], writes=[Wc])
            for kind in range(2):
                for hc in range(2):
                    idx = kind * 2 + hc
                    for l in range(32):
                        kb.op("pe", lambda e: e.matmul(pb1[:, idx:idx + 1], lhsT=w1[kind][:, l, hc * 128:(hc + 1) * 128],
                                                       rhs=posT[:, kind, l:l + 1], start=(l == 0), stop=(l == 31)),
                              reads=[Wc], writes=[pb1_b], inc=(l == 31))
            kb.op("dve", lambda e: e.tensor_copy(out=b1[:], in_=pb1[:]), reads=[pb1_b], writes=[Wc])
            for kind in range(2):
                for g in range(2):
                    gls = []
                    for hc in range(2):
                        idx = kind * 2 + hc
                        pm, pm_b = PSC.next()
                        for l in range(32):
                            kb.op("pe", lambda e: e.matmul(pm[:, 0:511], lhsT=w1[kind][:, l, hc * 128:(hc + 1) * 128],
                                                           rhs=kcg[:, kind * 2 + g, l:l + 8161:16], start=(l == 0), stop=(l == 31)),
                                  reads=[Wc, kcg_b], writes=[pm_b], inc=(l == 31))
                        xg, xg_b = XG.next()
                        x2, x2_b = X2.next()
                        gl, gl_b = GL.next()
                        kb.op("dve", lambda e: e.tensor_scalar(out=xg[:, 0:511], in0=pm[:, 0:511], scalar1=b1[:, idx:idx + 1],
                                                               scalar2=None, op0=ALU.add), reads=[pm_b, Wc], writes=[xg_b])
                        kb.op("pool", lambda e: e.tensor_tensor(out=x2[:, 0:511], in0=xg[:, 0:511], in1=xg[:, 0:511], op=ALU.mult),
                              reads=[xg_b], writes=[x2_b])
                        kb.op("dve", lambda e: e.tensor_scalar(out=x2[:, 0:511], in0=x2[:, 0:511], scalar1=0.044715, scalar2=1.0,
                                                               op0=ALU.mult, op1=ALU.add), reads=[x2_b], writes=[x2_b])
                        kb.op("pool", lambda e: e.tensor_tensor(out=x2[:, 0:511], in0=x2[:, 0:511], in1=xg[:, 0:511], op=ALU.mult),
                              reads=[x2_b, xg_b], writes=[x2_b])
                        kb.op("act", lambda e: e.activation(out=x2[:, 0:511], in_=x2[:, 0:511], func=AF.Sigmoid, scale=1.5957691216),
                              reads=[x2_b], writes=[x2_b])
                        kb.op("dve", lambda e: e.tensor_tensor(out=gl[:, 0:511], in0=x2[:, 0:511], in1=xg[:, 0:511], op=ALU.mult),
                              reads=[x2_b, xg_b], writes=[gl_b])
                        gls.append((gl, gl_b))
                    if kind == 0:
                        pk, pk_b = PSC.next()
                        for hc in range(2):
                            kb.op("pe", lambda e: e.matmul(pk[0:64, 0:511], lhsT=w2[0][:, hc, :], rhs=gls[hc][0][:, 0:511],
                                                           start=(hc == 0), stop=(hc == 1)),
                                  reads=[Wc, gls[hc][1]], writes=[pk_b], inc=(hc == 1))
                        kb.op("act", lambda e: e.activation(out=kcmp[0:64, g, 0:511], in_=pk[0:64, 0:511], func=AF.Copy),
                              reads=[pk_b], writes=[kcmp_b])
                    else:
                        for ct in range(4):
                            pv, pv_b = PSC.next()
                            for hc in range(2):
                                kb.op("pe", lambda e: e.matmul(pv[:, 0:64], lhsT=gls[hc][0][:, ct * 128:(ct + 1) * 128], rhs=w2[1][:, hc, :],
                                                               start=(hc == 0), stop=(hc == 1)),
                                      reads=[Wc, gls[hc][1]], writes=[pv_b], inc=(hc == 1))
                            kb.op("act", lambda e: e.activation(out=vcmp[:, ct, g, 0:64], in_=pv[:, 0:64], func=AF.Copy),
                                  reads=[pv_b], writes=[vcmp_b])
        kb.barrier()
        if getattr(c, "stop_mid", None) == "compress":
            dbg = c.dbg
            kb.dma(dbg["kcmp"][:], kcmp[:], reads=[kcmp_b])
            kb.dma(dbg["vcmp"][:], vcmp[:], reads=[vcmp_b])
            return
        with ExitStack() as es:
            R = Ctx()
            R.SPS = rot_tiles(nc, es, "sps", [128, 512], F32, 3, psum=True)
            R.OPS = rot_tiles(nc, es, "ops", [128, 512], F32, 2, psum=True)
            R.UPS = rot_tiles(nc, es, "ups", [128, 4, 128], F32, 2, psum=True)
            R.TPN = rot_tiles(nc, es, "tpn", [128, 4, 128], BF16, 1, psum=True)
            R.PSB = rot_tiles(nc, es, "psb", [128, 512], BF16, 4)
            KA = rot_tiles(nc, es, "ka", [96, T], BF16, 2).items
            VA = rot_tiles(nc, es, "va", [128, 64, 128], BF16, 2).items
            for va, va_b in VA:
                kb.op("pool", lambda e: e.memset(va[:], 1.0), writes=[va_b])
            QA = [[(sb(nc, es, "qa%d%d" % (pr, hi), [69, 512], BF16), Buf("qa")) for hi in range(4)] for pr in range(2)]
            OC = [[(sb(nc, es, "oc%d%d" % (pr, hi), [64, 512], F32), Buf("oc")) for hi in range(4)] for pr in range(2)]
            QMT = rot_tiles(nc, es, "qmt", [96, 512], BF16, 3)
            GB = rot_tiles(nc, es, "gb", [64, 3, 512], F32, 4)
            NMT = rot_tiles(nc, es, "nmt", [128, 512], BF16, 2)
            IMP = rot_tiles(nc, es, "imp", [128, 4, 128], F32, 2)
            DEN = rot_tiles(nc, es, "den", [128, 256], F32, 2)
            REC = rot_tiles(nc, es, "rec", [128, 256], F32, 2)
            RG = rot_tiles(nc, es, "rg", [64, 512], F32, 2)
            T1 = rot_tiles(nc, es, "t1", [64, 512], F32, 4)
            ACC = rot_tiles(nc, es, "acc", [64, 512], F32, 2)
            OST = rot_tiles(nc, es, "ost", [64, 512], BF16, 3)
            W1 = rot_tiles(nc, es, "w1k", [128, 128], F32, 2)
            W2 = rot_tiles(nc, es, "w2k", [128, 128], F32, 2)
            M8 = rot_tiles(nc, es, "m8", [128, 16], F32, 2)
            NMS = rot_tiles(nc, es, "nms", [128, 128], BF16, 4)
            RS = rot_tiles(nc, es, "rs", [128, 8], F32, 2)
            Gt = ds["G"].tensor
            c.stream = Stream(c, R)

            def norm_rec(O, O_b):
                den, den_b = DEN.next()
                rec, rec_b = REC.next()
                kb.op("dve", lambda e: e.tensor_scalar_max(out=den[0:64, :], in0=O[64:128, 0:256], scalar1=1e-30), reads=[O_b], writes=[den_b])
                kb.op("dve", lambda e: e.tensor_scalar_max(out=den[64:128, :], in0=O[64:128, 256:512], scalar1=1e-30), reads=[O_b], writes=[den_b])
                kb.op("dve", lambda e: e.reciprocal(out=rec[:, :], in_=den[:, :]), reads=[den_b], writes=[rec_b])
                return rec, rec_b

            def mul_rec(eng, out_t, in_t, rec, reads, writes):
                kb.op(eng, lambda e: e.tensor_tensor(out=out_t[0:64, 0:256], in0=in_t[0:64, 0:256], in1=rec[0:64, :], op=ALU.mult), reads=reads, writes=writes)
                kb.op(eng, lambda e: e.tensor_tensor(out=out_t[0:64, 256:512], in0=in_t[0:64, 256:512], in1=rec[64:128, :], op=ALU.mult), reads=reads, writes=writes)

            for g in range(2):
                (ks, ks_b), (kw, kw_b) = KA
                (vs, vs_b), (vw, vw_b) = VA
                kb.dma(ks[0:64, :], ds["KX"][2, g * 64:(g + 1) * 64, :], reads=[dsb["KX"]], writes=[ks_b])
                kb.dma(ks[64:69, :], dc["kaug"][:, :], writes=[ks_b])
                kb.dma(kw[0:64, :], ds["KX"][3, g * 64:(g + 1) * 64, :], reads=[dsb["KX"]], writes=[kw_b])
                kb.dma(kw[64:69, :], dc["kaug"][:, :], writes=[kw_b])
                for half in range(2):
                    hs = slice(half * 32, (half + 1) * 32)
                    rs_ = slice(half * 4096, (half + 1) * 4096)
                    kb.dma(vs[:, hs, 0:64], ds["VSW"][rs_, g * 64:(g + 1) * 64].rearrange("(kt p) d -> p kt d", p=128),
                           reads=[dsb["VSW"]], writes=[vs_b])
                    kb.dma(vw[:, hs, 0:64], ds["VSW"][rs_, 128 + g * 64:128 + (g + 1) * 64].rearrange("(kt p) d -> p kt d", p=128),
                           reads=[dsb["VSW"]], writes=[vw_b])
                CS = {}

                def cmp_begin(qc):
                    CS[qc] = {"imp": IMP.next(), "nmt": NMT.next(), "nms": []}

                def cmp_head(qc, hi, g=g):
                    t0 = qc * 512
                    pr = qc % 2
                    h = 4 * g + hi
                    imp, imp_b = CS[qc]["imp"]
                    qa, qa_b = QA[pr][hi]
                    kb.dma(qa[0:64, :], ds["QN"][h * 64:(h + 1) * 64, t0:t0 + 512], reads=[dsb["QN"]], writes=[qa_b])
                    kb.dma(qa[64:69, :], dc["qaug"][h, :, t0:t0 + 512], writes=[qa_b])
                    O, O_b = R.OPS.next()
                    U, U_b = R.UPS.next()
                    nct = qc // 4 + 1
                    for ct in range(nct):
                        m = qc - 4 * ct
                        qk = [(kcmp[0:69, g, ct * 128:(ct + 1) * 128], qa[0:69, :], [kcmp_b, qa_b])]
                        if m <= 4:
                            qk.append((ident[:, :], pmk[:, m, :], [ident_b, CB]))

                        def pv(P, P_b, ct=ct, nct=nct, O=O, O_b=O_b, U=U, U_b=U_b):
                            kb.op("pe", lambda e: e.matmul(O[:, :], lhsT=vcmp[:, ct, g, :], rhs=P[:, :], start=(ct == 0), stop=(ct == nct - 1)),
                                  reads=[vcmp_b, P_b], writes=[O_b], inc=True)
                            for s_ in range(4):
                                kb.op("pe", lambda e: e.matmul(U[:, s_, :], lhsT=P[:, s_ * 128:(s_ + 1) * 128], rhs=impm[:, ct, :],
                                                               start=(ct == 0), stop=(ct == nct - 1)),
                                      reads=[CB, P_b], writes=[U_b], inc=(s_ == 3))

                        def epi(hi=hi, pr=pr, O=O, O_b=O_b, U=U, U_b=U_b, imp=imp, imp_b=imp_b):
                            rec, rec_b = norm_rec(O, O_b)
                            oc, oc_b = OC[pr][hi]
                            mul_rec("dve", oc, O, rec, [O_b, rec_b], [oc_b])
                            rs, rs_b = RS.next()
                            kb.op("dve", lambda e: e.tensor_reduce(out=rs[:, 0:4], in_=U[:, :, :], axis=AX.X, op=ALU.add),
                                  reads=[U_b], writes=[rs_b])
                            kb.op("dve", lambda e: e.tensor_scalar(out=rs[:, 0:4], in0=rs[:, 0:4], scalar1=1e-30, scalar2=0.5,
                                                                   op0=ALU.max, op1=ALU.mult), reads=[rs_b], writes=[rs_b])
                            kb.op("dve", lambda e: e.reciprocal(out=rs[:, 4:8], in_=rs[:, 0:4]), reads=[rs_b], writes=[rs_b])
                            for s_ in range(4):
                                if hi == 0:
                                    kb.op("dve", lambda e: e.tensor_scalar(out=imp[:, s_, :], in0=U[:, s_, :], scalar1=rs[:, 4 + s_:5 + s_],
                                                                           scalar2=None, op0=ALU.mult), reads=[U_b, rs_b], writes=[imp_b])
                                else:
                                    kb.op("dve", lambda e: e.scalar_tensor_tensor(out=imp[:, s_, :], in0=U[:, s_, :], scalar=rs[:, 4 + s_:5 + s_],
                                                                                  in1=imp[:, s_, :], op0=ALU.mult, op1=ALU.add),
                                          reads=[U_b, rs_b, imp_b], writes=[imp_b])

                        c.stream.push(TileD(qk, 1.0, pv, epi if ct == nct - 1 else None))

                def topk_dve(qc):
                    imp, imp_b = CS[qc]["imp"]
                    for s_ in range(4):
                        qt = 4 * qc + s_
                        lo = 128 - 2 * qt
                        w, w_b = W1.next()
                        w2, w2_b = W2.next()
                        m8, m8_b = M8.next()
                        nms, nms_b = NMS.next()
                        kb.op("pool", lambda e: e.tensor_tensor(out=w[:, :], in0=imp[:, s_, :], in1=keept[:, lo:lo + 128], op=ALU.mult),
                              reads=[imp_b, CB], writes=[w_b])
                        kb.op("pool", lambda e: e.tensor_tensor(out=w[:, :], in0=w[:, :], in1=addt[:, lo:lo + 128], op=ALU.add),
                              reads=[w_b, CB], writes=[w_b])
                        kb.op("pool", lambda e: e.memset(w[:, 0:1], 1e6), reads=[w_b], writes=[w_b])
                        kb.op("dve", lambda e: e.max(out=m8[:, 0:8], in_=w[:, :]), reads=[w_b], writes=[m8_b])
                        kb.op("dve", lambda e: e.match_replace(out=w2[:, :], in_to_replace=m8[:, 0:8], in_values=w[:, :], imm_value=-2.0),
                              reads=[w_b, m8_b], writes=[w2_b])
                        kb.op("dve", lambda e: e.max(out=m8[:, 8:16], in_=w2[:, :]), reads=[w2_b, m8_b], writes=[m8_b])
                        kb.op("dve", lambda e: e.tensor_scalar(out=nms[:, :], in0=w[:, :], scalar1=m8[:, 15:16], scalar2=-1.0,
                                                               op0=ALU.is_ge, op1=ALU.add), reads=[w_b, m8_b], writes=[nms_b])
                        CS[qc]["nms"].append((nms, nms_b))

                def topk_pe(qc):
                    nmt, nmt_b = CS[qc]["nmt"]
                    tpn, tpn_b = R.TPN.next()
                    for s_ in range(4):
                        nms, nms_b = CS[qc]["nms"][s_]
                        kb.op("pe", lambda e: e.transpose(out=tpn[:, s_, :], in_=nms[:, :], identity=ident[:, :]),
                              reads=[nms_b, ident_b], writes=[tpn_b])
                    kb.op("act", lambda e: e.activation(out=nmt[:, :], in_=tpn[:, :, :].rearrange("p a b -> p (a b)"), func=AF.Copy),
                          reads=[tpn_b], writes=[nmt_b])

                def sw_head(qc, hi, g=g):
                    t0 = qc * 512
                    pr = qc % 2
                    h = 4 * g + hi
                    nmt, nmt_b = CS[qc]["nmt"]
                    qa, qa_b = QA[pr][hi]
                    gb, gb_b = GB.next()
                    kb.dma(gb[:, :, :], bass.AP(Gt, h * 3 * T + t0, [[0, 64], [T, 3], [1, 512]]), reads=[dsb["G"]], writes=[gb_b])
                    st = {}
                    O, O_b = R.OPS.next()
                    nk = 4 * qc + 4
                    k0 = max(0, int((t0 - 127 - ALIBI_CUT * (2.0 ** (h + 1))) // 128) + 1)
                    for kt in range(k0, nk):
                        d = kt - 4 * qc
                        lo = 128 * d if d > 0 else 0
                        qk = [(ks[0:69, kt * 128:(kt + 1) * 128], qa[0:69, lo:512], [ks_b, qa_b]),
                              (eexp[:, kt, :], nmt[:, lo:512], [CB, nmt_b])]
                        if d >= 0:
                            qk.append((ident[:, :], cmk[:, d, lo:lo + 128], [ident_b, CB], lo, lo + 128))

                        def pv(P, P_b, kt=kt, nk=nk, k0=k0, O=O, O_b=O_b, lo=lo):
                            kb.op("pe", lambda e: e.matmul(O[:, lo:512], lhsT=vs[:, kt, :], rhs=P[:, lo:512], start=(kt == k0), stop=(kt == nk - 1),
                                                           skip_group_check=True),
                                  reads=[vs_b, P_b], writes=[O_b], inc=True)

                        def epi(O=O, O_b=O_b, gb=gb, gb_b=gb_b, st=st):
                            rec, rec_b = norm_rec(O, O_b)
                            t1, t1_b = T1.next()
                            mul_rec("dve", t1, O, rec, [O_b, rec_b], [t1_b])
                            kb.op("pool", lambda e: e.tensor_tensor(out=t1[:, :], in0=t1[:, :], in1=gb[:, 1, :], op=ALU.mult),
                                  reads=[t1_b, gb_b], writes=[t1_b])
                            st["slc"] = (t1, t1_b)

                        c.stream.push(TileD(qk, 1.0, pv, epi if kt == nk - 1 else None, lo=lo, hi=512))
                    O, O_b = R.OPS.next()
                    wl = []
                    for d in range(4):
                        wl.append((4 * qc + d, cmk[:, d, 128 * d:128 * d + 128], 128 * d, 512, 128 * d))
                    for d in range(4):
                        kt = 4 * qc - 4 + d
                        if kt >= 0:
                            wl.append((kt, wmk[:, d, 128 * d:128 * d + 128], 0, 128 * d + 128, 128 * d))
                    nw = len(wl)
                    for wi, (kt, mk, wlo, whi, mlo) in enumerate(wl):
                        qk = [(kw[0:69, kt * 128:(kt + 1) * 128], qa[0:69, wlo:whi], [kw_b, qa_b]),
                              (ident[:, :], mk, [ident_b, CB], mlo, mlo + 128)]

                        def pv(P, P_b, kt=kt, wi=wi, nw=nw, O=O, O_b=O_b, wlo=wlo, whi=whi):
                            kb.op("pe", lambda e: e.matmul(O[:, wlo:whi], lhsT=vw[:, kt, :], rhs=P[:, wlo:whi], start=(wi == 0), stop=(wi == nw - 1),
                                                           skip_group_check=True),
                                  reads=[vw_b, P_b], writes=[O_b], inc=True)

                        def epi(h=h, hi=hi, pr=pr, t0=t0, O=O, O_b=O_b, gb=gb, gb_b=gb_b, st=st):
                            rec, rec_b = norm_rec(O, O_b)
                            t2, t2_b = T1.next()
                            mul_rec("dve", t2, O, rec, [O_b, rec_b], [t2_b])
                            kb.op("pool", lambda e: e.tensor_tensor(out=t2[:, :], in0=t2[:, :], in1=gb[:, 2, :], op=ALU.mult),
                                  reads=[t2_b, gb_b], writes=[t2_b])
                            t1, t1_b = st["slc"]
                            oc, oc_b = OC[pr][hi]
                            acc, acc_b = ACC.next()
                            ost, ost_b = OST.next()
                            kb.op("pool", lambda e: e.tensor_tensor(out=acc[:, :], in0=oc[:, :], in1=gb[:, 0, :], op=ALU.mult),
                                  reads=[oc_b, gb_b], writes=[acc_b])
                            kb.op("pool", lambda e: e.tensor_tensor(out=acc[:, :], in0=acc[:, :], in1=t1[:, :], op=ALU.add),
                                  reads=[acc_b, t1_b], writes=[acc_b])
                            kb.op("pool", lambda e: e.tensor_tensor(out=ost[:, :], in0=acc[:, :], in1=t2[:, :], op=ALU.add),
                                  reads=[acc_b, t2_b], writes=[ost_b])
                            kb.dma(ds["ON"][h * 64:(h + 1) * 64, t0:t0 + 512], ost[:, :], reads=[ost_b], writes=[dsb["ON"]], q="pool")

                        c.stream.push(TileD(qk, 1.0, pv, epi if wi == nw - 1 else None, lo=wlo, hi=whi))

                cmp_begin(0)
                for hi in range(4):
                    cmp_head(0, hi)
                c.stream.flush()
                topk_dve(0)
                topk_pe(0)
                for qc in range(NCH):
                    nxt = qc + 1 < NCH
                    if nxt:
                        cmp_begin(qc + 1)
                        cmp_head(qc + 1, 0)
                        cmp_head(qc + 1, 1)
                    sw_head(qc, 0)
                    if nxt:
                        cmp_head(qc + 1, 2)
                        cmp_head(qc + 1, 3)
                    sw_head(qc, 1)
                    sw_head(qc, 2)
                    if nxt:
                        topk_dve(qc + 1)
                    sw_head(qc, 3)
                    c.stream.flush()
                    if nxt:
                        topk_pe(qc + 1)
                    CS.pop(qc)
            c.stream.flush()
            if getattr(c, "stop_mid", None) == "nsa":
                return
            wst = sb(nc, es, "wst", [128, 1408], F32)
            wst_b = Buf("wst")
            wsb = sb(nc, es, "wsb", [128, 1408], BF16)
            wsb_b = Buf("wsb")
            wpieces = [(ft, a, hf) for ft in range(8) for a in range(2) for hf in range(2)]

            def wup_piece(ft, a, hf):
                c0 = a * DFF + hf * 1408
                kb.dma(wst[:, :], din["w_up"][ft * 128:(ft + 1) * 128, c0:c0 + 1408], writes=[wst_b])
                kb.op("pool", lambda e: e.tensor_copy(out=wsb[:, :], in_=wst[:, :]), reads=[wst_b], writes=[wsb_b])
                kb.dma(ds["WUP"][hf * 11:(hf + 1) * 11, :, a, ft, :].rearrange("j p c -> p j c"),
                       wsb[:, :].rearrange("p (j c) -> p j c", c=128), reads=[wsb_b], writes=[dsb["WUP"]], q="pool")

            for h in range(8):
                ka, ka_b = KA[h % 2]
                va, va_b = VA[h % 2]
                kb.dma(ka[0:64, :], ds["KN"][h * 64:(h + 1) * 64, :], reads=[dsb["KN"]], writes=[ka_b])
                kb.dma(ka[64:96, :], ds["KPE"][:, :], reads=[dsb["KPE"]], writes=[ka_b])
                for half in range(2):
                    hs = slice(half * 32, (half + 1) * 32)
                    rs_ = slice(half * 4096, (half + 1) * 4096)
                    kb.dma(va[:, hs, 0:64], ds["VM"][rs_, h * 64:(h + 1) * 64].rearrange("(kt p) d -> p kt d", p=128),
                           reads=[dsb["VM"]], writes=[va_b])
                qms = {}

                def qload(qc, h=h, qms=qms):
                    qm, qm_b = QMT.next()
                    kb.dma(qm[:, :], ds["QM"][h, :, qc * 512:(qc + 1) * 512], reads=[dsb["QM"]], writes=[qm_b])
                    qms[qc] = (qm, qm_b)

                qload(0)
                for qc in range(NCH):
                    t0 = qc * 512
                    if qc + 1 < NCH:
                        qload(qc + 1)
                    qm, qm_b = qms.pop(qc)
                    O, O_b = R.OPS.next()
                    nk = 4 * qc + 4
                    for kt in range(nk):
                        d = kt - 4 * qc
                        lo = 128 * d if d > 0 else 0
                        qk = [(ka[0:96, kt * 128:(kt + 1) * 128], qm[0:96, lo:512], [ka_b, qm_b])]
                        if d >= 0:
                            qk.append((ident[:, :], cmk[:, d, lo:lo + 128], [ident_b, CB], lo, lo + 128))

                        def pv(P, P_b, kt=kt, nk=nk, O=O, O_b=O_b, va=va, va_b=va_b, lo=lo):
                            kb.op("pe", lambda e: e.matmul(O[:, lo:512], lhsT=va[:, kt, :], rhs=P[:, lo:512], start=(kt == 0), stop=(kt == nk - 1),
                                                           skip_group_check=True),
                                  reads=[va_b, P_b], writes=[O_b], inc=True)

                        def epi(h=h, t0=t0, O=O, O_b=O_b):
                            rec, rec_b = norm_rec(O, O_b)
                            ost, ost_b = OST.next()
                            mul_rec("dve", ost, O, rec, [O_b, rec_b], [ost_b])
                            kb.dma(ds["OM"][h * 64:(h + 1) * 64, t0:t0 + 512], ost[:, :], reads=[ost_b], writes=[dsb["OM"]], q="pool")

                        c.stream.push(TileD(qk, MLA_SCALE, pv, epi if kt == nk - 1 else None, lo=lo, hi=512))
                    if wpieces:
                        wup_piece(*wpieces.pop(0))
            c.stream.flush()

CH3 = 256
NCH3 = T // CH3


def phase3(c):
    nc, kb = c.nc, c.kb
    din, dc, ds, dsb = c.din, c.dc, c.ds, c.dsb
    ident, ident_b = c.ident, c.ident_b
    with ExitStack() as es:
        W = Buf("p3w")
        wo = sb(nc, es, "wo", [128, 8, 1024], BF16)
        wpg = sb(nc, es, "wpg", [128, 8, 1024], BF16)
        wpl = sb(nc, es, "wpl", [128, 2, 1024], BF16)
        wd = sb(nc, es, "wd", [128, 22, 1024], BF16)
        lnv = sb(nc, es, "lnv", [128, 6, 1024], F32)
        for i, nm in enumerate(("ln1_g", "ln1_b", "ln2_g", "ln2_b", "ln3_g", "ln3_b")):
            kb.dma(lnv[:, i, :], din[nm][0:1, :].partition_broadcast(128), writes=[W])
        with ExitStack() as es1:
            STG = rot_tiles(nc, es1, "stg3", [128, 1024], F32, 4)
            k = 0
            for name, dst in (("w_out", wo), ("w_ple_gate", wpg)):
                for ft in range(8):
                    s_, s_b = STG.next()
                    kb.dma(s_[:, 0:1024], din[name][ft * 128:(ft + 1) * 128, :], writes=[s_b])
                    kb.op("dve" if k % 2 else "pool", lambda e: e.tensor_copy(out=dst[:, ft, :], in_=s_[:, 0:1024]), reads=[s_b], writes=[W])
                    k += 1
            for kt in range(2):
                s_, s_b = STG.next()
                kb.dma(s_[:, 0:1024], din["w_ple"][kt * 128:(kt + 1) * 128, :], writes=[s_b])
                kb.op("dve", lambda e: e.tensor_copy(out=wpl[:, kt, :], in_=s_[:, 0:1024]), reads=[s_b], writes=[W])
            for j in range(22):
                s_, s_b = STG.next()
                kb.dma(s_[:, 0:1024], din["w_down"][j * 128:(j + 1) * 128, :], writes=[s_b])
                kb.op("dve" if j % 2 else "pool", lambda e: e.tensor_copy(out=wd[:, j, :], in_=s_[:, 0:1024]), reads=[s_b], writes=[W])
        kb.barrier()
        XF = rot_tiles(nc, es, "xf3", [128, 1024], F32, 1)
        OT = rot_tiles(nc, es, "oT3", [128, 8, CH3], BF16, 1)
        PF = rot_tiles(nc, es, "pf3", [128, 2, 256], F32, 1)
        PB = rot_tiles(nc, es, "pb3", [128, 2, 256], BF16, 3)
        PT = rot_tiles(nc, es, "pT3", [128, 2, CH3], BF16, 1)
        Y = rot_tiles(nc, es, "y3", [128, 1024], F32, 2)
        Z = rot_tiles(nc, es, "z3", [128, 1024], F32, 1)
        X1 = rot_tiles(nc, es, "x1r", [128, 2, 1024], F32, 2)
        X2 = rot_tiles(nc, es, "x2r", [128, 2, 1024], F32, 2)
        XBF = rot_tiles(nc, es, "xbf3", [128, 1024], BF16, 2)
        X1T = rot_tiles(nc, es, "x1T", [128, 8, CH3], BF16, 2)
        X2T = rot_tiles(nc, es, "x2T", [128, 8, CH3], BF16, 2)
        AT = rot_tiles(nc, es, "aT", [128, 22, CH3], BF16, 1)
        WU = rot_tiles(nc, es, "wu", [128, 2, 8, 128], BF16, 3)
        SG = rot_tiles(nc, es, "sg", [128, CH3], F32, 2)
        ST = rot_tiles(nc, es, "st3", [128, 16], F32, 6)
        PA = rot_tiles(nc, es, "pa3", [128, 512], F32, 3, psum=True)
        PG = rot_tiles(nc, es, "pg3", [128, 2, CH3], F32, 4, psum=True)
        TP = rot_tiles(nc, es, "tp3", [128, 8, 128], BF16, 1, psum=True)

        epsv = sb(nc, es, "epsv", [128, 2], F32)
        kb.op("pool", lambda e: e.memset(epsv[:, 0:1], LN_EPS), writes=[W])
        kb.op("pool", lambda e: e.memset(epsv[:, 1:2], 4.0 * LN_EPS), writes=[W])

        def ln_pair_steps(items, st, st_b, gi, eps_col=0):
            def s1():
                for k, it_ in enumerate(items):
                    j_ap, j_b = it_.get("junk") or (it_["out_ap"], it_["out_b"])
                    y, y_b = it_["y"], it_["y_b"]
                    kb.op("act", lambda e: e.activation(out=j_ap, in_=y[:, :], func=AF.Square, accum_out=st[:, 8 * k + 2:8 * k + 3]),
                          reads=[y_b], writes=[j_b, st_b])

            def s2():
                for k in range(len(items)):
                    b = 8 * k
                    kb.op("dve", lambda e: e.tensor_scalar(out=st[:, b + 3:b + 4], in0=st[:, b:b + 1], scalar1=st[:, b + 1:b + 2], scalar2=1.0 / D,
                                                           op0=ALU.add, op1=ALU.mult), reads=[st_b], writes=[st_b])
                    kb.op("dve", lambda e: e.scalar_tensor_tensor(out=st[:, b + 4:b + 5], in0=st[:, b + 3:b + 4], scalar=-1.0, in1=st[:, b + 3:b + 4],
                                                                  op0=ALU.mult, op1=ALU.mult), reads=[st_b], writes=[st_b])
                    kb.op("dve", lambda e: e.scalar_tensor_tensor(out=st[:, b + 5:b + 6], in0=st[:, b + 2:b + 3], scalar=1.0 / D, in1=st[:, b + 4:b + 5],
                                                                  op0=ALU.mult, op1=ALU.add), reads=[st_b], writes=[st_b])

            def s3():
                n = len(items)
                src = st[:, 0:8 * n].rearrange("p (k c) -> p k c", c=8)[:, :, 5:6]
                dst = st[:, 0:8 * n].rearrange("p (k c) -> p k c", c=8)[:, :, 6:7]
                kb.op("act", lambda e: e.activation(out=dst, in_=src, func=AF.Sqrt, bias=epsv[:, eps_col:eps_col + 1]), reads=[st_b, W], writes=[st_b])
                src2 = st[:, 0:8 * n].rearrange("p (k c) -> p k c", c=8)[:, :, 6:7]
                dst2 = st[:, 0:8 * n].rearrange("p (k c) -> p k c", c=8)[:, :, 7:8]
                kb.op("dve", lambda e: e.reciprocal(out=dst2, in_=src2), reads=[st_b], writes=[st_b])

            def s4(k):
                def f():
                    it_ = items[k]
                    b = 8 * k
                    y, y_b = it_["y"], it_["y_b"]
                    z, z_b = Z.next()
                    kb.op("dve", lambda e: e.scalar_tensor_tensor(out=z[:, :], in0=y[:, :], scalar=st[:, b + 3:b + 4], in1=lnv[:, gi, :],
                                                                  op0=ALU.subtract, op1=ALU.mult), reads=[y_b, st_b, W], writes=[z_b])
                    kb.op("dve", lambda e: e.scalar_tensor_tensor(out=it_["out_ap"], in0=z[:, :], scalar=st[:, b + 7:b + 8], in1=lnv[:, gi + 1, :],
                                                                  op0=ALU.mult, op1=ALU.add), reads=[z_b, st_b, W], writes=[it_["out_b"]])
                    if it_.get("post") is not None:
                        it_["post"]()
                return f

            return [s1, s2, s3, s4(0), s4(1)]

        def cast_bf(src, src_b):
            xb, xb_b = XBF.next()
            kb.op("pool", lambda e: e.tensor_copy(out=xb[:, :], in_=src), reads=[src_b], writes=[xb_b])
            return xb, xb_b

        def transposes(xb, xb_b, s_, dstT, dstT_b):
            tp, tp_b = TP.next()
            for ft in range(8):
                kb.op("pe", lambda e: e.transpose(out=tp[:, ft, :], in_=xb[:, ft * 128:(ft + 1) * 128], identity=ident[:, :]),
                      reads=[xb_b, ident_b], writes=[tp_b], inc=(ft == 7))
            kb.op("act", lambda e: e.activation(out=dstT[:, :, s_ * 128:(s_ + 1) * 128], in_=tp[:, :, :], func=AF.Copy), reads=[tp_b], writes=[dstT_b])

        S = {}

        def A_mm(ch):
            t0 = ch * CH3
            st = S.setdefault(ch, {})
            oT, oT_b = OT.next()
            pf, pf_b = PF.next()
            kb.dma(oT[:, 0:4, :], ds["ON"].rearrange("(c p) t -> p c t", p=128)[:, :, t0:t0 + CH3], reads=[dsb["ON"]], writes=[oT_b])
            kb.dma(oT[:, 4:8, :], ds["OM"].rearrange("(c p) t -> p c t", p=128)[:, :, t0:t0 + CH3], reads=[dsb["OM"]], writes=[oT_b])
            kb.dma(pf[:, :, :], din["p"][t0:t0 + CH3, :].rearrange("(s p) d -> p s d", p=128), writes=[pf_b])
            pb_, pb_b = PB.next()
            kb.op("pool", lambda e: e.tensor_copy(out=pb_[:, :, :], in_=pf[:, :, :]), reads=[pf_b], writes=[pb_b])
            st["pb"] = (pb_, pb_b)
            st["x1"] = X1.next()
            st["x1T"] = X1T.next()
            st["ys"] = []
            sst, sst_b = ST.next()
            st["sst"] = (sst, sst_b)
            for s_ in range(2):
                y, y_b = Y.next()
                xf, xf_b = XF.next()
                kb.dma(xf[:, :], din["x"][t0 + s_ * 128:t0 + (s_ + 1) * 128, :], writes=[xf_b])
                for n in range(2):
                    pa, pa_b = PA.next()
                    for kt in range(8):
                        kb.op("pe", lambda e: e.matmul(pa[:, :], lhsT=oT[:, kt, s_ * 128:(s_ + 1) * 128], rhs=wo[:, kt, n * 512:(n + 1) * 512],
                                                       start=(kt == 0), stop=(kt == 7)), reads=[oT_b, W], writes=[pa_b], inc=(kt == 7))
                    kb.op("dve", lambda e: e.scalar_tensor_tensor(out=y[:, n * 512:(n + 1) * 512], in0=xf[:, n * 512:(n + 1) * 512], scalar=ALPHA,
                                                                  in1=pa[:, :], op0=ALU.mult, op1=ALU.add, accum_out=sst[:, 8 * s_ + n:8 * s_ + n + 1]),
                          reads=[xf_b, pa_b], writes=[y_b, sst_b])
                st["ys"].append((y, y_b))

        def A_ln(ch):
            st = S[ch]
            x1, x1_b = st["x1"]
            sst, sst_b = st["sst"]
            items = []
            for s_ in range(2):
                y, y_b = st["ys"][s_]

                def post(s_=s_):
                    st["xb%d" % s_] = cast_bf(x1[:, s_, :], x1_b)

                items.append(dict(y=y, y_b=y_b, out_ap=x1[:, s_, :], out_b=x1_b, post=post))
            return ln_pair_steps(items, sst, sst_b, 0)

        def A_tr(ch, s_):
            st = S[ch]
            x1T, x1T_b = st["x1T"]
            transposes(*st["xb%d" % s_], s_, x1T, x1T_b)

        def B_up(ch, j):
            st = S[ch]
            x1T, x1T_b = st["x1T"]
            if j == 0:
                st["aT"] = AT.next()
            aT, aT_b = st["aT"]
            wu, wu_b = WU.next()
            kb.dma(wu[:, :, :, :], ds["WUP"][j], reads=[dsb["WUP"]], writes=[wu_b])
            pg, pg_b = PG.next()
            for a in range(2):
                for ft in range(8):
                    kb.op("pe", lambda e: e.matmul(pg[:, a, :], lhsT=wu[:, a, ft, :], rhs=x1T[:, ft, :], start=(ft == 0), stop=(ft == 7)),
                          reads=[wu_b, x1T_b], writes=[pg_b], inc=(a == 1 and ft == 7))
            sg, sg_b = SG.next()
            kb.op("act", lambda e: e.activation(out=sg[:, :], in_=pg[:, 0, :], func=AF.Silu), reads=[pg_b], writes=[sg_b])
            kb.op("dve", lambda e: e.tensor_tensor(out=aT[:, j, :], in0=pg[:, 1, :], in1=sg[:, :], op=ALU.mult), reads=[pg_b, sg_b], writes=[aT_b])

        def B_down(ch):
            st = S[ch]
            aT, aT_b = st["aT"]
            x1, x1_b = st["x1"]
            st["x2"] = X2.next()
            st["x2T"] = X2T.next()
            x2, x2_b = st["x2"]
            ys = []
            sst, sst_b = ST.next()
            st["sst2"] = (sst, sst_b)
            for s_ in range(2):
                y, y_b = Y.next()
                for n in range(2):
                    pa, pa_b = PA.next()
                    for j in range(22):
                        kb.op("pe", lambda e: e.matmul(pa[:, :], lhsT=aT[:, j, s_ * 128:(s_ + 1) * 128], rhs=wd[:, j, n * 512:(n + 1) * 512],
                                                       start=(j == 0), stop=(j == 21)), reads=[aT_b, W], writes=[pa_b], inc=(j == 21))
                    kb.op("dve", lambda e: e.scalar_tensor_tensor(out=y[:, n * 512:(n + 1) * 512], in0=x1[:, s_, n * 512:(n + 1) * 512], scalar=ALPHA,
                                                                  in1=pa[:, :], op0=ALU.mult, op1=ALU.add, accum_out=sst[:, 8 * s_ + n:8 * s_ + n + 1]),
                          reads=[x1_b, pa_b], writes=[y_b, sst_b])
                ys.append((y, y_b))
            st["ys2"] = ys

        def B_ln(ch):
            st = S[ch]
            x2, x2_b = st["x2"]
            sst, sst_b = st["sst2"]
            items = []
            for s_ in range(2):
                y, y_b = st["ys2"][s_]

                def post(s_=s_):
                    st["x2b%d" % s_] = cast_bf(x2[:, s_, :], x2_b)

                items.append(dict(y=y, y_b=y_b, out_ap=x2[:, s_, :], out_b=x2_b, post=post))
            return ln_pair_steps(items, sst, sst_b, 2)

        def B_tr(ch, s_):
            st = S[ch]
            x2T, x2T_b = st["x2T"]
            transposes(*st["x2b%d" % s_], s_, x2T, x2T_b)

        def C_pre(ch):
            st = S[ch]
            pb_, pb_b = st["pb"]
            pT, pT_b = PT.next()
            tp, tp_b = TP.next()
            for s_ in range(2):
                for kt in range(2):
                    kb.op("pe", lambda e: e.transpose(out=tp[:, s_ * 2 + kt, :], in_=pb_[:, s_, kt * 128:(kt + 1) * 128], identity=ident[:, :]),
                          reads=[pb_b, ident_b], writes=[tp_b], inc=(s_ == 1 and kt == 1))
            for s_ in range(2):
                kb.op("act", lambda e: e.activation(out=pT[:, :, s_ * 128:(s_ + 1) * 128], in_=tp[:, 2 * s_:2 * s_ + 2, :], func=AF.Copy),
                      reads=[tp_b], writes=[pT_b])
            st["pT"] = (pT, pT_b)

        def C_mm(ch, s_):
            st = S[ch]
            x2, x2_b = st["x2"]
            x2T, x2T_b = st["x2T"]
            pT, pT_b = st["pT"]
            if s_ == 0:
                st["sst3"] = ST.next()
                st["ys3"] = []
            sst, sst_b = st["sst3"]
            y, y_b = Y.next()
            for n in range(2):
                pa, pa_b = PA.next()
                for kt in range(8):
                    kb.op("pe", lambda e: e.matmul(pa[:, :], lhsT=x2T[:, kt, s_ * 128:(s_ + 1) * 128], rhs=wpg[:, kt, n * 512:(n + 1) * 512],
                                                   start=(kt == 0), stop=(kt == 7)), reads=[x2T_b, W], writes=[pa_b], inc=(kt == 7))
                kb.op("act", lambda e: e.activation(out=y[:, n * 512:(n + 1) * 512], in_=pa[:, :], func=AF.Tanh, scale=0.5), reads=[pa_b], writes=[y_b])
                pp, pp_b = PA.next()
                for kt in range(2):
                    kb.op("pe", lambda e: e.matmul(pp[:, :], lhsT=pT[:, kt, s_ * 128:(s_ + 1) * 128], rhs=wpl[:, kt, n * 512:(n + 1) * 512],
                                                   start=(kt == 0), stop=(kt == 1)), reads=[pT_b, W], writes=[pp_b], inc=(kt == 1))
                kb.op("dve", lambda e: e.scalar_tensor_tensor(out=y[:, n * 512:(n + 1) * 512], in0=y[:, n * 512:(n + 1) * 512], scalar=1.0, in1=pp[:, :],
                                                              op0=ALU.add, op1=ALU.mult), reads=[pp_b, y_b], writes=[y_b])
                kb.op("dve", lambda e: e.scalar_tensor_tensor(out=y[:, n * 512:(n + 1) * 512], in0=x2[:, s_, n * 512:(n + 1) * 512], scalar=2.0 * ALPHA,
                                                              in1=y[:, n * 512:(n + 1) * 512], op0=ALU.mult, op1=ALU.add,
                                                              accum_out=sst[:, 8 * s_ + n:8 * s_ + n + 1]),
                      reads=[x2_b, y_b], writes=[y_b, sst_b])
            st["ys3"].append((y, y_b))

        def C_ln(ch):
            t0 = ch * CH3
            st = S[ch]
            sst, sst_b = st["sst3"]
            zj, zj_b = Z.items[0]
            items = []
            for s_ in range(2):
                y, y_b = st["ys3"][s_]

                def post(s_=s_, y=y, y_b=y_b):
                    kb.dma(c.out[t0 + s_ * 128:t0 + (s_ + 1) * 128, :], y[:, :], reads=[y_b], writes=[c.out_b], q="pool")
                    if s_ == 1:
                        S.pop(ch)

                items.append(dict(y=y, y_b=y_b, out_ap=y[:, :], out_b=y_b, post=post, junk=(zj[:, :], zj_b)))
            return ln_pair_steps(items, sst, sst_b, 4, eps_col=1)

        def run_all(steps):
            for f in steps:
                f()

        A_mm(0)
        run_all(A_ln(0))
        A_tr(0, 0)
        A_tr(0, 1)
        sched = {
            0: [("Bln", 0)], 1: [("Bln", 1)], 2: [("Bln", 2)], 3: [("Bln", 3)], 4: [("Bln", 4)], 5: [("Btr", 0)], 6: [("Btr", 1), ("Amm", 0)],
            7: [("Aln", 0)], 8: [("Aln", 1), ("Cpre", 0)], 9: [("Aln", 2)], 10: [("Aln", 3), ("Cmm", 0)], 11: [("Aln", 4)], 12: [("Cmm", 1)],
            13: [("Cln", 0), ("Atr", 0)], 14: [("Cln", 1)], 15: [("Cln", 2), ("Atr", 1)], 16: [("Cln", 3)], 17: [("Cln", 4)],
        }
        chains = {}

        def run_slot(it, j):
            nxt = it + 1 < NCH3
            prv = it >= 1
            for name, k in sched.get(j, []):
                if name == "Bln" and prv:
                    if k == 0:
                        chains["B"] = B_ln(it - 1)
                    chains["B"][k]()
                elif name == "Btr" and prv:
                    B_tr(it - 1, k)
                elif name == "Amm" and nxt:
                    A_mm(it + 1)
                elif name == "Aln" and nxt:
                    if k == 0:
                        chains["A"] = A_ln(it + 1)
                    chains["A"][k]()
                elif name == "Atr" and nxt:
                    A_tr(it + 1, k)
                elif name == "Cpre" and prv:
                    C_pre(it - 1)
                elif name == "Cmm" and prv:
                    C_mm(it - 1, k)
                elif name == "Cln" and prv:
                    if k == 0:
                        chains["C"] = C_ln(it - 1)
                    chains["C"][k]()

        for it in range(NCH3 + 1):
            if it < NCH3:
                for j in range(22):
                    B_up(it, j)
                    run_slot(it, j)
                B_down(it)
            else:
                for j in range(22):
                    run_slot(it, j)


_CACHE = {}


def kernel(**inputs):
    if "nc" not in _CACHE:
        _CACHE["nc"] = build_program()
        _CACHE["consts"] = host_constants()
    nc = _CACHE["nc"]
    consts = _CACHE["consts"]
    B = inputs["x"].shape[0]
    shared = {}
    for k, shp in IN_SPECS.items():
        if k in ("x", "p"):
            continue
        shared[k] = np.ascontiguousarray(np.asarray(inputs[k], dtype=np.float32)[0].reshape(shp))
    for k, v in consts.items():
        shared["c_" + k] = v
    in_maps = []
    for b in range(B):
        m = dict(shared)
        m["x"] = np.ascontiguousarray(np.asarray(inputs["x"], dtype=np.float32)[b])
        m["p"] = np.ascontiguousarray(np.asarray(inputs["p"], dtype=np.float32)[0, b])
        in_maps.append(m)
    res = run_bass_kernel_spmd(nc, in_maps, core_ids=list(range(B)))
    return np.stack([np.asarray(r["out"], dtype=np.float32) for r in res.results], 0)
```

```python
from contextlib import ExitStack
import numpy as np
import ml_dtypes
import concourse.bass as bass
import concourse.mybir as mybir
from concourse.bass_utils import run_bass_kernel_spmd

F32 = mybir.dt.float32
BF16 = mybir.dt.bfloat16
AF = mybir.ActivationFunctionType
ALU = mybir.AluOpType
AX = mybir.AxisListType
bf16 = ml_dtypes.bfloat16

T = 8192
D = 1024
NCH = T // 512
NKT = T // 128
DFF = 2816
BIG = 30000.0
ALPHA = 2.0 ** 0.25
LN_EPS = 1e-5
RMS_EPS = 1e-6
MLA_SCALE = 96.0 ** -0.5
ALIBI_CUT = 160.0


class Buf:
    __slots__ = ("name", "last_w", "readers")

    def __init__(self, name=""):
        self.name = name
        self.last_w = None
        self.readers = {}


class KB:
    NSLOT = 8

    def __init__(self, nc, es):
        self.nc = nc
        self.E = {"pe": nc.tensor, "act": nc.scalar, "dve": nc.vector, "pool": nc.gpsimd, "sp": nc.sync}
        self.sem = {}
        self.seq = {}
        self.known = {}
        for e in self.E:
            self.sem[("e", e)] = es.enter_context(nc.semaphore("s_" + e))
            self.seq[e] = 0
            self.known[e] = {}
        self.slots = {}
        self.slot_rr = {}
        for q in ("sp", "pool"):
            self.slots[q] = []
            self.slot_rr[q] = 0
            for i in range(self.NSLOT):
                key = ("d", q, i)
                self.sem[key] = es.enter_context(nc.semaphore("d_%s%d" % (q, i)))
                self.slots[q].append([key, 0])

    def _collect(self, reads, writes):
        deps = {}
        for b in reads:
            if b.last_w is not None:
                k, v = b.last_w
                o = deps.get(k)
                deps[k] = (max(v, o[0]) if o else v, True)
        for b in writes:
            if b.last_w is not None:
                k, v = b.last_w
                o = deps.get(k)
                if o is None:
                    deps[k] = (v, False)
                elif v > o[0]:
                    deps[k] = (v, o[1])
            for k, v in b.readers.items():
                o = deps.get(k)
                if o is None:
                    deps[k] = (v, False)
                elif v > o[0]:
                    deps[k] = (v, o[1])
        return deps

    def _wait(self, eng, deps):
        kn = self.known[eng]
        for k, (v, raw) in deps.items():
            if k == ("e", eng):
                if eng == "pe" or not raw:
                    continue
            if kn.get(k, 0) >= v:
                continue
            self.E[eng].wait_ge(self.sem[k], v)
            kn[k] = v

    def _record(self, dep, reads, writes):
        k, v = dep
        for b in reads:
            if b.readers.get(k, 0) < v:
                b.readers[k] = v
        for b in writes:
            b.last_w = dep
            b.readers = {}

    def op(self, eng, fn, reads=(), writes=(), inc=True):
        self._wait(eng, self._collect(reads, writes))
        ins = fn(self.E[eng])
        if inc:
            ins.then_inc(self.sem[("e", eng)], 1)
            self.seq[eng] += 1
            dep = (("e", eng), self.seq[eng])
        else:
            dep = (("e", eng), self.seq[eng] + 1)
        self._record(dep, reads, writes)
        return ins

    def dma(self, out, in_, reads=(), writes=(), q="sp", **kw):
        sl = self.slots[q][self.slot_rr[q]]
        self.slot_rr[q] = (self.slot_rr[q] + 1) % self.NSLOT
        deps = self._collect(reads, writes)
        if sl[1] > 0:
            o = deps.get(sl[0])
            if o is None or o[0] < sl[1]:
                deps[sl[0]] = (sl[1], True)
        self._wait(q, deps)
        self.E[q].dma_start(out=out, in_=in_, **kw).then_inc(self.sem[sl[0]], 16)
        sl[1] += 16
        self._record((sl[0], sl[1]), reads, writes)

    def barrier(self):
        cur = {}
        for e in self.E:
            cur[("e", e)] = self.seq[e]
        for q in self.slots:
            for key, cnt in self.slots[q]:
                cur[key] = cnt
        for e in self.E:
            kn = self.known[e]
            for k, v in cur.items():
                if v == 0 or k == ("e", e):
                    continue
                if kn.get(k, 0) >= v:
                    continue
                self.E[e].wait_ge(self.sem[k], v)
                kn[k] = v


class Rot:
    def __init__(self, items):
        self.items = items
        self.i = 0

    def next(self):
        it = self.items[self.i]
        self.i = (self.i + 1) % len(self.items)
        return it


def host_constants():
    c = {}
    c["ident"] = np.eye(128, dtype=np.float32).astype(bf16)
    pos = np.arange(T, dtype=np.float32)
    inv_freq = (10000.0 ** (-np.arange(16, dtype=np.float32) / 16)).astype(np.float32)
    ang = (pos[:, None] * inv_freq[None, :]).astype(np.float32)
    cos = np.cos(ang).astype(np.float32).T
    sin = np.sin(ang).astype(np.float32).T
    c["ropeC"] = np.ascontiguousarray(np.concatenate([cos, cos], 0))
    c["ropeS"] = np.ascontiguousarray(np.concatenate([-sin, sin], 0))
    p = np.arange(128)[:, None]
    j = np.arange(512)[None, :]
    def addmask(valid):
        return np.where(valid, 0.0, -BIG).astype(np.float32).astype(bf16)
    c["cmk"] = addmask(np.stack([(p + 128 * d <= j) for d in range(4)], 1))
    c["wmk"] = addmask(np.stack([(j < 128 * d + p) for d in range(4)], 1))
    c["pmk"] = addmask(np.stack([(16 * p + 31 <= 512 * m + j) for m in range(5)], 1))
    slopes = 2.0 ** (-np.arange(1, 9, dtype=np.float64))
    t = np.arange(T)
    qaug = np.zeros((8, 5, T), np.float32)
    for h in range(8):
        s = slopes[h]
        qaug[h, 0] = s
        qaug[h, 1] = s
        qaug[h, 2] = -s * (t // 128) * 128
        qaug[h, 3] = -s * (t % 128)
        qaug[h, 4] = -s * ((t % 128) - 31)
    c["qaug"] = qaug.astype(bf16)
    ka = np.zeros((5, T), np.float32)
    ka[0] = (t // 128) * 128
    ka[1] = t % 128
    ka[2] = 1
    ka[3] = 1
    c["kaug"] = ka.astype(bf16)
    cc = np.arange(512)
    kc = np.zeros((5, 512), np.float32)
    kc[0] = (cc // 128) * 2048
    kc[1] = (cc % 128) * 16
    kc[2] = 1
    kc[4] = 1
    c["kaugc"] = kc.astype(bf16)
    for k in ("qaug", "kaug", "kaugc"):
        assert np.all(c[k].astype(np.float32) == {"qaug": qaug, "kaug": ka, "kaugc": kc}[k])
    E = np.zeros((128, 64, 128), np.float32)
    for kt in range(64):
        E[2 * kt, kt, :64] = BIG
        E[2 * kt + 1, kt, 64:] = BIG
    c["eexp"] = E.astype(bf16)
    M = np.zeros((512, 128), np.float32)
    for cidx in range(511):
        for s in (cidx, cidx + 1):
            M[cidx, s // 4] += 1
    c["impm"] = M.astype(bf16)
    keep = np.zeros((128, 256), np.float32)
    add = np.zeros((128, 256), np.float32)
    pp = np.arange(128)
    for col in range(256):
        r = col - 128
        if r < -1:
            keep[:, col] = 1
        elif r == -1:
            keep[:, col] = (pp >= 64)
            add[:, col] = 1e6 * (pp < 64)
        elif r == 0:
            add[:, col] = 1e6
        elif r == 1:
            add[:, col] = np.where(pp >= 64, 1e6, -1.0)
        else:
            add[:, col] = -1.0
    c["keept"] = keep
    c["addt"] = add
    return c


CONST_SPECS = {
    "ident": ([128, 128], BF16), "ropeC": ([32, T], F32), "ropeS": ([32, T], F32),
    "cmk": ([128, 4, 512], BF16), "wmk": ([128, 4, 512], BF16), "pmk": ([128, 5, 512], BF16),
    "qaug": ([8, 5, T], BF16), "kaug": ([5, T], BF16), "kaugc": ([5, 512], BF16),
    "eexp": ([128, 64, 128], BF16), "impm": ([512, 128], BF16),
    "keept": ([128, 256], F32), "addt": ([128, 256], F32),
}

IN_SPECS = {
    "x": [T, D], "p": [T, 256], "w_in": [D, 1720], "w_ck1": [2048, 256], "w_ck2": [256, 64],
    "pos_ck": [32, 64], "w_cv1": [2048, 256], "w_cv2": [256, 64], "pos_cv": [32, 64],
    "mla_q_norm": [256, 1], "w_uq": [256, 768], "mla_kv_norm": [128, 1], "w_ukv": [128, 1024],
    "w_out": [D, D], "ln1_g": [1, D], "ln1_b": [1, D], "w_up": [D, 2 * DFF], "w_down": [DFF, D],
    "ln2_g": [1, D], "ln2_b": [1, D], "w_ple_gate": [D, D], "w_ple": [256, D],
    "ln3_g": [1, D], "ln3_b": [1, D],
}

SCRATCH_SPECS = {
    "QN": ([512, T], BF16),
    "KX": ([4, 128, T], BF16),
    "VSW": ([T, 256], BF16),
    "G": ([24, T], F32),
    "KPE": ([32, T], BF16),
    "QM": ([8, 96, T], BF16),
    "KN": ([512, T], BF16),
    "VM": ([T, 512], BF16),
    "ON": ([512, T], BF16),
    "OM": ([512, T], BF16),
    "WUP": ([22, 128, 2, 8, 128], BF16),
}


def sb(nc, es, name, shape, dt):
    return es.enter_context(nc.sbuf_tensor(name, list(shape), dt))


def ps(nc, es, name, shape, dt):
    return es.enter_context(nc.psum_tensor(name, list(shape), dt))


def rot_tiles(nc, es, name, shape, dt, n, psum=False):
    items = []
    for i in range(n):
        t = (ps if psum else sb)(nc, es, "%s%d" % (name, i), shape, dt)
        items.append((t, Buf("%s%d" % (name, i))))
    return Rot(items)


class Ctx:
    pass


def build_program(stop_after=None, debug_scratch=False, stop_mid=None):
    nc = bass.Bass("TRN2", target_bir_lowering=False)
    c = Ctx()
    c.nc = nc
    c.din = {k: nc.dram_tensor(k, v, F32, kind="ExternalInput").ap() for k, v in IN_SPECS.items()}
    c.dc = {k: nc.dram_tensor("c_" + k, v[0], v[1], kind="ExternalInput").ap() for k, v in CONST_SPECS.items()}
    skind = "ExternalOutput" if debug_scratch else "Internal"
    c.ds = {k: nc.dram_tensor("s_" + k, v[0], v[1], kind=skind).ap() for k, v in SCRATCH_SPECS.items()}
    c.dsb = {k: Buf("s_" + k) for k in SCRATCH_SPECS}
    c.out = nc.dram_tensor("out", [T, D], F32, kind="ExternalOutput").ap()
    c.out_b = Buf("out")
    c.stop_mid = stop_mid
    if stop_mid == "compress":
        c.dbg = {"kcmp": nc.dram_tensor("dbg_kcmp", [69, 2, 512], BF16, kind="ExternalOutput").ap(),
                 "vcmp": nc.dram_tensor("dbg_vcmp", [128, 4, 2, 128], BF16, kind="ExternalOutput").ap()}
    with ExitStack() as es0:
        kb = KB(nc, es0)
        c.kb = kb
        c.ident = sb(nc, es0, "ident", [128, 128], BF16)
        c.ident_b = Buf("ident")
        kb.dma(c.ident[:], c.dc["ident"][:], writes=[c.ident_b])
        phases = [phase1, phase2, phase3]
        for ph in phases:
            ph(c)
            kb.barrier()
            if stop_after == ph.__name__:
                break
        kb.barrier()
    return nc


def phase1(c):
    nc, kb = c.nc, c.kb
    din, dc, ds, dsb = c.din, c.dc, c.ds, c.dsb
    with ExitStack() as es:
        wtr = sb(nc, es, "wtr", [128, 8, 1048], BF16)
        wtok = sb(nc, es, "wtok", [128, 8, 640], BF16)
        wkpA = sb(nc, es, "wkpA", [128, 8, 96], BF16)
        wkpB = sb(nc, es, "wkpB", [128, 8, 96], BF16)
        wuqA = sb(nc, es, "wuqA", [128, 2, 8, 96], BF16)
        wuqB = sb(nc, es, "wuqB", [128, 2, 8, 96], BF16)
        wkvK = sb(nc, es, "wkvK", [128, 8, 64], BF16)
        wkvV = sb(nc, es, "wkvV", [128, 8, 64], BF16)
        gq = sb(nc, es, "gq", [128, 2], F32)
        gkv = sb(nc, es, "gkv", [128, 1], F32)
        W = Buf("p1w")
        es_stg = ExitStack()
        stg = rot_tiles(nc, es_stg, "stg", [128, 1720], F32, 2)
        kb.op("pool", lambda e: e.memset(wkpA[:], 0.0), writes=[W])
        kb.op("pool", lambda e: e.memset(wkpB[:], 0.0), writes=[W])
        kb.op("pool", lambda e: e.memset(wuqB[:], 0.0), writes=[W])
        for kt in range(2):
            kb.dma(gq[:, kt:kt + 1], din["mla_q_norm"][kt * 128:(kt + 1) * 128, :], writes=[W])
        kb.dma(gkv[:], din["mla_kv_norm"][:, :], writes=[W])
        for ft in range(8):
            s, sbuf_ = stg.next()
            kb.dma(s[:], din["w_in"][ft * 128:(ft + 1) * 128, :], writes=[sbuf_])
            kb.op("dve", lambda e: e.tensor_copy(out=wtr[:, ft, 0:896], in_=s[:, 0:896]), reads=[sbuf_], writes=[W])
            kb.op("pool", lambda e: e.tensor_copy(out=wtr[:, ft, 896:1024], in_=s[:, 1024:1152]), reads=[sbuf_], writes=[W])
            kb.op("pool", lambda e: e.tensor_copy(out=wtr[:, ft, 1024:1048], in_=s[:, 1280:1304]), reads=[sbuf_], writes=[W])
            kb.op("dve", lambda e: e.tensor_copy(out=wtok[:, ft, 0:384], in_=s[:, 1304:1688]), reads=[sbuf_], writes=[W])
            kb.op("pool", lambda e: e.tensor_copy(out=wtok[:, ft, 384:512], in_=s[:, 896:1024]), reads=[sbuf_], writes=[W])
            kb.op("pool", lambda e: e.tensor_copy(out=wtok[:, ft, 512:640], in_=s[:, 1152:1280]), reads=[sbuf_], writes=[W])
            kb.op("dve", lambda e: e.tensor_copy(out=wkpA[:, ft, 64:96], in_=s[:, 1688:1720]), reads=[sbuf_], writes=[W])
            kb.op("dve", lambda e: e.tensor_copy(out=wkpB[:, ft, 64:80], in_=s[:, 1704:1720]), reads=[sbuf_], writes=[W])
            kb.op("dve", lambda e: e.tensor_copy(out=wkpB[:, ft, 80:96], in_=s[:, 1688:1704]), reads=[sbuf_], writes=[W])
        for kt in range(2):
            s, sbuf_ = stg.next()
            kb.dma(s[:, 0:768], din["w_uq"][kt * 128:(kt + 1) * 128, :], writes=[sbuf_])
            s3 = s[:, 0:768].rearrange("p (h c) -> p h c", c=96)
            kb.op("dve", lambda e: e.tensor_scalar(out=wuqA[:, kt, :, :], in0=s3, scalar1=gq[:, kt:kt + 1], scalar2=None,
                                                   op0=ALU.mult), reads=[sbuf_, W], writes=[W])
            kb.op("dve", lambda e: e.tensor_scalar(out=wuqB[:, kt, :, 64:80], in0=s3[:, :, 80:96], scalar1=gq[:, kt:kt + 1],
                                                   scalar2=None, op0=ALU.mult), reads=[sbuf_, W], writes=[W])
            kb.op("dve", lambda e: e.tensor_scalar(out=wuqB[:, kt, :, 80:96], in0=s3[:, :, 64:80], scalar1=gq[:, kt:kt + 1],
                                                   scalar2=None, op0=ALU.mult), reads=[sbuf_, W], writes=[W])
        s, sbuf_ = stg.next()
        kb.dma(s[:, 0:1024], din["w_ukv"][:, :], writes=[sbuf_])
        s3 = s[:, 0:1024].rearrange("p (h c) -> p h c", c=128)
        kb.op("dve", lambda e: e.tensor_scalar(out=wkvK[:, :, :], in0=s3[:, :, 0:64], scalar1=gkv[:, 0:1], scalar2=None,
                                               op0=ALU.mult), reads=[sbuf_, W], writes=[W])
        kb.op("dve", lambda e: e.tensor_scalar(out=wkvV[:, :, :], in0=s3[:, :, 64:128], scalar1=gkv[:, 0:1], scalar2=None,
                                               op0=ALU.mult), reads=[sbuf_, W], writes=[W])
        kb.barrier()
        es_stg.close()
        XF = rot_tiles(nc, es, "xf", [128, 4, 1024], F32, 2)
        XT = rot_tiles(nc, es, "xT", [128, 8, 512], BF16, 2)
        STT = rot_tiles(nc, es, "stT", [128, 8, 512], BF16, 2)
        GST = rot_tiles(nc, es, "gst", [24, 512], F32, 2)
        RC = rot_tiles(nc, es, "rc", [96, 2, 512], F32, 3)
        KST = rot_tiles(nc, es, "kst", [96, 512], BF16, 2)
        RT = rot_tiles(nc, es, "rt", [96, 2, 512], F32, 2)
        VST = rot_tiles(nc, es, "vst", [128, 4, 256], BF16, 2)
        SS = rot_tiles(nc, es, "ss", [128, 4], F32, 2)
        JK = rot_tiles(nc, es, "jk", [128, 256], F32, 2)
        CN = rot_tiles(nc, es, "cn", [128, 384], BF16, 2)
        CT = rot_tiles(nc, es, "cT", [128, 3, 512], BF16, 2)
        QST = rot_tiles(nc, es, "qst", [96, 8, 512], BF16, 2)
        KNST = rot_tiles(nc, es, "knst", [128, 4, 512], BF16, 2)
        VMST = rot_tiles(nc, es, "vmst", [128, 4, 512], BF16, 2)
        TP = rot_tiles(nc, es, "tp", [128, 1024], BF16, 2, psum=True)
        TPQ = rot_tiles(nc, es, "tpq", [128, 3, 128], BF16, 1, psum=True)
        PS = rot_tiles(nc, es, "ps", [128, 512], F32, 5, psum=True)
        ident, ident_b = c.ident, c.ident_b
        xv = din["x"]
        XB = rot_tiles(nc, es, "xbb", [128, 4, 1024], BF16, 2)
        LS = {}

        def L_load(ch):
            t0 = ch * 512
            xf, xf_b = XF.next()
            kb.dma(xf[:], xv[t0:t0 + 512, :].rearrange("(s p) d -> p s d", p=128), writes=[xf_b])
            rc, rc_b = RC.next()
            kb.dma(rc[64:96, 0, :], dc["ropeC"][:, t0:t0 + 512], writes=[rc_b])
            kb.dma(rc[64:96, 1, :], dc["ropeS"][:, t0:t0 + 512], writes=[rc_b])
            LS[ch] = {"xf": (xf, xf_b), "rc": (rc, rc_b)}

        def L_cast(ch):
            xf, xf_b = LS[ch]["xf"]
            xb, xb_b = XB.next()
            kb.op("dve", lambda e: e.tensor_copy(out=xb[:, 0:3, :], in_=xf[:, 0:3, :]), reads=[xf_b], writes=[xb_b])
            kb.op("pool", lambda e: e.tensor_copy(out=xb[:, 3:4, :], in_=xf[:, 3:4, :]), reads=[xf_b], writes=[xb_b])
            LS[ch]["xb"] = (xb, xb_b)

        def L_tr(ch):
            xb, xb_b = LS[ch]["xb"]
            xT, xT_b = XT.next()
            for f2 in range(4):
                tp, tp_b = TP.next()
                for fi in range(2):
                    ft = f2 * 2 + fi
                    for s_ in range(4):
                        last = (fi == 1 and s_ == 3)
                        kb.op("pe", lambda e: e.transpose(out=tp[:, fi * 512 + s_ * 128: fi * 512 + (s_ + 1) * 128],
                                                          in_=xb[:, s_, ft * 128:(ft + 1) * 128], identity=ident[:]),
                              reads=[xb_b, ident_b], writes=[tp_b], inc=last)
                dst = xT[:, 2 * f2:2 * f2 + 2, :].rearrange("p a t -> p (a t)")
                kb.op("act" if f2 % 2 else "dve",
                      (lambda e: e.activation(out=dst, in_=tp[:], func=AF.Copy)) if f2 % 2 else
                      (lambda e: e.tensor_copy(out=dst, in_=tp[:])), reads=[tp_b], writes=[xT_b])
            LS[ch]["xT"] = (xT, xT_b)

        L_load(0)
        L_cast(0)
        L_tr(0)
        if NCH > 1:
            L_load(1)
        for ch in range(NCH):
            t0 = ch * 512
            xT, xT_b = LS[ch]["xT"]
            rc, rc_b = LS[ch]["rc"]
            if ch + 2 < NCH:
                L_load(ch + 2)
            if ch + 1 < NCH:
                L_cast(ch + 1)
            stT, stT_b = STT.next()
            for ct in range(8):
                pt, pt_b = PS.next()
                for ft in range(8):
                    kb.op("pe", lambda e: e.matmul(pt[:, :], lhsT=wtr[:, ft, ct * 128:(ct + 1) * 128], rhs=xT[:, ft, :],
                                                   start=(ft == 0), stop=(ft == 7)),
                          reads=[W, xT_b], writes=[pt_b], inc=(ft == 7))
                if ct < 4:
                    kb.op("act", lambda e: e.activation(out=stT[:, ct, :], in_=pt[:, :], func=AF.Copy, scale=0.125),
                          reads=[pt_b], writes=[stT_b])
                else:
                    kb.op("dve", lambda e: e.tensor_copy(out=stT[:, ct, :], in_=pt[:, :]), reads=[pt_b], writes=[stT_b])
            kb.dma(ds["QN"].rearrange("(c p) t -> p c t", p=128)[:, :, t0:t0 + 512], stT[:, 0:4, :],
                   reads=[stT_b], writes=[dsb["QN"]])
            kb.dma(ds["KX"].rearrange("k p t -> p k t")[:, :, t0:t0 + 512], stT[:, 4:8, :],
                   reads=[stT_b], writes=[dsb["KX"]])
            pt, pt_b = PS.next()
            for ft in range(8):
                kb.op("pe", lambda e: e.matmul(pt[0:24, :], lhsT=wtr[:, ft, 1024:1048], rhs=xT[:, ft, :],
                                               start=(ft == 0), stop=(ft == 7)),
                      reads=[W, xT_b], writes=[pt_b], inc=(ft == 7))
            gst, gst_b = GST.next()
            kb.op("act", lambda e: e.activation(out=gst[:, :], in_=pt[0:24, :], func=AF.Sigmoid), reads=[pt_b], writes=[gst_b])
            kb.dma(ds["G"][:, t0:t0 + 512], gst[:, :], reads=[gst_b], writes=[dsb["G"]])
            pa, pa_b = PS.next()
            for ft in range(8):
                kb.op("pe", lambda e: e.matmul(pa[0:96, :], lhsT=wkpA[:, ft, :], rhs=xT[:, ft, :], start=(ft == 0), stop=(ft == 7)),
                      reads=[W, xT_b], writes=[pa_b], inc=(ft == 7))
            pb, pb_b = PS.next()
            for ft in range(8):
                kb.op("pe", lambda e: e.matmul(pb[0:96, :], lhsT=wkpB[:, ft, :], rhs=xT[:, ft, :], start=(ft == 0), stop=(ft == 7)),
                      reads=[W, xT_b], writes=[pb_b], inc=(ft == 7))
            rt, rt_b = RT.next()
            kst, kst_b = KST.next()
            kb.op("dve", lambda e: e.tensor_tensor(out=rt[64:96, 0, :], in0=pa[64:96, :], in1=rc[64:96, 0, :], op=ALU.mult),
                  reads=[pa_b, rc_b], writes=[rt_b])
            kb.op("dve", lambda e: e.tensor_tensor(out=rt[64:96, 1, :], in0=pb[64:96, :], in1=rc[64:96, 1, :], op=ALU.mult),
                  reads=[pb_b, rc_b], writes=[rt_b])
            kb.op("pool", lambda e: e.tensor_tensor(out=kst[64:96, :], in0=rt[64:96, 0, :], in1=rt[64:96, 1, :], op=ALU.add),
                  reads=[rt_b], writes=[kst_b])
            kb.dma(ds["KPE"][:, t0:t0 + 512], kst[64:96, :], reads=[kst_b], writes=[dsb["KPE"]])
            if ch + 1 < NCH:
                L_tr(ch + 1)
            vst, vst_b = VST.next()
            cT, cT_b = CT.next()
            P1S = {}

            def tm_mm(s_):
                p1, p1_b = PS.next()
                for ft in range(8):
                    kb.op("pe", lambda e: e.matmul(p1[:, 0:384], lhsT=xT[:, ft, s_ * 128:(s_ + 1) * 128], rhs=wtok[:, ft, 0:384],
                                                   start=(ft == 0), stop=(ft == 7)),
                          reads=[W, xT_b], writes=[p1_b], inc=(ft == 7))
                p2, p2_b = PS.next()
                for ft in range(8):
                    kb.op("pe", lambda e: e.matmul(p2[:, 0:256], lhsT=xT[:, ft, s_ * 128:(s_ + 1) * 128], rhs=wtok[:, ft, 384:640],
                                                   start=(ft == 0), stop=(ft == 7)),
                          reads=[W, xT_b], writes=[p2_b], inc=(ft == 7))
                kb.op("dve", lambda e: e.tensor_copy(out=vst[:, s_, :], in_=p2[:, 0:256]), reads=[p2_b], writes=[vst_b])
                P1S[s_] = (p1, p1_b)

            def tm_chain(s_):
                p1, p1_b = P1S.pop(s_)
                ss, ss_b = SS.next()
                jk, jk_b = JK.next()
                kb.op("act", lambda e: e.activation(out=jk[:, 0:256], in_=p1[:, 0:256], func=AF.Square, accum_out=ss[:, 0:1]),
                      reads=[p1_b], writes=[jk_b, ss_b])
                kb.op("act", lambda e: e.activation(out=jk[:, 0:128], in_=p1[:, 256:384], func=AF.Square, accum_out=ss[:, 1:2]),
                      reads=[p1_b], writes=[jk_b, ss_b])
                kb.op("dve", lambda e: e.tensor_scalar(out=ss[:, 0:1], in0=ss[:, 0:1], scalar1=1.0 / 256, scalar2=RMS_EPS,
                                                       op0=ALU.mult, op1=ALU.add), reads=[ss_b], writes=[ss_b])
                kb.op("dve", lambda e: e.tensor_scalar(out=ss[:, 1:2], in0=ss[:, 1:2], scalar1=1.0 / 128, scalar2=RMS_EPS,
                                                       op0=ALU.mult, op1=ALU.add), reads=[ss_b], writes=[ss_b])
                kb.op("act", lambda e: e.activation(out=ss[:, 2:4], in_=ss[:, 0:2], func=AF.Sqrt), reads=[ss_b], writes=[ss_b])
                kb.op("dve", lambda e: e.reciprocal(out=ss[:, 0:2], in_=ss[:, 2:4]), reads=[ss_b], writes=[ss_b])
                cn, cn_b = CN.next()
                kb.op("dve", lambda e: e.tensor_scalar(out=cn[:, 0:256], in0=p1[:, 0:256], scalar1=ss[:, 0:1], scalar2=None,
                                                       op0=ALU.mult), reads=[p1_b, ss_b], writes=[cn_b])
                kb.op("dve", lambda e: e.tensor_scalar(out=cn[:, 256:384], in0=p1[:, 256:384], scalar1=ss[:, 1:2], scalar2=None,
                                                       op0=ALU.mult), reads=[p1_b, ss_b], writes=[cn_b])
                tq, tq_b = TPQ.next()
                for i in range(3):
                    kb.op("pe", lambda e: e.transpose(out=tq[:, i, :], in_=cn[:, i * 128:(i + 1) * 128], identity=ident[:]),
                          reads=[cn_b, ident_b], writes=[tq_b], inc=(i == 2))
                kb.op("act", lambda e: e.activation(out=cT[:, :, s_ * 128:(s_ + 1) * 128], in_=tq[:, :, :], func=AF.Copy),
                      reads=[tq_b], writes=[cT_b])

            tm_mm(0)
            for s_ in range(4):
                if s_ + 1 < 4:
                    tm_mm(s_ + 1)
                tm_chain(s_)
            kb.dma(ds["VSW"][t0:t0 + 512, :].rearrange("(s p) c -> p s c", p=128), vst[:, :, :], reads=[vst_b], writes=[dsb["VSW"]])
            qst, qst_b = QST.next()
            for h in range(8):
                pa, pa_b = PS.next()
                for kt in range(2):
                    kb.op("pe", lambda e: e.matmul(pa[0:96, :], lhsT=wuqA[:, kt, h, :], rhs=cT[:, kt, :], start=(kt == 0), stop=(kt == 1)),
                          reads=[W, cT_b], writes=[pa_b], inc=(kt == 1))
                pb, pb_b = PS.next()
                for kt in range(2):
                    kb.op("pe", lambda e: e.matmul(pb[0:96, :], lhsT=wuqB[:, kt, h, :], rhs=cT[:, kt, :], start=(kt == 0), stop=(kt == 1)),
                          reads=[W, cT_b], writes=[pb_b], inc=(kt == 1))
                kb.op("act", lambda e: e.activation(out=qst[0:64, h, :], in_=pa[0:64, :], func=AF.Copy), reads=[pa_b], writes=[qst_b])
                rt, rt_b = RT.next()
                kb.op("dve", lambda e: e.tensor_tensor(out=rt[64:96, 0, :], in0=pa[64:96, :], in1=rc[64:96, 0, :], op=ALU.mult),
                      reads=[pa_b, rc_b], writes=[rt_b])
                kb.op("dve", lambda e: e.tensor_tensor(out=rt[64:96, 1, :], in0=pb[64:96, :], in1=rc[64:96, 1, :], op=ALU.mult),
                      reads=[pb_b, rc_b], writes=[rt_b])
                kb.op("pool", lambda e: e.tensor_tensor(out=qst[64:96, h, :], in0=rt[64:96, 0, :], in1=rt[64:96, 1, :], op=ALU.add),
                      reads=[rt_b], writes=[qst_b])
            kb.dma(ds["QM"].rearrange("h r t -> r h t")[:, :, t0:t0 + 512], qst[:, :, :], reads=[qst_b], writes=[dsb["QM"]])
            knst, knst_b = KNST.next()
            for hp in range(4):
                pt, pt_b = PS.next()
                kb.op("pe", lambda e: e.matmul(pt[:, :], lhsT=wkvK[:, 2 * hp:2 * hp + 2, :].rearrange("p a c -> p (a c)"),
                                               rhs=cT[:, 2, :], start=True, stop=True), reads=[W, cT_b], writes=[pt_b])
                kb.op("act" if hp % 2 else "dve",
                      (lambda e: e.activation(out=knst[:, hp, :], in_=pt[:, :], func=AF.Copy)) if hp % 2 else
                      (lambda e: e.tensor_copy(out=knst[:, hp, :], in_=pt[:, :])), reads=[pt_b], writes=[knst_b])
            kb.dma(ds["KN"].rearrange("(c p) t -> p c t", p=128)[:, :, t0:t0 + 512], knst[:, :, :], reads=[knst_b], writes=[dsb["KN"]])
            vmst, vmst_b = VMST.next()
            for s_ in range(4):
                pt, pt_b = PS.next()
                kb.op("pe", lambda e: e.matmul(pt[:, :], lhsT=cT[:, 2, s_ * 128:(s_ + 1) * 128],
                                               rhs=wkvV[:, :, :].rearrange("p a c -> p (a c)"), start=True, stop=True),
                      reads=[W, cT_b], writes=[pt_b])
                kb.op("act" if s_ % 2 else "dve",
                      (lambda e: e.activation(out=vmst[:, s_, :], in_=pt[:, :], func=AF.Copy)) if s_ % 2 else
                      (lambda e: e.tensor_copy(out=vmst[:, s_, :], in_=pt[:, :])), reads=[pt_b], writes=[vmst_b])
            kb.dma(ds["VM"][t0:t0 + 512, :].rearrange("(s p) c -> p s c", p=128), vmst[:, :, :], reads=[vmst_b], writes=[dsb["VM"]])


class TileD:
    __slots__ = ("qk", "scale", "pv", "epi", "lo", "hi")

    def __init__(self, qk, scale, pv, epi=None, lo=0, hi=512):
        self.qk = qk
        self.scale = scale
        self.pv = pv
        self.epi = epi
        self.lo, self.hi = lo, hi


class Stream:
    def __init__(self, c, R, L=2):
        self.c, self.R, self.L = c, R, L
        self.pending = []

    def push(self, t):
        kb, R = self.c.kb, self.R
        S, S_b = R.SPS.next()
        nq = len(t.qk)
        lo, hi = t.lo, t.hi
        for a, ent in enumerate(t.qk):
            lh, rh, rb = ent[0], ent[1], ent[2]
            clo, chi = (ent[3], ent[4]) if len(ent) > 3 else (lo, hi)
            kb.op("pe", lambda e: e.matmul(S[:, clo:chi], lhsT=lh, rhs=rh, start=(a == 0), stop=(a == nq - 1), skip_group_check=True),
                  reads=rb, writes=[S_b], inc=(a == nq - 1))
        P, P_b = R.PSB.next()
        kb.op("act", lambda e: e.activation(out=P[:, lo:hi], in_=S[:, lo:hi], func=AF.Exp, scale=t.scale), reads=[S_b], writes=[P_b])
        self.pending.append((t, P, P_b))
        if len(self.pending) > self.L:
            self._pop()

    def _pop(self):
        t, P, P_b = self.pending.pop(0)
        t.pv(P, P_b)
        if t.epi is not None:
            t.epi()

    def flush(self):
        while self.pending:
            self._pop()


def run_stream(c, R, tiles, L=2):
    st = c.stream
    for t in tiles:
        st.push(t)


def phase2(c):
    nc, kb = c.nc, c.kb
    din, dc, ds, dsb = c.din, c.dc, c.ds, c.dsb
    ident, ident_b = c.ident, c.ident_b
    with ExitStack() as esA:
        kcmp = sb(nc, esA, "kcmp", [69, 2, 512], BF16)
        kcmp_b = Buf("kcmp")
        vcmp = sb(nc, esA, "vcmp", [128, 4, 2, 128], BF16)
        vcmp_b = Buf("vcmp")
        CB = Buf("consts2")
        cmk = sb(nc, esA, "cmk", [128, 4, 512], BF16)
        wmk = sb(nc, esA, "wmk", [128, 4, 512], BF16)
        pmk = sb(nc, esA, "pmk", [128, 5, 512], BF16)
        eexp = sb(nc, esA, "eexp", [128, 64, 128], BF16)
        impm = sb(nc, esA, "impm", [128, 4, 128], BF16)
        keept = sb(nc, esA, "keept", [128, 256], F32)
        addt = sb(nc, esA, "addt", [128, 256], F32)
        kb.dma(cmk[:], dc["cmk"][:], writes=[CB])
        kb.dma(wmk[:], dc["wmk"][:], writes=[CB])
        kb.dma(pmk[:], dc["pmk"][:], writes=[CB])
        kb.dma(eexp[:], dc["eexp"][:], writes=[CB])
        kb.dma(impm[:], dc["impm"].rearrange("(ct p) b -> p ct b", p=128), writes=[CB])
        kb.dma(keept[:], dc["keept"][:], writes=[CB])
        kb.dma(addt[:], dc["addt"][:], writes=[CB])
        kb.op("pool", lambda e: e.memset(kcmp[:], 0.0), writes=[kcmp_b])
        kb.op("pool", lambda e: e.memset(vcmp[:], 1.0), writes=[vcmp_b])
        for g in range(2):
            kb.dma(kcmp[64:69, g, :], dc["kaugc"][:, :], writes=[kcmp_b])
        with ExitStack() as es:
            kcg = sb(nc, es, "kcg", [64, 4, T], BF16)
            kcg_b = Buf("kcg")
            for kind in range(2):
                for g in range(2):
                    kb.dma(kcg[:, kind * 2 + g, :], ds["KX"][kind, g * 64:(g + 1) * 64, :], reads=[dsb["KX"]], writes=[kcg_b])
            w1s = sb(nc, es, "w1s", [64, 32, 256], F32)
            w1s_b = Buf("w1s")
            w1 = [sb(nc, es, "w1_%d" % k, [64, 32, 256], BF16) for k in range(2)]
            w2s = sb(nc, es, "w2s", [128, 2, 64], F32)
            w2s_b = Buf("w2s")
            w2 = [sb(nc, es, "w2_%d" % k, [128, 2, 64], BF16) for k in range(2)]
            posS = sb(nc, es, "posS", [32, 64], F32)
            posS_b = Buf("posS")
            posB = sb(nc, es, "posB", [32, 64], BF16)
            posB_b = Buf("posB")
            posT = sb(nc, es, "posT", [64, 2, 32], BF16)
            b1 = sb(nc, es, "b1", [128, 4], F32)
            Wc = Buf("wc")
            tpp = ps(nc, es, "tpp", [64, 32], BF16)
            tpp_b = Buf("tpp")
            pb1 = ps(nc, es, "pb1", [128, 4], F32)
            pb1_b = Buf("pb1")
            PSC = rot_tiles(nc, es, "psc", [128, 512], F32, 4, psum=True)
            XG = rot_tiles(nc, es, "xg", [128, 512], F32, 2)
            X2 = rot_tiles(nc, es, "x2", [128, 512], F32, 2)
            GL = rot_tiles(nc, es, "gl", [128, 512], BF16, 4)
            for gl, gl_b in GL.items:
                kb.op("pool", lambda e: e.memset(gl[:], 0.0), writes=[gl_b])
            names = [("w_ck1", "w_ck2", "pos_ck"), ("w_cv1", "w_cv2", "pos_cv")]
            for kind, (n1, n2, npos) in enumerate(names):
                kb.dma(w1s[:], din[n1].rearrange("(l d) n -> d l n", d=64), writes=[w1s_b])
                kb.op("dve", lambda e: e.tensor_copy(out=w1[kind][:, 0:16, :], in_=w1s[:, 0:16, :]), reads=[w1s_b], writes=[Wc])
                kb.op("pool", lambda e: e.tensor_copy(out=w1[kind][:, 16:32, :], in_=w1s[:, 16:32, :]), reads=[w1s_b], writes=[Wc])
                kb.dma(w2s[:], din[n2].rearrange("(c p) n -> p c n", p=128), writes=[w2s_b])
                kb.op("dve", lambda e: e.tensor_copy(out=w2[kind][:], in_=w2s[:]), reads=[w2s_b], writes=[Wc])
                kb.dma(posS[:], din[npos][:, :], writes=[posS_b])
                kb.op("dve", lambda e: e.tensor_copy(out=posB[:], in_=posS[:]), reads=[posS_b], writes=[posB_b])
                kb.op("pe", lambda e: e.transpose(out=tpp[:, :], in_=posB[:, :], identity=ident[0:32, 0:32]),
                      reads=[posB_b, ident_b], writes=[tpp_b])
                kb.op("dve", lambda e: e.tensor_copy(out=posT[:, kind, :], in_=tpp[:, :]), reads=[tpp_b], writes=[Wc])
            for kind in range(2):
                for hc in range(2):
                    idx = kind * 2 + hc
                    for l in range(32):
                        kb.op("pe", lambda e: e.matmul(pb1[:, idx:idx + 1], lhsT=w1[kind][:, l, hc * 128:(hc + 1) * 128],
                                                       rhs=posT[:, kind, l:l + 1], start=(l == 0), stop=(l == 31)),
                              reads=[Wc], writes=[pb1_b], inc=(l == 31))
            kb.op("dve", lambda e: e.tensor_copy(out=b1[:], in_=pb1[:]), reads=[pb1_b], writes=[Wc])
            for kind in range(2):
                for g in range(2):
                    gls = []
                    for hc in range(2):
                        idx = kind * 2 + hc
                        pm, pm_b = PSC.next()
                        for l in range(32):
                            kb.op("pe", lambda e: e.matmul(pm[:, 0:511], lhsT=w1[kind][:, l, hc * 128:(hc + 1) * 128],
                                                           rhs=kcg[:, kind * 2 + g, l:l + 8161:16], start=(l == 0), stop=(l == 31)),
                                  reads=[Wc, kcg_b], writes=[pm_b], inc=(l == 31))
                        xg, xg_b = XG.next()
                        x2, x2_b = X2.next()
                        gl, gl_b = GL.next()
                        kb.op("dve", lambda e: e.tensor_scalar(out=xg[:, 0:511], in0=pm[:, 0:511], scalar1=b1[:, idx:idx + 1],
                                                               scalar2=None, op0=ALU.add), reads=[pm_b, Wc], writes=[xg_b])
                        kb.op("pool", lambda e: e.tensor_tensor(out=x2[:, 0:511], in0=xg[:, 0:511], in1=xg[:, 0:511], op=ALU.mult),
                              reads=[xg_b], writes=[x2_b])
                        kb.op("dve", lambda e: e.tensor_scalar(out=x2[:, 0:511], in0=x2[:, 0:511], scalar1=0.044715, scalar2=1.0,
                                                               op0=ALU.mult, op1=ALU.add), reads=[x2_b], writes=[x2_b])
                        kb.op("pool", lambda e: e.tensor_tensor(out=x2[:, 0:511], in0=x2[:, 0:511], in1=xg[:, 0:511], op=ALU.mult),
                              reads=[x2_b, xg_b], writes=[x2_b])
                        kb.op("act", lambda e: e.activation(out=x2[:, 0:511], in_=x2[:, 0:511], func=AF.Sigmoid, scale=1.5957691216),
                              reads=[x2_b], writes=[x2_b])
                        kb.op("dve", lambda e: e.tensor_tensor(out=gl[:, 0:511], in0=x2[:, 0:511], in1=xg[:, 0:511], op=ALU.mult),
                              reads=[x2_b, xg_b], writes=[gl_b])
                        gls.append((gl, gl_b))
                    if kind == 0:
                        pk, pk_b = PSC.next()
                        for hc in range(2):
                            kb.op("pe", lambda e: e.matmul(pk[0:64, 0:511], lhsT=w2[0][:, hc, :], rhs=gls[hc][0][:, 0:511],
                                                           start=(hc == 0), stop=(hc == 1)),
                                  reads=[Wc, gls[hc][1]], writes=[pk_b], inc=(hc == 1))
                        kb.op("act", lambda e: e.activation(out=kcmp[0:64, g, 0:511], in_=pk[0:64, 0:511], func=AF.Copy),
                              reads=[pk_b], writes=[kcmp_b])
                    else:
                        for ct in range(4):
                            pv, pv_b = PSC.next()
                            for hc in range(2):
                                kb.op("pe", lambda e: e.matmul(pv[:, 0:64], lhsT=gls[hc][0][:, ct * 128:(ct + 1) * 128], rhs=w2[1][:, hc, :],
                                                               start=(hc == 0), stop=(hc == 1)),
                                      reads=[Wc, gls[hc][1]], writes=[pv_b], inc=(hc == 1))
                            kb.op("act", lambda e: e.activation(out=vcmp[:, ct, g, 0:64], in_=pv[:, 0:64], func=AF.Copy),
                                  reads=[pv_b], writes=[vcmp_b])
        kb.barrier()
        if getattr(c, "stop_mid", None) == "compress":
            dbg = c.dbg
            kb.dma(dbg["kcmp"][:], kcmp[:], reads=[kcmp_b])
            kb.dma(dbg["vcmp"][:], vcmp[:], reads=[vcmp_b])
            return
        with ExitStack() as es:
            R = Ctx()
            R.SPS = rot_tiles(nc, es, "sps", [128, 512], F32, 3, psum=True)
            R.OPS = rot_tiles(nc, es, "ops", [128, 512], F32, 2, psum=True)
            R.UPS = rot_tiles(nc, es, "ups", [128, 4, 128], F32, 2, psum=True)
            R.TPN = rot_tiles(nc, es, "tpn", [128, 4, 128], BF16, 1, psum=True)
            R.PSB = rot_tiles(nc, es, "psb", [128, 512], BF16, 4)
            KA = rot_tiles(nc, es, "ka", [96, T], BF16, 2).items
            VA = rot_tiles(nc, es, "va", [128, 64, 128], BF16, 2).items
            for va, va_b in VA:
                kb.op("pool", lambda e: e.memset(va[:], 1.0), writes=[va_b])
            QA = [[(sb(nc, es, "qa%d%d" % (pr, hi), [69, 512], BF16), Buf("qa")) for hi in range(4)] for pr in range(2)]
            OC = [[(sb(nc, es, "oc%d%d" % (pr, hi), [64, 512], F32), Buf("oc")) for hi in range(4)] for pr in range(2)]
            QMT = rot_tiles(nc, es, "qmt", [96, 512], BF16, 3)
            GB = rot_tiles(nc, es, "gb", [64, 3, 512], F32, 4)
            NMT = rot_tiles(nc, es, "nmt", [128, 512], BF16, 2)
            IMP = rot_tiles(nc, es, "imp", [128, 4, 128], F32, 2)
            DEN = rot_tiles(nc, es, "den", [128, 256], F32, 2)
            REC = rot_tiles(nc, es, "rec", [128, 256], F32, 2)
            RG = rot_tiles(nc, es, "rg", [64, 512], F32, 2)
            T1 = rot_tiles(nc, es, "t1", [64, 512], F32, 4)
            ACC = rot_tiles(nc, es, "acc", [64, 512], F32, 2)
            OST = rot_tiles(nc, es, "ost", [64, 512], BF16, 3)
            W1 = rot_tiles(nc, es, "w1k", [128, 128], F32, 2)
            W2 = rot_tiles(nc, es, "w2k", [128, 128], F32, 2)
            M8 = rot_tiles(nc, es, "m8", [128, 16], F32, 2)
            NMS = rot_tiles(nc, es, "nms", [128, 128], BF16, 4)
            RS = rot_tiles(nc, es, "rs", [128, 8], F32, 2)
            Gt = ds["G"].tensor
            c.stream = Stream(c, R)

            def norm_rec(O, O_b):
                den, den_b = DEN.next()
                rec, rec_b = REC.next()
                kb.op("dve", lambda e: e.tensor_scalar_max(out=den[0:64, :], in0=O[64:128, 0:256], scalar1=1e-30), reads=[O_b], writes=[den_b])
                kb.op("dve", lambda e: e.tensor_scalar_max(out=den[64:128, :], in0=O[64:128, 256:512], scalar1=1e-30), reads=[O_b], writes=[den_b])
                kb.op("dve", lambda e: e.reciprocal(out=rec[:, :], in_=den[:, :]), reads=[den_b], writes=[rec_b])
                return rec, rec_b

            def mul_rec(eng, out_t, in_t, rec, reads, writes):
                kb.op(eng, lambda e: e.tensor_tensor(out=out_t[0:64, 0:256], in0=in_t[0:64, 0:256], in1=rec[0:64, :], op=ALU.mult), reads=reads, writes=writes)
                kb.op(eng, lambda e: e.tensor_tensor(out=out_t[0:64, 256:512], in0=in_t[0:64, 256:512], in1=rec[64:128, :], op=ALU.mult), reads=reads, writes=writes)

            for g in range(2):
                (ks, ks_b), (kw, kw_b) = KA
                (vs, vs_b), (vw, vw_b) = VA
                kb.dma(ks[0:64, :], ds["KX"][2, g * 64:(g + 1) * 64, :], reads=[dsb["KX"]], writes=[ks_b])
                kb.dma(ks[64:69, :], dc["kaug"][:, :], writes=[ks_b])
                kb.dma(kw[0:64, :], ds["KX"][3, g * 64:(g + 1) * 64, :], reads=[dsb["KX"]], writes=[kw_b])
                kb.dma(kw[64:69, :], dc["kaug"][:, :], writes=[kw_b])
                for half in range(2):
                    hs = slice(half * 32, (half + 1) * 32)
                    rs_ = slice(half * 4096, (half + 1) * 4096)
                    kb.dma(vs[:, hs, 0:64], ds["VSW"][rs_, g * 64:(g + 1) * 64].rearrange("(kt p) d -> p kt d", p=128),
                           reads=[dsb["VSW"]], writes=[vs_b])
                    kb.dma(vw[:, hs, 0:64], ds["VSW"][rs_, 128 + g * 64:128 + (g + 1) * 64].rearrange("(kt p) d -> p kt d", p=128),
                           reads=[dsb["VSW"]], writes=[vw_b])
                CS = {}

                def cmp_begin(qc):
                    CS[qc] = {"imp": IMP.next(), "nmt": NMT.next(), "nms": []}

                def cmp_head(qc, hi, g=g):
                    t0 = qc * 512
                    pr = qc % 2
                    h = 4 * g + hi
                    imp, imp_b = CS[qc]["imp"]
                    qa, qa_b = QA[pr][hi]
                    kb.dma(qa[0:64, :], ds["QN"][h * 64:(h + 1) * 64, t0:t0 + 512], reads=[dsb["QN"]], writes=[qa_b])
                    kb.dma(qa[64:69, :], dc["qaug"][h, :, t0:t0 + 512], writes=[qa_b])
                    O, O_b = R.OPS.next()
                    U, U_b = R.UPS.next()
                    nct = qc // 4 + 1
                    ct0 = max(0, int((t0 - 2063 - ALIBI_CUT * (2.0 ** (h + 1))) // 2048) + 1)
                    for ct in range(ct0, nct):
                        m = qc - 4 * ct
                        qk = [(kcmp[0:69, g, ct * 128:(ct + 1) * 128], qa[0:69, :], [kcmp_b, qa_b])]
                        if m <= 4:
                            qk.append((ident[:, :], pmk[:, m, :], [ident_b, CB]))

                        def pv(P, P_b, ct=ct, nct=nct, ct0=ct0, O=O, O_b=O_b, U=U, U_b=U_b):
                            kb.op("pe", lambda e: e.matmul(O[:, :], lhsT=vcmp[:, ct, g, :], rhs=P[:, :], start=(ct == ct0), stop=(ct == nct - 1)),
                                  reads=[vcmp_b, P_b], writes=[O_b], inc=True)
                            for s_ in range(4):
                                kb.op("pe", lambda e: e.matmul(U[:, s_, :], lhsT=P[:, s_ * 128:(s_ + 1) * 128], rhs=impm[:, ct, :],
                                                               start=(ct == ct0), stop=(ct == nct - 1)),
                                      reads=[CB, P_b], writes=[U_b], inc=(s_ == 3))

                        def epi(hi=hi, pr=pr, O=O, O_b=O_b, U=U, U_b=U_b, imp=imp, imp_b=imp_b):
                            rec, rec_b = norm_rec(O, O_b)
                            oc, oc_b = OC[pr][hi]
                            mul_rec("dve", oc, O, rec, [O_b, rec_b], [oc_b])
                            rs, rs_b = RS.next()
                            kb.op("dve", lambda e: e.tensor_reduce(out=rs[:, 0:4], in_=U[:, :, :], axis=AX.X, op=ALU.add),
                                  reads=[U_b], writes=[rs_b])
                            kb.op("dve", lambda e: e.tensor_scalar(out=rs[:, 0:4], in0=rs[:, 0:4], scalar1=1e-30, scalar2=0.5,
                                                                   op0=ALU.max, op1=ALU.mult), reads=[rs_b], writes=[rs_b])
                            kb.op("dve", lambda e: e.reciprocal(out=rs[:, 4:8], in_=rs[:, 0:4]), reads=[rs_b], writes=[rs_b])
                            for s_ in range(4):
                                if hi == 0:
                                    kb.op("dve", lambda e: e.tensor_scalar(out=imp[:, s_, :], in0=U[:, s_, :], scalar1=rs[:, 4 + s_:5 + s_],
                                                                           scalar2=None, op0=ALU.mult), reads=[U_b, rs_b], writes=[imp_b])
                                else:
                                    kb.op("dve", lambda e: e.scalar_tensor_tensor(out=imp[:, s_, :], in0=U[:, s_, :], scalar=rs[:, 4 + s_:5 + s_],
                                                                                  in1=imp[:, s_, :], op0=ALU.mult, op1=ALU.add),
                                          reads=[U_b, rs_b, imp_b], writes=[imp_b])

                        c.stream.push(TileD(qk, 1.0, pv, epi if ct == nct - 1 else None))

                def topk_dve(qc):
                    imp, imp_b = CS[qc]["imp"]
                    for s_ in range(4):
                        qt = 4 * qc + s_
                        lo = 128 - 2 * qt
                        w, w_b = W1.next()
                        w2, w2_b = W2.next()
                        m8, m8_b = M8.next()
                        nms, nms_b = NMS.next()
                        kb.op("pool", lambda e: e.tensor_tensor(out=w[:, :], in0=imp[:, s_, :], in1=keept[:, lo:lo + 128], op=ALU.mult),
                              reads=[imp_b, CB], writes=[w_b])
                        kb.op("pool", lambda e: e.tensor_tensor(out=w[:, :], in0=w[:, :], in1=addt[:, lo:lo + 128], op=ALU.add),
                              reads=[w_b, CB], writes=[w_b])
                        kb.op("pool", lambda e: e.memset(w[:, 0:1], 1e6), reads=[w_b], writes=[w_b])
                        kb.op("dve", lambda e: e.max(out=m8[:, 0:8], in_=w[:, :]), reads=[w_b], writes=[m8_b])
                        kb.op("dve", lambda e: e.match_replace(out=w2[:, :], in_to_replace=m8[:, 0:8], in_values=w[:, :], imm_value=-2.0),
                              reads=[w_b, m8_b], writes=[w2_b])
                        kb.op("dve", lambda e: e.max(out=m8[:, 8:16], in_=w2[:, :]), reads=[w2_b, m8_b], writes=[m8_b])
                        kb.op("dve", lambda e: e.tensor_scalar(out=nms[:, :], in0=w[:, :], scalar1=m8[:, 15:16], scalar2=-1.0,
                                                               op0=ALU.is_ge, op1=ALU.add), reads=[w_b, m8_b], writes=[nms_b])
                        CS[qc]["nms"].append((nms, nms_b))

                def topk_pe(qc):
                    nmt, nmt_b = CS[qc]["nmt"]
                    tpn, tpn_b = R.TPN.next()
                    for s_ in range(4):
                        nms, nms_b = CS[qc]["nms"][s_]
                        kb.op("pe", lambda e: e.transpose(out=tpn[:, s_, :], in_=nms[:, :], identity=ident[:, :]),
                              reads=[nms_b, ident_b], writes=[tpn_b])
                    kb.op("act", lambda e: e.activation(out=nmt[:, :], in_=tpn[:, :, :].rearrange("p a b -> p (a b)"), func=AF.Copy),
                          reads=[tpn_b], writes=[nmt_b])

                def sw_head(qc, hi, g=g):
                    t0 = qc * 512
                    pr = qc % 2
                    h = 4 * g + hi
                    nmt, nmt_b = CS[qc]["nmt"]
                    qa, qa_b = QA[pr][hi]
                    gb, gb_b = GB.next()
                    kb.dma(gb[:, :, :], bass.AP(Gt, h * 3 * T + t0, [[0, 64], [T, 3], [1, 512]]), reads=[dsb["G"]], writes=[gb_b])
                    st = {}
                    O, O_b = R.OPS.next()
                    nk = 4 * qc + 4
                    k0 = max(0, int((t0 - 127 - ALIBI_CUT * (2.0 ** (h + 1))) // 128) + 1)
                    for kt in range(k0, nk):
                        d = kt - 4 * qc
                        lo = 128 * d if d > 0 else 0
                        qk = [(ks[0:69, kt * 128:(kt + 1) * 128], qa[0:69, lo:512], [ks_b, qa_b]),
                              (eexp[:, kt, :], nmt[:, lo:512], [CB, nmt_b])]
                        if d >= 0:
                            qk.append((ident[:, :], cmk[:, d, lo:lo + 128], [ident_b, CB], lo, lo + 128))

                        def pv(P, P_b, kt=kt, nk=nk, k0=k0, O=O, O_b=O_b, lo=lo):
                            kb.op("pe", lambda e: e.matmul(O[:, lo:512], lhsT=vs[:, kt, :], rhs=P[:, lo:512], start=(kt == k0), stop=(kt == nk - 1),
                                                           skip_group_check=True),
                                  reads=[vs_b, P_b], writes=[O_b], inc=True)

                        def epi(O=O, O_b=O_b, gb=gb, gb_b=gb_b, st=st):
                            rec, rec_b = norm_rec(O, O_b)
                            t1, t1_b = T1.next()
                            mul_rec("dve", t1, O, rec, [O_b, rec_b], [t1_b])
                            kb.op("pool", lambda e: e.tensor_tensor(out=t1[:, :], in0=t1[:, :], in1=gb[:, 1, :], op=ALU.mult),
                                  reads=[t1_b, gb_b], writes=[t1_b])
                            st["slc"] = (t1, t1_b)

                        c.stream.push(TileD(qk, 1.0, pv, epi if kt == nk - 1 else None, lo=lo, hi=512))
                    O, O_b = R.OPS.next()
                    wl = []
                    for d in range(4):
                        wl.append((4 * qc + d, cmk[:, d, 128 * d:128 * d + 128], 128 * d, 512, 128 * d))
                    for d in range(4):
                        kt = 4 * qc - 4 + d
                        if kt >= 0:
                            wl.append((kt, wmk[:, d, 128 * d:128 * d + 128], 0, 128 * d + 128, 128 * d))
                    nw = len(wl)
                    for wi, (kt, mk, wlo, whi, mlo) in enumerate(wl):
                        qk = [(kw[0:69, kt * 128:(kt + 1) * 128], qa[0:69, wlo:whi], [kw_b, qa_b]),
                              (ident[:, :], mk, [ident_b, CB], mlo, mlo + 128)]

                        def pv(P, P_b, kt=kt, wi=wi, nw=nw, O=O, O_b=O_b, wlo=wlo, whi=whi):
                            kb.op("pe", lambda e: e.matmul(O[:, wlo:whi], lhsT=vw[:, kt, :], rhs=P[:, wlo:whi], start=(wi == 0), stop=(wi == nw - 1),
                                                           skip_group_check=True),
                                  reads=[vw_b, P_b], writes=[O_b], inc=True)

                        def epi(h=h, hi=hi, pr=pr, t0=t0, O=O, O_b=O_b, gb=gb, gb_b=gb_b, st=st):
                            rec, rec_b = norm_rec(O, O_b)
                            t2, t2_b = T1.next()
                            mul_rec("dve", t2, O, rec, [O_b, rec_b], [t2_b])
                            kb.op("pool", lambda e: e.tensor_tensor(out=t2[:, :], in0=t2[:, :], in1=gb[:, 2, :], op=ALU.mult),
                                  reads=[t2_b, gb_b], writes=[t2_b])
                            t1, t1_b = st["slc"]
                            oc, oc_b = OC[pr][hi]
                            acc, acc_b = ACC.next()
                            ost, ost_b = OST.next()
                            kb.op("pool", lambda e: e.tensor_tensor(out=acc[:, :], in0=oc[:, :], in1=gb[:, 0, :], op=ALU.mult),
                                  reads=[oc_b, gb_b], writes=[acc_b])
                            kb.op("pool", lambda e: e.tensor_tensor(out=acc[:, :], in0=acc[:, :], in1=t1[:, :], op=ALU.add),
                                  reads=[acc_b, t1_b], writes=[acc_b])
                            kb.op("pool", lambda e: e.tensor_tensor(out=ost[:, :], in0=acc[:, :], in1=t2[:, :], op=ALU.add),
                                  reads=[acc_b, t2_b], writes=[ost_b])
                            kb.dma(ds["ON"][h * 64:(h + 1) * 64, t0:t0 + 512], ost[:, :], reads=[ost_b], writes=[dsb["ON"]], q="pool")

                        c.stream.push(TileD(qk, 1.0, pv, epi if wi == nw - 1 else None, lo=wlo, hi=whi))

                cmp_begin(0)
                for hi in range(4):
                    cmp_head(0, hi)
                c.stream.flush()
                topk_dve(0)
                topk_pe(0)
                for qc in range(NCH):
                    nxt = qc + 1 < NCH
                    if nxt:
                        cmp_begin(qc + 1)
                        cmp_head(qc + 1, 0)
                        cmp_head(qc + 1, 1)
                    sw_head(qc, 0)
                    if nxt:
                        cmp_head(qc + 1, 2)
                        cmp_head(qc + 1, 3)
                    sw_head(qc, 1)
                    sw_head(qc, 2)
                    if nxt:
                        topk_dve(qc + 1)
                    sw_head(qc, 3)
                    c.stream.flush()
                    if nxt:
                        topk_pe(qc + 1)
                    CS.pop(qc)
            c.stream.flush()
            if getattr(c, "stop_mid", None) == "nsa":
                return
            wst = sb(nc, es, "wst", [128, 1408], F32)
            wst_b = Buf("wst")
            wsb = sb(nc, es, "wsb", [128, 1408], BF16)
            wsb_b = Buf("wsb")
            wpieces = [(ft, a, hf) for ft in range(8) for a in range(2) for hf in range(2)]

            def wup_piece(ft, a, hf):
                c0 = a * DFF + hf * 1408
                kb.dma(wst[:, :], din["w_up"][ft * 128:(ft + 1) * 128, c0:c0 + 1408], writes=[wst_b])
                kb.op("pool", lambda e: e.tensor_copy(out=wsb[:, :], in_=wst[:, :]), reads=[wst_b], writes=[wsb_b])
                kb.dma(ds["WUP"][hf * 11:(hf + 1) * 11, :, a, ft, :].rearrange("j p c -> p j c"),
                       wsb[:, :].rearrange("p (j c) -> p j c", c=128), reads=[wsb_b], writes=[dsb["WUP"]], q="pool")

            for h in range(8):
                ka, ka_b = KA[h % 2]
                va, va_b = VA[h % 2]
                kb.dma(ka[0:64, :], ds["KN"][h * 64:(h + 1) * 64, :], reads=[dsb["KN"]], writes=[ka_b])
                kb.dma(ka[64:96, :], ds["KPE"][:, :], reads=[dsb["KPE"]], writes=[ka_b])
                for half in range(2):
                    hs = slice(half * 32, (half + 1) * 32)
                    rs_ = slice(half * 4096, (half + 1) * 4096)
                    kb.dma(va[:, hs, 0:64], ds["VM"][rs_, h * 64:(h + 1) * 64].rearrange("(kt p) d -> p kt d", p=128),
                           reads=[dsb["VM"]], writes=[va_b])
                qms = {}

                def qload(qc, h=h, qms=qms):
                    qm, qm_b = QMT.next()
                    kb.dma(qm[:, :], ds["QM"][h, :, qc * 512:(qc + 1) * 512], reads=[dsb["QM"]], writes=[qm_b])
                    qms[qc] = (qm, qm_b)

                qload(0)
                for qc in range(NCH):
                    t0 = qc * 512
                    if qc + 1 < NCH:
                        qload(qc + 1)
                    qm, qm_b = qms.pop(qc)
                    O, O_b = R.OPS.next()
                    nk = 4 * qc + 4
                    for kt in range(nk):
                        d = kt - 4 * qc
                        lo = 128 * d if d > 0 else 0
                        qk = [(ka[0:96, kt * 128:(kt + 1) * 128], qm[0:96, lo:512], [ka_b, qm_b])]
                        if d >= 0:
                            qk.append((ident[:, :], cmk[:, d, lo:lo + 128], [ident_b, CB], lo, lo + 128))

                        def pv(P, P_b, kt=kt, nk=nk, O=O, O_b=O_b, va=va, va_b=va_b, lo=lo):
                            kb.op("pe", lambda e: e.matmul(O[:, lo:512], lhsT=va[:, kt, :], rhs=P[:, lo:512], start=(kt == 0), stop=(kt == nk - 1),
                                                           skip_group_check=True),
                                  reads=[va_b, P_b], writes=[O_b], inc=True)

                        def epi(h=h, t0=t0, O=O, O_b=O_b):
                            rec, rec_b = norm_rec(O, O_b)
                            ost, ost_b = OST.next()
                            mul_rec("dve", ost, O, rec, [O_b, rec_b], [ost_b])
                            kb.dma(ds["OM"][h * 64:(h + 1) * 64, t0:t0 + 512], ost[:, :], reads=[ost_b], writes=[dsb["OM"]], q="pool")

                        c.stream.push(TileD(qk, MLA_SCALE, pv, epi if kt == nk - 1 else None, lo=lo, hi=512))
                    if wpieces:
                        wup_piece(*wpieces.pop(0))
            c.stream.flush()

CH3 = 256
NCH3 = T // CH3


def phase3(c):
    nc, kb = c.nc, c.kb
    din, dc, ds, dsb = c.din, c.dc, c.ds, c.dsb
    ident, ident_b = c.ident, c.ident_b
    with ExitStack() as es:
        W = Buf("p3w")
        wo = sb(nc, es, "wo", [128, 8, 1024], BF16)
        wpg = sb(nc, es, "wpg", [128, 8, 1024], BF16)
        wpl = sb(nc, es, "wpl", [128, 2, 1024], BF16)
        wd = sb(nc, es, "wd", [128, 22, 1024], BF16)
        lnv = sb(nc, es, "lnv", [128, 6, 1024], F32)
        for i, nm in enumerate(("ln1_g", "ln1_b", "ln2_g", "ln2_b", "ln3_g", "ln3_b")):
            kb.dma(lnv[:, i, :], din[nm][0:1, :].partition_broadcast(128), writes=[W])
        with ExitStack() as es1:
            STG = rot_tiles(nc, es1, "stg3", [128, 1024], F32, 4)
            k = 0
            for name, dst in (("w_out", wo), ("w_ple_gate", wpg)):
                for ft in range(8):
                    s_, s_b = STG.next()
                    kb.dma(s_[:, 0:1024], din[name][ft * 128:(ft + 1) * 128, :], writes=[s_b])
                    kb.op("dve" if k % 2 else "pool", lambda e: e.tensor_copy(out=dst[:, ft, :], in_=s_[:, 0:1024]), reads=[s_b], writes=[W])
                    k += 1
            for kt in range(2):
                s_, s_b = STG.next()
                kb.dma(s_[:, 0:1024], din["w_ple"][kt * 128:(kt + 1) * 128, :], writes=[s_b])
                kb.op("dve", lambda e: e.tensor_copy(out=wpl[:, kt, :], in_=s_[:, 0:1024]), reads=[s_b], writes=[W])
            for j in range(22):
                s_, s_b = STG.next()
                kb.dma(s_[:, 0:1024], din["w_down"][j * 128:(j + 1) * 128, :], writes=[s_b])
                kb.op("dve" if j % 2 else "pool", lambda e: e.tensor_copy(out=wd[:, j, :], in_=s_[:, 0:1024]), reads=[s_b], writes=[W])
        kb.barrier()
        XF = rot_tiles(nc, es, "xf3", [128, 1024], F32, 1)
        OT = rot_tiles(nc, es, "oT3", [128, 8, CH3], BF16, 1)
        PF = rot_tiles(nc, es, "pf3", [128, 2, 256], F32, 1)
        PB = rot_tiles(nc, es, "pb3", [128, 2, 256], BF16, 3)
        PT = rot_tiles(nc, es, "pT3", [128, 2, CH3], BF16, 1)
        Y = rot_tiles(nc, es, "y3", [128, 1024], F32, 2)
        Z = rot_tiles(nc, es, "z3", [128, 1024], F32, 1)
        X1 = rot_tiles(nc, es, "x1r", [128, 2, 1024], F32, 2)
        X2 = rot_tiles(nc, es, "x2r", [128, 2, 1024], F32, 2)
        XBF = rot_tiles(nc, es, "xbf3", [128, 1024], BF16, 2)
        X1T = rot_tiles(nc, es, "x1T", [128, 8, CH3], BF16, 2)
        X2T = rot_tiles(nc, es, "x2T", [128, 8, CH3], BF16, 2)
        AT = rot_tiles(nc, es, "aT", [128, 22, CH3], BF16, 1)
        WU = rot_tiles(nc, es, "wu", [128, 2, 8, 128], BF16, 3)
        SG = rot_tiles(nc, es, "sg", [128, CH3], F32, 2)
        ST = rot_tiles(nc, es, "st3", [128, 16], F32, 6)
        PA = rot_tiles(nc, es, "pa3", [128, 512], F32, 3, psum=True)
        PG = rot_tiles(nc, es, "pg3", [128, 2, CH3], F32, 4, psum=True)
        TP = rot_tiles(nc, es, "tp3", [128, 8, 128], BF16, 1, psum=True)

        epsv = sb(nc, es, "epsv", [128, 2], F32)
        kb.op("pool", lambda e: e.memset(epsv[:, 0:1], LN_EPS), writes=[W])
        kb.op("pool", lambda e: e.memset(epsv[:, 1:2], 4.0 * LN_EPS), writes=[W])

        def ln_pair_steps(items, st, st_b, gi, eps_col=0):
            def s1():
                for k, it_ in enumerate(items):
                    j_ap, j_b = it_.get("junk") or (it_["out_ap"], it_["out_b"])
                    y, y_b = it_["y"], it_["y_b"]
                    kb.op("act", lambda e: e.activation(out=j_ap, in_=y[:, :], func=AF.Square, accum_out=st[:, 8 * k + 2:8 * k + 3]),
                          reads=[y_b], writes=[j_b, st_b])

            def s2():
                for k in range(len(items)):
                    b = 8 * k
                    kb.op("dve", lambda e: e.tensor_scalar(out=st[:, b + 3:b + 4], in0=st[:, b:b + 1], scalar1=st[:, b + 1:b + 2], scalar2=1.0 / D,
                                                           op0=ALU.add, op1=ALU.mult), reads=[st_b], writes=[st_b])
                    kb.op("dve", lambda e: e.scalar_tensor_tensor(out=st[:, b + 4:b + 5], in0=st[:, b + 3:b + 4], scalar=-1.0, in1=st[:, b + 3:b + 4],
                                                                  op0=ALU.mult, op1=ALU.mult), reads=[st_b], writes=[st_b])
                    kb.op("dve", lambda e: e.scalar_tensor_tensor(out=st[:, b + 5:b + 6], in0=st[:, b + 2:b + 3], scalar=1.0 / D, in1=st[:, b + 4:b + 5],
                                                                  op0=ALU.mult, op1=ALU.add), reads=[st_b], writes=[st_b])

            def s3():
                n = len(items)
                src = st[:, 0:8 * n].rearrange("p (k c) -> p k c", c=8)[:, :, 5:6]
                dst = st[:, 0:8 * n].rearrange("p (k c) -> p k c", c=8)[:, :, 6:7]
                kb.op("act", lambda e: e.activation(out=dst, in_=src, func=AF.Sqrt, bias=epsv[:, eps_col:eps_col + 1]), reads=[st_b, W], writes=[st_b])
                src2 = st[:, 0:8 * n].rearrange("p (k c) -> p k c", c=8)[:, :, 6:7]
                dst2 = st[:, 0:8 * n].rearrange("p (k c) -> p k c", c=8)[:, :, 7:8]
                kb.op("dve", lambda e: e.reciprocal(out=dst2, in_=src2), reads=[st_b], writes=[st_b])

            def s4(k):
                def f():
                    it_ = items[k]
                    b = 8 * k
                    y, y_b = it_["y"], it_["y_b"]
                    z, z_b = Z.next()
                    kb.op("dve", lambda e: e.scalar_tensor_tensor(out=z[:, :], in0=y[:, :], scalar=st[:, b + 3:b + 4], in1=lnv[:, gi, :],
                                                                  op0=ALU.subtract, op1=ALU.mult), reads=[y_b, st_b, W], writes=[z_b])
                    kb.op("dve", lambda e: e.scalar_tensor_tensor(out=it_["out_ap"], in0=z[:, :], scalar=st[:, b + 7:b + 8], in1=lnv[:, gi + 1, :],
                                                                  op0=ALU.mult, op1=ALU.add), reads=[z_b, st_b, W], writes=[it_["out_b"]])
                    if it_.get("post") is not None:
                        it_["post"]()
                return f

            return [s1, s2, s3, s4(0), s4(1)]

        def cast_bf(src, src_b):
            xb, xb_b = XBF.next()
            kb.op("pool", lambda e: e.tensor_copy(out=xb[:, :], in_=src), reads=[src_b], writes=[xb_b])
            return xb, xb_b

        def transposes(xb, xb_b, s_, dstT, dstT_b):
            tp, tp_b = TP.next()
            for ft in range(8):
                kb.op("pe", lambda e: e.transpose(out=tp[:, ft, :], in_=xb[:, ft * 128:(ft + 1) * 128], identity=ident[:, :]),
                      reads=[xb_b, ident_b], writes=[tp_b], inc=(ft == 7))
            kb.op("act", lambda e: e.activation(out=dstT[:, :, s_ * 128:(s_ + 1) * 128], in_=tp[:, :, :], func=AF.Copy), reads=[tp_b], writes=[dstT_b])

        S = {}

        def A_mm(ch):
            t0 = ch * CH3
            st = S.setdefault(ch, {})
            oT, oT_b = OT.next()
            pf, pf_b = PF.next()
            kb.dma(oT[:, 0:4, :], ds["ON"].rearrange("(c p) t -> p c t", p=128)[:, :, t0:t0 + CH3], reads=[dsb["ON"]], writes=[oT_b])
            kb.dma(oT[:, 4:8, :], ds["OM"].rearrange("(c p) t -> p c t", p=128)[:, :, t0:t0 + CH3], reads=[dsb["OM"]], writes=[oT_b])
            kb.dma(pf[:, :, :], din["p"][t0:t0 + CH3, :].rearrange("(s p) d -> p s d", p=128), writes=[pf_b])
            pb_, pb_b = PB.next()
            kb.op("pool", lambda e: e.tensor_copy(out=pb_[:, :, :], in_=pf[:, :, :]), reads=[pf_b], writes=[pb_b])
            st["pb"] = (pb_, pb_b)
            st["x1"] = X1.next()
            st["x1T"] = X1T.next()
            st["ys"] = []
            sst, sst_b = ST.next()
            st["sst"] = (sst, sst_b)
            for s_ in range(2):
                y, y_b = Y.next()
                xf, xf_b = XF.next()
                kb.dma(xf[:, :], din["x"][t0 + s_ * 128:t0 + (s_ + 1) * 128, :], writes=[xf_b])
                for n in range(2):
                    pa, pa_b = PA.next()
                    for kt in range(8):
                        kb.op("pe", lambda e: e.matmul(pa[:, :], lhsT=oT[:, kt, s_ * 128:(s_ + 1) * 128], rhs=wo[:, kt, n * 512:(n + 1) * 512],
                                                       start=(kt == 0), stop=(kt == 7)), reads=[oT_b, W], writes=[pa_b], inc=(kt == 7))
                    kb.op("dve", lambda e: e.scalar_tensor_tensor(out=y[:, n * 512:(n + 1) * 512], in0=xf[:, n * 512:(n + 1) * 512], scalar=ALPHA,
                                                                  in1=pa[:, :], op0=ALU.mult, op1=ALU.add, accum_out=sst[:, 8 * s_ + n:8 * s_ + n + 1]),
                          reads=[xf_b, pa_b], writes=[y_b, sst_b])
                st["ys"].append((y, y_b))

        def A_ln(ch):
            st = S[ch]
            x1, x1_b = st["x1"]
            sst, sst_b = st["sst"]
            items = []
            for s_ in range(2):
                y, y_b = st["ys"][s_]

                def post(s_=s_):
                    st["xb%d" % s_] = cast_bf(x1[:, s_, :], x1_b)

                items.append(dict(y=y, y_b=y_b, out_ap=x1[:, s_, :], out_b=x1_b, post=post))
            return ln_pair_steps(items, sst, sst_b, 0)

        def A_tr(ch, s_):
            st = S[ch]
            x1T, x1T_b = st["x1T"]
            transposes(*st["xb%d" % s_], s_, x1T, x1T_b)

        def B_up(ch, j):
            st = S[ch]
            x1T, x1T_b = st["x1T"]
            if j == 0:
                st["aT"] = AT.next()
            aT, aT_b = st["aT"]
            wu, wu_b = WU.next()
            kb.dma(wu[:, :, :, :], ds["WUP"][j], reads=[dsb["WUP"]], writes=[wu_b])
            pg, pg_b = PG.next()
            for a in range(2):
                for ft in range(8):
                    kb.op("pe", lambda e: e.matmul(pg[:, a, :], lhsT=wu[:, a, ft, :], rhs=x1T[:, ft, :], start=(ft == 0), stop=(ft == 7)),
                          reads=[wu_b, x1T_b], writes=[pg_b], inc=(a == 1 and ft == 7))
            sg, sg_b = SG.next()
            kb.op("act", lambda e: e.activation(out=sg[:, :], in_=pg[:, 0, :], func=AF.Silu), reads=[pg_b], writes=[sg_b])
            kb.op("dve", lambda e: e.tensor_tensor(out=aT[:, j, :], in0=pg[:, 1, :], in1=sg[:, :], op=ALU.mult), reads=[pg_b, sg_b], writes=[aT_b])

        def B_down(ch):
            st = S[ch]
            aT, aT_b = st["aT"]
            x1, x1_b = st["x1"]
            st["x2"] = X2.next()
            st["x2T"] = X2T.next()
            x2, x2_b = st["x2"]
            ys = []
            sst, sst_b = ST.next()
            st["sst2"] = (sst, sst_b)
            for s_ in range(2):
                y, y_b = Y.next()
                for n in range(2):
                    pa, pa_b = PA.next()
                    for j in range(22):
                        kb.op("pe", lambda e: e.matmul(pa[:, :], lhsT=aT[:, j, s_ * 128:(s_ + 1) * 128], rhs=wd[:, j, n * 512:(n + 1) * 512],
                                                       start=(j == 0), stop=(j == 21)), reads=[aT_b, W], writes=[pa_b], inc=(j == 21))
                    kb.op("dve", lambda e: e.scalar_tensor_tensor(out=y[:, n * 512:(n + 1) * 512], in0=x1[:, s_, n * 512:(n + 1) * 512], scalar=ALPHA,
                                                                  in1=pa[:, :], op0=ALU.mult, op1=ALU.add, accum_out=sst[:, 8 * s_ + n:8 * s_ + n + 1]),
                          reads=[x1_b, pa_b], writes=[y_b, sst_b])
                ys.append((y, y_b))
            st["ys2"] = ys

        def B_ln(ch):
            st = S[ch]
            x2, x2_b = st["x2"]
            sst, sst_b = st["sst2"]
            items = []
            for s_ in range(2):
                y, y_b = st["ys2"][s_]

                def post(s_=s_):
                    st["x2b%d" % s_] = cast_bf(x2[:, s_, :], x2_b)

                items.append(dict(y=y, y_b=y_b, out_ap=x2[:, s_, :], out_b=x2_b, post=post))
            return ln_pair_steps(items, sst, sst_b, 2)

        def B_tr(ch, s_):
            st = S[ch]
            x2T, x2T_b = st["x2T"]
            transposes(*st["x2b%d" % s_], s_, x2T, x2T_b)

        def C_pre(ch):
            st = S[ch]
            pb_, pb_b = st["pb"]
            pT, pT_b = PT.next()
            tp, tp_b = TP.next()
            for s_ in range(2):
                for kt in range(2):
                    kb.op("pe", lambda e: e.transpose(out=tp[:, s_ * 2 + kt, :], in_=pb_[:, s_, kt * 128:(kt + 1) * 128], identity=ident[:, :]),
                          reads=[pb_b, ident_b], writes=[tp_b], inc=(s_ == 1 and kt == 1))
            for s_ in range(2):
                kb.op("act", lambda e: e.activation(out=pT[:, :, s_ * 128:(s_ + 1) * 128], in_=tp[:, 2 * s_:2 * s_ + 2, :], func=AF.Copy),
                      reads=[tp_b], writes=[pT_b])
            st["pT"] = (pT, pT_b)

        def C_mm(ch, s_):
            st = S[ch]
            x2, x2_b = st["x2"]
            x2T, x2T_b = st["x2T"]
            pT, pT_b = st["pT"]
            if s_ == 0:
                st["sst3"] = ST.next()
                st["ys3"] = []
            sst, sst_b = st["sst3"]
            y, y_b = Y.next()
            for n in range(2):
                pa, pa_b = PA.next()
                for kt in range(8):
                    kb.op("pe", lambda e: e.matmul(pa[:, :], lhsT=x2T[:, kt, s_ * 128:(s_ + 1) * 128], rhs=wpg[:, kt, n * 512:(n + 1) * 512],
                                                   start=(kt == 0), stop=(kt == 7)), reads=[x2T_b, W], writes=[pa_b], inc=(kt == 7))
                kb.op("act", lambda e: e.activation(out=y[:, n * 512:(n + 1) * 512], in_=pa[:, :], func=AF.Tanh, scale=0.5), reads=[pa_b], writes=[y_b])
                pp, pp_b = PA.next()
                for kt in range(2):
                    kb.op("pe", lambda e: e.matmul(pp[:, :], lhsT=pT[:, kt, s_ * 128:(s_ + 1) * 128], rhs=wpl[:, kt, n * 512:(n + 1) * 512],
                                                   start=(kt == 0), stop=(kt == 1)), reads=[pT_b, W], writes=[pp_b], inc=(kt == 1))
                kb.op("dve", lambda e: e.scalar_tensor_tensor(out=y[:, n * 512:(n + 1) * 512], in0=y[:, n * 512:(n + 1) * 512], scalar=1.0, in1=pp[:, :],
                                                              op0=ALU.add, op1=ALU.mult), reads=[pp_b, y_b], writes=[y_b])
                kb.op("dve", lambda e: e.scalar_tensor_tensor(out=y[:, n * 512:(n + 1) * 512], in0=x2[:, s_, n * 512:(n + 1) * 512], scalar=2.0 * ALPHA,
                                                              in1=y[:, n * 512:(n + 1) * 512], op0=ALU.mult, op1=ALU.add,
                                                              accum_out=sst[:, 8 * s_ + n:8 * s_ + n + 1]),
                      reads=[x2_b, y_b], writes=[y_b, sst_b])
            st["ys3"].append((y, y_b))

        def C_ln(ch):
            t0 = ch * CH3
            st = S[ch]
            sst, sst_b = st["sst3"]
            zj, zj_b = Z.items[0]
            items = []
            for s_ in range(2):
                y, y_b = st["ys3"][s_]

                def post(s_=s_, y=y, y_b=y_b):
                    kb.dma(c.out[t0 + s_ * 128:t0 + (s_ + 1) * 128, :], y[:, :], reads=[y_b], writes=[c.out_b], q="pool")
                    if s_ == 1:
                        S.pop(ch)

                items.append(dict(y=y, y_b=y_b, out_ap=y[:, :], out_b=y_b, post=post, junk=(zj[:, :], zj_b)))
            return ln_pair_steps(items, sst, sst_b, 4, eps_col=1)

        def run_all(steps):
            for f in steps:
                f()

        A_mm(0)
        run_all(A_ln(0))
        A_tr(0, 0)
        A_tr(0, 1)
        sched = {
            0: [("Bln", 0)], 1: [("Bln", 1)], 2: [("Bln", 2)], 3: [("Bln", 3)], 4: [("Bln", 4)], 5: [("Btr", 0)], 6: [("Btr", 1), ("Amm", 0)],
            7: [("Aln", 0)], 8: [("Aln", 1), ("Cpre", 0)], 9: [("Aln", 2)], 10: [("Aln", 3), ("Cmm", 0)], 11: [("Aln", 4)], 12: [("Cmm", 1)],
            13: [("Cln", 0), ("Atr", 0)], 14: [("Cln", 1)], 15: [("Cln", 2), ("Atr", 1)], 16: [("Cln", 3)], 17: [("Cln", 4)],
        }
        chains = {}

        def run_slot(it, j):
            nxt = it + 1 < NCH3
            prv = it >= 1
            for name, k in sched.get(j, []):
                if name == "Bln" and prv:
                    if k == 0:
                        chains["B"] = B_ln(it - 1)
                    chains["B"][k]()
                elif name == "Btr" and prv:
                    B_tr(it - 1, k)
                elif name == "Amm" and nxt:
                    A_mm(it + 1)
                elif name == "Aln" and nxt:
                    if k == 0:
                        chains["A"] = A_ln(it + 1)
                    chains["A"][k]()
                elif name == "Atr" and nxt:
                    A_tr(it + 1, k)
                elif name == "Cpre" and prv:
                    C_pre(it - 1)
                elif name == "Cmm" and prv:
                    C_mm(it - 1, k)
                elif name == "Cln" and prv:
                    if k == 0:
                        chains["C"] = C_ln(it - 1)
                    chains["C"][k]()

        for it in range(NCH3 + 1):
            if it < NCH3:
                for j in range(22):
                    B_up(it, j)
                    run_slot(it, j)
                B_down(it)
            else:
                for j in range(22):
                    run_slot(it, j)


_CACHE = {}


def kernel(**inputs):
    if "nc" not in _CACHE:
        _CACHE["nc"] = build_program()
        _CACHE["consts"] = host_constants()
    nc = _CACHE["nc"]
    consts = _CACHE["consts"]
    B = inputs["x"].shape[0]
    shared = {}
    for k, shp in IN_SPECS.items():
        if k in ("x", "p"):
            continue
        shared[k] = np.ascontiguousarray(np.asarray(inputs[k], dtype=np.float32)[0].reshape(shp))
    for k, v in consts.items():
        shared["c_" + k] = v
    in_maps = []
    for b in range(B):
        m = dict(shared)
        m["x"] = np.ascontiguousarray(np.asarray(inputs["x"], dtype=np.float32)[b])
        m["p"] = np.ascontiguousarray(np.asarray(inputs["p"], dtype=np.float32)[0, b])
        in_maps.append(m)
    res = run_bass_kernel_spmd(nc, in_maps, core_ids=list(range(B)))
    return np.stack([np.asarray(r["out"], dtype=np.float32) for r in res.results], 0)
```
